# Optimizing a Trainium2 kernel written in Bass

```python
import jax, jax.numpy as jnp
from jax import lax
import numpy as np

D_MODEL = 1024
BATCH = 16
SEQ = 256
DEPTH = 4
DEC_BATCH = 8
DEC_SEQ = 4096
PAST_LEN = 512

GRID_W = 64
EPS = 1e-6
ROPE_THETA = 10000.0
D_FF = 4 * D_MODEL
CONV_W = 3
Q_BLOCK = 128
N_EVEN = (DEPTH + 1) // 2
N_ODD = DEPTH // 2
GDN_HEADS = D_MODEL // 256
GDN_DK = 128
GDN_DV = 128
GDN_CHUNK = 64
GQA_HEADS = D_MODEL // 256
GQA_KV_HEADS = GQA_HEADS // 2
GQA_DH = 128
SSD_HEADS = D_MODEL // 128
SSD_P = 64
SSD_N = 128
SSD_GROUPS = 2
SSD_CHUNK = 64
SSD_D_INNER = SSD_HEADS * SSD_P
NA_HEADS = D_MODEL // 256
NA_DH = 128
NA_WIN_R = 8
NA_WIN_C = 16

EV_WIDTHS = (GDN_HEADS * GDN_DK, GDN_HEADS * GDN_DK, GDN_HEADS * GDN_DV, GDN_HEADS * GDN_DV,
             2 * GDN_HEADS, 2 * GDN_HEADS,
             GQA_HEADS * GQA_DH, GQA_KV_HEADS * GQA_DH, GQA_KV_HEADS * GQA_DH)
EV_IN = sum(EV_WIDTHS)
EV_MIX = GDN_HEADS * GDN_DV + GQA_HEADS * GQA_DH
OD_WIDTHS = (SSD_D_INNER, SSD_D_INNER, SSD_GROUPS * SSD_N, SSD_GROUPS * SSD_N, 2 * SSD_HEADS,
             NA_HEADS * NA_DH, NA_HEADS * NA_DH, NA_HEADS * NA_DH)
OD_IN = sum(OD_WIDTHS)
OD_MIX = SSD_D_INNER + NA_HEADS * NA_DH

kernel_name = 'hybrid_dit_gdn_gqa_ssd_natten_step'


def rmsnorm(x, w):
    xf = x.astype(jnp.float32)
    y = xf * lax.rsqrt(jnp.mean(xf * xf, axis=-1, keepdims=True) + EPS)
    return (y * w.astype(jnp.float32)).astype(x.dtype)


def l2norm(x):
    xf = x.astype(jnp.float32)
    return (xf * lax.rsqrt(jnp.sum(xf * xf, axis=-1, keepdims=True) + EPS)).astype(x.dtype)


def split_cols(t, widths):
    return jnp.split(t, [int(i) for i in np.cumsum(widths)[:-1]], axis=-1)


def depthwise_conv(x, w):
    pad = CONV_W // 2
    return lax.conv_general_dilated(x, w[:, None, :].astype(x.dtype), (1,), [(pad, pad)],
                                    dimension_numbers=('NWC', 'WIO', 'NWC'),
                                    feature_group_count=x.shape[-1])


def modulate(x, w_pre, shift, scale):
    return rmsnorm(x, w_pre) * (1 + scale) + shift


def rope_angles(n_tokens):
    half = GQA_DH // 2
    inv_freq = ROPE_THETA ** (-jnp.arange(0, half, 2, dtype=jnp.float32) / half)
    t = jnp.arange(n_tokens)
    row = (t // GRID_W).astype(jnp.float32)
    col = (t % GRID_W).astype(jnp.float32)
    return row[:, None] * inv_freq, col[:, None] * inv_freq


def rotate(x, ang):
    m = ang.shape[-1]
    cos = jnp.cos(ang)[None, :, None, :].astype(x.dtype)
    sin = jnp.sin(ang)[None, :, None, :].astype(x.dtype)
    x1, x2 = x[..., :m], x[..., m:]
    return jnp.concatenate([x1 * cos - x2 * sin, x2 * cos + x1 * sin], axis=-1)


def axial_rope(x, ang_r, ang_c):
    half = x.shape[-1] // 2
    return jnp.concatenate([rotate(x[..., :half], ang_r), rotate(x[..., half:], ang_c)], axis=-1)


def block_attention(q, k, v):
    bsz, seq_len, heads, dh = q.shape
    kv_heads = k.shape[2]
    rep = heads // kv_heads
    nb = seq_len // Q_BLOCK
    qb = q.reshape(bsz, nb, Q_BLOCK, kv_heads, rep, dh).transpose(1, 0, 2, 3, 4, 5)

    def one_block(q_blk):
        s = jnp.einsum('bqgrd,bkgd->bgrqk', q_blk, k, preferred_element_type=jnp.float32) * dh ** -0.5
        p = jax.nn.softmax(s, axis=-1).astype(v.dtype)
        return jnp.einsum('bgrqk,bkgd->bqgrd', p, v)

    o = lax.map(one_block, qb)
    return o.transpose(1, 0, 2, 3, 4, 5).reshape(bsz, seq_len, heads, dh)


def neighbourhood_attention(q, k, v, k_ctx, v_ctx, rpb):
    bsz, seq_len, heads, dh = q.shape
    rows = seq_len // GRID_W
    win_r = min(NA_WIN_R, rows)
    qg = q.reshape(bsz, rows, GRID_W, heads, dh)
    kg = k.reshape(bsz, rows, GRID_W, heads, dh)
    vg = v.reshape(bsz, rows, GRID_W, heads, dh)
    col = jnp.arange(GRID_W)
    col_start = jnp.clip(col - NA_WIN_C // 2, 0, GRID_W - NA_WIN_C)
    col_idx = col_start[:, None] + jnp.arange(NA_WIN_C)[None, :]
    col_off = col_idx - col[:, None] + (NA_WIN_C - 1)
    bias_col = rpb.astype(jnp.float32)[:, :, col_off]
    scale = dh ** -0.5

    def one_row(r):
        r0 = jnp.clip(r - win_r // 2, 0, rows - win_r)
        q_r = lax.dynamic_index_in_dim(qg, r, axis=1, keepdims=False)
        k_rows = lax.dynamic_slice_in_dim(kg, r0, win_r, axis=1)
        v_rows = lax.dynamic_slice_in_dim(vg, r0, win_r, axis=1)
        k_nb = k_rows[:, :, col_idx]
        v_nb = v_rows[:, :, col_idx]
        row_off = r0 + jnp.arange(win_r) - r + (NA_WIN_R - 1)
        bias = bias_col[:, row_off].transpose(0, 2, 1, 3)
        s_loc = jnp.einsum('bqhd,brqchd->bhqrc', q_r, k_nb, preferred_element_type=jnp.float32) * scale + bias[None]
        s_ctx = jnp.einsum('bqhd,bkhd->bhqk', q_r, k_ctx, preferred_element_type=jnp.float32) * scale
        n_loc = win_r * NA_WIN_C
        s = jnp.concatenate([s_loc.reshape(bsz, heads, GRID_W, n_loc), s_ctx], axis=-1)
        p = jax.nn.softmax(s, axis=-1).astype(v.dtype)
        p_loc = p[..., :n_loc].reshape(bsz, heads, GRID_W, win_r, NA_WIN_C)
        p_ctx = p[..., n_loc:]
        return jnp.einsum('bhqrc,brqchd->bqhd', p_loc, v_nb) + jnp.einsum('bhqk,bkhd->bqhd', p_ctx, v_ctx)

    o = lax.map(one_row, jnp.arange(rows))
    return o.transpose(1, 0, 2, 3, 4).reshape(bsz, seq_len, heads, dh)


def gated_delta_chunked(q, k, v, g, beta, s0):
    bsz, seq_len, heads, _ = q.shape
    dv = v.shape[-1]
    nc = seq_len // GDN_CHUNK
    f32 = jnp.float32

    def chunks(t):
        return t.astype(f32).reshape(bsz, nc, GDN_CHUNK, heads, -1).transpose(0, 1, 3, 2, 4)

    q, k, v = chunks(q), chunks(k), chunks(v)
    g = g.astype(f32).reshape(bsz, nc, GDN_CHUNK, heads).transpose(0, 1, 3, 2)
    beta = beta.astype(f32).reshape(bsz, nc, GDN_CHUNK, heads).transpose(0, 1, 3, 2)
    gc = jnp.cumsum(g, axis=-1)
    incl = jnp.tril(jnp.ones((GDN_CHUNK, GDN_CHUNK), dtype=bool))
    strict = jnp.tril(jnp.ones((GDN_CHUNK, GDN_CHUNK), dtype=bool), -1)
    decay = jnp.exp(jnp.where(incl, gc[..., :, None] - gc[..., None, :], -jnp.inf))
    kb = k * beta[..., None]
    a_mat = jnp.where(strict, jnp.einsum('bnhik,bnhjk->bnhij', kb, k) * decay, 0.0)
    t_mat = a_mat + jnp.eye(GDN_CHUNK, dtype=f32)
    rhs = jnp.concatenate([v * beta[..., None], kb * jnp.exp(gc)[..., None]], axis=-1)
    sol = lax.linalg.triangular_solve(t_mat, rhs, left_side=True, lower=True, unit_diagonal=True)
    u, w = sol[..., :dv], sol[..., dv:]
    qk = jnp.where(incl, jnp.einsum('bnhik,bnhjk->bnhij', q, k) * decay, 0.0)
    q_dec = q * jnp.exp(gc)[..., None]
    k_dec = k * jnp.exp(gc[..., -1:] - gc)[..., None]
    g_tot = jnp.exp(gc[..., -1])

    def step(s, xs):
        u_c, w_c, qk_c, qd_c, kd_c, gt_c = xs
        v_new = u_c - jnp.einsum('bhck,bhkv->bhcv', w_c, s)
        o = jnp.einsum('bhck,bhkv->bhcv', qd_c, s) + jnp.einsum('bhij,bhjv->bhiv', qk_c, v_new)
        s = s * gt_c[..., None, None] + jnp.einsum('bhck,bhcv->bhkv', kd_c, v_new)
        return s, o

    xs = tuple(jnp.moveaxis(t, 1, 0) for t in (u, w, qk, q_dec, k_dec, g_tot))
    s_fin, o = lax.scan(step, s0.astype(f32), xs)
    o = o.transpose(1, 0, 3, 2, 4).reshape(bsz, seq_len, heads, dv)
    return o, s_fin


def bidir_gated_delta(q, k, v, g, beta, s0):
    flip = lambda t: jnp.flip(t, axis=1)
    o_f, s_f = gated_delta_chunked(q, k, v, g[:, :, 0], beta[:, :, 0], s0[:, 0])
    o_b, s_b = gated_delta_chunked(flip(q), flip(k), flip(v), flip(g[:, :, 1]), flip(beta[:, :, 1]), s0[:, 1])
    return o_f + flip(o_b), jnp.stack([s_f, s_b], axis=1)


def ssd_chunked(x, dt, a, bm, cm, h0):
    bsz, seq_len, heads, p_dim = x.shape
    groups, n_dim = bm.shape[2], bm.shape[3]
    e = heads // groups
    nc = seq_len // SSD_CHUNK
    f32 = jnp.float32
    dtf = dt.astype(f32)
    xf = (x.astype(f32) * dtf[..., None]).reshape(bsz, nc, SSD_CHUNK, groups, e, p_dim)
    la = (dtf * a.astype(f32)).reshape(bsz, nc, SSD_CHUNK, groups, e)
    bc = bm.astype(f32).reshape(bsz, nc, SSD_CHUNK, groups, n_dim)
    cc = cm.astype(f32).reshape(bsz, nc, SSD_CHUNK, groups, n_dim)
    cs = jnp.cumsum(la, axis=2)
    incl = jnp.tril(jnp.ones((SSD_CHUNK, SSD_CHUNK), dtype=bool))[:, :, None, None]
    l_mat = jnp.exp(jnp.where(incl, cs[:, :, :, None] - cs[:, :, None, :], -jnp.inf))
    y_diag = jnp.einsum('bclgn,bcsgn,bclsge,bcsgep->bclgep', cc, bc, l_mat, xf)
    decay_states = jnp.exp(cs[:, :, -1:] - cs)
    states = jnp.einsum('bclgn,bclge,bclgep->bcgepn', bc, decay_states, xf)
    chunk_decay = jnp.exp(cs[:, :, -1])

    def step(h, xs):
        st, cd = xs
        return h * cd[..., None, None] + st, h

    h_fin, h_start = lax.scan(step, h0.astype(f32).reshape(bsz, groups, e, p_dim, n_dim),
                              (jnp.moveaxis(states, 1, 0), jnp.moveaxis(chunk_decay, 1, 0)))
    h_start = jnp.moveaxis(h_start, 0, 1)
    y_off = jnp.einsum('bclgn,bcgepn,bclge->bclgep', cc, h_start, jnp.exp(cs))
    y = (y_diag + y_off).reshape(bsz, seq_len, heads, p_dim)
    return y, h_fin.reshape(bsz, heads, p_dim, n_dim)


def bidir_ssd(x, dt, a, bm, cm, h0):
    flip = lambda t: jnp.flip(t, axis=1)
    y_f, h_f = ssd_chunked(x, dt[:, :, 0], a[0], bm, cm, h0[:, 0])
    y_b, h_b = ssd_chunked(flip(x), flip(dt[:, :, 1]), a[1], flip(bm), flip(cm), h0[:, 1])
    return y_f + flip(y_b), jnp.stack([h_f, h_b], axis=1)


def even_mixer(h, w_in, w_out, conv_w, a_log, dt_bias, gdn_norm, q_norm, k_norm, ang=None, ctx=None):
    bsz, seq_len, _ = h.shape
    qa, ka, va, za, ba, aa, qb, kb, vb = split_cols(h @ w_in, EV_WIDTHS)
    qkv = jax.nn.silu(depthwise_conv(jnp.concatenate([qa, ka, va], axis=-1), conv_w))
    qa, ka, va = split_cols(qkv, EV_WIDTHS[:3])
    qa = l2norm(qa.reshape(bsz, seq_len, GDN_HEADS, GDN_DK)) * GDN_DK ** -0.5
    ka = l2norm(ka.reshape(bsz, seq_len, GDN_HEADS, GDN_DK))
    va = va.reshape(bsz, seq_len, GDN_HEADS, GDN_DV)
    beta = jax.nn.sigmoid(ba.astype(jnp.float32)).reshape(bsz, seq_len, 2, GDN_HEADS)
    g = -jnp.exp(a_log.astype(jnp.float32)) * jax.nn.softplus(
        aa.astype(jnp.float32).reshape(bsz, seq_len, 2, GDN_HEADS) + dt_bias.astype(jnp.float32))
    s0 = jnp.zeros((bsz, 2, GDN_HEADS, GDN_DK, GDN_DV), jnp.float32) if ctx is None else ctx[0]
    o_a, s_a = bidir_gated_delta(qa, ka, va, g, beta, s0)
    o_a = rmsnorm(o_a.astype(h.dtype), gdn_norm) * jax.nn.silu(za.reshape(bsz, seq_len, GDN_HEADS, GDN_DV))
    qb = rmsnorm(qb.reshape(bsz, seq_len, GQA_HEADS, GQA_DH), q_norm)
    kb = rmsnorm(kb.reshape(bsz, seq_len, GQA_KV_HEADS, GQA_DH), k_norm)
    vb = vb.reshape(bsz, seq_len, GQA_KV_HEADS, GQA_DH)
    if ctx is None:
        o_b = block_attention(qb, kb, vb)
    else:
        o_b = block_attention(axial_rope(qb, *ang),
                              jnp.concatenate([axial_rope(kb, *ang), ctx[1].astype(kb.dtype)], axis=1),
                              jnp.concatenate([vb, ctx[2].astype(vb.dtype)], axis=1))
    y = jnp.concatenate([o_a.reshape(bsz, seq_len, -1), o_b.reshape(bsz, seq_len, -1)], axis=-1) @ w_out
    if ctx is None:
        return y, s_a, kb, vb
    return y


def odd_mixer(h, w_in, w_out, conv_w, conv_b, a_log, dt_bias, d_skip, ssd_norm, rpb, ctx=None):
    bsz, seq_len, _ = h.shape
    z, xc, bc, cc, dt_raw, qd, kd, vd = split_cols(h @ w_in, OD_WIDTHS)
    xbc = jax.nn.silu(depthwise_conv(jnp.concatenate([xc, bc, cc], axis=-1), conv_w) + conv_b)
    xc, bc, cc = split_cols(xbc, OD_WIDTHS[1:4])
    xh = xc.reshape(bsz, seq_len, SSD_HEADS, SSD_P)
    bm = bc.reshape(bsz, seq_len, SSD_GROUPS, SSD_N)
    cm = cc.reshape(bsz, seq_len, SSD_GROUPS, SSD_N)
    dt = jax.nn.softplus(dt_raw.astype(jnp.float32).reshape(bsz, seq_len, 2, SSD_HEADS) + dt_bias.astype(jnp.float32))
    a = -jnp.exp(a_log.astype(jnp.float32))
    h0 = jnp.zeros((bsz, 2, SSD_HEADS, SSD_P, SSD_N), jnp.float32) if ctx is None else ctx[0]
    y_c, s_c = bidir_ssd(xh, dt, a, bm, cm, h0)
    y_c = y_c + xh.astype(jnp.float32) * d_skip.astype(jnp.float32)[:, None]
    y_c = rmsnorm((y_c.reshape(bsz, seq_len, SSD_D_INNER) * jax.nn.silu(z.astype(jnp.float32))).astype(h.dtype), ssd_norm)
    qd = qd.reshape(bsz, seq_len, NA_HEADS, NA_DH)
    kd = kd.reshape(bsz, seq_len, NA_HEADS, NA_DH)
    vd = vd.reshape(bsz, seq_len, NA_HEADS, NA_DH)
    if ctx is None:
        o_d = block_attention(qd, kd, vd)
    else:
        o_d = neighbourhood_attention(qd, kd, vd, ctx[1].astype(kd.dtype), ctx[2].astype(vd.dtype), rpb)
    y = jnp.concatenate([y_c, o_d.reshape(bsz, seq_len, -1)], axis=-1) @ w_out
    if ctx is None:
        return y, s_c, kd, vd
    return y


def sq_relu_mlp(h, w1, w2):
    return jnp.square(jax.nn.relu(h @ w1)) @ w2


def setup_inputs(seed: int = 0) -> dict:
    key = jax.random.key(seed)
    ks = iter(jax.random.split(key, 48))
    nrm = lambda shape, s: jax.random.normal(next(ks), shape, jnp.float32) * s

    def a_log_init(shape):
        return jnp.log(jax.random.uniform(next(ks), shape, jnp.float32, 1.0, 16.0))

    def dt_bias_init(shape):
        dt = jnp.exp(jax.random.uniform(next(ks), shape, jnp.float32, np.log(1e-3), np.log(1e-1)))
        return dt + jnp.log(-jnp.expm1(-dt))

    return {
        'x_prompt': nrm((BATCH, SEQ, D_MODEL), 1.0),
        'x_sample': nrm((DEC_BATCH, DEC_SEQ, D_MODEL), 1.0),
        'state_gdn': nrm((DEC_BATCH, N_EVEN, 2, GDN_HEADS, GDN_DK, GDN_DV), 0.1),
        'cache_gqa_k': nrm((DEC_BATCH, N_EVEN, PAST_LEN, GQA_KV_HEADS, GQA_DH), 1.0),
        'cache_gqa_v': nrm((DEC_BATCH, N_EVEN, PAST_LEN, GQA_KV_HEADS, GQA_DH), 1.0),
        'state_ssd': nrm((DEC_BATCH, N_ODD, 2, SSD_HEADS, SSD_P, SSD_N), 0.1),
        'cache_na_k': nrm((DEC_BATCH, N_ODD, PAST_LEN, NA_HEADS, NA_DH), 1.0),
        'cache_na_v': nrm((DEC_BATCH, N_ODD, PAST_LEN, NA_HEADS, NA_DH), 1.0),
        'c': nrm((DEC_BATCH, D_MODEL), 1.0),
        'c_ctx': nrm((D_MODEL,), 1.0),
        'ada_w': nrm((DEPTH, D_MODEL, 6 * D_MODEL), D_MODEL ** -0.5),
        'ada_b': nrm((DEPTH, 6 * D_MODEL), 0.01),
        'norm_mix_pre': 1.0 + nrm((DEPTH, D_MODEL), 0.1),
        'norm_mix_post': 1.0 + nrm((DEPTH, D_MODEL), 0.1),
        'norm_mlp_pre': 1.0 + nrm((DEPTH, D_MODEL), 0.1),
        'norm_mlp_post': 1.0 + nrm((DEPTH, D_MODEL), 0.1),
        'mlp_w1': nrm((DEPTH, D_MODEL, D_FF), D_MODEL ** -0.5),
        'mlp_w2': nrm((DEPTH, D_FF, D_MODEL), D_FF ** -0.5),
        'ev_w_in': nrm((N_EVEN, D_MODEL, EV_IN), D_MODEL ** -0.5),
        'ev_w_out': nrm((N_EVEN, EV_MIX, D_MODEL), EV_MIX ** -0.5),
        'gdn_conv': nrm((N_EVEN, CONV_W, 2 * GDN_HEADS * GDN_DK + GDN_HEADS * GDN_DV), CONV_W ** -0.5),
        'gdn_a_log': a_log_init((N_EVEN, 2, GDN_HEADS)),
        'gdn_dt_bias': dt_bias_init((N_EVEN, 2, GDN_HEADS)),
        'gdn_norm': 1.0 + nrm((N_EVEN, GDN_DV), 0.1),
        'gqa_q_norm': 1.0 + nrm((N_EVEN, GQA_DH), 0.1),
        'gqa_k_norm': 1.0 + nrm((N_EVEN, GQA_DH), 0.1),
        'od_w_in': nrm((N_ODD, D_MODEL, OD_IN), D_MODEL ** -0.5),
        'od_w_out': nrm((N_ODD, OD_MIX, D_MODEL), OD_MIX ** -0.5),
        'ssd_conv': nrm((N_ODD, CONV_W, SSD_D_INNER + 2 * SSD_GROUPS * SSD_N), CONV_W ** -0.5),
        'ssd_conv_b': nrm((N_ODD, SSD_D_INNER + 2 * SSD_GROUPS * SSD_N), 0.02),
        'ssd_a_log': a_log_init((N_ODD, 2, SSD_HEADS)),
        'ssd_dt_bias': dt_bias_init((N_ODD, 2, SSD_HEADS)),
        'ssd_d': 1.0 + nrm((N_ODD, SSD_HEADS), 0.1),
        'ssd_norm': 1.0 + nrm((N_ODD, SSD_D_INNER), 0.1),
        'na_rpb': nrm((N_ODD, NA_HEADS, 2 * NA_WIN_R - 1, 2 * NA_WIN_C - 1), 0.1),
    }


def reference(x_prompt, x_sample, state_gdn, cache_gqa_k, cache_gqa_v, state_ssd, cache_na_k, cache_na_v,
              c, c_ctx, ada_w, ada_b, norm_mix_pre, norm_mix_post, norm_mlp_pre, norm_mlp_post,
              mlp_w1, mlp_w2, ev_w_in, ev_w_out, gdn_conv, gdn_a_log, gdn_dt_bias, gdn_norm,
              gqa_q_norm, gqa_k_norm, od_w_in, od_w_out, ssd_conv, ssd_conv_b, ssd_a_log, ssd_dt_bias,
              ssd_d, ssd_norm, na_rpb):
    ang = rope_angles(x_sample.shape[1])
    xp, xs = x_prompt, x_sample
    new_gdn, new_gk, new_gv, new_ssd, new_nk, new_nv = [], [], [], [], [], []
    for i in range(DEPTH):
        m_ctx = (jax.nn.silu(c_ctx) @ ada_w[i] + ada_b[i])[None, None, :]
        m_lat = (jax.nn.silu(c) @ ada_w[i] + ada_b[i])[:, None, :]
        sh1c, sc1c, g1c, sh2c, sc2c, g2c = jnp.split(m_ctx, 6, axis=-1)
        sh1l, sc1l, g1l, sh2l, sc2l, g2l = jnp.split(m_lat, 6, axis=-1)
        hc = modulate(xp, norm_mix_pre[i], sh1c, sc1c)
        hl = modulate(xs, norm_mix_pre[i], sh1l, sc1l)
        j = i // 2
        if i % 2 == 0:
            ew = (ev_w_in[j], ev_w_out[j], gdn_conv[j], gdn_a_log[j], gdn_dt_bias[j], gdn_norm[j],
                  gqa_q_norm[j], gqa_k_norm[j])
            yc, s_a, k_b, v_b = even_mixer(hc, *ew)
            yl = even_mixer(hl, *ew, ang=ang, ctx=(state_gdn[:, j], cache_gqa_k[:, j], cache_gqa_v[:, j]))
            new_gdn.append(s_a)
            new_gk.append(k_b)
            new_gv.append(v_b)
        else:
            ow = (od_w_in[j], od_w_out[j], ssd_conv[j], ssd_conv_b[j], ssd_a_log[j], ssd_dt_bias[j],
                  ssd_d[j], ssd_norm[j], na_rpb[j])
            yc, s_c, k_d, v_d = odd_mixer(hc, *ow)
            yl = odd_mixer(hl, *ow, ctx=(state_ssd[:, j], cache_na_k[:, j], cache_na_v[:, j]))
            new_ssd.append(s_c)
            new_nk.append(k_d)
            new_nv.append(v_d)
        xp = xp + g1c * rmsnorm(yc, norm_mix_post[i])
        xs = xs + g1l * rmsnorm(yl, norm_mix_post[i])
        xp = xp + g2c * rmsnorm(sq_relu_mlp(modulate(xp, norm_mlp_pre[i], sh2c, sc2c), mlp_w1[i], mlp_w2[i]), norm_mlp_post[i])
        xs = xs + g2l * rmsnorm(sq_relu_mlp(modulate(xs, norm_mlp_pre[i], sh2l, sc2l), mlp_w1[i], mlp_w2[i]), norm_mlp_post[i])
    return (xp, xs, jnp.stack(new_gdn, axis=1), jnp.stack(new_gk, axis=1), jnp.stack(new_gv, axis=1),
            jnp.stack(new_ssd, axis=1), jnp.stack(new_nk, axis=1), jnp.stack(new_nv, axis=1))
```

```python
import numpy as np
import concourse.bass as bass
import concourse.mybir as mybir
from concourse.bass_utils import run_bass_kernel_spmd

F32 = mybir.dt.float32
BF16 = mybir.dt.bfloat16
AF = mybir.ActivationFunctionType
ALU = mybir.AluOpType
AX = mybir.AxisListType

ENGS = ('pe', 'act', 'dve', 'pool', 'sp')
SEM_ROT = 20000
NDSEM = 12

D = 1024
NT = 4608
EPS = 1e-6
SEQS = [(0, 256, False), (256, 256, False), (512, 4096, True)]
EV_IN = 3088


class Prog:
    def __init__(self, nc, same_engine_sync=True):
        self.nc = nc
        self.same = same_engine_sync
        self.stream = {e: [] for e in ENGS}
        self.nsem = 0
        self.esem = {}
        self.ecnt = {}
        self.allsems = []
        for e in ENGS:
            self._new_esem(e)
        self.known = {e: {} for e in ENGS}
        self.res = {}
        self.dsems = {}
        self.dpos = {}
        self.n_ops = 0

    def _alloc_sem(self, name):
        self.nsem += 1
        s = self.nc.alloc_semaphore(name=f"{name}_{self.nsem}")
        return s

    def _new_esem(self, e):
        self.esem[e] = self._alloc_sem(f"e_{e}")
        self.ecnt[e] = 0

    def _deps(self, eng, reads, writes):
        deps = {}

        def add(tok):
            if tok is None:
                return
            sem, val, src = tok
            if src == eng and (eng == 'pe' or not self.same):
                return
            k = id(sem)
            if k not in deps or deps[k][1] < val:
                deps[k] = (sem, val)
        for r in reads:
            st = self.res.get(r)
            if st:
                add(st[0])
        for w in writes:
            st = self.res.get(w)
            if st:
                add(st[0])
                for t in st[1]:
                    add(t)
        out = []
        kn = self.known[eng]
        for k, (sem, val) in deps.items():
            if kn.get(k, 0) >= val:
                continue
            kn[k] = val
            out.append((sem, val))
        return out

    def _commit(self, tok, reads, writes):
        for w in writes:
            self.res[w] = [tok, []]
        for r in reads:
            st = self.res.setdefault(r, [None, []])
            st[1].append(tok)
            if len(st[1]) > 48:
                best = {}
                for t in st[1]:
                    k = id(t[0])
                    if k not in best or best[k][1] < t[1]:
                        best[k] = t
                st[1] = list(best.values())

    def op(self, eng, fn, reads=(), writes=()):
        waits = self._deps(eng, reads, writes)
        if self.ecnt[eng] >= SEM_ROT:
            self._new_esem(eng)
        sem = self.esem[eng]
        self.ecnt[eng] += 1
        tok = (sem, self.ecnt[eng], eng)
        self.stream[eng].append((waits, fn, sem, 1))
        self._commit(tok, reads, writes)
        self.n_ops += 1
        return tok

    def dma(self, q, out, in_, reads=(), writes=(), **kw):
        waits = self._deps(q, reads, writes)
        if q not in self.dsems:
            self.dsems[q] = [[self._alloc_sem(f"d_{q}"), 0] for _ in range(NDSEM)]
            self.dpos[q] = 0
        slot = self.dsems[q][self.dpos[q] % NDSEM]
        self.dpos[q] += 1
        sem = slot[0]
        kn = self.known[q]
        if slot[1] > 0 and kn.get(id(sem), 0) < slot[1]:
            waits.append((sem, slot[1]))
            kn[id(sem)] = slot[1]
        slot[1] += 16
        tok = (sem, slot[1], 'dma_' + q)
        self.stream[q].append((waits, lambda e, o=out, i=in_, k=kw: e.dma_start(out=o, in_=i, **k), sem, 16))
        self._commit(tok, reads, writes)
        self.n_ops += 1
        return tok

    def barrier(self):
        pts = []
        for e in ENGS:
            if self.ecnt[e] > 0:
                pts.append((self.esem[e], self.ecnt[e]))
        for q in self.dsems:
            for s in self.dsems[q]:
                if s[1] > 0:
                    pts.append((s[0], s[1]))
        for e in ENGS:
            kn = self.known[e]
            w = []
            for (s, v) in pts:
                if kn.get(id(s), 0) < v:
                    kn[id(s)] = v
                    w.append((s, v))
            if w:
                self.stream[e].append((w, None, None, 0))
        self.res = {}

    def emit(self):
        nc = self.nc
        with nc.Block() as block:
            def run(e, name):
                for waits, fn, sem, inc in self.stream[name]:
                    for (s, v) in waits:
                        e.wait_ge(s, v)
                    if fn is not None:
                        fn(e).then_inc(sem, inc)

            @block.tensor
            def _(e):
                run(e, 'pe')

            @block.scalar
            def _(e):
                run(e, 'act')

            @block.vector
            def _(e):
                run(e, 'dve')

            @block.gpsimd
            def _(e):
                run(e, 'pool')

            @block.sync
            def _(e):
                run(e, 'sp')


class Arena:
    def __init__(self, ap, nwords):
        self.ap = ap
        self.n = nwords
        self.off = 0
        self.uid = 0

    def reset(self, off=0):
        self.off = off

    def alloc(self, shape, dtype=F32):
        n = int(np.prod(shape))
        words = n if dtype == F32 else (n + 1) // 2
        words = (words + 7) // 8 * 8
        assert self.off + words <= self.n, f"arena overflow {self.off}+{words}>{self.n}"
        a = self.ap[:, self.off:self.off + words]
        self.off += words
        if dtype != F32:
            a = a.bitcast(dtype)
        a = a[:, 0:n]
        if len(shape) == 2:
            a = a.rearrange("p (a b) -> p a b", a=shape[0])
        elif len(shape) == 3:
            a = a.rearrange("p (a b c) -> p a b c", a=shape[0], b=shape[1])
        return a


def bc_last(a, n):
    return bass.AP(a.tensor, a.offset, [list(x) for x in a.ap] + [[0, n]])


def bc_mid(a, n):
    return bass.AP(a.tensor, a.offset, [list(a.ap[0]), [0, n]] + [list(x) for x in a.ap[1:]])


class K:
    def __init__(self, n_layers=4, upto=None, taps=()):
        self.n_layers = n_layers
        self.upto = upto
        self.taps = taps
        nc = bass.Bass("TRN2", target_bir_lowering=False)
        self.nc = nc
        self.P = Prog(nc)
        self.uid = 0
        di = lambda name, shape: nc.dram_tensor(name, list(shape), F32, kind="ExternalInput").ap()
        do = lambda name, shape: nc.dram_tensor(name, list(shape), F32, kind="ExternalOutput").ap()
        ds = lambda name, shape: nc.dram_tensor(name, list(shape), F32, kind="Internal").ap()
        self.xin = di("xin", [NT, D])
        self.cvec = di("cvec", [2, D])
        self.state_gdn = di("state_gdn", [2, 2, 4, 128, 128])
        self.cgk = di("cache_gqa_k", [2, 512, 2, 128])
        self.cgv = di("cache_gqa_v", [2, 512, 2, 128])
        self.state_ssd = di("state_ssd", [2, 2, 8, 64, 128])
        self.cnk = di("cache_na_k", [2, 512, 4, 128])
        self.cnv = di("cache_na_v", [2, 512, 4, 128])
        self.ada_w = di("ada_w", [4, D, 6 * D])
        self.ada_b = di("ada_b", [4, 6 * D])
        self.norms = {n: di(n, [4, D]) for n in ("norm_mix_pre", "norm_mix_post", "norm_mlp_pre", "norm_mlp_post")}
        self.mlp_w1 = di("mlp_w1", [4, D, 4 * D])
        self.mlp_w2 = di("mlp_w2", [4, 4 * D, D])
        self.ev_w_in = di("ev_w_in", [2, D, EV_IN])
        self.ev_w_out = di("ev_w_out", [2, D, D])
        self.gdn_conv = di("gdn_conv", [2, 3, 1536])
        self.gdn_a_log = di("gdn_a_log", [2, 8])
        self.gdn_dt_bias = di("gdn_dt_bias", [2, 8])
        self.gdn_norm = di("gdn_norm", [2, 128])
        self.gqa_q_norm = di("gqa_q_norm", [2, 128])
        self.gqa_k_norm = di("gqa_k_norm", [2, 128])
        self.od_w_in = di("od_w_in", [2, D, EV_IN])
        self.od_w_out = di("od_w_out", [2, D, D])
        self.ssd_conv = di("ssd_conv", [2, 3, 1024])
        self.ssd_conv_b = di("ssd_conv_b", [2, 1024])
        self.ssd_a_log = di("ssd_a_log", [2, 16])
        self.ssd_dt_bias = di("ssd_dt_bias", [2, 16])
        self.ssd_d = di("ssd_d", [2, 8])
        self.ssd_norm = di("ssd_norm", [2, 512])
        self.na_rpb = di("na_rpb", [2, 60, 31])
        self.c_ident = di("c_ident", [128, 128])
        self.c_masks = di("c_masks", [8, 64, 64])
        self.c_rope = di("c_rope", [2, 128, 4096])
        self.c_perm = di("c_perm", [128, 128])
        self.c_namask = di("c_namask", [64, 64])
        self.yout = do("yout", [NT, D])
        self.o_gdn = do("o_gdn", [2, 2, 2, 4, 128, 128])
        self.o_gk = do("o_gk", [2, 2, 256, 2, 128])
        self.o_gv = do("o_gv", [2, 2, 256, 2, 128])
        self.o_ssd = do("o_ssd", [2, 2, 2, 8, 64, 128])
        self.o_nk = do("o_nk", [2, 2, 256, 4, 128])
        self.o_nv = do("o_nv", [2, 2, 256, 4, 128])
        self.xres = ds("xres", [NT, D])
        self.modrow = ds("modrow", [4, 2, 6 * D])
        self.projT = ds("projT", [3200, NT])
        self.projN = ds("projN", [NT, EV_IN])
        self.mixN = ds("mixN", [NT, 512])
        self.mixT = ds("mixT", [512, NT])
        self.qkn = ds("qkn", [1024, NT])
        self.kvn = ds("kvn", [NT, 1024])
        self.rpbpad = ds("rpbpad", [60, 160])
        self.gb = ds("gb", [NT, 16])
        self.sb = ds("sb", [NT, 32])
        self.yf = ds("yf", [NT, 512])
        self.yb = ds("yb", [NT, 512])
        self.tapbufs = {}
        for (name, shape) in taps:
            self.tapbufs[name] = do("tap_" + name, shape)
        self.NW = 53000
        self.arena_t = nc.alloc_sbuf_tensor("arena", [128, self.NW], F32)
        self.A = Arena(self.arena_t.ap() if hasattr(self.arena_t, 'ap') else self.arena_t[:, :], self.NW)
        self.banks = []
        for i in range(8):
            t = nc.alloc_psum_tensor(f"bank{i}", [128, 512], F32)
            self.banks.append(t.ap() if hasattr(t, 'ap') else t[:, :])

    def bk(self, w, *aps):
        w = list(w)
        for a in aps:
            nm = getattr(getattr(a, 'tensor', None), 'name', '')
            if isinstance(nm, str) and nm.startswith('bank'):
                k = 'X' + nm
                if k not in w:
                    w.append(k)
        return w

    def mm(self, out, lhsT, rhs, start, stop, r, w):
        return self.P.op('pe', lambda e: e.matmul(out, lhsT=lhsT, rhs=rhs, start=start, stop=stop), r, self.bk(w, out))

    def tr(self, out, in_, ident, r, w):
        return self.P.op('pe', lambda e: e.transpose(out, in_, ident), r, self.bk(w, out))

    def act(self, out, in_, func, r, w, **kw):
        return self.P.op('act', lambda e: e.activation(out=out, in_=in_, func=func, **kw), r, self.bk(w, out, in_))

    def ts(self, eng, out, in0, s1, s2, op0, op1, r, w):
        w = self.bk(w, out, in0)
        if op1 is None:
            return self.P.op(eng, lambda e: e.tensor_scalar(out=out, in0=in0, scalar1=s1, scalar2=None, op0=op0), r, w)
        return self.P.op(eng, lambda e: e.tensor_scalar(out=out, in0=in0, scalar1=s1, scalar2=s2, op0=op0, op1=op1), r, w)

    def tt(self, eng, out, in0, in1, op, r, w):
        return self.P.op(eng, lambda e: e.tensor_tensor(out=out, in0=in0, in1=in1, op=op), r, self.bk(w, out, in0, in1))

    def stt(self, eng, out, in0, scalar, in1, op0, op1, r, w):
        return self.P.op(eng, lambda e: e.scalar_tensor_tensor(out=out, in0=in0, scalar=scalar, in1=in1, op0=op0, op1=op1), r,
                         self.bk(w, out, in0, in1))

    def recip(self, out, in_, r, w):
        return self.P.op('dve', lambda e: e.reciprocal(out=out, in_=in_), r, self.bk(w, out, in_))

    def cp(self, eng, out, in_, r, w):
        if eng == 'act':
            return self.act(out, in_, AF.Copy, r, w)
        return self.P.op(eng, lambda e: e.tensor_copy(out=out, in_=in_), r, self.bk(w, out, in_))

    def memset(self, eng, ap, val, w):
        return self.P.op(eng, lambda e: e.memset(ap, val), (), w)

    def key(self, name):
        self.uid += 1
        return f"{name}#{self.uid}"

    def tap(self, name, src_ap, reads):
        if name in self.tapbufs:
            self.P.dma('sp', self.tapbufs[name], src_ap, reads=reads)

    def phase_begin(self):
        self.P.barrier()
        A = self.A
        A.reset()
        self.ident = A.alloc([128])
        self.identb = A.alloc([128], BF16)
        self.P.dma('sp', self.ident, self.c_ident, writes=['ident'])
        self.cp('dve', self.identb, self.ident, ['ident'], ['identb'])
        self.ones = A.alloc([128])
        self.memset('pool', self.ones, 1.0, ['ones'])
        self.onesb = A.alloc([128], BF16)
        self.memset('pool', self.onesb, 1.0, ['onesb'])
        self.const_end = A.off

    def load_bcast(self, dst, row_ap, key, q='sp'):
        self.P.dma(q, dst, row_ap.partition_broadcast(128), writes=[key])

    def rstd_from_ss(self, rstd, ss, n, r, w):
        self.act(rstd, ss, AF.Sqrt, r, w, bias=EPS, scale=1.0 / n)
        self.P.op('dve', lambda e: e.reciprocal(out=rstd, in_=rstd), w, w)

    def phase_ada(self):
        self.phase_begin()
        A, P = self.A, self.P
        cT = A.alloc([2, 8])
        for v in range(2):
            P.dma('sp', cT[:, v, :], self.cvec[v, :].rearrange("(k p) -> p k", p=128), writes=['cT'],
                  allow_slow_non_contiguous=True)
        sc = A.alloc([2, 8])
        self.act(sc, cT, AF.Silu, ['cT'], ['sc'])
        L = A.alloc([2, 8, 128])
        for v in range(2):
            self.cp('dve', L[:, v, :, :], bc_last(sc[:, v, :], 128), ['sc'], ['L'])
        wbuf = [A.alloc([1536]) for _ in range(3)]
        bb = A.alloc([1536])
        ob = [A.alloc([1536]) for _ in range(2)]
        for l in range(self.n_layers):
            for g in range(4):
                c0 = g * 1536
                self.load_bcast(bb, self.ada_b[l, c0:c0 + 1536], 'bb', q='act')
                for k in range(8):
                    wb = wbuf[k % 3]
                    wk = f'wbuf{k % 3}'
                    P.dma('sp', wb, self.ada_w[l, k * 128:(k + 1) * 128, c0:c0 + 1536], writes=[wk])
                    for v in range(2):
                        for n in range(3):
                            self.mm(self.banks[v * 3 + n], L[:, v, k, :], wb[:, n * 512:(n + 1) * 512],
                                    k == 0, k == 7, [wk, 'L'], [f'bank{v * 3 + n}'])
                for v in range(2):
                    for n in range(3):
                        self.tt('dve', ob[v][:, n * 512:(n + 1) * 512], self.banks[v * 3 + n], bb[:, n * 512:(n + 1) * 512],
                                ALU.add, [f'bank{v * 3 + n}', 'bb'], [f'ob{v}'])
                    P.dma('act', self.modrow[l, v:v + 1, c0:c0 + 1536], ob[v][0:1, :], reads=[f'ob{v}'])

    def mod_vec(self, dst, l, v, idx, key, plus1=False):
        self.load_bcast(dst, self.modrow[l, v, idx * D:(idx + 1) * D], key, q='act')
        if plus1:
            self.ts('dve', dst, dst, 1.0, None, ALU.add, None, [key], [key])

    def phase_A(self, l, xsrc):
        even = (l % 2 == 0)
        j = l // 2
        w_in = self.ev_w_in[j] if even else self.od_w_in[j]
        if even:
            fm = [(c, 128) for c in range(0, 1024, 128)] + [(c, 128) for c in range(2064, 2832, 128)]
            tm = [(512, 512), (1024, 512), (1536, 512), (2048, 16), (2832, 256)]
        else:
            fm = [(c, 128) for c in range(1024, 1536, 128)] + [(c, 128) for c in range(1552, 2576, 128)]
            tm = [(0, 512), (512, 512), (1024, 256), (1536, 16), (2576, 512)]
        self.phase_begin()
        A, P = self.A, self.P
        W = A.alloc([8, EV_IN], BF16)
        for k in range(8):
            P.dma('pool', W[:, k, :], w_in[k * 128:(k + 1) * 128, :], writes=[f'W{k}'])
        Wk = [f'W{k}' for k in range(8)]
        tmpv = A.alloc([D])
        self.load_bcast(tmpv, self.norms["norm_mix_pre"][l, :], 'tmpv')
        A1 = []
        SH = []
        for v in range(2):
            a1 = A.alloc([D])
            sh = A.alloc([D])
            self.mod_vec(sh, l, v, 0, f'sh{v}')
            self.mod_vec(a1, l, v, 1, f'a1{v}', plus1=True)
            self.tt('dve', a1, a1, tmpv, ALU.mult, [f'a1{v}', 'tmpv'], [f'a1{v}'])
            A1.append(a1)
            SH.append(sh)
        xt = [A.alloc([D]) for _ in range(2)]
        junk = A.alloc([D], BF16)
        tmpf = A.alloc([D])
        hm = [A.alloc([D], BF16) for _ in range(2)]
        st = A.alloc([4])
        hmT = [A.alloc([8, 512], BF16) for _ in range(2)]
        stage = [A.alloc([512]) for _ in range(4)]
        ptr = [self.banks[0].bitcast(BF16), self.banks[1].bitcast(BF16)]
        nst = 0
        nps = 0
        for g in range(9):
            v = 0 if g == 0 else 1
            hT = hmT[g % 2]
            hk = f'hmT{g % 2}'
            for ti in range(4):
                t = g * 4 + ti
                s = t % 2
                P.dma('sp', xt[s], xsrc[t * 128:(t + 1) * 128, :], writes=[f'xt{s}'])
                self.act(junk, xt[s], AF.Square, [f'xt{s}'], ['junk', f'ss{s}'], accum_out=st[:, s:s + 1])
                self.rstd_from_ss(st[:, 2 + s:3 + s], st[:, s:s + 1], D, [f'ss{s}'], [f'rs{s}'])
                self.stt('dve', tmpf, xt[s], st[:, 2 + s:3 + s], A1[v], ALU.mult, ALU.mult,
                         [f'xt{s}', f'rs{s}', f'a1{v}'], ['tmpf'])
                self.tt('dve', hm[s], tmpf, SH[v], ALU.add, ['tmpf', f'sh{v}'], [f'hm{s}'])
                for k in range(8):
                    self.tr(ptr[s][:, k * 128:(k + 1) * 128], hm[s][:, k * 128:(k + 1) * 128], self.identb,
                            [f'hm{s}', 'identb'], [f'ptr{s}'])
                self.cp('act', hT[:, :, ti * 128:(ti + 1) * 128], ptr[s].rearrange("p (k t) -> p k t", k=8),
                        [f'ptr{s}'], [hk])
            for (c0, wd) in fm:
                b = 2 + nps % 3
                nps += 1
                for k in range(8):
                    self.mm(self.banks[b][0:wd, :], W[:, k, c0:c0 + wd], hT[:, k, :], k == 0, k == 7,
                            [Wk[k], hk], [f'bank{b}'])
                sg = nst % 4
                nst += 1
                self.cp('act' if nst % 2 else 'dve', stage[sg][0:wd, :], self.banks[b][0:wd, :], [f'bank{b}'], [f'stage{sg}'])
                P.dma('sp', self.projT[c0:c0 + wd, g * 512:(g + 1) * 512], stage[sg][0:wd, :], reads=[f'stage{sg}'])
            for ti in range(4):
                t = g * 4 + ti
                for (c0, wd) in tm:
                    b = 5 + nps % 3
                    nps += 1
                    for k in range(8):
                        self.mm(self.banks[b][:, 0:wd], hT[:, k, ti * 128:(ti + 1) * 128], W[:, k, c0:c0 + wd],
                                k == 0, k == 7, [Wk[k], hk], [f'bank{b}'])
                    sg = nst % 4
                    nst += 1
                    self.cp('act' if nst % 2 else 'dve', stage[sg][:, 0:wd], self.banks[b][:, 0:wd], [f'bank{b}'], [f'stage{sg}'])
                    P.dma('sp', self.projN[t * 128:(t + 1) * 128, c0:c0 + wd], stage[sg][:, 0:wd], reads=[f'stage{sg}'])

    def phase_C0(self, l, xsrc):
        even = (l % 2 == 0)
        j = l // 2
        w_out = self.ev_w_out[j] if even else self.od_w_out[j]
        self.phase_begin()
        A, P = self.A, self.P
        W = A.alloc([8, D], BF16)
        for k in range(8):
            P.dma('pool', W[:, k, :], w_out[k * 128:(k + 1) * 128, :], writes=[f'W{k}'])
        tmpv = A.alloc([D])
        self.load_bcast(tmpv, self.norms["norm_mix_post"][l, :], 'tmpv')
        G1 = []
        for v in range(2):
            g1 = A.alloc([D])
            self.mod_vec(g1, l, v, 2, f'g1{v}')
            self.tt('dve', g1, g1, tmpv, ALU.mult, [f'g1{v}', 'tmpv'], [f'g1{v}'])
            G1.append(g1)
        xt = [A.alloc([D]) for _ in range(2)]
        mn = [A.alloc([512], BF16) for _ in range(2)]
        mT = [A.alloc([4, 512], BF16) for _ in range(2)]
        mTn = [A.alloc([4, 128], BF16) for _ in range(2)]
        junk = A.alloc([D], BF16)
        tmpf = A.alloc([D])
        st = A.alloc([8])
        ptr = [self.banks[0].bitcast(BF16), self.banks[1].bitcast(BF16)]
        for g in range(9):
            v = 0 if g == 0 else 1
            gs = g % 2
            for k in range(4):
                P.dma('pool', mT[gs][:, k, :], self.mixT[k * 128:(k + 1) * 128, g * 512:(g + 1) * 512], writes=[f'mT{gs}'])
            for ti in range(4):
                t = g * 4 + ti
                s = t % 2
                P.dma('sp', xt[s], xsrc[t * 128:(t + 1) * 128, :], writes=[f'xt{s}'])
                P.dma('pool', mn[s], self.mixN[t * 128:(t + 1) * 128, :], writes=[f'mn{s}'])
                for k in range(4):
                    self.tr(ptr[s][:, k * 128:(k + 1) * 128], mn[s][:, k * 128:(k + 1) * 128], self.identb,
                            [f'mn{s}', 'identb'], [f'ptr{s}'])
                self.cp('act', mTn[s], ptr[s][:, 0:512].rearrange("p (k t) -> p k t", k=4), [f'ptr{s}'], [f'mTn{s}'])
                for h in range(2):
                    b = 2 + 2 * s + h
                    for k in range(8):
                        lhs = mTn[s][:, k, :] if k < 4 else mT[gs][:, k - 4, ti * 128:(ti + 1) * 128]
                        self.mm(self.banks[b], lhs, W[:, k, h * 512:(h + 1) * 512], k == 0, k == 7,
                                [f'W{k}', f'mTn{s}', f'mT{gs}'], [f'bank{b}'])
                    self.act(junk[:, 0:512], self.banks[b], AF.Square, [f'bank{b}'], ['junk', f'ss{s}{h}'],
                             accum_out=st[:, 2 * s + h:2 * s + h + 1])
                self.tt('dve', st[:, 4 + s:5 + s], st[:, 2 * s:2 * s + 1], st[:, 2 * s + 1:2 * s + 2], ALU.add,
                        [f'ss{s}0', f'ss{s}1'], [f'sst{s}'])
                self.rstd_from_ss(st[:, 6 + s:7 + s], st[:, 4 + s:5 + s], D, [f'sst{s}'], [f'rs{s}'])
                for h in range(2):
                    b = 2 + 2 * s + h
                    self.stt('dve', tmpf[:, h * 512:(h + 1) * 512], self.banks[b], st[:, 6 + s:7 + s],
                             G1[v][:, h * 512:(h + 1) * 512], ALU.mult, ALU.mult, [f'bank{b}', f'rs{s}', f'g1{v}'], ['tmpf'])
                self.tt('pool', xt[s], xt[s], tmpf, ALU.add, [f'xt{s}', 'tmpf'], [f'xt{s}'])
                P.dma('sp', self.xres[t * 128:(t + 1) * 128, :], xt[s], reads=[f'xt{s}'])

    def phase_C1(self, l, xdst):
        self.phase_begin()
        A, P = self.A, self.P
        W1 = A.alloc([8, 4 * D], BF16)
        W2 = A.alloc([32, D], BF16)
        for k in range(8):
            P.dma('pool', W1[:, k, :], self.mlp_w1[l, k * 128:(k + 1) * 128, :], writes=[f'W1{k}'])
        for k in range(32):
            P.dma('pool', W2[:, k, :], self.mlp_w2[l, k * 128:(k + 1) * 128, :], writes=[f'W2{k}'])
        W1k = [f'W1{k}' for k in range(8)]
        cur = {}
        cur['sh'] = A.alloc([D])
        cur['a2'] = A.alloc([D])
        cur['g2'] = A.alloc([D])
        xt = [A.alloc([D]) for _ in range(2)]
        junk = A.alloc([D], BF16)
        tmpf = A.alloc([D])
        hm = [A.alloc([D], BF16) for _ in range(2)]
        st = A.alloc([16])
        hmT = A.alloc([8, 512], BF16)
        h1T = A.alloc([32, 512], BF16)
        r1 = [A.alloc([512]) for _ in range(2)]
        ptr = [self.banks[0].bitcast(BF16), self.banks[1].bitcast(BF16)]

        def load_vecs(v):
            if cur.get('v') == v:
                return
            cur['v'] = v
            self.mod_vec(cur['sh'], l, v, 3, 'sh2')
            self.mod_vec(cur['a2'], l, v, 4, 'a2', plus1=True)
            self.load_bcast(tmpf, self.norms["norm_mlp_pre"][l, :], 'tmpf')
            self.tt('dve', cur['a2'], cur['a2'], tmpf, ALU.mult, ['a2', 'tmpf'], ['a2'])
            self.mod_vec(cur['g2'], l, v, 5, 'g2')
            self.load_bcast(tmpf, self.norms["norm_mlp_post"][l, :], 'tmpf')
            self.tt('dve', cur['g2'], cur['g2'], tmpf, ALU.mult, ['g2', 'tmpf'], ['g2'])

        nx = 0
        for g in range(9):
            v = 0 if g == 0 else 1
            load_vecs(v)
            for ti in range(4):
                t = g * 4 + ti
                s = nx % 2
                nx += 1
                P.dma('sp', xt[s], self.xres[t * 128:(t + 1) * 128, :], writes=[f'xt{s}'])
                self.act(junk, xt[s], AF.Square, [f'xt{s}'], ['junk', f'ss{s}'], accum_out=st[:, s:s + 1])
                self.rstd_from_ss(st[:, 2 + s:3 + s], st[:, s:s + 1], D, [f'ss{s}'], [f'rs{s}'])
                self.stt('dve', tmpf, xt[s], st[:, 2 + s:3 + s], cur['a2'], ALU.mult, ALU.mult,
                         [f'xt{s}', f'rs{s}', 'a2'], ['tmpf'])
                self.tt('dve', hm[s], tmpf, cur['sh'], ALU.add, ['tmpf', 'sh2'], [f'hm{s}'])
                for k in range(8):
                    self.tr(ptr[s][:, k * 128:(k + 1) * 128], hm[s][:, k * 128:(k + 1) * 128], self.identb,
                            [f'hm{s}', 'identb'], [f'ptr{s}'])
                self.cp('act', hmT[:, :, ti * 128:(ti + 1) * 128], ptr[s].rearrange("p (k t) -> p k t", k=8),
                        [f'ptr{s}'], ['hmT'])
            for f in range(32):
                b = 2 + f % 2
                for k in range(8):
                    self.mm(self.banks[b], W1[:, k, f * 128:(f + 1) * 128], hmT[:, k, :], k == 0, k == 7,
                            [W1k[k], 'hmT'], [f'bank{b}'])
                rs = f % 2
                self.act(r1[rs], self.banks[b], AF.Relu, [f'bank{b}'], [f'r1{rs}'])
                self.tt('dve' if f % 4 < 3 else 'pool', h1T[:, f, :], r1[rs], r1[rs], ALU.mult, [f'r1{rs}'], [f'h1T{f}'])
            for ti in range(4):
                t = g * 4 + ti
                s = nx % 2
                nx += 1
                P.dma('sp', xt[s], self.xres[t * 128:(t + 1) * 128, :], writes=[f'xt{s}'])
                for h in range(2):
                    b = 4 + 2 * s + h
                    for f in range(32):
                        self.mm(self.banks[b], h1T[:, f, ti * 128:(ti + 1) * 128], W2[:, f, h * 512:(h + 1) * 512],
                                f == 0, f == 31, [f'W2{f}', f'h1T{f}'], [f'bank{b}'])
                    self.act(junk[:, 0:512], self.banks[b], AF.Square, [f'bank{b}'], ['junk', f'q{s}{h}'],
                             accum_out=st[:, 4 + 2 * s + h:5 + 2 * s + h])
                self.tt('dve', st[:, 8 + s:9 + s], st[:, 4 + 2 * s:5 + 2 * s], st[:, 5 + 2 * s:6 + 2 * s], ALU.add,
                        [f'q{s}0', f'q{s}1'], [f'qt{s}'])
                self.rstd_from_ss(st[:, 10 + s:11 + s], st[:, 8 + s:9 + s], D, [f'qt{s}'], [f'qr{s}'])
                for h in range(2):
                    b = 4 + 2 * s + h
                    self.stt('dve', tmpf[:, h * 512:(h + 1) * 512], self.banks[b], st[:, 10 + s:11 + s],
                             cur['g2'][:, h * 512:(h + 1) * 512], ALU.mult, ALU.mult, [f'bank{b}', f'qr{s}', 'g2'], ['tmpf'])
                self.tt('pool', xt[s], xt[s], tmpf, ALU.add, [f'xt{s}', 'tmpf'], [f'xt{s}'])
                P.dma('sp', xdst[t * 128:(t + 1) * 128, :], xt[s], reads=[f'xt{s}'])

    def build(self):
        P = self.P
        self.phase_ada()
        if self.upto == 'ada':
            return self.finish()
        if self.upto in ('T1', 'S1'):
            self.phase_A(1, self.xin)
            self.phase_odd_mix(1)
            return self.finish()
        for l in range(self.n_layers):
            xsrc = self.xin if l == 0 else self.xres
            self.phase_A(l, xsrc)
            if self.upto == f'A{l}':
                return self.finish()
            if l % 2 == 0:
                self.phase_even_mix(l)
            else:
                self.phase_odd_mix(l)
            if self.upto in (f'M{l}', 'G1', 'G2', 'G3', 'S1'):
                return self.finish()
            self.phase_C0(l, xsrc)
            last = (l == self.n_layers - 1)
            self.phase_C1(l, self.yout if last else self.xres)
        return self.finish()

    def finish(self):
        P = self.P
        P.barrier()
        for name, buf in self.tapbufs.items():
            src = getattr(self, name)
            P.dma('sp', buf, src)
        P.barrier()
        P.emit()
        return self.nc

    def phase_even_mix(self, l):
        raise NotImplementedError

    def phase_odd_mix(self, l):
        raise NotImplementedError


def host_consts():
    ident = np.eye(128, dtype=np.float32)
    t = np.arange(64)
    masks = np.zeros((8, 64, 64), np.float32)
    masks[0] = (t[:, None] <= t[None, :])
    masks[1] = (t[:, None] > t[None, :])
    masks[2] = (t[:, None] >= t[None, :])
    masks[3] = (t[:, None] < t[None, :])
    masks[4] = ((t[:, None] // 32) == (t[None, :] // 32))
    masks[5] = 1.0 - masks[4]
    half = 64
    inv_freq = (10000.0 ** (-np.arange(0, half, 2, dtype=np.float32) / half)).astype(np.float32)
    tok = np.arange(4096)
    row = (tok // 64).astype(np.float32)
    col = (tok % 64).astype(np.float32)
    ang = np.concatenate([row[None, :] * inv_freq[:, None], row[None, :] * inv_freq[:, None],
                          col[None, :] * inv_freq[:, None], col[None, :] * inv_freq[:, None]], 0)
    C = np.cos(ang).astype(np.float32)
    S = np.sin(ang).astype(np.float32)
    sign = np.ones((128, 1), np.float32)
    sign[0:32] = -1.0
    sign[64:96] = -1.0
    rope = np.stack([C, S * sign]).astype(np.float32)
    perm = np.zeros((128, 128), np.float32)
    for p in range(128):
        blk = p // 64 * 64
        q = p - blk
        partner = blk + (q + 32) % 64
        perm[partner, p] = 1.0
    q = np.arange(64)
    kc = np.arange(64)
    cs = np.clip(q - 8, 0, 48)
    namask = ((kc[None, :] >= cs[:, None]) & (kc[None, :] < cs[:, None] + 16)).astype(np.float32)
    return dict(c_ident=ident, c_masks=masks, c_rope=rope, c_perm=perm, c_namask=namask)


def make_in_maps(inp):
    f = lambda a: np.ascontiguousarray(a, dtype=np.float32)
    consts = host_consts()
    shared = {}
    for n in ("ada_w", "ada_b", "norm_mix_pre", "norm_mix_post", "norm_mlp_pre", "norm_mlp_post", "mlp_w1", "mlp_w2",
              "ev_w_in", "ev_w_out", "gdn_conv", "gdn_norm", "gqa_q_norm", "gqa_k_norm", "od_w_in", "od_w_out",
              "ssd_conv", "ssd_conv_b", "ssd_d", "ssd_norm"):
        shared[n] = f(inp[n])
    shared["gdn_a_log"] = f(inp["gdn_a_log"]).reshape(2, 8)
    shared["gdn_dt_bias"] = f(inp["gdn_dt_bias"]).reshape(2, 8)
    shared["ssd_a_log"] = f(inp["ssd_a_log"]).reshape(2, 16)
    shared["ssd_dt_bias"] = f(inp["ssd_dt_bias"]).reshape(2, 16)
    shared["na_rpb"] = f(inp["na_rpb"]).reshape(2, 60, 31)
    shared.update(consts)
    maps = []
    for c in range(8):
        m = dict(shared)
        m["xin"] = f(np.concatenate([inp["x_prompt"][2 * c], inp["x_prompt"][2 * c + 1], inp["x_sample"][c]], 0))
        m["cvec"] = f(np.stack([inp["c_ctx"], inp["c"][c]], 0))
        m["state_gdn"] = f(inp["state_gdn"][c])
        m["cache_gqa_k"] = f(inp["cache_gqa_k"][c])
        m["cache_gqa_v"] = f(inp["cache_gqa_v"][c])
        m["state_ssd"] = f(inp["state_ssd"][c])
        m["cache_na_k"] = f(inp["cache_na_k"][c])
        m["cache_na_v"] = f(inp["cache_na_v"][c])
        maps.append(m)
    return maps


def _prep_qk(self, x, T, wcol, rope, out, tmp, rs, tables, pos0, kx, ko, bank):
    step = 512
    for c0 in range(0, T, step):
        w = min(step, T - c0)
        xs = x[:, c0:c0 + w]
        if wcol is not None:
            self.act(tmp[:, 0:w], xs, AF.Square, [kx], ['pq_tmp'])
            self.mm(self.banks[bank][:, 0:w], self.ones, tmp[:, 0:w], True, True, ['ones', 'pq_tmp'], [f'bank{bank}'])
            self.act(rs[:, 0:w], self.banks[bank][:, 0:w], AF.Sqrt, [f'bank{bank}'], ['pq_rs'], bias=EPS, scale=1.0 / 128)
            self.P.op('dve', lambda e, a=rs[:, 0:w]: e.reciprocal(out=a, in_=a), ['pq_rs'], ['pq_rs'])
            self.stt('dve', xs, xs, wcol, rs[:, 0:w], ALU.mult, ALU.mult, [kx, 'pq_rs', 'pq_w'], [kx])
        if rope:
            C, S, perm = tables
            self.mm(self.banks[bank][:, 0:w], perm, xs, True, True, ['perm', kx], [f'bank{bank}'])
            self.tt('dve', tmp[:, 0:w], self.banks[bank][:, 0:w], S[:, pos0 + c0:pos0 + c0 + w], ALU.mult,
                    [f'bank{bank}', 'ropeS'], ['pq_tmp'])
            self.tt('pool', rs[:, 0:w], xs, C[:, pos0 + c0:pos0 + c0 + w], ALU.mult, [kx, 'ropeC'], ['pq_rs'])
            self.tt('dve', out[:, c0:c0 + w], tmp[:, 0:w], rs[:, 0:w], ALU.add, ['pq_tmp', 'pq_rs'], [ko])
        else:
            self.cp('dve', out[:, c0:c0 + w], xs, [kx], [ko])


def _attn_dense(self, l, qrow0, krow0, vcol0, nq, nkv, sample_ctx, norm, rope, kout, vout, only_prompts=False):
    j = l // 2
    A, P = self.A, self.P
    rep = nq // nkv
    scale = 128 ** -0.5
    TKM = 4096 + 512
    kTb = A.alloc([TKM], BF16)
    Vb = A.alloc([36, 128], BF16)
    qTb = A.alloc([4096], BF16)
    xk = A.alloc([4096])
    xq = A.alloc([4096])
    tmp = A.alloc([512])
    rs = A.alloc([512])
    pt = [A.alloc([512], BF16) for _ in range(3)]
    rden = A.alloc([512])
    osb = [A.alloc([512]) for _ in range(2)]
    ctk = A.alloc([4, 128])
    kst = A.alloc([2, 128])
    tables = None
    if rope:
        C = A.alloc([4096])
        S = A.alloc([4096])
        perm = A.alloc([128])
        P.dma('sp', C, self.c_rope[0], writes=['ropeC'])
        P.dma('sp', S, self.c_rope[1], writes=['ropeS'])
        P.dma('sp', perm, self.c_perm, writes=['perm'])
        tables = (C, S, perm)
    wq = wk = None
    if norm is not None:
        wq = A.alloc([1])
        wk = A.alloc([1])
        P.dma('sp', wq, norm[0].rearrange("(p o) -> p o", o=1), writes=['pq_w'])
        P.dma('sp', wk, norm[1].rearrange("(p o) -> p o", o=1), writes=['pq_w'])
    no = 0
    npt = 0
    for si, (t0, T, is_s) in enumerate(SEQS):
        if only_prompts and is_s:
            continue
        Tk = T + (512 if (is_s and sample_ctx is not None) else 0)
        nkc = Tk // 128
        for g in range(nkv):
            P.dma('sp', xk[:, 0:T], self.projT[krow0 + g * 128:krow0 + (g + 1) * 128, t0:t0 + T], writes=['xk'])
            _prep_qk(self, xk, T, wk, rope and is_s, kTb, tmp, rs, tables, 0, 'xk', 'kTb', 6)
            if not is_s and kout is not None:
                for a in range(T // 128):
                    self.tr(self.banks[7][:, 0:128], xk[:, a * 128:(a + 1) * 128], self.ident, ['xk', 'ident'], ['bank7'])
                    self.cp('act', kst[:, a, :], self.banks[7][:, 0:128], ['bank7'], ['kst'])
                P.dma('sp', kout[si, j, :, g, :].rearrange("(a p) d -> p a d", p=128), kst, reads=['kst'])
            if Tk > T:
                ck, cv = sample_ctx
                P.dma('sp', ctk, ck[j, :, g, :].rearrange("(a p) d -> p a d", p=128), writes=['ctk'])
                for a in range(4):
                    self.tr(self.banks[7][:, 0:128], ctk[:, a, :], self.ident, ['ctk', 'ident'], ['bank7'])
                    self.cp('act', kTb[:, T + a * 128:T + (a + 1) * 128], self.banks[7][:, 0:128], ['bank7'], ['kTb'])
                P.dma('pool', Vb[:, T // 128:T // 128 + 4, :], cv[j, :, g, :].rearrange("(a p) d -> p a d", p=128), writes=['Vb'])
            P.dma('pool', Vb[:, 0:T // 128, :],
                  self.projN[t0:t0 + T, vcol0 + g * 128:vcol0 + (g + 1) * 128].rearrange("(a p) d -> p a d", p=128), writes=['Vb'])
            if not is_s and vout is not None:
                P.dma('act', vout[si, j, :, g, :], self.projN[t0:t0 + T, vcol0 + g * 128:vcol0 + (g + 1) * 128])
            for r_ in range(rep):
                h = g * rep + r_
                P.dma('sp', xq[:, 0:T], self.projT[qrow0 + h * 128:qrow0 + (h + 1) * 128, t0:t0 + T], writes=['xq'])
                _prep_qk(self, xq, T, wq, rope and is_s, qTb, tmp, rs, tables, 0, 'xq', 'qTb', 6)
                QW = min(512, T)
                for qc in range(T // QW):
                    q0 = qc * QW
                    bo = 2 + 2 * (no % 2)
                    def smm(kc_, bs_):
                        self.mm(self.banks[bs_][:, 0:QW], kTb[:, kc_ * 128:(kc_ + 1) * 128], qTb[:, q0:q0 + QW], True, True,
                                ['kTb', 'qTb'], [f'bank{bs_}'])
                    smm(0, npt % 2)
                    for kc in range(nkc):
                        bs = npt % 2
                        p_ = pt[npt % 3]
                        pk = f'pt{npt % 3}'
                        npt += 1
                        if kc + 1 < nkc:
                            smm(kc + 1, npt % 2)
                        self.act(p_[:, 0:QW], self.banks[bs][:, 0:QW], AF.Exp, [f'bank{bs}'], [pk], scale=scale)
                        self.mm(self.banks[bo][:, 0:QW], Vb[:, kc, :], p_[:, 0:QW], kc == 0, kc == nkc - 1, ['Vb', pk], [f'bank{bo}'])
                        self.mm(self.banks[bo + 1][:, 0:QW], self.onesb, p_[:, 0:QW], kc == 0, kc == nkc - 1, ['onesb', pk], [f'bank{bo + 1}'])
                    self.recip(rden[:, 0:QW], self.banks[bo + 1][:, 0:QW], [f'bank{bo + 1}'], ['rden'])
                    ob = osb[no % 2]
                    okk = f'osb{no % 2}'
                    no += 1
                    self.tt('dve', ob[:, 0:QW], self.banks[bo][:, 0:QW], rden[:, 0:QW], ALU.mult, [f'bank{bo}', 'rden'], [okk])
                    P.dma('sp', self.mixT[h * 128:(h + 1) * 128, t0 + q0:t0 + q0 + QW], ob[:, 0:QW], reads=[okk])


K._attn_dense = _attn_dense


def _gdn(self, l):
    j = l // 2
    A, P = self.A, self.P
    NB = 4
    wb = [A.alloc([1024]) for _ in range(3)]
    for k in range(3):
        self.load_bcast(wb[k], self.gdn_conv[j, k, 512:1536], f'wb{k}')
    xm = [A.alloc([1024]) for _ in range(2)]
    x0 = [A.alloc([1024]) for _ in range(2)]
    xp = [A.alloc([1024]) for _ in range(2)]
    sq = A.alloc([512])
    ss4 = A.alloc([4])
    nt = 0
    for (t0, T, is_s) in SEQS:
        for a in range(T // 128):
            s = nt % 2
            nt += 1
            r0 = t0 + a * 128
            first, last = (a == 0), (a == T // 128 - 1)
            src = self.projN
            P.dma('sp', x0[s], src[r0:r0 + 128, 512:1536], writes=[f'x0{s}'])
            if first:
                self.memset('pool', xm[s], 0.0, [f'xm{s}'])
                P.dma('sp', xm[s][1:128, :], src[r0:r0 + 127, 512:1536], writes=[f'xm{s}'])
            else:
                P.dma('sp', xm[s], src[r0 - 1:r0 + 127, 512:1536], writes=[f'xm{s}'])
            if last:
                self.memset('pool', xp[s], 0.0, [f'xp{s}'])
                P.dma('sp', xp[s][0:127, :], src[r0 + 1:r0 + 128, 512:1536], writes=[f'xp{s}'])
            else:
                P.dma('sp', xp[s], src[r0 + 1:r0 + 129, 512:1536], writes=[f'xp{s}'])
            self.tt('dve', x0[s], x0[s], wb[1], ALU.mult, [f'x0{s}', 'wb1'], [f'x0{s}'])
            self.tt('pool', xm[s], xm[s], wb[0], ALU.mult, [f'xm{s}', 'wb0'], [f'xm{s}'])
            self.tt('pool', xp[s], xp[s], wb[2], ALU.mult, [f'xp{s}', 'wb2'], [f'xp{s}'])
            self.tt('dve', x0[s], x0[s], xm[s], ALU.add, [f'x0{s}', f'xm{s}'], [f'x0{s}'])
            self.tt('dve', x0[s], x0[s], xp[s], ALU.add, [f'x0{s}', f'xp{s}'], [f'x0{s}'])
            self.act(x0[s], x0[s], AF.Silu, [f'x0{s}'], [f'x0{s}'])
            self.tt('dve', sq, x0[s][:, 0:512], x0[s][:, 0:512], ALU.mult, [f'x0{s}'], ['sq'])
            self.P.op('dve', lambda e, o=ss4, i=sq.rearrange("p (h d) -> p h d", h=4): e.tensor_reduce(out=o, in_=i, axis=AX.X, op=ALU.add),
                      ['sq'], ['ss4'])
            self.act(ss4, ss4, AF.Sqrt, ['ss4'], ['ss4'], bias=EPS, scale=1.0)
            self.P.op('dve', lambda e, o=ss4: e.reciprocal(out=o, in_=o), ['ss4'], ['ss4'])
            kv = x0[s][:, 0:512].rearrange("p (h d) -> p h d", h=4)
            self.tt('dve', kv, kv, bc_last(ss4, 128), ALU.mult, [f'x0{s}', 'ss4'], [f'x0{s}'])
            P.dma('sp', self.kvn[r0:r0 + 128, :], x0[s], reads=[f'x0{s}'])
    raw = A.alloc([36, 16])
    for a in range(36):
        P.dma('sp', raw[:, a, :], self.projN[a * 128:(a + 1) * 128, 2048:2064], writes=['raw'])
    dtb = A.alloc([8])
    nea = A.alloc([8])
    self.load_bcast(dtb, self.gdn_dt_bias[j, :], 'dtb')
    self.load_bcast(nea, self.gdn_a_log[j, :], 'nea')
    self.act(nea, nea, AF.Exp, ['nea'], ['nea'])
    gbt = A.alloc([36, 16])
    self.act(gbt[:, :, 0:8], raw[:, :, 0:8], AF.Sigmoid, ['raw'], ['gbt'])
    self.tt('dve', raw[:, :, 8:16], raw[:, :, 8:16], bc_mid(dtb, 36), ALU.add, ['raw', 'dtb'], ['raw'])
    self.act(raw[:, :, 8:16], raw[:, :, 8:16], AF.Exp, ['raw'], ['raw'])
    self.act(raw[:, :, 8:16], raw[:, :, 8:16], AF.Ln, ['raw'], ['raw'], bias=1.0)
    self.tt('dve', raw[:, :, 8:16], raw[:, :, 8:16], bc_mid(nea, 36), ALU.mult, ['raw', 'nea'], ['raw'])
    self.ts('dve', gbt[:, :, 8:16], raw[:, :, 8:16], -1.0, None, ALU.mult, None, ['raw'], ['gbt'])
    for a in range(36):
        P.dma('sp', self.gb[a * 128:(a + 1) * 128, :], gbt[:, a, :], reads=['gbt'])
    if self.upto == 'G1':
        return
    P.barrier()
    A.reset(self.const_end)
    xf = A.alloc([NT])
    acc = A.alloc([NT])
    cw = A.alloc([3])
    tq = A.alloc([512])
    rq = A.alloc([512])
    for c in range(8):
        P.dma('sp', xf, self.projT[c * 128:(c + 1) * 128, :], writes=['xf'])
        P.dma('sp', cw, self.gdn_conv[j, :, c * 128:(c + 1) * 128].rearrange("k p -> p k"), writes=['cw'], allow_slow_non_contiguous=True)
        self.ts('dve', acc, xf, cw[:, 1:2], None, ALU.mult, None, ['xf', 'cw'], ['acc'])
        for (t0, T, is_s) in SEQS:
            self.stt('dve', acc[:, t0 + 1:t0 + T], xf[:, t0:t0 + T - 1], cw[:, 0:1], acc[:, t0 + 1:t0 + T], ALU.mult, ALU.add,
                     ['xf', 'cw', 'acc'], ['acc'])
            self.stt('dve', acc[:, t0:t0 + T - 1], xf[:, t0 + 1:t0 + T], cw[:, 2:3], acc[:, t0:t0 + T - 1], ALU.mult, ALU.add,
                     ['xf', 'cw', 'acc'], ['acc'])
        self.act(acc, acc, AF.Silu, ['acc'], ['acc'])
        for g in range(9):
            sl = acc[:, g * 512:(g + 1) * 512]
            self.act(tq, sl, AF.Square, ['acc'], ['tq'])
            self.mm(self.banks[0], self.ones, tq, True, True, ['ones', 'tq'], ['bank0'])
            self.act(rq, self.banks[0], AF.Sqrt, ['bank0'], ['rq'], bias=EPS, scale=1.0)
            self.P.op('dve', lambda e, o=rq: e.reciprocal(out=o, in_=o), ['rq'], ['rq'])
            self.stt('dve', sl, sl, (128 ** -0.5) if c < 4 else 1.0, rq, ALU.mult, ALU.mult, ['acc', 'rq'], ['acc'])
        P.dma('sp', self.qkn[c * 128:(c + 1) * 128, :], acc, reads=['acc'])
    if self.upto == 'G2':
        return
    P.barrier()
    A.reset(self.const_end)
    msk = A.alloc([6, 64])
    P.dma('sp', msk[0:64], self.c_masks[0:6].rearrange("m t i -> t m i"), writes=['msk'])
    m_bd, m_off = msk[0:64, 4], msk[0:64, 5]
    gnw = A.alloc([128])
    self.load_bcast(gnw, self.gdn_norm[j, :], 'gnw')
    qT = A.alloc([4096])
    kT = A.alloc([4096])
    ktok = A.alloc([64, 128])
    vtok = A.alloc([64, 128])
    oacc = A.alloc([64, 128])
    gbc = A.alloc([64, 16])
    zt = A.alloc([8, 128])
    sqb = A.alloc([8, 128])
    ssn = A.alloc([64])
    X = []
    for d in range(2):
        x = {}
        for nm in ('gcol', 'bcol', 'gc', 'eg', 'edec', 'beg', 'gtb'):
            x[nm] = A.alloc([64])
        x['S'] = A.alloc([128])
        for nm in ('Gm', 'Em', 'ETm', 'A0', 'A1', 'B0', 'B1', 'Q', 'Pm', 'Ao', 'Yb', 'qkT', 'wT'):
            x[nm] = A.alloc([NB, 64])
        for nm in ('vb', 'kbg', 'kdec', 'u'):
            x[nm] = A.alloc([NB, 128])
        for nm in ('vnew', 'tqs', 'tmp2'):
            x[nm] = A.alloc([128])
        X.append(x)
    idb = self.ident[0:64, 0:64]
    v3 = lambda bk, off=0: bk[0:64, off:off + NB * 64].rearrange("p (c i) -> p c i", c=NB)

    def chain(d, si, t0, T, is_s, h):
        x = X[d]
        K_ = lambda n: f'{n}_{d}'
        nch = T // 64
        ci = d * 4 + h
        B0, B1, B2, B3 = self.banks[4 * d:4 * d + 4]
        k0, k1, k2, k3 = [f'bk{4 * d + i}' for i in range(4)]
        m_cum, m_oth, m_strict, m_incl = (msk[0:64, 0], msk[0:64, 1], msk[0:64, 1], msk[0:64, 0]) if d == 0 else \
                                         (msk[0:64, 2], msk[0:64, 3], msk[0:64, 3], msk[0:64, 2])
        gcol, bcol, gc, eg, edec, beg, gtb, S = (x[n] for n in ('gcol', 'bcol', 'gc', 'eg', 'edec', 'beg', 'gtb', 'S'))
        Gm, Em, ETm, Q, Pm, Ao, Yb, qkT, wT = (x[n] for n in ('Gm', 'Em', 'ETm', 'Q', 'Pm', 'Ao', 'Yb', 'qkT', 'wT'))
        vb, kbg, kdec, u, vnew, tqs, tmp2 = (x[n] for n in ('vb', 'kbg', 'kdec', 'u', 'vnew', 'tqs', 'tmp2'))
        Ak, Bk = [x['A0'], x['A1']], [x['B0'], x['B1']]
        self.cp('dve', bcol[0:64, 0:nch], gbc[0:64, 0:nch, ci], ['gbc'], [K_('bcol')])
        self.cp('dve', gcol[0:64, 0:nch], gbc[0:64, 0:nch, 8 + ci], ['gbc'], [K_('gcol')])
        yield
        self.mm(B0[0:64, 0:nch], m_cum, gcol[0:64, 0:nch], True, True, ['msk', K_('gcol')], [k0])
        self.mm(B1[:, 0:nch], self.ones[0:64, :], gcol[0:64, 0:nch], True, True, ['ones', K_('gcol')], [k1])
        yield
        self.cp('dve', gc[0:64, 0:nch], B0[0:64, 0:nch], [k0], [K_('gc')])
        self.act(eg[0:64, 0:nch], B0[0:64, 0:nch], AF.Exp, [k0], [K_('eg')])
        self.act(gtb[:, 0:nch], B1[:, 0:nch], AF.Exp, [k1], [K_('gtb')])
        yield
        self.tt('dve', edec[0:64, 0:nch], B1[0:64, 0:nch], gc[0:64, 0:nch], ALU.subtract, [k1, K_('gc')], [K_('edec')])
        self.tt('dve', beg[0:64, 0:nch], bcol[0:64, 0:nch], eg[0:64, 0:nch], ALU.mult, [K_('bcol'), K_('eg')], [K_('beg')])
        yield
        self.act(edec[0:64, 0:nch], edec[0:64, 0:nch], AF.Exp, [K_('edec')], [K_('edec')])
        if is_s:
            P.dma('sp', S, self.state_gdn[j, d, h], writes=[K_('S')])
        else:
            self.memset('pool', S, 0.0, [K_('S')])
        yield
        nbat = nch // NB
        border = range(nbat) if d == 0 else range(nbat - 1, -1, -1)
        for bi in border:
            c0 = bi * NB
            cs = slice(c0, c0 + NB)
            self.tt('dve', Gm[0:64], bc_last(gcol[0:64, cs], 64), bc_mid(m_cum, NB), ALU.mult, [K_('gcol'), 'msk'], [K_('Gm')])
            yield
            for c in range(NB):
                self.mm(B0[0:64, c * 64:(c + 1) * 64], Gm[0:64, c, :], m_oth, True, True, [K_('Gm'), 'msk'], [k0])
            for c in range(NB):
                self.mm(B0[0:64, 256 + c * 64:256 + (c + 1) * 64], m_oth, Gm[0:64, c, :], True, True, [K_('Gm'), 'msk'], [k0])
            for c in range(NB):
                tk = slice((c0 + c) * 64, (c0 + c + 1) * 64)
                self.mm(B1[0:64, c * 64:(c + 1) * 64], kT[:, tk], kT[:, tk], True, True, ['kT'], [k1])
            for c in range(NB):
                tk = slice((c0 + c) * 64, (c0 + c + 1) * 64)
                self.mm(B1[0:64, 256 + c * 64:256 + (c + 1) * 64], kT[:, tk], qT[:, tk], True, True, ['kT', 'qT'], [k1])
            yield
            self.act(Em[0:64], v3(B0), AF.Exp, [k0], [K_('Em')])
            self.act(ETm[0:64], v3(B0, 256), AF.Exp, [k0], [K_('ETm')])
            yield
            self.tt('dve', Em[0:64], Em[0:64], bc_mid(m_strict, NB), ALU.mult, [K_('Em'), 'msk'], [K_('Em')])
            self.tt('pool', ETm[0:64], ETm[0:64], bc_mid(m_incl, NB), ALU.mult, [K_('ETm'), 'msk'], [K_('ETm')])
            yield
            a0, b0 = Ak[0], Bk[0]
            self.tt('dve', a0[0:64], v3(B1), Em[0:64], ALU.mult, [k1, K_('Em')], [K_('A0')])
            self.tt('dve', qkT[0:64], v3(B1, 256), ETm[0:64], ALU.mult, [k1, K_('ETm')], [K_('qkT')])
            yield
            self.tt('dve', a0[0:64], a0[0:64], bc_last(bcol[0:64, cs], 64), ALU.mult, [K_('A0'), K_('bcol')], [K_('A0')])
            yield
            for c in range(NB):
                self.tr(B2[0:64, c * 64:(c + 1) * 64], a0[0:64, c, :], idb, [K_('A0'), 'ident'], [k2])
            yield
            self.cp('act', b0[0:64], v3(B2), [k2], [K_('B0')])
            self.tt('dve', Ao[0:64], a0[0:64], bc_mid(m_off, NB), ALU.mult, [K_('A0'), 'msk'], [K_('Ao')])
            yield
            self.tt('dve', a0[0:64], a0[0:64], bc_mid(m_bd, NB), ALU.mult, [K_('A0'), 'msk'], [K_('A0')])
            self.tt('pool', b0[0:64], b0[0:64], bc_mid(m_bd, NB), ALU.mult, [K_('B0'), 'msk'], [K_('B0')])
            yield
            self.tt('pool', Q[0:64], bc_mid(idb, NB), b0[0:64], ALU.subtract, ['ident', K_('B0')], [K_('Q')])
            self.tt('dve', Pm[0:64], bc_mid(idb, NB), a0[0:64], ALU.subtract, ['ident', K_('A0')], [K_('Pm')])
            yield
            for kk in range(1, 5):
                ao, bo_ = Ak[(kk - 1) % 2], Bk[(kk - 1) % 2]
                an, bn = Ak[kk % 2], Bk[kk % 2]
                ko, kn_ = K_(f'A{(kk - 1) % 2}'), K_(f'A{kk % 2}')
                lo, ln_ = K_(f'B{(kk - 1) % 2}'), K_(f'B{kk % 2}')
                for c in range(NB):
                    self.mm(B2[0:64, c * 64:(c + 1) * 64], ao[0:64, c, :], bo_[0:64, c, :], True, True, [ko, lo], [k2])
                for c in range(NB):
                    self.mm(B3[0:64, c * 64:(c + 1) * 64], bo_[0:64, c, :], ao[0:64, c, :], True, True, [ko, lo], [k3])
                yield
                self.cp('act', bn[0:64], v3(B2), [k2], [ln_])
                self.cp('dve', an[0:64], v3(B3), [k3], [kn_])
                yield
                for c in range(NB):
                    self.mm(B2[0:64, 256 + c * 64:256 + (c + 1) * 64], an[0:64, c, :], Q[0:64, c, :], True, True, [kn_, K_('Q')], [k2])
                for c in range(NB):
                    self.mm(B3[0:64, 256 + c * 64:256 + (c + 1) * 64], bn[0:64, c, :], Pm[0:64, c, :], True, True, [ln_, K_('Pm')], [k3])
                yield
                self.tt('dve', Q[0:64], Q[0:64], v3(B2, 256), ALU.add, [K_('Q'), k2], [K_('Q')])
                self.tt('dve', Pm[0:64], Pm[0:64], v3(B3, 256), ALU.add, [K_('Pm'), k3], [K_('Pm')])
                yield
            for c in range(NB):
                self.mm(B2[0:64, c * 64:(c + 1) * 64], Ao[0:64, c, :], Q[0:64, c, :], True, True, [K_('Ao'), K_('Q')], [k2])
            yield
            self.cp('act', Yb[0:64], v3(B2), [k2], [K_('Yb')])
            self.tt('dve', vb[0:64], vtok[0:64, cs, :], bc_last(bcol[0:64, cs], 128), ALU.mult, ['vtok', K_('bcol')], [K_('vb')])
            self.tt('dve', kbg[0:64], ktok[0:64, cs, :], bc_last(beg[0:64, cs], 128), ALU.mult, ['ktok', K_('beg')], [K_('kbg')])
            yield
            for c in range(NB):
                self.mm(B3[0:64, c * 64:(c + 1) * 64], Pm[0:64, c, :], Yb[0:64, c, :], True, True, [K_('Pm'), K_('Yb')], [k3])
            yield
            self.tt('dve', Q[0:64], Q[0:64], v3(B3), ALU.subtract, [K_('Q'), k3], [K_('Q')])
            self.tt('pool', kdec[0:64], ktok[0:64, cs, :], bc_last(edec[0:64, cs], 128), ALU.mult, ['ktok', K_('edec')], [K_('kdec')])
            yield
            for c in range(NB):
                self.mm(B0[0:64, c * 128:(c + 1) * 128], Q[0:64, c, :], vb[0:64, c, :], True, True, [K_('Q'), K_('vb')], [k0])
            for c in range(NB):
                self.mm(B1[:, c * 64:(c + 1) * 64], kbg[0:64, c, :], Q[0:64, c, :], True, True, [K_('Q'), K_('kbg')], [k1])
            yield
            self.cp('act', u[0:64], B0[0:64, :].rearrange("p (c i) -> p c i", c=NB), [k0], [K_('u')])
            self.cp('dve', wT, B1[:, 0:NB * 64].rearrange("p (c i) -> p c i", c=NB), [k1], [K_('wT')])
            yield
            corder = range(NB) if d == 0 else range(NB - 1, -1, -1)
            for c in corder:
                ch = c0 + c
                tk = slice(ch * 64, (ch + 1) * 64)
                self.mm(B2[0:64, 0:128], wT[:, c, :], S, True, True, [K_('wT'), K_('S')], [k2])
                self.mm(B3[0:64, 0:128], qT[:, tk], S, True, True, ['qT', K_('S')], [k3])
                yield
                self.tt('dve', vnew[0:64], u[0:64, c, :], B2[0:64, 0:128], ALU.subtract, [K_('u'), k2], [K_('vnew')])
                self.act(tqs[0:64], B3[0:64, 0:128], AF.Copy, [k3, K_('eg')], [K_('tqs')], scale=eg[0:64, ch:ch + 1])
                yield
                self.mm(B2[0:64, 128:256], qkT[0:64, c, :], vnew[0:64], True, True, [K_('qkT'), K_('vnew')], [k2])
                self.mm(B3[:, 128:256], kdec[0:64, c, :], vnew[0:64], True, True, [K_('kdec'), K_('vnew')], [k3])
                yield
                self.tt('dve', tmp2[0:64], tqs[0:64], B2[0:64, 128:256], ALU.add, [K_('tqs'), k2], [K_('tmp2')])
                self.stt('dve', S, S, gtb[:, ch:ch + 1], B3[:, 128:256], ALU.mult, ALU.add, [K_('S'), K_('gtb'), k3], [K_('S')])
                yield
                self.tt('pool', oacc[0:64, ch, :], oacc[0:64, ch, :], tmp2[0:64], ALU.add, [f'oacc{ch}', K_('tmp2')], [f'oacc{ch}'])
        if not is_s:
            P.dma('sp', self.o_gdn[si, j, d, h], S, reads=[K_('S')])

    for si, (t0, T, is_s) in enumerate(SEQS):
        nch = T // 64
        for h in range(4):
            P.dma('sp', qT[:, 0:T], self.qkn[h * 128:(h + 1) * 128, t0:t0 + T], writes=['qT'])
            P.dma('sp', kT[:, 0:T], self.qkn[512 + h * 128:512 + (h + 1) * 128, t0:t0 + T], writes=['kT'])
            for c8 in range(0, nch, 4):
                rr = slice(t0 + c8 * 64, t0 + (c8 + 4) * 64)
                P.dma('sp', ktok[0:64, c8:c8 + 4, :], self.kvn[rr, h * 128:(h + 1) * 128].rearrange("(c t) f -> t c f", t=64), writes=['ktok'])
                P.dma('act', vtok[0:64, c8:c8 + 4, :], self.kvn[rr, 512 + h * 128:512 + (h + 1) * 128].rearrange("(c t) f -> t c f", t=64), writes=['vtok'])
                P.dma('sp', gbc[0:64, c8:c8 + 4, :], self.gb[rr, :].rearrange("(c t) f -> t c f", t=64), writes=['gbc'])
            okeys = [f'oacc{c}' for c in range(nch)]
            self.memset('pool', oacc[0:64, 0:nch, :], 0.0, okeys)
            gens = [chain(0, si, t0, T, is_s, h), chain(1, si, t0, T, is_s, h)]
            while gens:
                for g_ in list(gens):
                    try:
                        next(g_)
                    except StopIteration:
                        gens.remove(g_)
            for c0 in range(0, nch, 8):
                n = min(8, nch - c0)
                cs = slice(c0, c0 + n)
                ok_ = okeys[c0:c0 + n]
                o = oacc[0:64, cs, :]
                self.tt('dve', sqb[0:64, 0:n, :], o, o, ALU.mult, ok_, ['sqb'])
                self.P.op('dve', lambda e, oo=ssn[0:64, 0:n], i=sqb[0:64, 0:n, :]: e.tensor_reduce(out=oo, in_=i, axis=AX.X, op=ALU.add), ['sqb'], ['ssn'])
                self.act(ssn[0:64, 0:n], ssn[0:64, 0:n], AF.Sqrt, ['ssn'], ['ssn'], bias=EPS, scale=1.0 / 128)
                self.P.op('dve', lambda e, oo=ssn[0:64, 0:n]: e.reciprocal(out=oo, in_=oo), ['ssn'], ['ssn'])
                for c4 in range(0, n, 4):
                    P.dma('sp', zt[0:64, c4:c4 + 4, :],
                          self.projN[t0 + (c0 + c4) * 64:t0 + (c0 + c4 + 4) * 64, 1536 + h * 128:1536 + (h + 1) * 128].rearrange("(c t) f -> t c f", t=64), writes=['zt'])
                self.act(zt[0:64, 0:n, :], zt[0:64, 0:n, :], AF.Silu, ['zt'], ['zt'])
                self.tt('dve', sqb[0:64, 0:n, :], o, bc_last(ssn[0:64, 0:n], 128), ALU.mult, ok_ + ['ssn'], ['sqb'])
                self.tt('pool', sqb[0:64, 0:n, :], sqb[0:64, 0:n, :], bc_mid(gnw[0:64], n), ALU.mult, ['sqb', 'gnw'], ['sqb'])
                self.tt('dve', sqb[0:64, 0:n, :], sqb[0:64, 0:n, :], zt[0:64, 0:n, :], ALU.mult, ['sqb', 'zt'], ['sqb'])
                for c4 in range(0, n, 4):
                    P.dma('act', self.mixN[t0 + (c0 + c4) * 64:t0 + (c0 + c4 + 4) * 64, h * 128:(h + 1) * 128].rearrange("(c t) f -> t c f", t=64),
                          sqb[0:64, c4:c4 + 4, :], reads=['sqb'])


def phase_even_mix(self, l):
    j = l // 2
    self.phase_begin()
    _gdn(self, l)
    if self.upto in ('G1', 'G2', 'G3'):
        return
    self.P.barrier()
    self.A.reset(self.const_end)
    _attn_dense(self, l, 2064, 2576, 2832, 4, 2, (self.cgk, self.cgv), (self.gqa_q_norm[j], self.gqa_k_norm[j]), True,
                self.o_gk, self.o_gv)


K.phase_even_mix = phase_even_mix


_CACHE = {}


def kernel(**inputs):
    if 'nc' not in _CACHE:
        kb = K(n_layers=4)
        _CACHE['nc'] = kb.build()
    nc = _CACHE['nc']
    in_maps = make_in_maps(inputs)
    res = run_bass_kernel_spmd(nc, in_maps, core_ids=list(range(8)))
    rs = res.results
    y = np.stack([r["yout"] for r in rs], 0)
    y_prompt = np.ascontiguousarray(y[:, :512].reshape(16, 256, 1024))
    y_sample = np.ascontiguousarray(y[:, 512:])
    cat = lambda n: np.concatenate([r[n] for r in rs], 0)
    return (y_prompt, y_sample, cat("o_gdn"), cat("o_gk"), cat("o_gv"), cat("o_ssd"), cat("o_nk"), cat("o_nv"))


def _ssd(self, l):
    j = l // 2
    A, P = self.A, self.P
    B = self.banks
    W = 768
    wb = [A.alloc([W]) for _ in range(3)]
    for k in range(3):
        self.load_bcast(wb[k], self.ssd_conv[j, k, 0:W], f'wb{k}')
    cbb = A.alloc([W])
    self.load_bcast(cbb, self.ssd_conv_b[j, 0:W], 'cbb')
    xm = [A.alloc([W]) for _ in range(2)]
    x0 = [A.alloc([W]) for _ in range(2)]
    xp = [A.alloc([W]) for _ in range(2)]
    nt = 0
    src = self.projN
    for (t0, T, is_s) in SEQS:
        for a in range(T // 128):
            s = nt % 2
            nt += 1
            r0 = t0 + a * 128
            first, last = (a == 0), (a == T // 128 - 1)
            P.dma('sp', x0[s], src[r0:r0 + 128, 512:1280], writes=[f'x0{s}'])
            if first:
                self.memset('pool', xm[s], 0.0, [f'xm{s}'])
                P.dma('sp', xm[s][1:128, :], src[r0:r0 + 127, 512:1280], writes=[f'xm{s}'])
            else:
                P.dma('sp', xm[s], src[r0 - 1:r0 + 127, 512:1280], writes=[f'xm{s}'])
            if last:
                self.memset('pool', xp[s], 0.0, [f'xp{s}'])
                P.dma('sp', xp[s][0:127, :], src[r0 + 1:r0 + 128, 512:1280], writes=[f'xp{s}'])
            else:
                P.dma('sp', xp[s], src[r0 + 1:r0 + 129, 512:1280], writes=[f'xp{s}'])
            self.tt('dve', x0[s], x0[s], wb[1], ALU.mult, [f'x0{s}', 'wb1'], [f'x0{s}'])
            self.tt('pool', xm[s], xm[s], wb[0], ALU.mult, [f'xm{s}', 'wb0'], [f'xm{s}'])
            self.tt('pool', xp[s], xp[s], wb[2], ALU.mult, [f'xp{s}', 'wb2'], [f'xp{s}'])
            self.tt('dve', x0[s], x0[s], xm[s], ALU.add, [f'x0{s}', f'xm{s}'], [f'x0{s}'])
            self.tt('dve', x0[s], x0[s], xp[s], ALU.add, [f'x0{s}', f'xp{s}'], [f'x0{s}'])
            self.tt('dve', x0[s], x0[s], cbb, ALU.add, [f'x0{s}', 'cbb'], [f'x0{s}'])
            self.act(x0[s], x0[s], AF.Silu, [f'x0{s}'], [f'x0{s}'])
            P.dma('sp', self.kvn[r0:r0 + 128, 0:W], x0[s], reads=[f'x0{s}'])
    raw = A.alloc([36, 16])
    for a in range(36):
        P.dma('sp', raw[:, a, :], self.projN[a * 128:(a + 1) * 128, 1536:1552], writes=['raw'])
    dtb = A.alloc([16])
    nea = A.alloc([16])
    self.load_bcast(dtb, self.ssd_dt_bias[j, :], 'dtb')
    self.load_bcast(nea, self.ssd_a_log[j, :], 'nea')
    self.act(nea, nea, AF.Exp, ['nea'], ['nea'])
    sbt = A.alloc([36, 32])
    self.tt('dve', raw, raw, bc_mid(dtb, 36), ALU.add, ['raw', 'dtb'], ['raw'])
    self.act(raw, raw, AF.Exp, ['raw'], ['raw'])
    self.act(sbt[:, :, 0:16], raw, AF.Ln, ['raw'], ['sbt'], bias=1.0)
    self.tt('dve', raw, sbt[:, :, 0:16], bc_mid(nea, 36), ALU.mult, ['sbt', 'nea'], ['raw'])
    self.ts('dve', sbt[:, :, 16:32], raw, -1.0, None, ALU.mult, None, ['raw'], ['sbt'])
    for a in range(36):
        P.dma('sp', self.sb[a * 128:(a + 1) * 128, :], sbt[:, a, :], reads=['sbt'])
    P.barrier()
    A.reset(self.const_end)
    xf = A.alloc([NT])
    acc = A.alloc([NT])
    cw = A.alloc([4])
    for c in range(4):
        ch0 = 512 + c * 128
        P.dma('sp', xf, self.projT[1024 + c * 128:1024 + (c + 1) * 128, :], writes=['xf'])
        P.dma('sp', cw[:, 0:3], self.ssd_conv[j, :, ch0:ch0 + 128].rearrange("k p -> p k"), writes=['cw'], allow_slow_non_contiguous=True)
        P.dma('sp', cw[:, 3:4], self.ssd_conv_b[j, ch0:ch0 + 128].rearrange("(p o) -> p o", o=1), writes=['cw'])
        self.ts('dve', acc, xf, cw[:, 1:2], cw[:, 3:4], ALU.mult, ALU.add, ['xf', 'cw'], ['acc'])
        for (t0, T, is_s) in SEQS:
            self.stt('dve', acc[:, t0 + 1:t0 + T], xf[:, t0:t0 + T - 1], cw[:, 0:1], acc[:, t0 + 1:t0 + T], ALU.mult, ALU.add,
                     ['xf', 'cw', 'acc'], ['acc'])
            self.stt('dve', acc[:, t0:t0 + T - 1], xf[:, t0 + 1:t0 + T], cw[:, 2:3], acc[:, t0:t0 + T - 1], ALU.mult, ALU.add,
                     ['xf', 'cw', 'acc'], ['acc'])
        self.act(acc, acc, AF.Silu, ['acc'], ['acc'])
        P.dma('sp', self.qkn[c * 128:(c + 1) * 128, :], acc, reads=['acc'])
    P.barrier()
    A.reset(self.const_end)
    msk = A.alloc([4, 64])
    P.dma('sp', msk[0:64], self.c_masks[0:4].rearrange("m t i -> t m i"), writes=['msk'])
    snw = A.alloc([512])
    self.load_bcast(snw, self.ssd_norm[j, :], 'snw')
    dsk = A.alloc([8])
    self.load_bcast(dsk, self.ssd_d[j, :], 'dsk')
    bcT = A.alloc([2, 4096])
    ccT = A.alloc([2, 4096])
    sca = A.alloc([64, 32])
    id64 = self.ident[0:64, 0:64]
    X = []
    for d in range(2):
        x = {}
        for nm in ('lad', 'dtd', 'cs', 'ecs', 'edec', 'cdec'):
            x[nm] = A.alloc([64, 8])
        for nm in ('hT', 'Gm', 'LT', 'MT', 'xdt', 'xd2', 'yo', 'yy'):
            x[nm] = A.alloc([8, 64])
        x['xb'] = [A.alloc([768]) for _ in range(2)]
        x['h0'] = A.alloc([8, 128])
        X.append(x)
    ydst = [self.yf, self.yb]

    def chain(d, si, t0, T, is_s):
        x = X[d]
        K_ = lambda n: f'{n}_{d}'
        nch = T // 64
        B0, B1, B2, B3 = self.banks[4 * d:4 * d + 4]
        k0, k1, k2, k3 = [f'bk{4 * d + i}' for i in range(4)]
        m_cum, m_oth, m_incl = (msk[0:64, 0], msk[0:64, 1], msk[0:64, 0]) if d == 0 else (msk[0:64, 2], msk[0:64, 3], msk[0:64, 2])
        lad, dtd, cs, ecs, edec, cdec, hT = (x[n] for n in ('lad', 'dtd', 'cs', 'ecs', 'edec', 'cdec', 'hT'))
        Gm, LT, MT, xdt, xd2, yo, yy, h0 = (x[n] for n in ('Gm', 'LT', 'MT', 'xdt', 'xd2', 'yo', 'yy', 'h0'))
        self.cp('dve', dtd[0:64, 0:nch, :], sca[0:64, 0:nch, d * 8:d * 8 + 8], ['sca'], [K_('dtd')])
        self.cp('dve', lad[0:64, 0:nch, :], sca[0:64, 0:nch, 16 + d * 8:16 + d * 8 + 8], ['sca'], [K_('lad')])
        yield
        n8 = nch * 8
        fl = lambda a_, p_=64: a_[0:p_, 0:nch, :].rearrange("p c h -> p (c h)")
        self.mm(B0[0:64, 0:n8], m_cum, fl(lad), True, True, ['msk', K_('lad')], [k0])
        self.mm(B1[:, 0:n8], self.ones[0:64, :], fl(lad), True, True, ['ones', K_('lad')], [k1])
        yield
        self.cp('dve', fl(cs), B0[0:64, 0:n8], [k0], [K_('cs')])
        self.act(fl(ecs), B0[0:64, 0:n8], AF.Exp, [k0], [K_('ecs')])
        self.act(fl(cdec, 128), B1[:, 0:n8], AF.Exp, [k1], [K_('cdec')])
        yield
        self.tt('dve', fl(edec), B1[0:64, 0:n8], fl(cs), ALU.subtract, [k1, K_('cs')], [K_('edec')])
        yield
        self.act(fl(edec), fl(edec), AF.Exp, [K_('edec')], [K_('edec')])
        if is_s:
            for hh in range(8):
                P.dma('sp' if d == 0 else 'act', h0[0:64, hh, :], self.state_ssd[j, d, hh], writes=[K_('h0')])
            yield
            for hh in range(8):
                self.tr(B2[:, hh * 64:(hh + 1) * 64], h0[0:64, hh, :], id64, [K_('h0'), 'ident'], [k2])
            yield
            self.cp('dve', hT, B2.rearrange("p (h q) -> p h q", h=8), [k2], [K_('hT')])
        else:
            self.memset('pool', hT, 0.0, [K_('hT')])
        yield
        corder = range(nch) if d == 0 else range(nch - 1, -1, -1)
        nx = 0
        for c in corder:
            r0 = t0 + c * 64
            tk = slice(c * 64, (c + 1) * 64)
            s = nx % 2
            nx += 1
            xk = K_(f'xb{s}')
            xbs = x['xb'][s]
            P.dma('sp' if d == 0 else 'act', xbs[0:64, :], self.kvn[r0:r0 + 64, 0:768], writes=[xk])
            self.tt('dve', Gm[0:64], bc_last(lad[0:64, c, :], 64), bc_mid(m_cum, 8), ALU.mult, [K_('lad'), 'msk'], [K_('Gm')])
            yield
            x3 = xbs[0:64, 0:512].rearrange("p (h q) -> p h q", h=8)
            self.mm(B0[0:64, :], m_oth, Gm[0:64].rearrange("p h i -> p (h i)"), True, True, ['msk', K_('Gm')], [k0])
            for g in range(2):
                self.mm(B1[0:64, g * 64:(g + 1) * 64], bcT[:, g, tk], ccT[:, g, tk], True, True, ['bcT', 'ccT'], [k1])
            for g in range(2):
                self.mm(B2[0:64, g * 256:(g + 1) * 256], ccT[:, g, tk], hT[:, g * 4:(g + 1) * 4, :].rearrange("p h q -> p (h q)"),
                        True, True, ['ccT', K_('hT')], [k2])
            self.tt('pool', xdt[0:64], x3, bc_last(dtd[0:64, c, :], 64), ALU.mult, [xk, K_('dtd')], [K_('xdt')])
            yield
            self.act(LT[0:64], B0[0:64, :].rearrange("p (h i) -> p h i", h=8), AF.Exp, [k0], [K_('LT')])
            self.tt('dve', yo[0:64], B2[0:64, :].rearrange("p (h q) -> p h q", h=8), bc_last(ecs[0:64, c, :], 64), ALU.mult,
                    [k2, K_('ecs')], [K_('yo')])
            self.tt('pool', xd2[0:64], xdt[0:64], bc_last(edec[0:64, c, :], 64), ALU.mult, [K_('xdt'), K_('edec')], [K_('xd2')])
            yield
            self.tt('pool', LT[0:64], LT[0:64], bc_mid(m_incl, 8), ALU.mult, [K_('LT'), 'msk'], [K_('LT')])
            for g in range(2):
                self.mm(B3[:, g * 256:(g + 1) * 256], xbs[0:64, 512 + g * 128:512 + (g + 1) * 128],
                        xd2[0:64, g * 4:(g + 1) * 4, :].rearrange("p h q -> p (h q)"), True, True, [xk, K_('xd2')], [k3])
            yield
            for g in range(2):
                self.tt('dve', MT[0:64, g * 4:(g + 1) * 4, :], LT[0:64, g * 4:(g + 1) * 4, :], bc_mid(B1[0:64, g * 64:(g + 1) * 64], 4),
                        ALU.mult, [K_('LT'), k1], [K_('MT')])
            self.tt('pool', hT, hT, bc_last(cdec[:, c, :], 64), ALU.mult, [K_('hT'), K_('cdec')], [K_('hT')])
            yield
            for hh in range(8):
                self.mm(B2[0:64, hh * 64:(hh + 1) * 64], MT[0:64, hh, :], xdt[0:64, hh, :], True, True, [K_('MT'), K_('xdt')], [k2])
            self.tt('dve', hT, hT, B3.rearrange("p (h q) -> p h q", h=8), ALU.add, [K_('hT'), k3], [K_('hT')])
            yield
            self.tt('dve', yy[0:64], yo[0:64], B2[0:64, :].rearrange("p (h q) -> p h q", h=8), ALU.add, [K_('yo'), k2], [K_('yy')])
            yield
            P.dma('sp' if d == 0 else 'act', ydst[d][r0:r0 + 64, :], yy[0:64].rearrange("p h q -> p (h q)"), reads=[K_('yy')],
                  writes=[f'ydram{d}'])
        if not is_s:
            hs = h0
            for hh in range(8):
                self.tr(B0[0:64, (hh % 4) * 128:(hh % 4 + 1) * 128] if hh < 4 else B1[0:64, (hh % 4) * 128:(hh % 4 + 1) * 128],
                        hT[:, hh, :], self.ident, [K_('hT'), 'ident'], [k0 if hh < 4 else k1])
            yield
            self.cp('dve', hs[0:64, 0:4, :], B0[0:64, :].rearrange("p (h n) -> p h n", h=4), [k0], [K_('h0')])
            self.cp('act', hs[0:64, 4:8, :], B1[0:64, :].rearrange("p (h n) -> p h n", h=4), [k1], [K_('h0')])
            yield
            P.dma('sp', self.o_ssd[si, j, d].rearrange("h p n -> p h n"), hs[0:64], reads=[K_('h0')])

    for si, (t0, T, is_s) in enumerate(SEQS):
        nch = T // 64
        for g in range(2):
            P.dma('sp', bcT[:, g, 0:T], self.qkn[g * 128:(g + 1) * 128, t0:t0 + T], writes=['bcT'])
            P.dma('act', ccT[:, g, 0:T], self.qkn[256 + g * 128:256 + (g + 1) * 128, t0:t0 + T], writes=['ccT'])
        for c4 in range(0, nch, 4):
            P.dma('sp', sca[0:64, c4:c4 + 4, :], self.sb[t0 + c4 * 64:t0 + (c4 + 4) * 64, :].rearrange("(c t) f -> t c f", t=64), writes=['sca'])
        gens = [chain(0, si, t0, T, is_s), chain(1, si, t0, T, is_s)]
        while gens:
            for g_ in list(gens):
                try:
                    next(g_)
                except StopIteration:
                    gens.remove(g_)
    P.barrier()
    A.reset(self.const_end)
    snw = A.alloc([512])
    self.load_bcast(snw, self.ssd_norm[j, :], 'snw')
    dsk = A.alloc([8])
    self.load_bcast(dsk, self.ssd_d[j, :], 'dsk')
    ya = [A.alloc([512]) for _ in range(2)]
    yb_ = [A.alloc([512]) for _ in range(2)]
    xx = [A.alloc([512]) for _ in range(2)]
    zz = [A.alloc([512]) for _ in range(2)]
    sq = A.alloc([512])
    st = A.alloc([4])
    for t in range(36):
        s = t % 2
        rr = slice(t * 128, (t + 1) * 128)
        P.dma('sp', ya[s], self.yf[rr, :], writes=[f'ya{s}'])
        P.dma('act', yb_[s], self.yb[rr, :], writes=[f'yb{s}'])
        P.dma('sp', xx[s], self.kvn[rr, 0:512], writes=[f'xx{s}'])
        P.dma('act', zz[s], self.projN[rr, 0:512], writes=[f'zz{s}'])
        self.tt('dve', ya[s], ya[s], yb_[s], ALU.add, [f'ya{s}', f'yb{s}'], [f'ya{s}'])
        x3 = xx[s].rearrange("p (h q) -> p h q", h=8)
        self.tt('pool', x3, x3, bc_last(dsk, 64), ALU.mult, [f'xx{s}', 'dsk'], [f'xx{s}'])
        self.act(zz[s], zz[s], AF.Silu, [f'zz{s}'], [f'zz{s}'])
        self.tt('dve', ya[s], ya[s], xx[s], ALU.add, [f'ya{s}', f'xx{s}'], [f'ya{s}'])
        self.tt('dve', ya[s], ya[s], zz[s], ALU.mult, [f'ya{s}', f'zz{s}'], [f'ya{s}'])
        self.act(sq, ya[s], AF.Square, [f'ya{s}'], ['sq', f'ssq{s}'], accum_out=st[:, s:s + 1])
        self.act(st[:, 2 + s:3 + s], st[:, s:s + 1], AF.Sqrt, [f'ssq{s}'], [f'rsq{s}'], bias=EPS, scale=1.0 / 512)
        self.recip(st[:, 2 + s:3 + s], st[:, 2 + s:3 + s], [f'rsq{s}'], [f'rsq{s}'])
        self.stt('dve', yb_[s], ya[s], st[:, 2 + s:3 + s], snw, ALU.mult, ALU.mult, [f'ya{s}', f'rsq{s}', 'snw'], [f'yb{s}'])
        P.dma('sp', self.mixN[rr, :], yb_[s], reads=[f'yb{s}'])


def _na_sample(self, l):
    j = l // 2
    A, P = self.A, self.P
    B = self.banks
    t0 = 512
    scale = 128 ** -0.5
    BIG = 30000.0
    kdT = A.alloc([4, 4096], BF16)
    qdT = A.alloc([4, 4096], BF16)
    Ve = A.alloc([32, 512], BF16)
    Vo = A.alloc([31, 512], BF16)
    kcT = A.alloc([4, 512], BF16)
    Vc = A.alloc([4, 512], BF16)
    Bt = A.alloc([60, 64])
    nm = A.alloc([64])
    ngm = A.alloc([64])
    Z = A.alloc([160])
    ctk = A.alloc([4, 128])
    for h in range(4):
        P.dma('pool', kdT[:, h, :], self.projT[2064 + h * 128:2064 + (h + 1) * 128, t0:t0 + 4096], writes=['kdT'])
        P.dma('pool', qdT[:, h, :], self.projT[1552 + h * 128:1552 + (h + 1) * 128, t0:t0 + 4096], writes=['qdT'])
    for a4 in range(0, 32, 4):
        P.dma('pool', Ve[:, a4:a4 + 4, :], self.projN[t0 + a4 * 128:t0 + (a4 + 4) * 128, 2576:3088].rearrange("(a p) f -> p a f", p=128), writes=['Ve'])
    for a4 in range(0, 31, 4):
        n = min(4, 31 - a4)
        P.dma('pool', Vo[:, a4:a4 + n, :], self.projN[t0 + 64 + a4 * 128:t0 + 64 + (a4 + n) * 128, 2576:3088].rearrange("(a p) f -> p a f", p=128), writes=['Vo'])
    P.dma('pool', Vc, self.cnv[j].rearrange("(a p) h d -> p a (h d)", p=128), writes=['Vc'])
    for h in range(4):
        P.dma('sp', ctk, self.cnk[j, :, h, :].rearrange("(a p) d -> p a d", p=128), writes=['ctk'])
        for a in range(4):
            self.tr(B[0][:, a * 128:(a + 1) * 128], ctk[:, a, :], self.ident, ['ctk', 'ident'], ['b0'])
        self.cp('act', kcT[:, h, :], B[0], ['b0'], ['kcT'])
    self.memset('pool', Z[0:60, :], 0.0, ['Z'])
    P.dma('sp', Z[0:60, 64:95], self.na_rpb[j], writes=['Z'])
    tz = P.dma('sp', self.rpbpad, Z[0:60, :], reads=['Z'], writes=['rpbpad'])
    for q in range(64):
        src = bass.AP(self.rpbpad.tensor, 79 - q, [[0, 1], [160, 60], [1, 64]])
        P.dma('sp' if q % 2 else 'act', Bt[q:q + 1, :, :], src, reads=['rpbpad'], writes=['Bt'])
    P.dma('sp', nm[0:64, :], self.c_namask, writes=['nm'])
    self.ts('dve', ngm[0:64, :], nm[0:64, :], -1.0, BIG, ALU.add, ALU.mult, ['nm'], ['ngm'])
    self.tt('dve', Bt[0:64], Bt[0:64], bc_mid(nm[0:64, :], 60), ALU.mult, ['Bt', 'nm'], ['Bt'])
    self.tt('dve', Bt[0:64], Bt[0:64], bc_mid(ngm[0:64, :], 60), ALU.add, ['Bt', 'ngm'], ['Bt'])
    idb64 = self.identb[0:64, 0:64]
    XN = []
    for ch in range(2):
        XN.append(dict(ssb=A.alloc([1024]), pf=A.alloc([1024]), pb=A.alloc([1024], BF16), pT=[A.alloc([512], BF16) for _ in range(2)],
                       st=A.alloc([4]), osb=A.alloc([512])))

    def chain(cn):
        x = XN[cn]
        K_ = lambda n: f'{n}_{cn}'
        BS, BC, BT, BO = B[4 * cn:4 * cn + 4]
        kS, kC, kT, kO = [f'bk{4 * cn + i}' for i in range(4)]
        ptb = BT.bitcast(BF16)
        ssb, pf, pb, st, osb = x['ssb'], x['pf'], x['pb'], x['st'], x['osb']
        heads = (2 * cn, 2 * cn + 1)
        npt = 0
        for r in range(64):
            r0 = min(max(r - 4, 0), 56)
            dr0 = r0 - r + 7
            Vx, ti0, vk = (Ve, r0 // 2, 'Ve') if r0 % 2 == 0 else (Vo, (r0 - 1) // 2, 'Vo')
            for hi, h in enumerate(heads):
                qs = qdT[:, h, r * 64:(r + 1) * 64]
                self.mm(BS[0:64, :], qs, kdT[:, h, r0 * 64:r0 * 64 + 512], True, True, ['qdT', 'kdT'], [kS])
                self.mm(BC[0:64, :], qs, kcT[:, h, :], True, True, ['qdT', 'kcT'], [kC])
                yield
                self.stt('dve', ssb[0:64, 0:512].rearrange("p (a k) -> p a k", a=8), BS[0:64, :].rearrange("p (a k) -> p a k", a=8), scale,
                         Bt[0:64, h * 15 + dr0:h * 15 + dr0 + 8, :], ALU.mult, ALU.add, [kS, 'Bt'], [K_('ssb')])
                self.act(ssb[0:64, 512:1024], BC[0:64, :], AF.Copy, [kC], [K_('ssb')], scale=scale)
                yield
                self.P.op('dve', lambda e, o=st[0:64, 0:1], i=ssb[0:64, :]: e.tensor_reduce(out=o, in_=i, axis=AX.X, op=ALU.max),
                          [K_('ssb')], [K_('mx')])
                yield
                self.ts('dve', st[0:64, 1:2], st[0:64, 0:1], -1.0, None, ALU.mult, None, [K_('mx')], [K_('nmx')])
                yield
                self.act(pf[0:64, :], ssb[0:64, :], AF.Exp, [K_('ssb'), K_('nmx')], [K_('pf'), K_('sm')], bias=st[0:64, 1:2], accum_out=st[0:64, 2:3])
                yield
                self.recip(st[0:64, 3:4], st[0:64, 2:3], [K_('sm')], [K_('rsm')])
                yield
                self.ts('dve', pb[0:64, :], pf[0:64, :], st[0:64, 3:4], None, ALU.mult, None, [K_('pf'), K_('rsm')], [K_('pb')])
                yield
                for kc in range(8):
                    self.tr(ptb[:, kc * 64:(kc + 1) * 64], pb[0:64, kc * 128:(kc + 1) * 128], idb64, [K_('pb'), 'identb'], [kT])
                yield
                pt_ = x['pT'][npt % 2]
                pk = K_(f'pT{npt % 2}')
                npt += 1
                self.cp('act', pt_, ptb[:, 0:512], [kT], [pk])
                yield
                oc = hi * 256 + (r % 4) * 64
                for jj in range(8):
                    lhs = Vx[:, ti0 + jj, h * 128:(h + 1) * 128] if jj < 4 else Vc[:, jj - 4, h * 128:(h + 1) * 128]
                    self.mm(BO[:, oc:oc + 64], lhs, pt_[:, jj * 64:(jj + 1) * 64], jj == 0, jj == 7, [vk, 'Vc', pk], [kO])
                yield
            if r % 4 == 3:
                self.cp('dve' if cn else 'act', osb, BO, [kO], [K_('osb')])
                yield
                for hi, h in enumerate(heads):
                    P.dma('sp' if cn else 'act', self.mixT[h * 128:(h + 1) * 128, t0 + (r - 3) * 64:t0 + (r + 1) * 64],
                          osb[:, hi * 256:(hi + 1) * 256], reads=[K_('osb')])

    gens = [chain(0), chain(1)]
    while gens:
        for g_ in list(gens):
            try:
                next(g_)
            except StopIteration:
                gens.remove(g_)


def phase_odd_mix(self, l):
    j = l // 2
    self.phase_begin()
    _ssd(self, l)
    if self.upto == 'S1':
        return
    self.P.barrier()
    self.A.reset(self.const_end)
    _attn_dense(self, l, 1552, 2064, 2576, 4, 4, None, None, False, self.o_nk, self.o_nv, only_prompts=True)
    self.P.barrier()
    self.A.reset(self.const_end)
    _na_sample(self, l)


K.phase_odd_mix = phase_odd_mix
```

```python
import numpy as np
import concourse.bass as bass
import concourse.mybir as mybir
from concourse.bass_utils import run_bass_kernel_spmd

F32 = mybir.dt.float32
BF16 = mybir.dt.bfloat16
AF = mybir.ActivationFunctionType
ALU = mybir.AluOpType
AX = mybir.AxisListType

ENGS = ('pe', 'act', 'dve', 'pool', 'sp')
SEM_ROT = 20000
NDSEM = 12

D = 1024
NT = 4608
EPS = 1e-6
SEQS = [(0, 256, False), (256, 256, False), (512, 4096, True)]
EV_IN = 3088


class Prog:
    def __init__(self, nc, same_engine_sync=True):
        self.nc = nc
        self.same = same_engine_sync
        self.stream = {e: [] for e in ENGS}
        self.nsem = 0
        self.esem = {}
        self.ecnt = {}
        self.allsems = []
        for e in ENGS:
            self._new_esem(e)
        self.known = {e: {} for e in ENGS}
        self.res = {}
        self.dsems = {}
        self.dpos = {}
        self.n_ops = 0

    def _alloc_sem(self, name):
        self.nsem += 1
        s = self.nc.alloc_semaphore(name=f"{name}_{self.nsem}")
        return s

    def _new_esem(self, e):
        self.esem[e] = self._alloc_sem(f"e_{e}")
        self.ecnt[e] = 0

    def _deps(self, eng, reads, writes):
        deps = {}

        def add(tok):
            if tok is None:
                return
            sem, val, src = tok
            if src == eng and (eng == 'pe' or not self.same):
                return
            k = id(sem)
            if k not in deps or deps[k][1] < val:
                deps[k] = (sem, val)
        for r in reads:
            st = self.res.get(r)
            if st:
                add(st[0])
        for w in writes:
            st = self.res.get(w)
            if st:
                add(st[0])
                for t in st[1]:
                    add(t)
        out = []
        kn = self.known[eng]
        for k, (sem, val) in deps.items():
            if kn.get(k, 0) >= val:
                continue
            kn[k] = val
            out.append((sem, val))
        return out

    def _commit(self, tok, reads, writes):
        for w in writes:
            self.res[w] = [tok, []]
        for r in reads:
            st = self.res.setdefault(r, [None, []])
            st[1].append(tok)
            if len(st[1]) > 48:
                best = {}
                for t in st[1]:
                    k = id(t[0])
                    if k not in best or best[k][1] < t[1]:
                        best[k] = t
                st[1] = list(best.values())

    def op(self, eng, fn, reads=(), writes=()):
        waits = self._deps(eng, reads, writes)
        if self.ecnt[eng] >= SEM_ROT:
            self._new_esem(eng)
        sem = self.esem[eng]
        self.ecnt[eng] += 1
        tok = (sem, self.ecnt[eng], eng)
        self.stream[eng].append((waits, fn, sem, 1))
        self._commit(tok, reads, writes)
        self.n_ops += 1
        return tok

    def dma(self, q, out, in_, reads=(), writes=(), **kw):
        waits = self._deps(q, reads, writes)
        if q not in self.dsems:
            self.dsems[q] = [[self._alloc_sem(f"d_{q}"), 0] for _ in range(NDSEM)]
            self.dpos[q] = 0
        slot = self.dsems[q][self.dpos[q] % NDSEM]
        self.dpos[q] += 1
        sem = slot[0]
        kn = self.known[q]
        if slot[1] > 0 and kn.get(id(sem), 0) < slot[1]:
            waits.append((sem, slot[1]))
            kn[id(sem)] = slot[1]
        slot[1] += 16
        tok = (sem, slot[1], 'dma_' + q)
        self.stream[q].append((waits, lambda e, o=out, i=in_, k=kw: e.dma_start(out=o, in_=i, **k), sem, 16))
        self._commit(tok, reads, writes)
        self.n_ops += 1
        return tok

    def barrier(self):
        pts = []
        for e in ENGS:
            if self.ecnt[e] > 0:
                pts.append((self.esem[e], self.ecnt[e]))
        for q in self.dsems:
            for s in self.dsems[q]:
                if s[1] > 0:
                    pts.append((s[0], s[1]))
        for e in ENGS:
            kn = self.known[e]
            w = []
            for (s, v) in pts:
                if kn.get(id(s), 0) < v:
                    kn[id(s)] = v
                    w.append((s, v))
            if w:
                self.stream[e].append((w, None, None, 0))
        self.res = {}

    def emit(self):
        nc = self.nc
        with nc.Block() as block:
            def run(e, name):
                for waits, fn, sem, inc in self.stream[name]:
                    for (s, v) in waits:
                        e.wait_ge(s, v)
                    if fn is not None:
                        fn(e).then_inc(sem, inc)

            @block.tensor
            def _(e):
                run(e, 'pe')

            @block.scalar
            def _(e):
                run(e, 'act')

            @block.vector
            def _(e):
                run(e, 'dve')

            @block.gpsimd
            def _(e):
                run(e, 'pool')

            @block.sync
            def _(e):
                run(e, 'sp')


class Arena:
    def __init__(self, ap, nwords):
        self.ap = ap
        self.n = nwords
        self.off = 0
        self.uid = 0

    def reset(self, off=0):
        self.off = off

    def alloc(self, shape, dtype=F32):
        n = int(np.prod(shape))
        words = n if dtype == F32 else (n + 1) // 2
        words = (words + 7) // 8 * 8
        assert self.off + words <= self.n, f"arena overflow {self.off}+{words}>{self.n}"
        a = self.ap[:, self.off:self.off + words]
        self.off += words
        if dtype != F32:
            a = a.bitcast(dtype)
        a = a[:, 0:n]
        if len(shape) == 2:
            a = a.rearrange("p (a b) -> p a b", a=shape[0])
        elif len(shape) == 3:
            a = a.rearrange("p (a b c) -> p a b c", a=shape[0], b=shape[1])
        return a


def bc_last(a, n):
    return bass.AP(a.tensor, a.offset, [list(x) for x in a.ap] + [[0, n]])


def bc_mid(a, n):
    return bass.AP(a.tensor, a.offset, [list(a.ap[0]), [0, n]] + [list(x) for x in a.ap[1:]])


class K:
    def __init__(self, n_layers=4, upto=None, taps=()):
        self.n_layers = n_layers
        self.upto = upto
        self.taps = taps
        nc = bass.Bass("TRN2", target_bir_lowering=False)
        self.nc = nc
        self.P = Prog(nc)
        self.uid = 0
        di = lambda name, shape: nc.dram_tensor(name, list(shape), F32, kind="ExternalInput").ap()
        do = lambda name, shape: nc.dram_tensor(name, list(shape), F32, kind="ExternalOutput").ap()
        ds = lambda name, shape: nc.dram_tensor(name, list(shape), F32, kind="Internal").ap()
        self.xin = di("xin", [NT, D])
        self.cvec = di("cvec", [2, D])
        self.state_gdn = di("state_gdn", [2, 2, 4, 128, 128])
        self.cgk = di("cache_gqa_k", [2, 512, 2, 128])
        self.cgv = di("cache_gqa_v", [2, 512, 2, 128])
        self.state_ssd = di("state_ssd", [2, 2, 8, 64, 128])
        self.cnk = di("cache_na_k", [2, 512, 4, 128])
        self.cnv = di("cache_na_v", [2, 512, 4, 128])
        self.ada_w = di("ada_w", [4, D, 6 * D])
        self.ada_b = di("ada_b", [4, 6 * D])
        self.norms = {n: di(n, [4, D]) for n in ("norm_mix_pre", "norm_mix_post", "norm_mlp_pre", "norm_mlp_post")}
        self.mlp_w1 = di("mlp_w1", [4, D, 4 * D])
        self.mlp_w2 = di("mlp_w2", [4, 4 * D, D])
        self.ev_w_in = di("ev_w_in", [2, D, EV_IN])
        self.ev_w_out = di("ev_w_out", [2, D, D])
        self.gdn_conv = di("gdn_conv", [2, 3, 1536])
        self.gdn_a_log = di("gdn_a_log", [2, 8])
        self.gdn_dt_bias = di("gdn_dt_bias", [2, 8])
        self.gdn_norm = di("gdn_norm", [2, 128])
        self.gqa_q_norm = di("gqa_q_norm", [2, 128])
        self.gqa_k_norm = di("gqa_k_norm", [2, 128])
        self.od_w_in = di("od_w_in", [2, D, EV_IN])
        self.od_w_out = di("od_w_out", [2, D, D])
        self.ssd_conv = di("ssd_conv", [2, 3, 1024])
        self.ssd_conv_b = di("ssd_conv_b", [2, 1024])
        self.ssd_a_log = di("ssd_a_log", [2, 16])
        self.ssd_dt_bias = di("ssd_dt_bias", [2, 16])
        self.ssd_d = di("ssd_d", [2, 8])
        self.ssd_norm = di("ssd_norm", [2, 512])
        self.na_rpb = di("na_rpb", [2, 60, 31])
        self.c_ident = di("c_ident", [128, 128])
        self.c_masks = di("c_masks", [8, 64, 64])
        self.c_rope = di("c_rope", [2, 128, 4096])
        self.c_perm = di("c_perm", [128, 128])
        self.c_namask = di("c_namask", [64, 64])
        self.yout = do("yout", [NT, D])
        self.o_gdn = do("o_gdn", [2, 2, 2, 4, 128, 128])
        self.o_gk = do("o_gk", [2, 2, 256, 2, 128])
        self.o_gv = do("o_gv", [2, 2, 256, 2, 128])
        self.o_ssd = do("o_ssd", [2, 2, 2, 8, 64, 128])
        self.o_nk = do("o_nk", [2, 2, 256, 4, 128])
        self.o_nv = do("o_nv", [2, 2, 256, 4, 128])
        self.xres = ds("xres", [NT, D])
        self.modrow = ds("modrow", [4, 2, 6 * D])
        self.projT = ds("projT", [3200, NT])
        self.projN = ds("projN", [NT, EV_IN])
        self.mixN = ds("mixN", [NT, 512])
        self.mixT = ds("mixT", [512, NT])
        self.qkn = ds("qkn", [1024, NT])
        self.kvn = ds("kvn", [NT, 1024])
        self.rpbpad = ds("rpbpad", [60, 160])
        self.gb = ds("gb", [NT, 16])
        self.sb = ds("sb", [NT, 32])
        self.yf = ds("yf", [NT, 512])
        self.yb = ds("yb", [NT, 512])
        self.og0 = ds("og0", [NT, 512])
        self.og1 = ds("og1", [NT, 512])
        self.tapbufs = {}
        for (name, shape) in taps:
            self.tapbufs[name] = do("tap_" + name, shape)
        self.NW = 53000
        self.arena_t = nc.alloc_sbuf_tensor("arena", [128, self.NW], F32)
        self.A = Arena(self.arena_t.ap() if hasattr(self.arena_t, 'ap') else self.arena_t[:, :], self.NW)
        self.banks = []
        for i in range(8):
            t = nc.alloc_psum_tensor(f"bank{i}", [128, 512], F32)
            self.banks.append(t.ap() if hasattr(t, 'ap') else t[:, :])

    def bk(self, w, *aps):
        w = list(w)
        for a in aps:
            nm = getattr(getattr(a, 'tensor', None), 'name', '')
            if isinstance(nm, str) and nm.startswith('bank'):
                k = 'X' + nm
                if k not in w:
                    w.append(k)
        return w

    def mm(self, out, lhsT, rhs, start, stop, r, w):
        return self.P.op('pe', lambda e: e.matmul(out, lhsT=lhsT, rhs=rhs, start=start, stop=stop), r, self.bk(w, out))

    def tr(self, out, in_, ident, r, w):
        return self.P.op('pe', lambda e: e.transpose(out, in_, ident), r, self.bk(w, out))

    def act(self, out, in_, func, r, w, **kw):
        return self.P.op('act', lambda e: e.activation(out=out, in_=in_, func=func, **kw), r, self.bk(w, out, in_))

    def ts(self, eng, out, in0, s1, s2, op0, op1, r, w):
        w = self.bk(w, out, in0)
        if op1 is None:
            return self.P.op(eng, lambda e: e.tensor_scalar(out=out, in0=in0, scalar1=s1, scalar2=None, op0=op0), r, w)
        return self.P.op(eng, lambda e: e.tensor_scalar(out=out, in0=in0, scalar1=s1, scalar2=s2, op0=op0, op1=op1), r, w)

    def tt(self, eng, out, in0, in1, op, r, w):
        return self.P.op(eng, lambda e: e.tensor_tensor(out=out, in0=in0, in1=in1, op=op), r, self.bk(w, out, in0, in1))

    def stt(self, eng, out, in0, scalar, in1, op0, op1, r, w):
        return self.P.op(eng, lambda e: e.scalar_tensor_tensor(out=out, in0=in0, scalar=scalar, in1=in1, op0=op0, op1=op1), r,
                         self.bk(w, out, in0, in1))

    def recip(self, out, in_, r, w):
        return self.P.op('dve', lambda e: e.reciprocal(out=out, in_=in_), r, self.bk(w, out, in_))

    def cp(self, eng, out, in_, r, w):
        if eng == 'act':
            return self.act(out, in_, AF.Copy, r, w)
        return self.P.op(eng, lambda e: e.tensor_copy(out=out, in_=in_), r, self.bk(w, out, in_))

    def memset(self, eng, ap, val, w):
        return self.P.op(eng, lambda e: e.memset(ap, val), (), w)

    def key(self, name):
        self.uid += 1
        return f"{name}#{self.uid}"

    def tap(self, name, src_ap, reads):
        if name in self.tapbufs:
            self.P.dma('sp', self.tapbufs[name], src_ap, reads=reads)

    def phase_begin(self):
        self.P.barrier()
        A = self.A
        A.reset()
        self.ident = A.alloc([128])
        self.identb = A.alloc([128], BF16)
        self.P.dma('sp', self.ident, self.c_ident, writes=['ident'])
        self.cp('dve', self.identb, self.ident, ['ident'], ['identb'])
        self.ones = A.alloc([128])
        self.memset('pool', self.ones, 1.0, ['ones'])
        self.onesb = A.alloc([128], BF16)
        self.memset('pool', self.onesb, 1.0, ['onesb'])
        self.const_end = A.off

    def load_bcast(self, dst, row_ap, key, q='sp'):
        self.P.dma(q, dst, row_ap.partition_broadcast(128), writes=[key])

    def rstd_from_ss(self, rstd, ss, n, r, w):
        self.act(rstd, ss, AF.Sqrt, r, w, bias=EPS, scale=1.0 / n)
        self.P.op('dve', lambda e: e.reciprocal(out=rstd, in_=rstd), w, w)

    def phase_ada(self):
        self.phase_begin()
        A, P = self.A, self.P
        cT = A.alloc([2, 8])
        for v in range(2):
            P.dma('sp', cT[:, v, :], self.cvec[v, :].rearrange("(k p) -> p k", p=128), writes=['cT'],
                  allow_slow_non_contiguous=True)
        sc = A.alloc([2, 8])
        self.act(sc, cT, AF.Silu, ['cT'], ['sc'])
        L = A.alloc([2, 8, 128])
        for v in range(2):
            self.cp('dve', L[:, v, :, :], bc_last(sc[:, v, :], 128), ['sc'], ['L'])
        wbuf = [A.alloc([1536]) for _ in range(3)]
        bb = A.alloc([1536])
        ob = [A.alloc([1536]) for _ in range(2)]
        for l in range(self.n_layers):
            for g in range(4):
                c0 = g * 1536
                self.load_bcast(bb, self.ada_b[l, c0:c0 + 1536], 'bb', q='act')
                for k in range(8):
                    wb = wbuf[k % 3]
                    wk = f'wbuf{k % 3}'
                    P.dma('sp', wb, self.ada_w[l, k * 128:(k + 1) * 128, c0:c0 + 1536], writes=[wk])
                    for v in range(2):
                        for n in range(3):
                            self.mm(self.banks[v * 3 + n], L[:, v, k, :], wb[:, n * 512:(n + 1) * 512],
                                    k == 0, k == 7, [wk, 'L'], [f'bank{v * 3 + n}'])
                for v in range(2):
                    for n in range(3):
                        self.tt('dve', ob[v][:, n * 512:(n + 1) * 512], self.banks[v * 3 + n], bb[:, n * 512:(n + 1) * 512],
                                ALU.add, [f'bank{v * 3 + n}', 'bb'], [f'ob{v}'])
                    P.dma('act', self.modrow[l, v:v + 1, c0:c0 + 1536], ob[v][0:1, :], reads=[f'ob{v}'])

    def mod_vec(self, dst, l, v, idx, key, plus1=False):
        self.load_bcast(dst, self.modrow[l, v, idx * D:(idx + 1) * D], key, q='act')
        if plus1:
            self.ts('dve', dst, dst, 1.0, None, ALU.add, None, [key], [key])

    def phase_A(self, l, xsrc):
        even = (l % 2 == 0)
        j = l // 2
        w_in = self.ev_w_in[j] if even else self.od_w_in[j]
        if even:
            fm = [(c, 128) for c in range(0, 1024, 128)] + [(c, 128) for c in range(2064, 2832, 128)]
            tm = [(512, 512), (1024, 512), (1536, 512), (2048, 16), (2832, 256)]
        else:
            fm = [(c, 128) for c in range(1024, 1536, 128)] + [(c, 128) for c in range(1552, 2576, 128)]
            tm = [(0, 512), (512, 512), (1024, 256), (1536, 16), (2576, 512)]
        self.phase_begin()
        A, P = self.A, self.P
        W = A.alloc([8, EV_IN], BF16)
        for k in range(8):
            P.dma('pool', W[:, k, :], w_in[k * 128:(k + 1) * 128, :], writes=[f'W{k}'])
        Wk = [f'W{k}' for k in range(8)]
        tmpv = A.alloc([D])
        self.load_bcast(tmpv, self.norms["norm_mix_pre"][l, :], 'tmpv')
        A1 = []
        SH = []
        for v in range(2):
            a1 = A.alloc([D])
            sh = A.alloc([D])
            self.mod_vec(sh, l, v, 0, f'sh{v}')
            self.mod_vec(a1, l, v, 1, f'a1{v}', plus1=True)
            self.tt('dve', a1, a1, tmpv, ALU.mult, [f'a1{v}', 'tmpv'], [f'a1{v}'])
            A1.append(a1)
            SH.append(sh)
        xt = [A.alloc([D]) for _ in range(2)]
        junk = A.alloc([D], BF16)
        tmpf = A.alloc([D])
        hm = [A.alloc([D], BF16) for _ in range(2)]
        st = A.alloc([4])
        hmT = [A.alloc([8, 512], BF16) for _ in range(2)]
        stage = [A.alloc([512]) for _ in range(4)]
        ptr = [self.banks[0].bitcast(BF16), self.banks[1].bitcast(BF16)]
        nst = 0
        nps = 0
        for g in range(9):
            v = 0 if g == 0 else 1
            hT = hmT[g % 2]
            hk = f'hmT{g % 2}'
            for ti in range(4):
                t = g * 4 + ti
                s = t % 2
                P.dma('sp', xt[s], xsrc[t * 128:(t + 1) * 128, :], writes=[f'xt{s}'])
                self.act(junk, xt[s], AF.Square, [f'xt{s}'], ['junk', f'ss{s}'], accum_out=st[:, s:s + 1])
                self.rstd_from_ss(st[:, 2 + s:3 + s], st[:, s:s + 1], D, [f'ss{s}'], [f'rs{s}'])
                self.stt('dve', tmpf, xt[s], st[:, 2 + s:3 + s], A1[v], ALU.mult, ALU.mult,
                         [f'xt{s}', f'rs{s}', f'a1{v}'], ['tmpf'])
                self.tt('dve', hm[s], tmpf, SH[v], ALU.add, ['tmpf', f'sh{v}'], [f'hm{s}'])
                for k in range(8):
                    self.tr(ptr[s][:, k * 128:(k + 1) * 128], hm[s][:, k * 128:(k + 1) * 128], self.identb,
                            [f'hm{s}', 'identb'], [f'ptr{s}'])
                self.cp('act', hT[:, :, ti * 128:(ti + 1) * 128], ptr[s].rearrange("p (k t) -> p k t", k=8),
                        [f'ptr{s}'], [hk])
            for (c0, wd) in fm:
                b = 2 + nps % 3
                nps += 1
                for k in range(8):
                    self.mm(self.banks[b][0:wd, :], W[:, k, c0:c0 + wd], hT[:, k, :], k == 0, k == 7,
                            [Wk[k], hk], [f'bank{b}'])
                sg = nst % 4
                nst += 1
                self.cp('act' if nst % 2 else 'dve', stage[sg][0:wd, :], self.banks[b][0:wd, :], [f'bank{b}'], [f'stage{sg}'])
                P.dma('sp', self.projT[c0:c0 + wd, g * 512:(g + 1) * 512], stage[sg][0:wd, :], reads=[f'stage{sg}'])
            for ti in range(4):
                t = g * 4 + ti
                for (c0, wd) in tm:
                    b = 5 + nps % 3
                    nps += 1
                    for k in range(8):
                        self.mm(self.banks[b][:, 0:wd], hT[:, k, ti * 128:(ti + 1) * 128], W[:, k, c0:c0 + wd],
                                k == 0, k == 7, [Wk[k], hk], [f'bank{b}'])
                    sg = nst % 4
                    nst += 1
                    self.cp('act' if nst % 2 else 'dve', stage[sg][:, 0:wd], self.banks[b][:, 0:wd], [f'bank{b}'], [f'stage{sg}'])
                    P.dma('sp', self.projN[t * 128:(t + 1) * 128, c0:c0 + wd], stage[sg][:, 0:wd], reads=[f'stage{sg}'])

    def phase_C0(self, l, xsrc):
        even = (l % 2 == 0)
        j = l // 2
        w_out = self.ev_w_out[j] if even else self.od_w_out[j]
        self.phase_begin()
        A, P = self.A, self.P
        W = A.alloc([8, D], BF16)
        for k in range(8):
            P.dma('pool', W[:, k, :], w_out[k * 128:(k + 1) * 128, :], writes=[f'W{k}'])
        tmpv = A.alloc([D])
        self.load_bcast(tmpv, self.norms["norm_mix_post"][l, :], 'tmpv')
        G1 = []
        for v in range(2):
            g1 = A.alloc([D])
            self.mod_vec(g1, l, v, 2, f'g1{v}')
            self.tt('dve', g1, g1, tmpv, ALU.mult, [f'g1{v}', 'tmpv'], [f'g1{v}'])
            G1.append(g1)
        xt = [A.alloc([D]) for _ in range(2)]
        mn = [A.alloc([512], BF16) for _ in range(2)]
        mT = [A.alloc([4, 512], BF16) for _ in range(2)]
        mTn = [A.alloc([4, 128], BF16) for _ in range(2)]
        junk = A.alloc([D], BF16)
        tmpf = A.alloc([D])
        st = A.alloc([8])
        ptr = [self.banks[0].bitcast(BF16), self.banks[1].bitcast(BF16)]
        for g in range(9):
            v = 0 if g == 0 else 1
            gs = g % 2
            for k in range(4):
                P.dma('pool', mT[gs][:, k, :], self.mixT[k * 128:(k + 1) * 128, g * 512:(g + 1) * 512], writes=[f'mT{gs}'])
            for ti in range(4):
                t = g * 4 + ti
                s = t % 2
                P.dma('sp', xt[s], xsrc[t * 128:(t + 1) * 128, :], writes=[f'xt{s}'])
                P.dma('pool', mn[s], self.mixN[t * 128:(t + 1) * 128, :], writes=[f'mn{s}'])
                for k in range(4):
                    self.tr(ptr[s][:, k * 128:(k + 1) * 128], mn[s][:, k * 128:(k + 1) * 128], self.identb,
                            [f'mn{s}', 'identb'], [f'ptr{s}'])
                self.cp('act', mTn[s], ptr[s][:, 0:512].rearrange("p (k t) -> p k t", k=4), [f'ptr{s}'], [f'mTn{s}'])
                for h in range(2):
                    b = 2 + 2 * s + h
                    for k in range(8):
                        lhs = mTn[s][:, k, :] if k < 4 else mT[gs][:, k - 4, ti * 128:(ti + 1) * 128]
                        self.mm(self.banks[b], lhs, W[:, k, h * 512:(h + 1) * 512], k == 0, k == 7,
                                [f'W{k}', f'mTn{s}', f'mT{gs}'], [f'bank{b}'])
                    self.act(junk[:, 0:512], self.banks[b], AF.Square, [f'bank{b}'], ['junk', f'ss{s}{h}'],
                             accum_out=st[:, 2 * s + h:2 * s + h + 1])
                self.tt('dve', st[:, 4 + s:5 + s], st[:, 2 * s:2 * s + 1], st[:, 2 * s + 1:2 * s + 2], ALU.add,
                        [f'ss{s}0', f'ss{s}1'], [f'sst{s}'])
                self.rstd_from_ss(st[:, 6 + s:7 + s], st[:, 4 + s:5 + s], D, [f'sst{s}'], [f'rs{s}'])
                for h in range(2):
                    b = 2 + 2 * s + h
                    self.stt('dve', tmpf[:, h * 512:(h + 1) * 512], self.banks[b], st[:, 6 + s:7 + s],
                             G1[v][:, h * 512:(h + 1) * 512], ALU.mult, ALU.mult, [f'bank{b}', f'rs{s}', f'g1{v}'], ['tmpf'])
                self.tt('pool', xt[s], xt[s], tmpf, ALU.add, [f'xt{s}', 'tmpf'], [f'xt{s}'])
                P.dma('sp', self.xres[t * 128:(t + 1) * 128, :], xt[s], reads=[f'xt{s}'])

    def phase_C1(self, l, xdst):
        self.phase_begin()
        A, P = self.A, self.P
        W1 = A.alloc([8, 4 * D], BF16)
        W2 = A.alloc([32, D], BF16)
        for k in range(8):
            P.dma('pool', W1[:, k, :], self.mlp_w1[l, k * 128:(k + 1) * 128, :], writes=[f'W1{k}'])
        for k in range(32):
            P.dma('pool', W2[:, k, :], self.mlp_w2[l, k * 128:(k + 1) * 128, :], writes=[f'W2{k}'])
        W1k = [f'W1{k}' for k in range(8)]
        cur = {}
        cur['sh'] = A.alloc([D])
        cur['a2'] = A.alloc([D])
        cur['g2'] = A.alloc([D])
        xt = [A.alloc([D]) for _ in range(2)]
        junk = A.alloc([D], BF16)
        tmpf = A.alloc([D])
        hm = [A.alloc([D], BF16) for _ in range(2)]
        st = A.alloc([16])
        hmT = A.alloc([8, 512], BF16)
        h1T = A.alloc([32, 512], BF16)
        r1 = [A.alloc([512]) for _ in range(2)]
        ptr = [self.banks[0].bitcast(BF16), self.banks[1].bitcast(BF16)]

        def load_vecs(v):
            if cur.get('v') == v:
                return
            cur['v'] = v
            self.mod_vec(cur['sh'], l, v, 3, 'sh2')
            self.mod_vec(cur['a2'], l, v, 4, 'a2', plus1=True)
            self.load_bcast(tmpf, self.norms["norm_mlp_pre"][l, :], 'tmpf')
            self.tt('dve', cur['a2'], cur['a2'], tmpf, ALU.mult, ['a2', 'tmpf'], ['a2'])
            self.mod_vec(cur['g2'], l, v, 5, 'g2')
            self.load_bcast(tmpf, self.norms["norm_mlp_post"][l, :], 'tmpf')
            self.tt('dve', cur['g2'], cur['g2'], tmpf, ALU.mult, ['g2', 'tmpf'], ['g2'])

        nx = 0
        for g in range(9):
            v = 0 if g == 0 else 1
            load_vecs(v)
            for ti in range(4):
                t = g * 4 + ti
                s = nx % 2
                nx += 1
                P.dma('sp', xt[s], self.xres[t * 128:(t + 1) * 128, :], writes=[f'xt{s}'])
                self.act(junk, xt[s], AF.Square, [f'xt{s}'], ['junk', f'ss{s}'], accum_out=st[:, s:s + 1])
                self.rstd_from_ss(st[:, 2 + s:3 + s], st[:, s:s + 1], D, [f'ss{s}'], [f'rs{s}'])
                self.stt('dve', tmpf, xt[s], st[:, 2 + s:3 + s], cur['a2'], ALU.mult, ALU.mult,
                         [f'xt{s}', f'rs{s}', 'a2'], ['tmpf'])
                self.tt('dve', hm[s], tmpf, cur['sh'], ALU.add, ['tmpf', 'sh2'], [f'hm{s}'])
                for k in range(8):
                    self.tr(ptr[s][:, k * 128:(k + 1) * 128], hm[s][:, k * 128:(k + 1) * 128], self.identb,
                            [f'hm{s}', 'identb'], [f'ptr{s}'])
                self.cp('act', hmT[:, :, ti * 128:(ti + 1) * 128], ptr[s].rearrange("p (k t) -> p k t", k=8),
                        [f'ptr{s}'], ['hmT'])
            for f in range(32):
                b = 2 + f % 2
                for k in range(8):
                    self.mm(self.banks[b], W1[:, k, f * 128:(f + 1) * 128], hmT[:, k, :], k == 0, k == 7,
                            [W1k[k], 'hmT'], [f'bank{b}'])
                rs = f % 2
                self.act(r1[rs], self.banks[b], AF.Relu, [f'bank{b}'], [f'r1{rs}'])
                self.tt('dve' if f % 4 < 3 else 'pool', h1T[:, f, :], r1[rs], r1[rs], ALU.mult, [f'r1{rs}'], [f'h1T{f}'])
            for ti in range(4):
                t = g * 4 + ti
                s = nx % 2
                nx += 1
                P.dma('sp', xt[s], self.xres[t * 128:(t + 1) * 128, :], writes=[f'xt{s}'])
                for h in range(2):
                    b = 4 + 2 * s + h
                    for f in range(32):
                        self.mm(self.banks[b], h1T[:, f, ti * 128:(ti + 1) * 128], W2[:, f, h * 512:(h + 1) * 512],
                                f == 0, f == 31, [f'W2{f}', f'h1T{f}'], [f'bank{b}'])
                    self.act(junk[:, 0:512], self.banks[b], AF.Square, [f'bank{b}'], ['junk', f'q{s}{h}'],
                             accum_out=st[:, 4 + 2 * s + h:5 + 2 * s + h])
                self.tt('dve', st[:, 8 + s:9 + s], st[:, 4 + 2 * s:5 + 2 * s], st[:, 5 + 2 * s:6 + 2 * s], ALU.add,
                        [f'q{s}0', f'q{s}1'], [f'qt{s}'])
                self.rstd_from_ss(st[:, 10 + s:11 + s], st[:, 8 + s:9 + s], D, [f'qt{s}'], [f'qr{s}'])
                for h in range(2):
                    b = 4 + 2 * s + h
                    self.stt('dve', tmpf[:, h * 512:(h + 1) * 512], self.banks[b], st[:, 10 + s:11 + s],
                             cur['g2'][:, h * 512:(h + 1) * 512], ALU.mult, ALU.mult, [f'bank{b}', f'qr{s}', 'g2'], ['tmpf'])
                self.tt('pool', xt[s], xt[s], tmpf, ALU.add, [f'xt{s}', 'tmpf'], [f'xt{s}'])
                P.dma('sp', xdst[t * 128:(t + 1) * 128, :], xt[s], reads=[f'xt{s}'])

    def build(self):
        P = self.P
        self.phase_ada()
        if self.upto == 'ada':
            return self.finish()
        if self.upto in ('T1', 'S1'):
            self.phase_A(1, self.xin)
            self.phase_odd_mix(1)
            return self.finish()
        for l in range(self.n_layers):
            xsrc = self.xin if l == 0 else self.xres
            self.phase_A(l, xsrc)
            if self.upto == f'A{l}':
                return self.finish()
            if l % 2 == 0:
                self.phase_even_mix(l)
            else:
                self.phase_odd_mix(l)
            if self.upto in (f'M{l}', 'G1', 'G2', 'G3', 'S1'):
                return self.finish()
            self.phase_C0(l, xsrc)
            last = (l == self.n_layers - 1)
            self.phase_C1(l, self.yout if last else self.xres)
        return self.finish()

    def finish(self):
        P = self.P
        P.barrier()
        for name, buf in self.tapbufs.items():
            src = getattr(self, name)
            P.dma('sp', buf, src)
        P.barrier()
        P.emit()
        return self.nc

    def phase_even_mix(self, l):
        raise NotImplementedError

    def phase_odd_mix(self, l):
        raise NotImplementedError


def host_consts():
    ident = np.eye(128, dtype=np.float32)
    t = np.arange(64)
    masks = np.zeros((8, 64, 64), np.float32)
    masks[0] = (t[:, None] <= t[None, :])
    masks[1] = (t[:, None] > t[None, :])
    masks[2] = (t[:, None] >= t[None, :])
    masks[3] = (t[:, None] < t[None, :])
    masks[4] = ((t[:, None] // 32) == (t[None, :] // 32))
    masks[5] = 1.0 - masks[4]
    half = 64
    inv_freq = (10000.0 ** (-np.arange(0, half, 2, dtype=np.float32) / half)).astype(np.float32)
    tok = np.arange(4096)
    row = (tok // 64).astype(np.float32)
    col = (tok % 64).astype(np.float32)
    ang = np.concatenate([row[None, :] * inv_freq[:, None], row[None, :] * inv_freq[:, None],
                          col[None, :] * inv_freq[:, None], col[None, :] * inv_freq[:, None]], 0)
    C = np.cos(ang).astype(np.float32)
    S = np.sin(ang).astype(np.float32)
    sign = np.ones((128, 1), np.float32)
    sign[0:32] = -1.0
    sign[64:96] = -1.0
    rope = np.stack([C, S * sign]).astype(np.float32)
    perm = np.zeros((128, 128), np.float32)
    for p in range(128):
        blk = p // 64 * 64
        q = p - blk
        partner = blk + (q + 32) % 64
        perm[partner, p] = 1.0
    q = np.arange(64)
    kc = np.arange(64)
    cs = np.clip(q - 8, 0, 48)
    namask = ((kc[None, :] >= cs[:, None]) & (kc[None, :] < cs[:, None] + 16)).astype(np.float32)
    return dict(c_ident=ident, c_masks=masks, c_rope=rope, c_perm=perm, c_namask=namask)


def make_in_maps(inp):
    f = lambda a: np.ascontiguousarray(a, dtype=np.float32)
    consts = host_consts()
    shared = {}
    for n in ("ada_w", "ada_b", "norm_mix_pre", "norm_mix_post", "norm_mlp_pre", "norm_mlp_post", "mlp_w1", "mlp_w2",
              "ev_w_in", "ev_w_out", "gdn_conv", "gdn_norm", "gqa_q_norm", "gqa_k_norm", "od_w_in", "od_w_out",
              "ssd_conv", "ssd_conv_b", "ssd_d", "ssd_norm"):
        shared[n] = f(inp[n])
    shared["gdn_a_log"] = f(inp["gdn_a_log"]).reshape(2, 8)
    shared["gdn_dt_bias"] = f(inp["gdn_dt_bias"]).reshape(2, 8)
    shared["ssd_a_log"] = f(inp["ssd_a_log"]).reshape(2, 16)
    shared["ssd_dt_bias"] = f(inp["ssd_dt_bias"]).reshape(2, 16)
    shared["na_rpb"] = f(inp["na_rpb"]).reshape(2, 60, 31)
    shared.update(consts)
    maps = []
    for c in range(8):
        m = dict(shared)
        m["xin"] = f(np.concatenate([inp["x_prompt"][2 * c], inp["x_prompt"][2 * c + 1], inp["x_sample"][c]], 0))
        m["cvec"] = f(np.stack([inp["c_ctx"], inp["c"][c]], 0))
        m["state_gdn"] = f(inp["state_gdn"][c])
        m["cache_gqa_k"] = f(inp["cache_gqa_k"][c])
        m["cache_gqa_v"] = f(inp["cache_gqa_v"][c])
        m["state_ssd"] = f(inp["state_ssd"][c])
        m["cache_na_k"] = f(inp["cache_na_k"][c])
        m["cache_na_v"] = f(inp["cache_na_v"][c])
        maps.append(m)
    return maps


def _prep_qk(self, x, T, wcol, rope, out, tmp, rs, tables, pos0, kx, ko, bank):
    step = 512
    for c0 in range(0, T, step):
        w = min(step, T - c0)
        xs = x[:, c0:c0 + w]
        if wcol is not None:
            self.act(tmp[:, 0:w], xs, AF.Square, [kx], ['pq_tmp'])
            self.mm(self.banks[bank][:, 0:w], self.ones, tmp[:, 0:w], True, True, ['ones', 'pq_tmp'], [f'bank{bank}'])
            self.act(rs[:, 0:w], self.banks[bank][:, 0:w], AF.Sqrt, [f'bank{bank}'], ['pq_rs'], bias=EPS, scale=1.0 / 128)
            self.P.op('dve', lambda e, a=rs[:, 0:w]: e.reciprocal(out=a, in_=a), ['pq_rs'], ['pq_rs'])
            self.stt('dve', xs, xs, wcol, rs[:, 0:w], ALU.mult, ALU.mult, [kx, 'pq_rs', 'pq_w'], [kx])
        if rope:
            C, S, perm = tables
            self.mm(self.banks[bank][:, 0:w], perm, xs, True, True, ['perm', kx], [f'bank{bank}'])
            self.tt('dve', tmp[:, 0:w], self.banks[bank][:, 0:w], S[:, pos0 + c0:pos0 + c0 + w], ALU.mult,
                    [f'bank{bank}', 'ropeS'], ['pq_tmp'])
            self.tt('pool', rs[:, 0:w], xs, C[:, pos0 + c0:pos0 + c0 + w], ALU.mult, [kx, 'ropeC'], ['pq_rs'])
            self.tt('dve', out[:, c0:c0 + w], tmp[:, 0:w], rs[:, 0:w], ALU.add, ['pq_tmp', 'pq_rs'], [ko])
        else:
            self.cp('dve', out[:, c0:c0 + w], xs, [kx], [ko])


def _attn_dense(self, l, qrow0, krow0, vcol0, nq, nkv, sample_ctx, norm, rope, kout, vout, only_prompts=False):
    j = l // 2
    A, P = self.A, self.P
    rep = nq // nkv
    scale = 128 ** -0.5
    TKM = 4096 + 512
    kTb = A.alloc([TKM], BF16)
    Vb = A.alloc([36, 128], BF16)
    qTb = A.alloc([4096], BF16)
    xk = A.alloc([4096])
    xq = A.alloc([4096])
    tmp = A.alloc([512])
    rs = A.alloc([512])
    pt = [A.alloc([512], BF16) for _ in range(3)]
    rden = A.alloc([512])
    osb = [A.alloc([512]) for _ in range(2)]
    ctk = A.alloc([4, 128])
    kst = A.alloc([2, 128])
    tables = None
    if rope:
        C = A.alloc([4096])
        S = A.alloc([4096])
        perm = A.alloc([128])
        P.dma('sp', C, self.c_rope[0], writes=['ropeC'])
        P.dma('sp', S, self.c_rope[1], writes=['ropeS'])
        P.dma('sp', perm, self.c_perm, writes=['perm'])
        tables = (C, S, perm)
    wq = wk = None
    if norm is not None:
        wq = A.alloc([1])
        wk = A.alloc([1])
        P.dma('sp', wq, norm[0].rearrange("(p o) -> p o", o=1), writes=['pq_w'])
        P.dma('sp', wk, norm[1].rearrange("(p o) -> p o", o=1), writes=['pq_w'])
    no = 0
    npt = 0
    for si, (t0, T, is_s) in enumerate(SEQS):
        if only_prompts and is_s:
            continue
        Tk = T + (512 if (is_s and sample_ctx is not None) else 0)
        nkc = Tk // 128
        for g in range(nkv):
            P.dma('sp', xk[:, 0:T], self.projT[krow0 + g * 128:krow0 + (g + 1) * 128, t0:t0 + T], writes=['xk'])
            _prep_qk(self, xk, T, wk, rope and is_s, kTb, tmp, rs, tables, 0, 'xk', 'kTb', 6)
            if not is_s and kout is not None:
                for a in range(T // 128):
                    self.tr(self.banks[7][:, 0:128], xk[:, a * 128:(a + 1) * 128], self.ident, ['xk', 'ident'], ['bank7'])
                    self.cp('act', kst[:, a, :], self.banks[7][:, 0:128], ['bank7'], ['kst'])
                P.dma('sp', kout[si, j, :, g, :].rearrange("(a p) d -> p a d", p=128), kst, reads=['kst'])
            if Tk > T:
                ck, cv = sample_ctx
                P.dma('sp', ctk, ck[j, :, g, :].rearrange("(a p) d -> p a d", p=128), writes=['ctk'])
                for a in range(4):
                    self.tr(self.banks[7][:, 0:128], ctk[:, a, :], self.ident, ['ctk', 'ident'], ['bank7'])
                    self.cp('act', kTb[:, T + a * 128:T + (a + 1) * 128], self.banks[7][:, 0:128], ['bank7'], ['kTb'])
                P.dma('pool', Vb[:, T // 128:T // 128 + 4, :], cv[j, :, g, :].rearrange("(a p) d -> p a d", p=128), writes=['Vb'])
            P.dma('pool', Vb[:, 0:T // 128, :],
                  self.projN[t0:t0 + T, vcol0 + g * 128:vcol0 + (g + 1) * 128].rearrange("(a p) d -> p a d", p=128), writes=['Vb'])
            if not is_s and vout is not None:
                P.dma('act', vout[si, j, :, g, :], self.projN[t0:t0 + T, vcol0 + g * 128:vcol0 + (g + 1) * 128])
            for r_ in range(rep):
                h = g * rep + r_
                P.dma('sp', xq[:, 0:T], self.projT[qrow0 + h * 128:qrow0 + (h + 1) * 128, t0:t0 + T], writes=['xq'])
                _prep_qk(self, xq, T, wq, rope and is_s, qTb, tmp, rs, tables, 0, 'xq', 'qTb', 6)
                QW = min(512, T)
                for qc in range(T // QW):
                    q0 = qc * QW
                    bo = 2 + 2 * (no % 2)
                    def smm(kc_, bs_):
                        self.mm(self.banks[bs_][:, 0:QW], kTb[:, kc_ * 128:(kc_ + 1) * 128], qTb[:, q0:q0 + QW], True, True,
                                ['kTb', 'qTb'], [f'bank{bs_}'])
                    smm(0, npt % 2)
                    for kc in range(nkc):
                        bs = npt % 2
                        p_ = pt[npt % 3]
                        pk = f'pt{npt % 3}'
                        npt += 1
                        if kc + 1 < nkc:
                            smm(kc + 1, npt % 2)
                        self.act(p_[:, 0:QW], self.banks[bs][:, 0:QW], AF.Exp, [f'bank{bs}'], [pk], scale=scale)
                        self.mm(self.banks[bo][:, 0:QW], Vb[:, kc, :], p_[:, 0:QW], kc == 0, kc == nkc - 1, ['Vb', pk], [f'bank{bo}'])
                        self.mm(self.banks[bo + 1][:, 0:QW], self.onesb, p_[:, 0:QW], kc == 0, kc == nkc - 1, ['onesb', pk], [f'bank{bo + 1}'])
                    self.recip(rden[:, 0:QW], self.banks[bo + 1][:, 0:QW], [f'bank{bo + 1}'], ['rden'])
                    ob = osb[no % 2]
                    okk = f'osb{no % 2}'
                    no += 1
                    self.tt('dve', ob[:, 0:QW], self.banks[bo][:, 0:QW], rden[:, 0:QW], ALU.mult, [f'bank{bo}', 'rden'], [okk])
                    P.dma('sp', self.mixT[h * 128:(h + 1) * 128, t0 + q0:t0 + q0 + QW], ob[:, 0:QW], reads=[okk])


K._attn_dense = _attn_dense


def _gdn(self, l):
    j = l // 2
    A, P = self.A, self.P
    NB = 4
    wb = [A.alloc([1024]) for _ in range(3)]
    for k in range(3):
        self.load_bcast(wb[k], self.gdn_conv[j, k, 512:1536], f'wb{k}')
    xm = [A.alloc([1024]) for _ in range(2)]
    x0 = [A.alloc([1024]) for _ in range(2)]
    xp = [A.alloc([1024]) for _ in range(2)]
    sq = A.alloc([512])
    ss4 = A.alloc([4])
    nt = 0
    for (t0, T, is_s) in SEQS:
        for a in range(T // 128):
            s = nt % 2
            nt += 1
            r0 = t0 + a * 128
            first, last = (a == 0), (a == T // 128 - 1)
            src = self.projN
            P.dma('sp', x0[s], src[r0:r0 + 128, 512:1536], writes=[f'x0{s}'])
            if first:
                self.memset('pool', xm[s], 0.0, [f'xm{s}'])
                P.dma('act', xm[s][1:128, :], src[r0:r0 + 127, 512:1536], writes=[f'xm{s}'])
            else:
                P.dma('act', xm[s], src[r0 - 1:r0 + 127, 512:1536], writes=[f'xm{s}'])
            if last:
                self.memset('pool', xp[s], 0.0, [f'xp{s}'])
                P.dma('act', xp[s][0:127, :], src[r0 + 1:r0 + 128, 512:1536], writes=[f'xp{s}'])
            else:
                P.dma('act', xp[s], src[r0 + 1:r0 + 129, 512:1536], writes=[f'xp{s}'])
            self.tt('dve', x0[s], x0[s], wb[1], ALU.mult, [f'x0{s}', 'wb1'], [f'x0{s}'])
            self.tt('pool', xm[s], xm[s], wb[0], ALU.mult, [f'xm{s}', 'wb0'], [f'xm{s}'])
            self.tt('pool', xp[s], xp[s], wb[2], ALU.mult, [f'xp{s}', 'wb2'], [f'xp{s}'])
            self.tt('dve', x0[s], x0[s], xm[s], ALU.add, [f'x0{s}', f'xm{s}'], [f'x0{s}'])
            self.tt('dve', x0[s], x0[s], xp[s], ALU.add, [f'x0{s}', f'xp{s}'], [f'x0{s}'])
            self.act(x0[s], x0[s], AF.Silu, [f'x0{s}'], [f'x0{s}'])
            self.tt('dve', sq, x0[s][:, 0:512], x0[s][:, 0:512], ALU.mult, [f'x0{s}'], ['sq'])
            self.P.op('dve', lambda e, o=ss4, i=sq.rearrange("p (h d) -> p h d", h=4): e.tensor_reduce(out=o, in_=i, axis=AX.X, op=ALU.add),
                      ['sq'], ['ss4'])
            self.act(ss4, ss4, AF.Sqrt, ['ss4'], ['ss4'], bias=EPS, scale=1.0)
            self.P.op('dve', lambda e, o=ss4: e.reciprocal(out=o, in_=o), ['ss4'], ['ss4'])
            kv = x0[s][:, 0:512].rearrange("p (h d) -> p h d", h=4)
            self.tt('dve', kv, kv, bc_last(ss4, 128), ALU.mult, [f'x0{s}', 'ss4'], [f'x0{s}'])
            P.dma('sp', self.kvn[r0:r0 + 128, :], x0[s], reads=[f'x0{s}'])
    raw = A.alloc([36, 16])
    for a in range(36):
        P.dma('sp', raw[:, a, :], self.projN[a * 128:(a + 1) * 128, 2048:2064], writes=['raw'])
    dtb = A.alloc([8])
    nea = A.alloc([8])
    self.load_bcast(dtb, self.gdn_dt_bias[j, :], 'dtb')
    self.load_bcast(nea, self.gdn_a_log[j, :], 'nea')
    self.act(nea, nea, AF.Exp, ['nea'], ['nea'])
    gbt = A.alloc([36, 16])
    self.act(gbt[:, :, 0:8], raw[:, :, 0:8], AF.Sigmoid, ['raw'], ['gbt'])
    self.tt('dve', raw[:, :, 8:16], raw[:, :, 8:16], bc_mid(dtb, 36), ALU.add, ['raw', 'dtb'], ['raw'])
    self.act(raw[:, :, 8:16], raw[:, :, 8:16], AF.Exp, ['raw'], ['raw'])
    self.act(raw[:, :, 8:16], raw[:, :, 8:16], AF.Ln, ['raw'], ['raw'], bias=1.0)
    self.tt('dve', raw[:, :, 8:16], raw[:, :, 8:16], bc_mid(nea, 36), ALU.mult, ['raw', 'nea'], ['raw'])
    self.ts('dve', gbt[:, :, 8:16], raw[:, :, 8:16], -1.0, None, ALU.mult, None, ['raw'], ['gbt'])
    for a in range(36):
        P.dma('sp', self.gb[a * 128:(a + 1) * 128, :], gbt[:, a, :], reads=['gbt'])
    if self.upto == 'G1':
        return
    P.barrier()
    A.reset(self.const_end)
    xf = A.alloc([NT])
    acc = A.alloc([NT])
    cw = A.alloc([3])
    tq = A.alloc([512])
    rq = A.alloc([512])
    for c in range(8):
        P.dma('sp', xf, self.projT[c * 128:(c + 1) * 128, :], writes=['xf'])
        P.dma('sp', cw, self.gdn_conv[j, :, c * 128:(c + 1) * 128].rearrange("k p -> p k"), writes=['cw'], allow_slow_non_contiguous=True)
        self.ts('dve', acc, xf, cw[:, 1:2], None, ALU.mult, None, ['xf', 'cw'], ['acc'])
        for (t0, T, is_s) in SEQS:
            self.stt('dve', acc[:, t0 + 1:t0 + T], xf[:, t0:t0 + T - 1], cw[:, 0:1], acc[:, t0 + 1:t0 + T], ALU.mult, ALU.add,
                     ['xf', 'cw', 'acc'], ['acc'])
            self.stt('dve', acc[:, t0:t0 + T - 1], xf[:, t0 + 1:t0 + T], cw[:, 2:3], acc[:, t0:t0 + T - 1], ALU.mult, ALU.add,
                     ['xf', 'cw', 'acc'], ['acc'])
        self.act(acc, acc, AF.Silu, ['acc'], ['acc'])
        for g in range(9):
            sl = acc[:, g * 512:(g + 1) * 512]
            self.act(tq, sl, AF.Square, ['acc'], ['tq'])
            self.mm(self.banks[0], self.ones, tq, True, True, ['ones', 'tq'], ['bank0'])
            self.act(rq, self.banks[0], AF.Sqrt, ['bank0'], ['rq'], bias=EPS, scale=1.0)
            self.P.op('dve', lambda e, o=rq: e.reciprocal(out=o, in_=o), ['rq'], ['rq'])
            self.stt('dve', sl, sl, (128 ** -0.5) if c < 4 else 1.0, rq, ALU.mult, ALU.mult, ['acc', 'rq'], ['acc'])
        P.dma('sp', self.qkn[c * 128:(c + 1) * 128, :], acc, reads=['acc'])
    if self.upto == 'G2':
        return
    P.barrier()
    A.reset(self.const_end)
    msk = A.alloc([6, 64])
    P.dma('sp', msk[0:64], self.c_masks[0:6].rearrange("m t i -> t m i"), writes=['msk'])
    m_bd, m_off = msk[0:64, 4], msk[0:64, 5]
    qTs = [A.alloc([4096]) for _ in range(2)]
    kTs = [A.alloc([4096]) for _ in range(2)]
    gbc = A.alloc([64, 16])
    X = []
    for ci_ in range(4):
        x = {}
        for nm in ('gcol', 'bcol', 'gc', 'eg', 'edec', 'beg', 'gtb'):
            x[nm] = A.alloc([64])
        x['S'] = A.alloc([128])
        for nm in ('Gm', 'Em', 'ETm', 'A0', 'A1', 'B0', 'B1', 'Q', 'Pm', 'qkT', 'wT'):
            x[nm] = A.alloc([NB, 64])
        x['Ao'] = x['Em']
        x['Yb'] = x['Gm']
        for nm in ('vb', 'kbg', 'kdec', 'u', 'ktb', 'vtb'):
            x[nm] = A.alloc([NB, 128])
        for nm in ('vnew', 'tqs', 'tmp2'):
            x[nm] = A.alloc([128])
        X.append(x)
    og = [self.og0, self.og1]
    idb = self.ident[0:64, 0:64]
    v3 = lambda bk, off=0: bk[0:64, off:off + NB * 64].rearrange("p (c i) -> p c i", c=NB)

    def chain(ci_, hs, h, d, si, t0, T, is_s):
        x = X[ci_]
        K_ = lambda n: f"{ {'Ao': 'Em', 'Yb': 'Gm'}.get(n, n) }_{ci_}"
        qT, kT, kqT, kkT = qTs[hs], kTs[hs], f'qT{hs}', f'kT{hs}'
        ktb, vtb = x['ktb'], x['vtb']
        dq = 'sp' if ci_ % 2 == 0 else 'act'
        nch = T // 64
        ci = d * 4 + h
        B0 = B2 = self.banks[2 * ci_]
        B1 = B3 = self.banks[2 * ci_ + 1]
        k0 = k2 = f'bk{2 * ci_}'
        k1 = k3 = f'bk{2 * ci_ + 1}'
        m_cum, m_oth, m_strict, m_incl = (msk[0:64, 0], msk[0:64, 1], msk[0:64, 1], msk[0:64, 0]) if d == 0 else \
                                         (msk[0:64, 2], msk[0:64, 3], msk[0:64, 3], msk[0:64, 2])
        gcol, bcol, gc, eg, edec, beg, gtb, S = (x[n] for n in ('gcol', 'bcol', 'gc', 'eg', 'edec', 'beg', 'gtb', 'S'))
        Gm, Em, ETm, Q, Pm, Ao, Yb, qkT, wT = (x[n] for n in ('Gm', 'Em', 'ETm', 'Q', 'Pm', 'Ao', 'Yb', 'qkT', 'wT'))
        vb, kbg, kdec, u, vnew, tqs, tmp2 = (x[n] for n in ('vb', 'kbg', 'kdec', 'u', 'vnew', 'tqs', 'tmp2'))
        Ak, Bk = [x['A0'], x['A1']], [x['B0'], x['B1']]
        self.cp('dve', bcol[0:64, 0:nch], gbc[0:64, 0:nch, ci], ['gbc'], [K_('bcol')])
        self.cp('dve', gcol[0:64, 0:nch], gbc[0:64, 0:nch, 8 + ci], ['gbc'], [K_('gcol')])
        yield
        self.mm(B0[0:64, 0:nch], m_cum, gcol[0:64, 0:nch], True, True, ['msk', K_('gcol')], [k0])
        self.mm(B1[:, 0:nch], self.ones[0:64, :], gcol[0:64, 0:nch], True, True, ['ones', K_('gcol')], [k1])
        yield
        self.cp('dve', gc[0:64, 0:nch], B0[0:64, 0:nch], [k0], [K_('gc')])
        self.act(eg[0:64, 0:nch], B0[0:64, 0:nch], AF.Exp, [k0], [K_('eg')])
        self.act(gtb[:, 0:nch], B1[:, 0:nch], AF.Exp, [k1], [K_('gtb')])
        yield
        self.tt('dve', edec[0:64, 0:nch], B1[0:64, 0:nch], gc[0:64, 0:nch], ALU.subtract, [k1, K_('gc')], [K_('edec')])
        self.tt('dve', beg[0:64, 0:nch], bcol[0:64, 0:nch], eg[0:64, 0:nch], ALU.mult, [K_('bcol'), K_('eg')], [K_('beg')])
        yield
        self.act(edec[0:64, 0:nch], edec[0:64, 0:nch], AF.Exp, [K_('edec')], [K_('edec')])
        if is_s:
            P.dma('sp', S, self.state_gdn[j, d, h], writes=[K_('S')])
        else:
            self.memset('pool', S, 0.0, [K_('S')])
        yield
        nbat = nch // NB
        border = range(nbat) if d == 0 else range(nbat - 1, -1, -1)
        blist = list(border)

        def load_kv(bi_):
            rr = slice(t0 + bi_ * NB * 64, t0 + (bi_ + 1) * NB * 64)
            P.dma(dq, ktb[0:64], self.kvn[rr, h * 128:(h + 1) * 128].rearrange("(c t) f -> t c f", t=64), writes=[K_('ktb')])
            P.dma(dq, vtb[0:64], self.kvn[rr, 512 + h * 128:512 + (h + 1) * 128].rearrange("(c t) f -> t c f", t=64), writes=[K_('vtb')])
        load_kv(blist[0])
        for bidx, bi in enumerate(blist):
            c0 = bi * NB
            cs = slice(c0, c0 + NB)
            self.tt('pool', Gm[0:64], bc_last(gcol[0:64, cs], 64), bc_mid(m_cum, NB), ALU.mult, [K_('gcol'), 'msk'], [K_('Gm')])
            yield
            for c in range(NB):
                self.mm(B0[0:64, c * 64:(c + 1) * 64], Gm[0:64, c, :], m_oth, True, True, [K_('Gm'), 'msk'], [k0])
            for c in range(NB):
                self.mm(B0[0:64, 256 + c * 64:256 + (c + 1) * 64], m_oth, Gm[0:64, c, :], True, True, [K_('Gm'), 'msk'], [k0])
            for c in range(NB):
                tk = slice((c0 + c) * 64, (c0 + c + 1) * 64)
                self.mm(B1[0:64, c * 64:(c + 1) * 64], kT[:, tk], kT[:, tk], True, True, [kkT], [k1])
            for c in range(NB):
                tk = slice((c0 + c) * 64, (c0 + c + 1) * 64)
                self.mm(B1[0:64, 256 + c * 64:256 + (c + 1) * 64], kT[:, tk], qT[:, tk], True, True, [kkT, kqT], [k1])
            yield
            self.act(Em[0:64], v3(B0), AF.Exp, [k0], [K_('Em')])
            self.act(ETm[0:64], v3(B0, 256), AF.Exp, [k0], [K_('ETm')])
            yield
            self.tt('dve', Em[0:64], Em[0:64], bc_mid(m_strict, NB), ALU.mult, [K_('Em'), 'msk'], [K_('Em')])
            self.tt('pool', ETm[0:64], ETm[0:64], bc_mid(m_incl, NB), ALU.mult, [K_('ETm'), 'msk'], [K_('ETm')])
            yield
            a0, b0 = Ak[0], Bk[0]
            self.tt('dve', a0[0:64], v3(B1), Em[0:64], ALU.mult, [k1, K_('Em')], [K_('A0')])
            self.tt('dve', qkT[0:64], v3(B1, 256), ETm[0:64], ALU.mult, [k1, K_('ETm')], [K_('qkT')])
            yield
            self.tt('pool', a0[0:64], a0[0:64], bc_last(bcol[0:64, cs], 64), ALU.mult, [K_('A0'), K_('bcol')], [K_('A0')])
            yield
            for c in range(NB):
                self.tr(B2[0:64, c * 64:(c + 1) * 64], a0[0:64, c, :], idb, [K_('A0'), 'ident'], [k2])
            yield
            self.cp('act', b0[0:64], v3(B2), [k2], [K_('B0')])
            self.tt('pool', Ao[0:64], a0[0:64], bc_mid(m_off, NB), ALU.mult, [K_('A0'), 'msk'], [K_('Ao')])
            yield
            self.tt('pool', a0[0:64], a0[0:64], bc_mid(m_bd, NB), ALU.mult, [K_('A0'), 'msk'], [K_('A0')])
            self.tt('pool', b0[0:64], b0[0:64], bc_mid(m_bd, NB), ALU.mult, [K_('B0'), 'msk'], [K_('B0')])
            yield
            self.tt('pool', Q[0:64], bc_mid(idb, NB), b0[0:64], ALU.subtract, ['ident', K_('B0')], [K_('Q')])
            self.tt('pool', Pm[0:64], bc_mid(idb, NB), a0[0:64], ALU.subtract, ['ident', K_('A0')], [K_('Pm')])
            yield
            for kk in range(1, 5):
                ao, bo_ = Ak[(kk - 1) % 2], Bk[(kk - 1) % 2]
                an, bn = Ak[kk % 2], Bk[kk % 2]
                ko, kn_ = K_(f'A{(kk - 1) % 2}'), K_(f'A{kk % 2}')
                lo, ln_ = K_(f'B{(kk - 1) % 2}'), K_(f'B{kk % 2}')
                for c in range(NB):
                    self.mm(B2[0:64, c * 64:(c + 1) * 64], ao[0:64, c, :], bo_[0:64, c, :], True, True, [ko, lo], [k2])
                for c in range(NB):
                    self.mm(B3[0:64, c * 64:(c + 1) * 64], bo_[0:64, c, :], ao[0:64, c, :], True, True, [ko, lo], [k3])
                yield
                self.cp('act', bn[0:64], v3(B2), [k2], [ln_])
                self.cp('act', an[0:64], v3(B3), [k3], [kn_])
                yield
                for c in range(NB):
                    self.mm(B2[0:64, 256 + c * 64:256 + (c + 1) * 64], an[0:64, c, :], Q[0:64, c, :], True, True, [kn_, K_('Q')], [k2])
                for c in range(NB):
                    self.mm(B3[0:64, 256 + c * 64:256 + (c + 1) * 64], bn[0:64, c, :], Pm[0:64, c, :], True, True, [ln_, K_('Pm')], [k3])
                yield
                self.tt('dve', Q[0:64], Q[0:64], v3(B2, 256), ALU.add, [K_('Q'), k2], [K_('Q')])
                self.tt('dve', Pm[0:64], Pm[0:64], v3(B3, 256), ALU.add, [K_('Pm'), k3], [K_('Pm')])
                yield
            for c in range(NB):
                self.mm(B2[0:64, c * 64:(c + 1) * 64], Ao[0:64, c, :], Q[0:64, c, :], True, True, [K_('Ao'), K_('Q')], [k2])
            yield
            self.cp('act', Yb[0:64], v3(B2), [k2], [K_('Yb')])
            self.tt('pool', vb[0:64], vtb[0:64], bc_last(bcol[0:64, cs], 128), ALU.mult, [K_('vtb'), K_('bcol')], [K_('vb')])
            self.tt('pool', kbg[0:64], ktb[0:64], bc_last(beg[0:64, cs], 128), ALU.mult, [K_('ktb'), K_('beg')], [K_('kbg')])
            self.tt('pool', kdec[0:64], ktb[0:64], bc_last(edec[0:64, cs], 128), ALU.mult, [K_('ktb'), K_('edec')], [K_('kdec')])
            if bidx + 1 < len(blist):
                load_kv(blist[bidx + 1])
            yield
            for c in range(NB):
                self.mm(B3[0:64, c * 64:(c + 1) * 64], Pm[0:64, c, :], Yb[0:64, c, :], True, True, [K_('Pm'), K_('Yb')], [k3])
            yield
            self.tt('dve', Q[0:64], Q[0:64], v3(B3), ALU.subtract, [K_('Q'), k3], [K_('Q')])
            yield
            for c in range(NB):
                self.mm(B0[0:64, c * 128:(c + 1) * 128], Q[0:64, c, :], vb[0:64, c, :], True, True, [K_('Q'), K_('vb')], [k0])
            for c in range(NB):
                self.mm(B1[:, c * 64:(c + 1) * 64], kbg[0:64, c, :], Q[0:64, c, :], True, True, [K_('Q'), K_('kbg')], [k1])
            yield
            self.cp('act', u[0:64], B0[0:64, :].rearrange("p (c i) -> p c i", c=NB), [k0], [K_('u')])
            self.cp('act', wT, B1[:, 0:NB * 64].rearrange("p (c i) -> p c i", c=NB), [k1], [K_('wT')])
            yield
            corder = range(NB) if d == 0 else range(NB - 1, -1, -1)
            for c in corder:
                ch = c0 + c
                tk = slice(ch * 64, (ch + 1) * 64)
                self.mm(B2[0:64, 0:128], wT[:, c, :], S, True, True, [K_('wT'), K_('S')], [k2])
                self.mm(B3[0:64, 0:128], qT[:, tk], S, True, True, [kqT, K_('S')], [k3])
                yield
                self.tt('dve', vnew[0:64], u[0:64, c, :], B2[0:64, 0:128], ALU.subtract, [K_('u'), k2], [K_('vnew')])
                self.act(tqs[0:64], B3[0:64, 0:128], AF.Copy, [k3, K_('eg')], [K_('tqs')], scale=eg[0:64, ch:ch + 1])
                yield
                self.mm(B2[0:64, 128:256], qkT[0:64, c, :], vnew[0:64], True, True, [K_('qkT'), K_('vnew')], [k2])
                self.mm(B3[:, 128:256], kdec[0:64, c, :], vnew[0:64], True, True, [K_('kdec'), K_('vnew')], [k3])
                yield
                self.tt('dve', tmp2[0:64], tqs[0:64], B2[0:64, 128:256], ALU.add, [K_('tqs'), k2], [K_('tmp2')])
                self.stt('dve', S, S, gtb[:, ch:ch + 1], B3[:, 128:256], ALU.mult, ALU.add, [K_('S'), K_('gtb'), k3], [K_('S')])
                P.dma(dq, og[d][t0 + ch * 64:t0 + (ch + 1) * 64, h * 128:(h + 1) * 128], tmp2[0:64], reads=[K_('tmp2')])
        if not is_s:
            P.dma('sp', self.o_gdn[si, j, d, h], S, reads=[K_('S')])

    for si, (t0, T, is_s) in enumerate(SEQS):
        nch = T // 64
        for c8 in range(0, nch, 4):
            rr = slice(t0 + c8 * 64, t0 + (c8 + 4) * 64)
            P.dma('sp', gbc[0:64, c8:c8 + 4, :], self.gb[rr, :].rearrange("(c t) f -> t c f", t=64), writes=['gbc'])
        for pair in ((0, 1), (2, 3)):
            gens = []
            for hs, h in enumerate(pair):
                P.dma('sp', qTs[hs][:, 0:T], self.qkn[h * 128:(h + 1) * 128, t0:t0 + T], writes=[f'qT{hs}'])
                P.dma('act', kTs[hs][:, 0:T], self.qkn[512 + h * 128:512 + (h + 1) * 128, t0:t0 + T], writes=[f'kT{hs}'])
                for d in range(2):
                    gens.append(chain(hs * 2 + d, hs, h, d, si, t0, T, is_s))
            while gens:
                for g_ in list(gens):
                    try:
                        next(g_)
                    except StopIteration:
                        gens.remove(g_)
    P.barrier()
    A.reset(self.const_end)
    gnw = A.alloc([128])
    self.load_bcast(gnw, self.gdn_norm[j, :], 'gnw')
    oa = [A.alloc([512]) for _ in range(2)]
    ob_ = [A.alloc([512]) for _ in range(2)]
    zz = [A.alloc([512]) for _ in range(2)]
    sq = A.alloc([512])
    ss4 = A.alloc([8])
    for t in range(36):
        s = t % 2
        rr = slice(t * 128, (t + 1) * 128)
        P.dma('sp', oa[s], self.og0[rr, :], writes=[f'oa{s}'])
        P.dma('act', ob_[s], self.og1[rr, :], writes=[f'ob{s}'])
        P.dma('sp', zz[s], self.projN[rr, 1536:2048], writes=[f'zz{s}'])
        self.tt('dve', oa[s], oa[s], ob_[s], ALU.add, [f'oa{s}', f'ob{s}'], [f'oa{s}'])
        self.act(zz[s], zz[s], AF.Silu, [f'zz{s}'], [f'zz{s}'])
        self.tt('pool', sq, oa[s], oa[s], ALU.mult, [f'oa{s}'], ['sq'])
        r4 = ss4[:, 4 * s:4 * s + 4]
        self.P.op('dve', lambda e, o=r4, i=sq.rearrange("p (h d) -> p h d", h=4): e.tensor_reduce(out=o, in_=i, axis=AX.X, op=ALU.add),
                  ['sq'], [f'ss4{s}'])
        self.act(r4, r4, AF.Sqrt, [f'ss4{s}'], [f'ss4{s}'], bias=EPS, scale=1.0 / 128)
        self.recip(r4, r4, [f'ss4{s}'], [f'ss4{s}'])
        o3 = oa[s].rearrange("p (h d) -> p h d", h=4)
        self.tt('dve', o3, o3, bc_last(r4, 128), ALU.mult, [f'oa{s}', f'ss4{s}'], [f'oa{s}'])
        self.tt('pool', o3, o3, bc_mid(gnw, 4), ALU.mult, [f'oa{s}', 'gnw'], [f'oa{s}'])
        self.tt('dve', ob_[s], oa[s], zz[s], ALU.mult, [f'oa{s}', f'zz{s}'], [f'ob{s}'])
        P.dma('sp', self.mixN[rr, :], ob_[s], reads=[f'ob{s}'])


def phase_even_mix(self, l):
    j = l // 2
    self.phase_begin()
    _gdn(self, l)
    if self.upto in ('G1', 'G2', 'G3'):
        return
    self.P.barrier()
    self.A.reset(self.const_end)
    _attn_dense(self, l, 2064, 2576, 2832, 4, 2, (self.cgk, self.cgv), (self.gqa_q_norm[j], self.gqa_k_norm[j]), True,
                self.o_gk, self.o_gv)


K.phase_even_mix = phase_even_mix


_CACHE = {}


def kernel(**inputs):
    if 'nc' not in _CACHE:
        kb = K(n_layers=4)
        _CACHE['nc'] = kb.build()
    nc = _CACHE['nc']
    in_maps = make_in_maps(inputs)
    res = run_bass_kernel_spmd(nc, in_maps, core_ids=list(range(8)))
    rs = res.results
    y = np.stack([r["yout"] for r in rs], 0)
    y_prompt = np.ascontiguousarray(y[:, :512].reshape(16, 256, 1024))
    y_sample = np.ascontiguousarray(y[:, 512:])
    cat = lambda n: np.concatenate([r[n] for r in rs], 0)
    return (y_prompt, y_sample, cat("o_gdn"), cat("o_gk"), cat("o_gv"), cat("o_ssd"), cat("o_nk"), cat("o_nv"))


def _ssd(self, l):
    j = l // 2
    A, P = self.A, self.P
    B = self.banks
    W = 768
    wb = [A.alloc([W]) for _ in range(3)]
    for k in range(3):
        self.load_bcast(wb[k], self.ssd_conv[j, k, 0:W], f'wb{k}')
    cbb = A.alloc([W])
    self.load_bcast(cbb, self.ssd_conv_b[j, 0:W], 'cbb')
    xm = [A.alloc([W]) for _ in range(2)]
    x0 = [A.alloc([W]) for _ in range(2)]
    xp = [A.alloc([W]) for _ in range(2)]
    nt = 0
    src = self.projN
    for (t0, T, is_s) in SEQS:
        for a in range(T // 128):
            s = nt % 2
            nt += 1
            r0 = t0 + a * 128
            first, last = (a == 0), (a == T // 128 - 1)
            P.dma('sp', x0[s], src[r0:r0 + 128, 512:1280], writes=[f'x0{s}'])
            if first:
                self.memset('pool', xm[s], 0.0, [f'xm{s}'])
                P.dma('act', xm[s][1:128, :], src[r0:r0 + 127, 512:1280], writes=[f'xm{s}'])
            else:
                P.dma('act', xm[s], src[r0 - 1:r0 + 127, 512:1280], writes=[f'xm{s}'])
            if last:
                self.memset('pool', xp[s], 0.0, [f'xp{s}'])
                P.dma('act', xp[s][0:127, :], src[r0 + 1:r0 + 128, 512:1280], writes=[f'xp{s}'])
            else:
                P.dma('act', xp[s], src[r0 + 1:r0 + 129, 512:1280], writes=[f'xp{s}'])
            self.tt('dve', x0[s], x0[s], wb[1], ALU.mult, [f'x0{s}', 'wb1'], [f'x0{s}'])
            self.tt('pool', xm[s], xm[s], wb[0], ALU.mult, [f'xm{s}', 'wb0'], [f'xm{s}'])
            self.tt('pool', xp[s], xp[s], wb[2], ALU.mult, [f'xp{s}', 'wb2'], [f'xp{s}'])
            self.tt('dve', x0[s], x0[s], xm[s], ALU.add, [f'x0{s}', f'xm{s}'], [f'x0{s}'])
            self.tt('dve', x0[s], x0[s], xp[s], ALU.add, [f'x0{s}', f'xp{s}'], [f'x0{s}'])
            self.tt('dve', x0[s], x0[s], cbb, ALU.add, [f'x0{s}', 'cbb'], [f'x0{s}'])
            self.act(x0[s], x0[s], AF.Silu, [f'x0{s}'], [f'x0{s}'])
            P.dma('sp', self.kvn[r0:r0 + 128, 0:W], x0[s], reads=[f'x0{s}'])
    raw = A.alloc([36, 16])
    for a in range(36):
        P.dma('sp', raw[:, a, :], self.projN[a * 128:(a + 1) * 128, 1536:1552], writes=['raw'])
    dtb = A.alloc([16])
    nea = A.alloc([16])
    self.load_bcast(dtb, self.ssd_dt_bias[j, :], 'dtb')
    self.load_bcast(nea, self.ssd_a_log[j, :], 'nea')
    self.act(nea, nea, AF.Exp, ['nea'], ['nea'])
    sbt = A.alloc([36, 32])
    self.tt('dve', raw, raw, bc_mid(dtb, 36), ALU.add, ['raw', 'dtb'], ['raw'])
    self.act(raw, raw, AF.Exp, ['raw'], ['raw'])
    self.act(sbt[:, :, 0:16], raw, AF.Ln, ['raw'], ['sbt'], bias=1.0)
    self.tt('dve', raw, sbt[:, :, 0:16], bc_mid(nea, 36), ALU.mult, ['sbt', 'nea'], ['raw'])
    self.ts('dve', sbt[:, :, 16:32], raw, -1.0, None, ALU.mult, None, ['raw'], ['sbt'])
    for a in range(36):
        P.dma('sp', self.sb[a * 128:(a + 1) * 128, :], sbt[:, a, :], reads=['sbt'])
    P.barrier()
    A.reset(self.const_end)
    xf = A.alloc([NT])
    acc = A.alloc([NT])
    cw = A.alloc([4])
    for c in range(4):
        ch0 = 512 + c * 128
        P.dma('sp', xf, self.projT[1024 + c * 128:1024 + (c + 1) * 128, :], writes=['xf'])
        P.dma('sp', cw[:, 0:3], self.ssd_conv[j, :, ch0:ch0 + 128].rearrange("k p -> p k"), writes=['cw'], allow_slow_non_contiguous=True)
        P.dma('sp', cw[:, 3:4], self.ssd_conv_b[j, ch0:ch0 + 128].rearrange("(p o) -> p o", o=1), writes=['cw'])
        self.ts('dve', acc, xf, cw[:, 1:2], cw[:, 3:4], ALU.mult, ALU.add, ['xf', 'cw'], ['acc'])
        for (t0, T, is_s) in SEQS:
            self.stt('dve', acc[:, t0 + 1:t0 + T], xf[:, t0:t0 + T - 1], cw[:, 0:1], acc[:, t0 + 1:t0 + T], ALU.mult, ALU.add,
                     ['xf', 'cw', 'acc'], ['acc'])
            self.stt('dve', acc[:, t0:t0 + T - 1], xf[:, t0 + 1:t0 + T], cw[:, 2:3], acc[:, t0:t0 + T - 1], ALU.mult, ALU.add,
                     ['xf', 'cw', 'acc'], ['acc'])
        self.act(acc, acc, AF.Silu, ['acc'], ['acc'])
        P.dma('sp', self.qkn[c * 128:(c + 1) * 128, :], acc, reads=['acc'])
    P.barrier()
    A.reset(self.const_end)
    msk = A.alloc([4, 64])
    P.dma('sp', msk[0:64], self.c_masks[0:4].rearrange("m t i -> t m i"), writes=['msk'])
    snw = A.alloc([512])
    self.load_bcast(snw, self.ssd_norm[j, :], 'snw')
    dsk = A.alloc([8])
    self.load_bcast(dsk, self.ssd_d[j, :], 'dsk')
    bcT = A.alloc([2, 4096])
    ccT = A.alloc([2, 4096])
    sca = A.alloc([64, 32])
    id64 = self.ident[0:64, 0:64]
    X = []
    for d in range(2):
        x = {}
        for nm in ('lad', 'dtd', 'cs', 'ecs', 'edec', 'cdec'):
            x[nm] = A.alloc([64, 8])
        for nm in ('hT', 'Gm', 'LT', 'MT', 'xdt', 'xd2', 'yo', 'yy'):
            x[nm] = A.alloc([8, 64])
        x['xb'] = [A.alloc([768]) for _ in range(2)]
        x['h0'] = A.alloc([8, 128])
        X.append(x)
    ydst = [self.yf, self.yb]

    def chain(d, si, t0, T, is_s):
        x = X[d]
        K_ = lambda n: f'{n}_{d}'
        nch = T // 64
        B0, B1, B2, B3 = self.banks[4 * d:4 * d + 4]
        k0, k1, k2, k3 = [f'bk{4 * d + i}' for i in range(4)]
        m_cum, m_oth, m_incl = (msk[0:64, 0], msk[0:64, 1], msk[0:64, 0]) if d == 0 else (msk[0:64, 2], msk[0:64, 3], msk[0:64, 2])
        lad, dtd, cs, ecs, edec, cdec, hT = (x[n] for n in ('lad', 'dtd', 'cs', 'ecs', 'edec', 'cdec', 'hT'))
        Gm, LT, MT, xdt, xd2, yo, yy, h0 = (x[n] for n in ('Gm', 'LT', 'MT', 'xdt', 'xd2', 'yo', 'yy', 'h0'))
        self.cp('dve', dtd[0:64, 0:nch, :], sca[0:64, 0:nch, d * 8:d * 8 + 8], ['sca'], [K_('dtd')])
        self.cp('dve', lad[0:64, 0:nch, :], sca[0:64, 0:nch, 16 + d * 8:16 + d * 8 + 8], ['sca'], [K_('lad')])
        yield
        n8 = nch * 8
        fl = lambda a_, p_=64: a_[0:p_, 0:nch, :].rearrange("p c h -> p (c h)")
        self.mm(B0[0:64, 0:n8], m_cum, fl(lad), True, True, ['msk', K_('lad')], [k0])
        self.mm(B1[:, 0:n8], self.ones[0:64, :], fl(lad), True, True, ['ones', K_('lad')], [k1])
        yield
        self.cp('dve', fl(cs), B0[0:64, 0:n8], [k0], [K_('cs')])
        self.act(fl(ecs), B0[0:64, 0:n8], AF.Exp, [k0], [K_('ecs')])
        self.act(fl(cdec, 128), B1[:, 0:n8], AF.Exp, [k1], [K_('cdec')])
        yield
        self.tt('dve', fl(edec), B1[0:64, 0:n8], fl(cs), ALU.subtract, [k1, K_('cs')], [K_('edec')])
        yield
        self.act(fl(edec), fl(edec), AF.Exp, [K_('edec')], [K_('edec')])
        if is_s:
            for hh in range(8):
                P.dma('sp' if d == 0 else 'act', h0[0:64, hh, :], self.state_ssd[j, d, hh], writes=[K_('h0')])
            yield
            for hh in range(8):
                self.tr(B2[:, hh * 64:(hh + 1) * 64], h0[0:64, hh, :], id64, [K_('h0'), 'ident'], [k2])
            yield
            self.cp('dve', hT, B2.rearrange("p (h q) -> p h q", h=8), [k2], [K_('hT')])
        else:
            self.memset('pool', hT, 0.0, [K_('hT')])
        yield
        corder = range(nch) if d == 0 else range(nch - 1, -1, -1)
        nx = 0
        for c in corder:
            r0 = t0 + c * 64
            tk = slice(c * 64, (c + 1) * 64)
            s = nx % 2
            nx += 1
            xk = K_(f'xb{s}')
            xbs = x['xb'][s]
            P.dma('sp' if d == 0 else 'act', xbs[0:64, :], self.kvn[r0:r0 + 64, 0:768], writes=[xk])
            self.tt('dve', Gm[0:64], bc_last(lad[0:64, c, :], 64), bc_mid(m_cum, 8), ALU.mult, [K_('lad'), 'msk'], [K_('Gm')])
            yield
            x3 = xbs[0:64, 0:512].rearrange("p (h q) -> p h q", h=8)
            self.mm(B0[0:64, :], m_oth, Gm[0:64].rearrange("p h i -> p (h i)"), True, True, ['msk', K_('Gm')], [k0])
            for g in range(2):
                self.mm(B1[0:64, g * 64:(g + 1) * 64], bcT[:, g, tk], ccT[:, g, tk], True, True, ['bcT', 'ccT'], [k1])
            for g in range(2):
                self.mm(B2[0:64, g * 256:(g + 1) * 256], ccT[:, g, tk], hT[:, g * 4:(g + 1) * 4, :].rearrange("p h q -> p (h q)"),
                        True, True, ['ccT', K_('hT')], [k2])
            self.tt('pool', xdt[0:64], x3, bc_last(dtd[0:64, c, :], 64), ALU.mult, [xk, K_('dtd')], [K_('xdt')])
            yield
            self.act(LT[0:64], B0[0:64, :].rearrange("p (h i) -> p h i", h=8), AF.Exp, [k0], [K_('LT')])
            self.tt('dve', yo[0:64], B2[0:64, :].rearrange("p (h q) -> p h q", h=8), bc_last(ecs[0:64, c, :], 64), ALU.mult,
                    [k2, K_('ecs')], [K_('yo')])
            self.tt('pool', xd2[0:64], xdt[0:64], bc_last(edec[0:64, c, :], 64), ALU.mult, [K_('xdt'), K_('edec')], [K_('xd2')])
            yield
            self.tt('pool', LT[0:64], LT[0:64], bc_mid(m_incl, 8), ALU.mult, [K_('LT'), 'msk'], [K_('LT')])
            for g in range(2):
                self.mm(B3[:, g * 256:(g + 1) * 256], xbs[0:64, 512 + g * 128:512 + (g + 1) * 128],
                        xd2[0:64, g * 4:(g + 1) * 4, :].rearrange("p h q -> p (h q)"), True, True, [xk, K_('xd2')], [k3])
            yield
            for g in range(2):
                self.tt('dve', MT[0:64, g * 4:(g + 1) * 4, :], LT[0:64, g * 4:(g + 1) * 4, :], bc_mid(B1[0:64, g * 64:(g + 1) * 64], 4),
                        ALU.mult, [K_('LT'), k1], [K_('MT')])
            self.tt('pool', hT, hT, bc_last(cdec[:, c, :], 64), ALU.mult, [K_('hT'), K_('cdec')], [K_('hT')])
            yield
            for hh in range(8):
                self.mm(B2[0:64, hh * 64:(hh + 1) * 64], MT[0:64, hh, :], xdt[0:64, hh, :], True, True, [K_('MT'), K_('xdt')], [k2])
            self.tt('dve', hT, hT, B3.rearrange("p (h q) -> p h q", h=8), ALU.add, [K_('hT'), k3], [K_('hT')])
            yield
            self.tt('dve', yy[0:64], yo[0:64], B2[0:64, :].rearrange("p (h q) -> p h q", h=8), ALU.add, [K_('yo'), k2], [K_('yy')])
            yield
            P.dma('sp' if d == 0 else 'act', ydst[d][r0:r0 + 64, :], yy[0:64].rearrange("p h q -> p (h q)"), reads=[K_('yy')],
                  writes=[f'ydram{d}'])
        if not is_s:
            hs = h0
            for hh in range(8):
                self.tr(B0[0:64, (hh % 4) * 128:(hh % 4 + 1) * 128] if hh < 4 else B1[0:64, (hh % 4) * 128:(hh % 4 + 1) * 128],
                        hT[:, hh, :], self.ident, [K_('hT'), 'ident'], [k0 if hh < 4 else k1])
            yield
            self.cp('dve', hs[0:64, 0:4, :], B0[0:64, :].rearrange("p (h n) -> p h n", h=4), [k0], [K_('h0')])
            self.cp('act', hs[0:64, 4:8, :], B1[0:64, :].rearrange("p (h n) -> p h n", h=4), [k1], [K_('h0')])
            yield
            P.dma('sp', self.o_ssd[si, j, d].rearrange("h p n -> p h n"), hs[0:64], reads=[K_('h0')])

    for si, (t0, T, is_s) in enumerate(SEQS):
        nch = T // 64
        for g in range(2):
            P.dma('sp', bcT[:, g, 0:T], self.qkn[g * 128:(g + 1) * 128, t0:t0 + T], writes=['bcT'])
            P.dma('act', ccT[:, g, 0:T], self.qkn[256 + g * 128:256 + (g + 1) * 128, t0:t0 + T], writes=['ccT'])
        for c4 in range(0, nch, 4):
            P.dma('sp', sca[0:64, c4:c4 + 4, :], self.sb[t0 + c4 * 64:t0 + (c4 + 4) * 64, :].rearrange("(c t) f -> t c f", t=64), writes=['sca'])
        gens = [chain(0, si, t0, T, is_s), chain(1, si, t0, T, is_s)]
        while gens:
            for g_ in list(gens):
                try:
                    next(g_)
                except StopIteration:
                    gens.remove(g_)
    P.barrier()
    A.reset(self.const_end)
    snw = A.alloc([512])
    self.load_bcast(snw, self.ssd_norm[j, :], 'snw')
    dsk = A.alloc([8])
    self.load_bcast(dsk, self.ssd_d[j, :], 'dsk')
    ya = [A.alloc([512]) for _ in range(2)]
    yb_ = [A.alloc([512]) for _ in range(2)]
    xx = [A.alloc([512]) for _ in range(2)]
    zz = [A.alloc([512]) for _ in range(2)]
    sq = A.alloc([512])
    st = A.alloc([4])
    for t in range(36):
        s = t % 2
        rr = slice(t * 128, (t + 1) * 128)
        P.dma('sp', ya[s], self.yf[rr, :], writes=[f'ya{s}'])
        P.dma('act', yb_[s], self.yb[rr, :], writes=[f'yb{s}'])
        P.dma('sp', xx[s], self.kvn[rr, 0:512], writes=[f'xx{s}'])
        P.dma('act', zz[s], self.projN[rr, 0:512], writes=[f'zz{s}'])
        self.tt('dve', ya[s], ya[s], yb_[s], ALU.add, [f'ya{s}', f'yb{s}'], [f'ya{s}'])
        x3 = xx[s].rearrange("p (h q) -> p h q", h=8)
        self.tt('pool', x3, x3, bc_last(dsk, 64), ALU.mult, [f'xx{s}', 'dsk'], [f'xx{s}'])
        self.act(zz[s], zz[s], AF.Silu, [f'zz{s}'], [f'zz{s}'])
        self.tt('dve', ya[s], ya[s], xx[s], ALU.add, [f'ya{s}', f'xx{s}'], [f'ya{s}'])
        self.tt('dve', ya[s], ya[s], zz[s], ALU.mult, [f'ya{s}', f'zz{s}'], [f'ya{s}'])
        self.act(sq, ya[s], AF.Square, [f'ya{s}'], ['sq', f'ssq{s}'], accum_out=st[:, s:s + 1])
        self.act(st[:, 2 + s:3 + s], st[:, s:s + 1], AF.Sqrt, [f'ssq{s}'], [f'rsq{s}'], bias=EPS, scale=1.0 / 512)
        self.recip(st[:, 2 + s:3 + s], st[:, 2 + s:3 + s], [f'rsq{s}'], [f'rsq{s}'])
        self.stt('dve', yb_[s], ya[s], st[:, 2 + s:3 + s], snw, ALU.mult, ALU.mult, [f'ya{s}', f'rsq{s}', 'snw'], [f'yb{s}'])
        P.dma('sp', self.mixN[rr, :], yb_[s], reads=[f'yb{s}'])


def _na_sample(self, l):
    j = l // 2
    A, P = self.A, self.P
    B = self.banks
    t0 = 512
    scale = 128 ** -0.5
    BIG = 30000.0
    kdT = A.alloc([4, 4096], BF16)
    qdT = A.alloc([4, 4096], BF16)
    Ve = A.alloc([32, 512], BF16)
    Vo = A.alloc([31, 512], BF16)
    kcT = A.alloc([4, 512], BF16)
    Vc = A.alloc([4, 512], BF16)
    Bt = A.alloc([60, 64])
    nm = A.alloc([64])
    ngm = A.alloc([64])
    Z = A.alloc([160])
    ctk = A.alloc([4, 128])
    for h in range(4):
        P.dma('pool', kdT[:, h, :], self.projT[2064 + h * 128:2064 + (h + 1) * 128, t0:t0 + 4096], writes=['kdT'])
        P.dma('pool', qdT[:, h, :], self.projT[1552 + h * 128:1552 + (h + 1) * 128, t0:t0 + 4096], writes=['qdT'])
    for a4 in range(0, 32, 4):
        P.dma('pool', Ve[:, a4:a4 + 4, :], self.projN[t0 + a4 * 128:t0 + (a4 + 4) * 128, 2576:3088].rearrange("(a p) f -> p a f", p=128), writes=['Ve'])
    for a4 in range(0, 31, 4):
        n = min(4, 31 - a4)
        P.dma('pool', Vo[:, a4:a4 + n, :], self.projN[t0 + 64 + a4 * 128:t0 + 64 + (a4 + n) * 128, 2576:3088].rearrange("(a p) f -> p a f", p=128), writes=['Vo'])
    P.dma('pool', Vc, self.cnv[j].rearrange("(a p) h d -> p a (h d)", p=128), writes=['Vc'])
    for h in range(4):
        P.dma('sp', ctk, self.cnk[j, :, h, :].rearrange("(a p) d -> p a d", p=128), writes=['ctk'])
        for a in range(4):
            self.tr(B[0][:, a * 128:(a + 1) * 128], ctk[:, a, :], self.ident, ['ctk', 'ident'], ['b0'])
        self.cp('act', kcT[:, h, :], B[0], ['b0'], ['kcT'])
    self.memset('pool', Z[0:60, :], 0.0, ['Z'])
    P.dma('sp', Z[0:60, 64:95], self.na_rpb[j], writes=['Z'])
    tz = P.dma('sp', self.rpbpad, Z[0:60, :], reads=['Z'], writes=['rpbpad'])
    for q in range(64):
        src = bass.AP(self.rpbpad.tensor, 79 - q, [[0, 1], [160, 60], [1, 64]])
        P.dma('sp' if q % 2 else 'act', Bt[q:q + 1, :, :], src, reads=['rpbpad'], writes=['Bt'])
    P.dma('sp', nm[0:64, :], self.c_namask, writes=['nm'])
    self.ts('dve', ngm[0:64, :], nm[0:64, :], -1.0, BIG, ALU.add, ALU.mult, ['nm'], ['ngm'])
    self.tt('dve', Bt[0:64], Bt[0:64], bc_mid(nm[0:64, :], 60), ALU.mult, ['Bt', 'nm'], ['Bt'])
    self.tt('dve', Bt[0:64], Bt[0:64], bc_mid(ngm[0:64, :], 60), ALU.add, ['Bt', 'ngm'], ['Bt'])
    idb64 = self.identb[0:64, 0:64]
    XN = []
    for ch in range(2):
        XN.append(dict(ssb=A.alloc([1024]), pf=A.alloc([1024]), pb=A.alloc([1024], BF16), pT=[A.alloc([512], BF16) for _ in range(2)],
                       st=A.alloc([4]), osb=A.alloc([512])))

    def chain(cn):
        x = XN[cn]
        K_ = lambda n: f'{n}_{cn}'
        BS, BC, BT, BO = B[4 * cn:4 * cn + 4]
        kS, kC, kT, kO = [f'bk{4 * cn + i}' for i in range(4)]
        ptb = BT.bitcast(BF16)
        ssb, pf, pb, st, osb = x['ssb'], x['pf'], x['pb'], x['st'], x['osb']
        heads = (2 * cn, 2 * cn + 1)
        npt = 0
        for r in range(64):
            r0 = min(max(r - 4, 0), 56)
            dr0 = r0 - r + 7
            Vx, ti0, vk = (Ve, r0 // 2, 'Ve') if r0 % 2 == 0 else (Vo, (r0 - 1) // 2, 'Vo')
            for hi, h in enumerate(heads):
                qs = qdT[:, h, r * 64:(r + 1) * 64]
                self.mm(BS[0:64, :], qs, kdT[:, h, r0 * 64:r0 * 64 + 512], True, True, ['qdT', 'kdT'], [kS])
                self.mm(BC[0:64, :], qs, kcT[:, h, :], True, True, ['qdT', 'kcT'], [kC])
                yield
                self.stt('dve', ssb[0:64, 0:512].rearrange("p (a k) -> p a k", a=8), BS[0:64, :].rearrange("p (a k) -> p a k", a=8), scale,
                         Bt[0:64, h * 15 + dr0:h * 15 + dr0 + 8, :], ALU.mult, ALU.add, [kS, 'Bt'], [K_('ssb')])
                self.act(ssb[0:64, 512:1024], BC[0:64, :], AF.Copy, [kC], [K_('ssb')], scale=scale)
                yield
                self.P.op('dve', lambda e, o=st[0:64, 0:1], i=ssb[0:64, :]: e.tensor_reduce(out=o, in_=i, axis=AX.X, op=ALU.max),
                          [K_('ssb')], [K_('mx')])
                yield
                self.ts('dve', st[0:64, 1:2], st[0:64, 0:1], -1.0, None, ALU.mult, None, [K_('mx')], [K_('nmx')])
                yield
                self.act(pf[0:64, :], ssb[0:64, :], AF.Exp, [K_('ssb'), K_('nmx')], [K_('pf'), K_('sm')], bias=st[0:64, 1:2], accum_out=st[0:64, 2:3])
                yield
                self.recip(st[0:64, 3:4], st[0:64, 2:3], [K_('sm')], [K_('rsm')])
                yield
                self.ts('dve', pb[0:64, :], pf[0:64, :], st[0:64, 3:4], None, ALU.mult, None, [K_('pf'), K_('rsm')], [K_('pb')])
                yield
                for kc in range(8):
                    self.tr(ptb[:, kc * 64:(kc + 1) * 64], pb[0:64, kc * 128:(kc + 1) * 128], idb64, [K_('pb'), 'identb'], [kT])
                yield
                pt_ = x['pT'][npt % 2]
                pk = K_(f'pT{npt % 2}')
                npt += 1
                self.cp('act', pt_, ptb[:, 0:512], [kT], [pk])
                yield
                oc = hi * 256 + (r % 4) * 64
                for jj in range(8):
                    lhs = Vx[:, ti0 + jj, h * 128:(h + 1) * 128] if jj < 4 else Vc[:, jj - 4, h * 128:(h + 1) * 128]
                    self.mm(BO[:, oc:oc + 64], lhs, pt_[:, jj * 64:(jj + 1) * 64], jj == 0, jj == 7, [vk, 'Vc', pk], [kO])
                yield
            if r % 4 == 3:
                self.cp('dve' if cn else 'act', osb, BO, [kO], [K_('osb')])
                yield
                for hi, h in enumerate(heads):
                    P.dma('sp' if cn else 'act', self.mixT[h * 128:(h + 1) * 128, t0 + (r - 3) * 64:t0 + (r + 1) * 64],
                          osb[:, hi * 256:(hi + 1) * 256], reads=[K_('osb')])

    gens = [chain(0), chain(1)]
    while gens:
        for g_ in list(gens):
            try:
                next(g_)
            except StopIteration:
                gens.remove(g_)


def phase_odd_mix(self, l):
    j = l // 2
    self.phase_begin()
    _ssd(self, l)
    if self.upto == 'S1':
        return
    self.P.barrier()
    self.A.reset(self.const_end)
    _attn_dense(self, l, 1552, 2064, 2576, 4, 4, None, None, False, self.o_nk, self.o_nv, only_prompts=True)
    self.P.barrier()
    self.A.reset(self.const_end)
    _na_sample(self, l)


K.phase_odd_mix = phase_odd_mix
```

```python
import numpy as np
import concourse.bass as bass
import concourse.mybir as mybir
from concourse.bass_utils import run_bass_kernel_spmd

F32 = mybir.dt.float32
BF16 = mybir.dt.bfloat16
AF = mybir.ActivationFunctionType
ALU = mybir.AluOpType
AX = mybir.AxisListType

ENGS = ('pe', 'act', 'dve', 'pool', 'sp')
SEM_ROT = 20000
NDSEM = 12

D = 1024
NT = 4608
EPS = 1e-6
SEQS = [(0, 256, False), (256, 256, False), (512, 4096, True)]
EV_IN = 3088
MERGE_CONV_LOADS = False


class Prog:
    def __init__(self, nc, same_engine_sync=True):
        self.nc = nc
        self.same = same_engine_sync
        self.stream = {e: [] for e in ENGS}
        self.nsem = 0
        self.esem = {}
        self.ecnt = {}
        self.allsems = []
        for e in ENGS:
            self._new_esem(e)
        self.known = {e: {} for e in ENGS}
        self.res = {}
        self.dsems = {}
        self.dpos = {}
        self.n_ops = 0

    def _alloc_sem(self, name):
        self.nsem += 1
        s = self.nc.alloc_semaphore(name=f"{name}_{self.nsem}")
        return s

    def _new_esem(self, e):
        self.esem[e] = self._alloc_sem(f"e_{e}")
        self.ecnt[e] = 0

    def _deps(self, eng, reads, writes):
        deps = {}

        def add(tok):
            if tok is None:
                return
            sem, val, src = tok
            if src == eng and (eng == 'pe' or not self.same):
                return
            k = id(sem)
            if k not in deps or deps[k][1] < val:
                deps[k] = (sem, val)
        for r in reads:
            st = self.res.get(r)
            if st:
                add(st[0])
        for w in writes:
            st = self.res.get(w)
            if st:
                add(st[0])
                for t in st[1]:
                    add(t)
        out = []
        kn = self.known[eng]
        for k, (sem, val) in deps.items():
            if kn.get(k, 0) >= val:
                continue
            kn[k] = val
            out.append((sem, val))
        return out

    def _commit(self, tok, reads, writes):
        for w in writes:
            self.res[w] = [tok, []]
        for r in reads:
            st = self.res.setdefault(r, [None, []])
            st[1].append(tok)
            if len(st[1]) > 48:
                best = {}
                for t in st[1]:
                    k = id(t[0])
                    if k not in best or best[k][1] < t[1]:
                        best[k] = t
                st[1] = list(best.values())

    def op(self, eng, fn, reads=(), writes=()):
        waits = self._deps(eng, reads, writes)
        if self.ecnt[eng] >= SEM_ROT:
            self._new_esem(eng)
        sem = self.esem[eng]
        self.ecnt[eng] += 1
        tok = (sem, self.ecnt[eng], eng)
        self.stream[eng].append((waits, fn, sem, 1))
        self._commit(tok, reads, writes)
        self.n_ops += 1
        return tok

    def dma(self, q, out, in_, reads=(), writes=(), **kw):
        waits = self._deps(q, reads, writes)
        if q not in self.dsems:
            self.dsems[q] = [[self._alloc_sem(f"d_{q}"), 0] for _ in range(NDSEM)]
            self.dpos[q] = 0
        slot = self.dsems[q][self.dpos[q] % NDSEM]
        self.dpos[q] += 1
        sem = slot[0]
        kn = self.known[q]
        if slot[1] > 0 and kn.get(id(sem), 0) < slot[1]:
            waits.append((sem, slot[1]))
            kn[id(sem)] = slot[1]
        slot[1] += 16
        tok = (sem, slot[1], 'dma_' + q)
        self.stream[q].append((waits, lambda e, o=out, i=in_, k=kw: e.dma_start(out=o, in_=i, **k), sem, 16))
        self._commit(tok, reads, writes)
        self.n_ops += 1
        return tok

    def barrier(self):
        pts = []
        for e in ENGS:
            if self.ecnt[e] > 0:
                pts.append((self.esem[e], self.ecnt[e]))
        for q in self.dsems:
            for s in self.dsems[q]:
                if s[1] > 0:
                    pts.append((s[0], s[1]))
        for e in ENGS:
            kn = self.known[e]
            w = []
            for (s, v) in pts:
                if kn.get(id(s), 0) < v:
                    kn[id(s)] = v
                    w.append((s, v))
            if w:
                self.stream[e].append((w, None, None, 0))
        self.res = {}

    def emit(self):
        nc = self.nc
        with nc.Block() as block:
            def run(e, name):
                for waits, fn, sem, inc in self.stream[name]:
                    for (s, v) in waits:
                        e.wait_ge(s, v)
                    if fn is not None:
                        fn(e).then_inc(sem, inc)

            @block.tensor
            def _(e):
                run(e, 'pe')

            @block.scalar
            def _(e):
                run(e, 'act')

            @block.vector
            def _(e):
                run(e, 'dve')

            @block.gpsimd
            def _(e):
                run(e, 'pool')

            @block.sync
            def _(e):
                run(e, 'sp')


class Arena:
    def __init__(self, ap, nwords):
        self.ap = ap
        self.n = nwords
        self.off = 0
        self.uid = 0

    def reset(self, off=0):
        self.off = off

    def alloc(self, shape, dtype=F32):
        n = int(np.prod(shape))
        words = n if dtype == F32 else (n + 1) // 2
        words = (words + 7) // 8 * 8
        assert self.off + words <= self.n, f"arena overflow {self.off}+{words}>{self.n}"
        a = self.ap[:, self.off:self.off + words]
        self.off += words
        if dtype != F32:
            a = a.bitcast(dtype)
        a = a[:, 0:n]
        if len(shape) == 2:
            a = a.rearrange("p (a b) -> p a b", a=shape[0])
        elif len(shape) == 3:
            a = a.rearrange("p (a b c) -> p a b c", a=shape[0], b=shape[1])
        return a


def bc_last(a, n):
    return bass.AP(a.tensor, a.offset, [list(x) for x in a.ap] + [[0, n]])


def bc_mid(a, n):
    return bass.AP(a.tensor, a.offset, [list(a.ap[0]), [0, n]] + [list(x) for x in a.ap[1:]])


class K:
    def __init__(self, n_layers=4, upto=None, taps=()):
        self.n_layers = n_layers
        self.upto = upto
        self.taps = taps
        nc = bass.Bass("TRN2", target_bir_lowering=False)
        self.nc = nc
        self.P = Prog(nc)
        self.uid = 0
        di = lambda name, shape: nc.dram_tensor(name, list(shape), F32, kind="ExternalInput").ap()
        do = lambda name, shape: nc.dram_tensor(name, list(shape), F32, kind="ExternalOutput").ap()
        ds = lambda name, shape: nc.dram_tensor(name, list(shape), F32, kind="Internal").ap()
        self.xin = di("xin", [NT, D])
        self.cvec = di("cvec", [2, D])
        self.state_gdn = di("state_gdn", [2, 2, 4, 128, 128])
        self.cgk = di("cache_gqa_k", [2, 512, 2, 128])
        self.cgv = di("cache_gqa_v", [2, 512, 2, 128])
        self.state_ssd = di("state_ssd", [2, 2, 8, 64, 128])
        self.cnk = di("cache_na_k", [2, 512, 4, 128])
        self.cnv = di("cache_na_v", [2, 512, 4, 128])
        self.ada_w = di("ada_w", [4, D, 6 * D])
        self.ada_b = di("ada_b", [4, 6 * D])
        self.norms = {n: di(n, [4, D]) for n in ("norm_mix_pre", "norm_mix_post", "norm_mlp_pre", "norm_mlp_post")}
        self.mlp_w1 = di("mlp_w1", [4, D, 4 * D])
        self.mlp_w2 = di("mlp_w2", [4, 4 * D, D])
        self.ev_w_in = di("ev_w_in", [2, D, EV_IN])
        self.ev_w_out = di("ev_w_out", [2, D, D])
        self.gdn_conv = di("gdn_conv", [2, 3, 1536])
        self.gdn_a_log = di("gdn_a_log", [2, 8])
        self.gdn_dt_bias = di("gdn_dt_bias", [2, 8])
        self.gdn_norm = di("gdn_norm", [2, 128])
        self.gqa_q_norm = di("gqa_q_norm", [2, 128])
        self.gqa_k_norm = di("gqa_k_norm", [2, 128])
        self.od_w_in = di("od_w_in", [2, D, EV_IN])
        self.od_w_out = di("od_w_out", [2, D, D])
        self.ssd_conv = di("ssd_conv", [2, 3, 1024])
        self.ssd_conv_b = di("ssd_conv_b", [2, 1024])
        self.ssd_a_log = di("ssd_a_log", [2, 16])
        self.ssd_dt_bias = di("ssd_dt_bias", [2, 16])
        self.ssd_d = di("ssd_d", [2, 8])
        self.ssd_norm = di("ssd_norm", [2, 512])
        self.na_rpb = di("na_rpb", [2, 60, 31])
        self.c_ident = di("c_ident", [128, 128])
        self.c_masks = di("c_masks", [8, 64, 64])
        self.c_rope = di("c_rope", [2, 128, 4096])
        self.c_perm = di("c_perm", [128, 128])
        self.c_namask = di("c_namask", [64, 64])
        self.yout = do("yout", [NT, D])
        self.o_gdn = do("o_gdn", [2, 2, 2, 4, 128, 128])
        self.o_gk = do("o_gk", [2, 2, 256, 2, 128])
        self.o_gv = do("o_gv", [2, 2, 256, 2, 128])
        self.o_ssd = do("o_ssd", [2, 2, 2, 8, 64, 128])
        self.o_nk = do("o_nk", [2, 2, 256, 4, 128])
        self.o_nv = do("o_nv", [2, 2, 256, 4, 128])
        self.xres = ds("xres", [NT, D])
        self.modrow = ds("modrow", [4, 2, 6 * D])
        self.projT = ds("projT", [3200, NT])
        self.projN = ds("projN", [NT, EV_IN])
        self.mixN = ds("mixN", [NT, 512])
        self.mixT = ds("mixT", [512, NT])
        self.qkn = ds("qkn", [1024, NT])
        self.kvn = ds("kvn", [NT, 1024])
        self.rpbpad = ds("rpbpad", [60, 160])
        self.gb = ds("gb", [NT, 16])
        self.sb = ds("sb", [NT, 32])
        self.yf = ds("yf", [NT, 512])
        self.yb = ds("yb", [NT, 512])
        self.og0 = ds("og0", [NT, 512])
        self.og1 = ds("og1", [NT, 512])
        self.tapbufs = {}
        for (name, shape) in taps:
            self.tapbufs[name] = do("tap_" + name, shape)
        self.NW = 53000
        self.arena_t = nc.alloc_sbuf_tensor("arena", [128, self.NW], F32)
        self.A = Arena(self.arena_t.ap() if hasattr(self.arena_t, 'ap') else self.arena_t[:, :], self.NW)
        self.banks = []
        for i in range(8):
            t = nc.alloc_psum_tensor(f"bank{i}", [128, 512], F32)
            self.banks.append(t.ap() if hasattr(t, 'ap') else t[:, :])

    def bk(self, w, *aps):
        w = list(w)
        for a in aps:
            nm = getattr(getattr(a, 'tensor', None), 'name', '')
            if isinstance(nm, str) and nm.startswith('bank'):
                k = 'X' + nm
                if k not in w:
                    w.append(k)
        return w

    def mm(self, out, lhsT, rhs, start, stop, r, w):
        return self.P.op('pe', lambda e: e.matmul(out, lhsT=lhsT, rhs=rhs, start=start, stop=stop), r, self.bk(w, out))

    def tr(self, out, in_, ident, r, w):
        return self.P.op('pe', lambda e: e.transpose(out, in_, ident), r, self.bk(w, out))

    def act(self, out, in_, func, r, w, **kw):
        return self.P.op('act', lambda e: e.activation(out=out, in_=in_, func=func, **kw), r, self.bk(w, out, in_))

    def ts(self, eng, out, in0, s1, s2, op0, op1, r, w):
        w = self.bk(w, out, in0)
        if op1 is None:
            return self.P.op(eng, lambda e: e.tensor_scalar(out=out, in0=in0, scalar1=s1, scalar2=None, op0=op0), r, w)
        return self.P.op(eng, lambda e: e.tensor_scalar(out=out, in0=in0, scalar1=s1, scalar2=s2, op0=op0, op1=op1), r, w)

    def tt(self, eng, out, in0, in1, op, r, w):
        return self.P.op(eng, lambda e: e.tensor_tensor(out=out, in0=in0, in1=in1, op=op), r, self.bk(w, out, in0, in1))

    def stt(self, eng, out, in0, scalar, in1, op0, op1, r, w):
        return self.P.op(eng, lambda e: e.scalar_tensor_tensor(out=out, in0=in0, scalar=scalar, in1=in1, op0=op0, op1=op1), r,
                         self.bk(w, out, in0, in1))

    def recip(self, out, in_, r, w):
        return self.P.op('dve', lambda e: e.reciprocal(out=out, in_=in_), r, self.bk(w, out, in_))

    def cp(self, eng, out, in_, r, w):
        if eng == 'act':
            return self.act(out, in_, AF.Copy, r, w)
        return self.P.op(eng, lambda e: e.tensor_copy(out=out, in_=in_), r, self.bk(w, out, in_))

    def memset(self, eng, ap, val, w):
        return self.P.op(eng, lambda e: e.memset(ap, val), (), w)

    def key(self, name):
        self.uid += 1
        return f"{name}#{self.uid}"

    def tap(self, name, src_ap, reads):
        if name in self.tapbufs:
            self.P.dma('sp', self.tapbufs[name], src_ap, reads=reads)

    def phase_begin(self):
        self.P.barrier()
        A = self.A
        A.reset()
        self.ident = A.alloc([128])
        self.identb = A.alloc([128], BF16)
        self.P.dma('sp', self.ident, self.c_ident, writes=['ident'])
        self.cp('dve', self.identb, self.ident, ['ident'], ['identb'])
        self.ones = A.alloc([128])
        self.memset('pool', self.ones, 1.0, ['ones'])
        self.onesb = A.alloc([128], BF16)
        self.memset('pool', self.onesb, 1.0, ['onesb'])
        self.const_end = A.off

    def load_bcast(self, dst, row_ap, key, q='sp'):
        self.P.dma(q, dst, row_ap.partition_broadcast(128), writes=[key])

    def rstd_from_ss(self, rstd, ss, n, r, w):
        self.act(rstd, ss, AF.Sqrt, r, w, bias=EPS, scale=1.0 / n)
        self.P.op('dve', lambda e: e.reciprocal(out=rstd, in_=rstd), w, w)

    def phase_ada(self):
        self.phase_begin()
        A, P = self.A, self.P
        cT = A.alloc([2, 8])
        for v in range(2):
            P.dma('sp', cT[:, v, :], self.cvec[v, :].rearrange("(k p) -> p k", p=128), writes=['cT'],
                  allow_slow_non_contiguous=True)
        sc = A.alloc([2, 8])
        self.act(sc, cT, AF.Silu, ['cT'], ['sc'])
        L = A.alloc([2, 8, 128])
        for v in range(2):
            self.cp('dve', L[:, v, :, :], bc_last(sc[:, v, :], 128), ['sc'], ['L'])
        wbuf = [A.alloc([1536]) for _ in range(3)]
        bb = A.alloc([1536])
        ob = [A.alloc([1536]) for _ in range(2)]
        for l in range(self.n_layers):
            for g in range(4):
                c0 = g * 1536
                self.load_bcast(bb, self.ada_b[l, c0:c0 + 1536], 'bb', q='act')
                for k in range(8):
                    wb = wbuf[k % 3]
                    wk = f'wbuf{k % 3}'
                    P.dma('sp', wb, self.ada_w[l, k * 128:(k + 1) * 128, c0:c0 + 1536], writes=[wk])
                    for v in range(2):
                        for n in range(3):
                            self.mm(self.banks[v * 3 + n], L[:, v, k, :], wb[:, n * 512:(n + 1) * 512],
                                    k == 0, k == 7, [wk, 'L'], [f'bank{v * 3 + n}'])
                for v in range(2):
                    for n in range(3):
                        self.tt('dve', ob[v][:, n * 512:(n + 1) * 512], self.banks[v * 3 + n], bb[:, n * 512:(n + 1) * 512],
                                ALU.add, [f'bank{v * 3 + n}', 'bb'], [f'ob{v}'])
                    P.dma('act', self.modrow[l, v:v + 1, c0:c0 + 1536], ob[v][0:1, :], reads=[f'ob{v}'])

    def mod_vec(self, dst, l, v, idx, key, plus1=False):
        self.load_bcast(dst, self.modrow[l, v, idx * D:(idx + 1) * D], key, q='act')
        if plus1:
            self.ts('dve', dst, dst, 1.0, None, ALU.add, None, [key], [key])

    def phase_A(self, l, xsrc):
        even = (l % 2 == 0)
        j = l // 2
        w_in = self.ev_w_in[j] if even else self.od_w_in[j]
        if even:
            fm = [(c, 128) for c in range(0, 1024, 128)] + [(c, 128) for c in range(2064, 2832, 128)]
            tm = [(512, 512), (1024, 512), (1536, 512), (2048, 16), (2832, 256)]
        else:
            fm = [(c, 128) for c in range(1024, 1536, 128)] + [(c, 128) for c in range(1552, 2576, 128)]
            tm = [(0, 512), (512, 512), (1024, 256), (1536, 16), (2576, 512)]
        self.phase_begin()
        A, P = self.A, self.P
        W = A.alloc([8, EV_IN], BF16)
        for k in range(8):
            P.dma('pool', W[:, k, :], w_in[k * 128:(k + 1) * 128, :], writes=[f'W{k}'])
        Wk = [f'W{k}' for k in range(8)]
        tmpv = A.alloc([D])
        self.load_bcast(tmpv, self.norms["norm_mix_pre"][l, :], 'tmpv')
        A1 = []
        SH = []
        for v in range(2):
            a1 = A.alloc([D])
            sh = A.alloc([D])
            self.mod_vec(sh, l, v, 0, f'sh{v}')
            self.mod_vec(a1, l, v, 1, f'a1{v}', plus1=True)
            self.tt('dve', a1, a1, tmpv, ALU.mult, [f'a1{v}', 'tmpv'], [f'a1{v}'])
            A1.append(a1)
            SH.append(sh)
        xt = [A.alloc([D]) for _ in range(2)]
        junk = A.alloc([D], BF16)
        tmpf = A.alloc([D])
        hm = [A.alloc([D], BF16) for _ in range(2)]
        st = A.alloc([4])
        hmT = [A.alloc([8, 512], BF16) for _ in range(2)]
        stage = [A.alloc([512]) for _ in range(4)]
        ptr = [self.banks[0].bitcast(BF16), self.banks[1].bitcast(BF16)]
        nst = 0
        nps = 0
        for g in range(9):
            v = 0 if g == 0 else 1
            hT = hmT[g % 2]
            hk = f'hmT{g % 2}'
            for ti in range(4):
                t = g * 4 + ti
                s = t % 2
                P.dma('sp', xt[s], xsrc[t * 128:(t + 1) * 128, :], writes=[f'xt{s}'])
                self.act(junk, xt[s], AF.Square, [f'xt{s}'], ['junk', f'ss{s}'], accum_out=st[:, s:s + 1])
                self.rstd_from_ss(st[:, 2 + s:3 + s], st[:, s:s + 1], D, [f'ss{s}'], [f'rs{s}'])
                self.stt('dve', tmpf, xt[s], st[:, 2 + s:3 + s], A1[v], ALU.mult, ALU.mult,
                         [f'xt{s}', f'rs{s}', f'a1{v}'], ['tmpf'])
                self.tt('dve', hm[s], tmpf, SH[v], ALU.add, ['tmpf', f'sh{v}'], [f'hm{s}'])
                for k in range(8):
                    self.tr(ptr[s][:, k * 128:(k + 1) * 128], hm[s][:, k * 128:(k + 1) * 128], self.identb,
                            [f'hm{s}', 'identb'], [f'ptr{s}'])
                self.cp('act', hT[:, :, ti * 128:(ti + 1) * 128], ptr[s].rearrange("p (k t) -> p k t", k=8),
                        [f'ptr{s}'], [hk])
            for (c0, wd) in fm:
                b = 2 + nps % 3
                nps += 1
                for k in range(8):
                    self.mm(self.banks[b][0:wd, :], W[:, k, c0:c0 + wd], hT[:, k, :], k == 0, k == 7,
                            [Wk[k], hk], [f'bank{b}'])
                sg = nst % 4
                nst += 1
                self.cp('act' if nst % 2 else 'dve', stage[sg][0:wd, :], self.banks[b][0:wd, :], [f'bank{b}'], [f'stage{sg}'])
                P.dma('sp', self.projT[c0:c0 + wd, g * 512:(g + 1) * 512], stage[sg][0:wd, :], reads=[f'stage{sg}'])
            for ti in range(4):
                t = g * 4 + ti
                for (c0, wd) in tm:
                    b = 5 + nps % 3
                    nps += 1
                    for k in range(8):
                        self.mm(self.banks[b][:, 0:wd], hT[:, k, ti * 128:(ti + 1) * 128], W[:, k, c0:c0 + wd],
                                k == 0, k == 7, [Wk[k], hk], [f'bank{b}'])
                    sg = nst % 4
                    nst += 1
                    self.cp('act' if nst % 2 else 'dve', stage[sg][:, 0:wd], self.banks[b][:, 0:wd], [f'bank{b}'], [f'stage{sg}'])
                    P.dma('sp', self.projN[t * 128:(t + 1) * 128, c0:c0 + wd], stage[sg][:, 0:wd], reads=[f'stage{sg}'])

    def phase_C0(self, l, xsrc):
        even = (l % 2 == 0)
        j = l // 2
        w_out = self.ev_w_out[j] if even else self.od_w_out[j]
        self.phase_begin()
        A, P = self.A, self.P
        W1pre = A.alloc([8, 4 * D], BF16)
        W = A.alloc([8, D], BF16)
        for k in range(8):
            P.dma('pool', W[:, k, :], w_out[k * 128:(k + 1) * 128, :], writes=[f'W{k}'])
        tmpv = A.alloc([D])
        self.load_bcast(tmpv, self.norms["norm_mix_post"][l, :], 'tmpv')
        G1 = []
        for v in range(2):
            g1 = A.alloc([D])
            self.mod_vec(g1, l, v, 2, f'g1{v}')
            self.tt('dve', g1, g1, tmpv, ALU.mult, [f'g1{v}', 'tmpv'], [f'g1{v}'])
            G1.append(g1)
        xt = [A.alloc([D]) for _ in range(2)]
        mn = [A.alloc([512], BF16) for _ in range(2)]
        mT = [A.alloc([4, 512], BF16) for _ in range(2)]
        mTn = [A.alloc([4, 128], BF16) for _ in range(2)]
        junk = A.alloc([D], BF16)
        tmpf = A.alloc([D])
        st = A.alloc([8])
        ptr = [self.banks[0].bitcast(BF16), self.banks[1].bitcast(BF16)]
        for g in range(9):
            v = 0 if g == 0 else 1
            gs = g % 2
            for k in range(4):
                P.dma('pool', mT[gs][:, k, :], self.mixT[k * 128:(k + 1) * 128, g * 512:(g + 1) * 512], writes=[f'mT{gs}'])
            for ti in range(4):
                t = g * 4 + ti
                s = t % 2
                P.dma('sp', xt[s], xsrc[t * 128:(t + 1) * 128, :], writes=[f'xt{s}'])
                P.dma('pool', mn[s], self.mixN[t * 128:(t + 1) * 128, :], writes=[f'mn{s}'])
                for k in range(4):
                    self.tr(ptr[s][:, k * 128:(k + 1) * 128], mn[s][:, k * 128:(k + 1) * 128], self.identb,
                            [f'mn{s}', 'identb'], [f'ptr{s}'])
                self.cp('act', mTn[s], ptr[s][:, 0:512].rearrange("p (k t) -> p k t", k=4), [f'ptr{s}'], [f'mTn{s}'])
                for h in range(2):
                    b = 2 + 2 * s + h
                    for k in range(8):
                        lhs = mTn[s][:, k, :] if k < 4 else mT[gs][:, k - 4, ti * 128:(ti + 1) * 128]
                        self.mm(self.banks[b], lhs, W[:, k, h * 512:(h + 1) * 512], k == 0, k == 7,
                                [f'W{k}', f'mTn{s}', f'mT{gs}'], [f'bank{b}'])
                    self.act(junk[:, 0:512], self.banks[b], AF.Square, [f'bank{b}'], ['junk', f'ss{s}{h}'],
                             accum_out=st[:, 2 * s + h:2 * s + h + 1])
                self.tt('dve', st[:, 4 + s:5 + s], st[:, 2 * s:2 * s + 1], st[:, 2 * s + 1:2 * s + 2], ALU.add,
                        [f'ss{s}0', f'ss{s}1'], [f'sst{s}'])
                self.rstd_from_ss(st[:, 6 + s:7 + s], st[:, 4 + s:5 + s], D, [f'sst{s}'], [f'rs{s}'])
                for h in range(2):
                    b = 2 + 2 * s + h
                    self.stt('dve', tmpf[:, h * 512:(h + 1) * 512], self.banks[b], st[:, 6 + s:7 + s],
                             G1[v][:, h * 512:(h + 1) * 512], ALU.mult, ALU.mult, [f'bank{b}', f'rs{s}', f'g1{v}'], ['tmpf'])
                self.tt('pool', xt[s], xt[s], tmpf, ALU.add, [f'xt{s}', 'tmpf'], [f'xt{s}'])
                P.dma('sp', self.xres[t * 128:(t + 1) * 128, :], xt[s], reads=[f'xt{s}'])
            if g < 8:
                P.dma('pool', W1pre[:, g, :], self.mlp_w1[l, g * 128:(g + 1) * 128, :], writes=[f'W1pre{g}'])
        self.w1_prefetched = l

    def phase_C1(self, l, xdst):
        self.phase_begin()
        A, P = self.A, self.P
        W1 = A.alloc([8, 4 * D], BF16)
        W2 = A.alloc([32, D], BF16)
        if getattr(self, 'w1_prefetched', None) != l:
            for k in range(8):
                P.dma('pool', W1[:, k, :], self.mlp_w1[l, k * 128:(k + 1) * 128, :], writes=[f'W1{k}'])
        for k in range(32):
            P.dma('pool', W2[:, k, :], self.mlp_w2[l, k * 128:(k + 1) * 128, :], writes=[f'W2{k}'])
        W1k = [f'W1{k}' for k in range(8)]
        cur = {}
        cur['sh'] = A.alloc([D])
        cur['a2'] = A.alloc([D])
        cur['g2'] = A.alloc([D])
        xt = [A.alloc([D]) for _ in range(2)]
        junk = A.alloc([D], BF16)
        tmpf = A.alloc([D])
        hm = [A.alloc([D], BF16) for _ in range(2)]
        st = A.alloc([16])
        hmT = A.alloc([8, 512], BF16)
        h1T = A.alloc([32, 512], BF16)
        r1 = [A.alloc([512]) for _ in range(2)]
        ptr = [self.banks[0].bitcast(BF16), self.banks[1].bitcast(BF16)]

        def load_vecs(v):
            if cur.get('v') == v:
                return
            cur['v'] = v
            self.mod_vec(cur['sh'], l, v, 3, 'sh2')
            self.mod_vec(cur['a2'], l, v, 4, 'a2', plus1=True)
            self.load_bcast(tmpf, self.norms["norm_mlp_pre"][l, :], 'tmpf')
            self.tt('dve', cur['a2'], cur['a2'], tmpf, ALU.mult, ['a2', 'tmpf'], ['a2'])
            self.mod_vec(cur['g2'], l, v, 5, 'g2')
            self.load_bcast(tmpf, self.norms["norm_mlp_post"][l, :], 'tmpf')
            self.tt('dve', cur['g2'], cur['g2'], tmpf, ALU.mult, ['g2', 'tmpf'], ['g2'])

        nx = 0
        for g in range(9):
            v = 0 if g == 0 else 1
            load_vecs(v)
            for ti in range(4):
                t = g * 4 + ti
                s = nx % 2
                nx += 1
                P.dma('sp', xt[s], self.xres[t * 128:(t + 1) * 128, :], writes=[f'xt{s}'])
                self.act(junk, xt[s], AF.Square, [f'xt{s}'], ['junk', f'ss{s}'], accum_out=st[:, s:s + 1])
                self.rstd_from_ss(st[:, 2 + s:3 + s], st[:, s:s + 1], D, [f'ss{s}'], [f'rs{s}'])
                self.stt('dve', tmpf, xt[s], st[:, 2 + s:3 + s], cur['a2'], ALU.mult, ALU.mult,
                         [f'xt{s}', f'rs{s}', 'a2'], ['tmpf'])
                self.tt('dve', hm[s], tmpf, cur['sh'], ALU.add, ['tmpf', 'sh2'], [f'hm{s}'])
                for k in range(8):
                    self.tr(ptr[s][:, k * 128:(k + 1) * 128], hm[s][:, k * 128:(k + 1) * 128], self.identb,
                            [f'hm{s}', 'identb'], [f'ptr{s}'])
                self.cp('act', hmT[:, :, ti * 128:(ti + 1) * 128], ptr[s].rearrange("p (k t) -> p k t", k=8),
                        [f'ptr{s}'], ['hmT'])
            for f in range(32):
                b = 2 + f % 2
                for k in range(8):
                    self.mm(self.banks[b], W1[:, k, f * 128:(f + 1) * 128], hmT[:, k, :], k == 0, k == 7,
                            [W1k[k], 'hmT'], [f'bank{b}'])
                rs = f % 2
                self.act(r1[rs], self.banks[b], AF.Relu, [f'bank{b}'], [f'r1{rs}'])
                self.tt('dve' if f % 4 < 3 else 'pool', h1T[:, f, :], r1[rs], r1[rs], ALU.mult, [f'r1{rs}'], [f'h1T{f}'])
            for ti in range(4):
                t = g * 4 + ti
                s = nx % 2
                nx += 1
                P.dma('sp', xt[s], self.xres[t * 128:(t + 1) * 128, :], writes=[f'xt{s}'])
                for h in range(2):
                    b = 4 + 2 * s + h
                    for f in range(32):
                        self.mm(self.banks[b], h1T[:, f, ti * 128:(ti + 1) * 128], W2[:, f, h * 512:(h + 1) * 512],
                                f == 0, f == 31, [f'W2{f}', f'h1T{f}'], [f'bank{b}'])
                    self.act(junk[:, 0:512], self.banks[b], AF.Square, [f'bank{b}'], ['junk', f'q{s}{h}'],
                             accum_out=st[:, 4 + 2 * s + h:5 + 2 * s + h])
                self.tt('dve', st[:, 8 + s:9 + s], st[:, 4 + 2 * s:5 + 2 * s], st[:, 5 + 2 * s:6 + 2 * s], ALU.add,
                        [f'q{s}0', f'q{s}1'], [f'qt{s}'])
                self.rstd_from_ss(st[:, 10 + s:11 + s], st[:, 8 + s:9 + s], D, [f'qt{s}'], [f'qr{s}'])
                for h in range(2):
                    b = 4 + 2 * s + h
                    self.stt('dve', tmpf[:, h * 512:(h + 1) * 512], self.banks[b], st[:, 10 + s:11 + s],
                             cur['g2'][:, h * 512:(h + 1) * 512], ALU.mult, ALU.mult, [f'bank{b}', f'qr{s}', 'g2'], ['tmpf'])
                self.tt('pool', xt[s], xt[s], tmpf, ALU.add, [f'xt{s}', 'tmpf'], [f'xt{s}'])
                P.dma('sp', xdst[t * 128:(t + 1) * 128, :], xt[s], reads=[f'xt{s}'])

    def build(self):
        P = self.P
        self.phase_ada()
        if self.upto == 'ada':
            return self.finish()
        if self.upto in ('T1', 'S1'):
            self.phase_A(1, self.xin)
            self.phase_odd_mix(1)
            return self.finish()
        for l in range(self.n_layers):
            xsrc = self.xin if l == 0 else self.xres
            self.phase_A(l, xsrc)
            if self.upto == f'A{l}':
                return self.finish()
            if l % 2 == 0:
                self.phase_even_mix(l)
            else:
                self.phase_odd_mix(l)
            if self.upto in (f'M{l}', 'G1', 'G2', 'G3', 'S1'):
                return self.finish()
            self.phase_C0(l, xsrc)
            last = (l == self.n_layers - 1)
            self.phase_C1(l, self.yout if last else self.xres)
        return self.finish()

    def finish(self):
        P = self.P
        P.barrier()
        for name, buf in self.tapbufs.items():
            src = getattr(self, name)
            P.dma('sp', buf, src)
        P.barrier()
        P.emit()
        return self.nc

    def phase_even_mix(self, l):
        raise NotImplementedError

    def phase_odd_mix(self, l):
        raise NotImplementedError


def host_consts():
    ident = np.eye(128, dtype=np.float32)
    t = np.arange(64)
    masks = np.zeros((8, 64, 64), np.float32)
    masks[0] = (t[:, None] <= t[None, :])
    masks[1] = (t[:, None] > t[None, :])
    masks[2] = (t[:, None] >= t[None, :])
    masks[3] = (t[:, None] < t[None, :])
    masks[4] = ((t[:, None] // 32) == (t[None, :] // 32))
    masks[5] = 1.0 - masks[4]
    half = 64
    inv_freq = (10000.0 ** (-np.arange(0, half, 2, dtype=np.float32) / half)).astype(np.float32)
    tok = np.arange(4096)
    row = (tok // 64).astype(np.float32)
    col = (tok % 64).astype(np.float32)
    ang = np.concatenate([row[None, :] * inv_freq[:, None], row[None, :] * inv_freq[:, None],
                          col[None, :] * inv_freq[:, None], col[None, :] * inv_freq[:, None]], 0)
    C = np.cos(ang).astype(np.float32)
    S = np.sin(ang).astype(np.float32)
    sign = np.ones((128, 1), np.float32)
    sign[0:32] = -1.0
    sign[64:96] = -1.0
    rope = np.stack([C, S * sign]).astype(np.float32)
    perm = np.zeros((128, 128), np.float32)
    for p in range(128):
        blk = p // 64 * 64
        q = p - blk
        partner = blk + (q + 32) % 64
        perm[partner, p] = 1.0
    q = np.arange(64)
    kc = np.arange(64)
    cs = np.clip(q - 8, 0, 48)
    namask = ((kc[None, :] >= cs[:, None]) & (kc[None, :] < cs[:, None] + 16)).astype(np.float32)
    return dict(c_ident=ident, c_masks=masks, c_rope=rope, c_perm=perm, c_namask=namask)


def make_in_maps(inp):
    f = lambda a: np.ascontiguousarray(a, dtype=np.float32)
    consts = host_consts()
    shared = {}
    for n in ("ada_w", "ada_b", "norm_mix_pre", "norm_mix_post", "norm_mlp_pre", "norm_mlp_post", "mlp_w1", "mlp_w2",
              "ev_w_in", "ev_w_out", "gdn_conv", "gdn_norm", "gqa_q_norm", "gqa_k_norm", "od_w_in", "od_w_out",
              "ssd_conv", "ssd_conv_b", "ssd_d", "ssd_norm"):
        shared[n] = f(inp[n])
    shared["gdn_a_log"] = f(inp["gdn_a_log"]).reshape(2, 8)
    shared["gdn_dt_bias"] = f(inp["gdn_dt_bias"]).reshape(2, 8)
    shared["ssd_a_log"] = f(inp["ssd_a_log"]).reshape(2, 16)
    shared["ssd_dt_bias"] = f(inp["ssd_dt_bias"]).reshape(2, 16)
    shared["na_rpb"] = f(inp["na_rpb"]).reshape(2, 60, 31)
    shared.update(consts)
    maps = []
    for c in range(8):
        m = dict(shared)
        m["xin"] = f(np.concatenate([inp["x_prompt"][2 * c], inp["x_prompt"][2 * c + 1], inp["x_sample"][c]], 0))
        m["cvec"] = f(np.stack([inp["c_ctx"], inp["c"][c]], 0))
        m["state_gdn"] = f(inp["state_gdn"][c])
        m["cache_gqa_k"] = f(inp["cache_gqa_k"][c])
        m["cache_gqa_v"] = f(inp["cache_gqa_v"][c])
        m["state_ssd"] = f(inp["state_ssd"][c])
        m["cache_na_k"] = f(inp["cache_na_k"][c])
        m["cache_na_v"] = f(inp["cache_na_v"][c])
        maps.append(m)
    return maps


def _prep_qk(self, x, T, wcol, rope, out, tmp, rs, tables, pos0, kx, ko, bank):
    step = 512
    for c0 in range(0, T, step):
        w = min(step, T - c0)
        xs = x[:, c0:c0 + w]
        if wcol is not None:
            self.act(tmp[:, 0:w], xs, AF.Square, [kx], ['pq_tmp'])
            self.mm(self.banks[bank][:, 0:w], self.ones, tmp[:, 0:w], True, True, ['ones', 'pq_tmp'], [f'bank{bank}'])
            self.act(rs[:, 0:w], self.banks[bank][:, 0:w], AF.Sqrt, [f'bank{bank}'], ['pq_rs'], bias=EPS, scale=1.0 / 128)
            self.P.op('dve', lambda e, a=rs[:, 0:w]: e.reciprocal(out=a, in_=a), ['pq_rs'], ['pq_rs'])
            self.stt('dve', xs, xs, wcol, rs[:, 0:w], ALU.mult, ALU.mult, [kx, 'pq_rs', 'pq_w'], [kx])
        if rope:
            C, S, perm = tables
            self.mm(self.banks[bank][:, 0:w], perm, xs, True, True, ['perm', kx], [f'bank{bank}'])
            self.tt('dve', tmp[:, 0:w], self.banks[bank][:, 0:w], S[:, pos0 + c0:pos0 + c0 + w], ALU.mult,
                    [f'bank{bank}', 'ropeS'], ['pq_tmp'])
            self.tt('pool', rs[:, 0:w], xs, C[:, pos0 + c0:pos0 + c0 + w], ALU.mult, [kx, 'ropeC'], ['pq_rs'])
            self.tt('dve', out[:, c0:c0 + w], tmp[:, 0:w], rs[:, 0:w], ALU.add, ['pq_tmp', 'pq_rs'], [ko])
        else:
            self.cp('dve', out[:, c0:c0 + w], xs, [kx], [ko])


def _attn_dense(self, l, qrow0, krow0, vcol0, nq, nkv, sample_ctx, norm, rope, kout, vout, only_prompts=False):
    j = l // 2
    A, P = self.A, self.P
    rep = nq // nkv
    scale = 128 ** -0.5
    TKM = 4096 + 512
    kTb = A.alloc([TKM], BF16)
    Vb = A.alloc([36, 128], BF16)
    qTb = A.alloc([4096], BF16)
    xk = A.alloc([4096])
    xq = A.alloc([4096])
    tmp = A.alloc([512])
    rs = A.alloc([512])
    pt = [A.alloc([512], BF16) for _ in range(3)]
    rden = A.alloc([512])
    osb = [A.alloc([512]) for _ in range(2)]
    ctk = A.alloc([4, 128])
    kst = A.alloc([2, 128])
    tables = None
    if rope:
        C = A.alloc([4096])
        S = A.alloc([4096])
        perm = A.alloc([128])
        P.dma('sp', C, self.c_rope[0], writes=['ropeC'])
        P.dma('sp', S, self.c_rope[1], writes=['ropeS'])
        P.dma('sp', perm, self.c_perm, writes=['perm'])
        tables = (C, S, perm)
    wq = wk = None
    if norm is not None:
        wq = A.alloc([1])
        wk = A.alloc([1])
        P.dma('sp', wq, norm[0].rearrange("(p o) -> p o", o=1), writes=['pq_w'])
        P.dma('sp', wk, norm[1].rearrange("(p o) -> p o", o=1), writes=['pq_w'])
    no = 0
    npt = 0
    for si, (t0, T, is_s) in enumerate(SEQS):
        if only_prompts and is_s:
            continue
        Tk = T + (512 if (is_s and sample_ctx is not None) else 0)
        nkc = Tk // 128
        for g in range(nkv):
            P.dma('sp', xk[:, 0:T], self.projT[krow0 + g * 128:krow0 + (g + 1) * 128, t0:t0 + T], writes=['xk'])
            _prep_qk(self, xk, T, wk, rope and is_s, kTb, tmp, rs, tables, 0, 'xk', 'kTb', 6)
            if not is_s and kout is not None:
                for a in range(T // 128):
                    self.tr(self.banks[7][:, 0:128], xk[:, a * 128:(a + 1) * 128], self.ident, ['xk', 'ident'], ['bank7'])
                    self.cp('act', kst[:, a, :], self.banks[7][:, 0:128], ['bank7'], ['kst'])
                P.dma('sp', kout[si, j, :, g, :].rearrange("(a p) d -> p a d", p=128), kst, reads=['kst'])
            if Tk > T:
                ck, cv = sample_ctx
                P.dma('sp', ctk, ck[j, :, g, :].rearrange("(a p) d -> p a d", p=128), writes=['ctk'])
                for a in range(4):
                    self.tr(self.banks[7][:, 0:128], ctk[:, a, :], self.ident, ['ctk', 'ident'], ['bank7'])
                    self.cp('act', kTb[:, T + a * 128:T + (a + 1) * 128], self.banks[7][:, 0:128], ['bank7'], ['kTb'])
                P.dma('pool', Vb[:, T // 128:T // 128 + 4, :], cv[j, :, g, :].rearrange("(a p) d -> p a d", p=128), writes=['Vb'])
            P.dma('pool', Vb[:, 0:T // 128, :],
                  self.projN[t0:t0 + T, vcol0 + g * 128:vcol0 + (g + 1) * 128].rearrange("(a p) d -> p a d", p=128), writes=['Vb'])
            if not is_s and vout is not None:
                P.dma('act', vout[si, j, :, g, :], self.projN[t0:t0 + T, vcol0 + g * 128:vcol0 + (g + 1) * 128])
            for r_ in range(rep):
                h = g * rep + r_
                P.dma('sp', xq[:, 0:T], self.projT[qrow0 + h * 128:qrow0 + (h + 1) * 128, t0:t0 + T], writes=['xq'])
                _prep_qk(self, xq, T, wq, rope and is_s, qTb, tmp, rs, tables, 0, 'xq', 'qTb', 6)
                QW = min(512, T)
                for qc in range(T // QW):
                    q0 = qc * QW
                    bo = 2 + 2 * (no % 2)
                    def smm(kc_, bs_):
                        self.mm(self.banks[bs_][:, 0:QW], kTb[:, kc_ * 128:(kc_ + 1) * 128], qTb[:, q0:q0 + QW], True, True,
                                ['kTb', 'qTb'], [f'bank{bs_}'])
                    smm(0, npt % 2)
                    for kc in range(nkc):
                        bs = npt % 2
                        p_ = pt[npt % 3]
                        pk = f'pt{npt % 3}'
                        npt += 1
                        if kc + 1 < nkc:
                            smm(kc + 1, npt % 2)
                        self.act(p_[:, 0:QW], self.banks[bs][:, 0:QW], AF.Exp, [f'bank{bs}'], [pk], scale=scale)
                        self.mm(self.banks[bo][:, 0:QW], Vb[:, kc, :], p_[:, 0:QW], kc == 0, kc == nkc - 1, ['Vb', pk], [f'bank{bo}'])
                        self.mm(self.banks[bo + 1][:, 0:QW], self.onesb, p_[:, 0:QW], kc == 0, kc == nkc - 1, ['onesb', pk], [f'bank{bo + 1}'])
                    self.recip(rden[:, 0:QW], self.banks[bo + 1][:, 0:QW], [f'bank{bo + 1}'], ['rden'])
                    ob = osb[no % 2]
                    okk = f'osb{no % 2}'
                    no += 1
                    self.tt('dve', ob[:, 0:QW], self.banks[bo][:, 0:QW], rden[:, 0:QW], ALU.mult, [f'bank{bo}', 'rden'], [okk])
                    P.dma('sp', self.mixT[h * 128:(h + 1) * 128, t0 + q0:t0 + q0 + QW], ob[:, 0:QW], reads=[okk])


K._attn_dense = _attn_dense


def _gdn(self, l):
    j = l // 2
    A, P = self.A, self.P
    NB = 4
    wb = [A.alloc([1024]) for _ in range(3)]
    for k in range(3):
        self.load_bcast(wb[k], self.gdn_conv[j, k, 512:1536], f'wb{k}')
    x3s = [A.alloc([3, 1024]) for _ in range(4)]
    xm = [x_[:, 0, :] for x_ in x3s]
    x0 = [x_[:, 1, :] for x_ in x3s]
    xp = [x_[:, 2, :] for x_ in x3s]
    sq = A.alloc([512])
    ss4 = A.alloc([4])
    nt = 0
    for (t0, T, is_s) in SEQS:
        for a in range(T // 128):
            s = nt % 4
            nt += 1
            r0 = t0 + a * 128
            first, last = (a == 0), (a == T // 128 - 1)
            src = self.projN
            if MERGE_CONV_LOADS and not first and not last:
                wd_ = 1024
                src3 = bass.AP(src.tensor, (r0 - 1) * EV_IN + 512, [[EV_IN, 128], [EV_IN, 3], [1, wd_]])
                P.dma('sp' if nt % 2 else 'act', x3s[s], src3, writes=[f'x0{s}', f'xm{s}', f'xp{s}'])
            else:
                P.dma('sp', x0[s], src[r0:r0 + 128, 512:1536], writes=[f'x0{s}'])
                if first:
                    self.memset('pool', xm[s], 0.0, [f'xm{s}'])
                    P.dma('act', xm[s][1:128, :], src[r0:r0 + 127, 512:1536], writes=[f'xm{s}'])
                else:
                    P.dma('act', xm[s], src[r0 - 1:r0 + 127, 512:1536], writes=[f'xm{s}'])
                if last:
                    self.memset('pool', xp[s], 0.0, [f'xp{s}'])
                    P.dma('act', xp[s][0:127, :], src[r0 + 1:r0 + 128, 512:1536], writes=[f'xp{s}'])
                else:
                    P.dma('act', xp[s], src[r0 + 1:r0 + 129, 512:1536], writes=[f'xp{s}'])
            self.tt('dve', x0[s], x0[s], wb[1], ALU.mult, [f'x0{s}', 'wb1'], [f'x0{s}'])
            self.tt('pool', xm[s], xm[s], wb[0], ALU.mult, [f'xm{s}', 'wb0'], [f'xm{s}'])
            self.tt('pool', xp[s], xp[s], wb[2], ALU.mult, [f'xp{s}', 'wb2'], [f'xp{s}'])
            self.tt('dve', x0[s], x0[s], xm[s], ALU.add, [f'x0{s}', f'xm{s}'], [f'x0{s}'])
            self.tt('dve', x0[s], x0[s], xp[s], ALU.add, [f'x0{s}', f'xp{s}'], [f'x0{s}'])
            self.act(x0[s], x0[s], AF.Silu, [f'x0{s}'], [f'x0{s}'])
            self.tt('dve', sq, x0[s][:, 0:512], x0[s][:, 0:512], ALU.mult, [f'x0{s}'], ['sq'])
            self.P.op('dve', lambda e, o=ss4, i=sq.rearrange("p (h d) -> p h d", h=4): e.tensor_reduce(out=o, in_=i, axis=AX.X, op=ALU.add),
                      ['sq'], ['ss4'])
            self.act(ss4, ss4, AF.Sqrt, ['ss4'], ['ss4'], bias=EPS, scale=1.0)
            self.P.op('dve', lambda e, o=ss4: e.reciprocal(out=o, in_=o), ['ss4'], ['ss4'])
            kv = x0[s][:, 0:512].rearrange("p (h d) -> p h d", h=4)
            self.tt('dve', kv, kv, bc_last(ss4, 128), ALU.mult, [f'x0{s}', 'ss4'], [f'x0{s}'])
            P.dma('sp', self.kvn[r0:r0 + 128, :], x0[s], reads=[f'x0{s}'])
    raw = A.alloc([36, 16])
    for a in range(36):
        P.dma('sp', raw[:, a, :], self.projN[a * 128:(a + 1) * 128, 2048:2064], writes=['raw'])
    dtb = A.alloc([8])
    nea = A.alloc([8])
    self.load_bcast(dtb, self.gdn_dt_bias[j, :], 'dtb')
    self.load_bcast(nea, self.gdn_a_log[j, :], 'nea')
    self.act(nea, nea, AF.Exp, ['nea'], ['nea'])
    gbt = A.alloc([36, 16])
    self.act(gbt[:, :, 0:8], raw[:, :, 0:8], AF.Sigmoid, ['raw'], ['gbt'])
    self.tt('dve', raw[:, :, 8:16], raw[:, :, 8:16], bc_mid(dtb, 36), ALU.add, ['raw', 'dtb'], ['raw'])
    self.act(raw[:, :, 8:16], raw[:, :, 8:16], AF.Exp, ['raw'], ['raw'])
    self.act(raw[:, :, 8:16], raw[:, :, 8:16], AF.Ln, ['raw'], ['raw'], bias=1.0)
    self.tt('dve', raw[:, :, 8:16], raw[:, :, 8:16], bc_mid(nea, 36), ALU.mult, ['raw', 'nea'], ['raw'])
    self.ts('dve', gbt[:, :, 8:16], raw[:, :, 8:16], -1.0, None, ALU.mult, None, ['raw'], ['gbt'])
    for a in range(36):
        P.dma('sp', self.gb[a * 128:(a + 1) * 128, :], gbt[:, a, :], reads=['gbt'])
    if self.upto == 'G1':
        return
    P.barrier()
    A.reset(self.const_end)
    xfs = [A.alloc([NT]) for _ in range(2)]
    accs = [A.alloc([NT]) for _ in range(2)]
    cws = [A.alloc([3]) for _ in range(2)]
    tq = A.alloc([512])
    rq = A.alloc([512])
    for c in range(8):
        xf, acc, cw = xfs[c % 2], accs[c % 2], cws[c % 2]
        P.dma('sp' if c % 2 == 0 else 'act', xf, self.projT[c * 128:(c + 1) * 128, :], writes=[f'xf{c % 2}'])
        P.dma('sp', cw, self.gdn_conv[j, :, c * 128:(c + 1) * 128].rearrange("k p -> p k"), writes=[f'cw{c % 2}'], allow_slow_non_contiguous=True)
        self.ts('dve', acc, xf, cw[:, 1:2], None, ALU.mult, None, [f'xf{c % 2}', f'cw{c % 2}'], [f'acc{c % 2}'])
        for (t0, T, is_s) in SEQS:
            self.stt('dve', acc[:, t0 + 1:t0 + T], xf[:, t0:t0 + T - 1], cw[:, 0:1], acc[:, t0 + 1:t0 + T], ALU.mult, ALU.add,
                     [f'xf{c % 2}', f'cw{c % 2}', f'acc{c % 2}'], [f'acc{c % 2}'])
            self.stt('dve', acc[:, t0:t0 + T - 1], xf[:, t0 + 1:t0 + T], cw[:, 2:3], acc[:, t0:t0 + T - 1], ALU.mult, ALU.add,
                     [f'xf{c % 2}', f'cw{c % 2}', f'acc{c % 2}'], [f'acc{c % 2}'])
        self.act(acc, acc, AF.Silu, [f'acc{c % 2}'], [f'acc{c % 2}'])
        for g in range(9):
            sl = acc[:, g * 512:(g + 1) * 512]
            self.act(tq, sl, AF.Square, [f'acc{c % 2}'], ['tq'])
            self.mm(self.banks[0], self.ones, tq, True, True, ['ones', 'tq'], ['bank0'])
            self.act(rq, self.banks[0], AF.Sqrt, ['bank0'], ['rq'], bias=EPS, scale=1.0)
            self.P.op('dve', lambda e, o=rq: e.reciprocal(out=o, in_=o), ['rq'], ['rq'])
            self.stt('dve', sl, sl, (128 ** -0.5) if c < 4 else 1.0, rq, ALU.mult, ALU.mult, [f'acc{c % 2}', 'rq'], [f'acc{c % 2}'])
        P.dma('sp' if c % 2 == 0 else 'act', self.qkn[c * 128:(c + 1) * 128, :], acc, reads=[f'acc{c % 2}'])
    if self.upto == 'G2':
        return
    P.barrier()
    A.reset(self.const_end)
    msk = A.alloc([6, 64])
    P.dma('sp', msk[0:64], self.c_masks[0:6].rearrange("m t i -> t m i"), writes=['msk'])
    m_bd, m_off = msk[0:64, 4], msk[0:64, 5]
    qTs = [A.alloc([4096]) for _ in range(2)]
    kTs = [A.alloc([4096]) for _ in range(2)]
    gbc = A.alloc([64, 16])
    X = []
    for ci_ in range(4):
        x = {}
        for nm in ('gcol', 'bcol', 'gc', 'eg', 'edec', 'beg', 'gtb'):
            x[nm] = A.alloc([64])
        x['S'] = A.alloc([128])
        for nm in ('Gm', 'Em', 'ETm', 'A0', 'A1', 'B0', 'B1', 'Q', 'Pm', 'qkT', 'wT'):
            x[nm] = A.alloc([NB, 64])
        x['Ao'] = x['Em']
        x['Yb'] = x['Gm']
        for nm in ('vb', 'kbg', 'kdec', 'u', 'ktb', 'vtb'):
            x[nm] = A.alloc([NB, 128])
        for nm in ('vnew', 'tqs', 'tmp2'):
            x[nm] = A.alloc([128])
        X.append(x)
    og = [self.og0, self.og1]
    idb = self.ident[0:64, 0:64]
    v3 = lambda bk, off=0: bk[0:64, off:off + NB * 64].rearrange("p (c i) -> p c i", c=NB)

    def chain(ci_, hs, h, d, si, t0, T, is_s):
        x = X[ci_]
        K_ = lambda n: f"{ {'Ao': 'Em', 'Yb': 'Gm'}.get(n, n) }_{ci_}"
        qT, kT, kqT, kkT = qTs[hs], kTs[hs], f'qT{hs}', f'kT{hs}'
        ktb, vtb = x['ktb'], x['vtb']
        dq = 'sp' if ci_ % 2 == 0 else 'act'
        nch = T // 64
        ci = d * 4 + h
        B0 = B2 = self.banks[2 * ci_]
        B1 = B3 = self.banks[2 * ci_ + 1]
        k0 = k2 = f'bk{2 * ci_}'
        k1 = k3 = f'bk{2 * ci_ + 1}'
        m_cum, m_oth, m_strict, m_incl = (msk[0:64, 0], msk[0:64, 1], msk[0:64, 1], msk[0:64, 0]) if d == 0 else \
                                         (msk[0:64, 2], msk[0:64, 3], msk[0:64, 3], msk[0:64, 2])
        gcol, bcol, gc, eg, edec, beg, gtb, S = (x[n] for n in ('gcol', 'bcol', 'gc', 'eg', 'edec', 'beg', 'gtb', 'S'))
        Gm, Em, ETm, Q, Pm, Ao, Yb, qkT, wT = (x[n] for n in ('Gm', 'Em', 'ETm', 'Q', 'Pm', 'Ao', 'Yb', 'qkT', 'wT'))
        vb, kbg, kdec, u, vnew, tqs, tmp2 = (x[n] for n in ('vb', 'kbg', 'kdec', 'u', 'vnew', 'tqs', 'tmp2'))
        Ak, Bk = [x['A0'], x['A1']], [x['B0'], x['B1']]
        self.cp('dve', bcol[0:64, 0:nch], gbc[0:64, 0:nch, ci], ['gbc'], [K_('bcol')])
        self.cp('dve', gcol[0:64, 0:nch], gbc[0:64, 0:nch, 8 + ci], ['gbc'], [K_('gcol')])
        yield
        self.mm(B0[0:64, 0:nch], m_cum, gcol[0:64, 0:nch], True, True, ['msk', K_('gcol')], [k0])
        self.mm(B1[:, 0:nch], self.ones[0:64, :], gcol[0:64, 0:nch], True, True, ['ones', K_('gcol')], [k1])
        yield
        self.cp('dve', gc[0:64, 0:nch], B0[0:64, 0:nch], [k0], [K_('gc')])
        self.act(eg[0:64, 0:nch], B0[0:64, 0:nch], AF.Exp, [k0], [K_('eg')])
        self.act(gtb[:, 0:nch], B1[:, 0:nch], AF.Exp, [k1], [K_('gtb')])
        yield
        self.tt('dve', edec[0:64, 0:nch], B1[0:64, 0:nch], gc[0:64, 0:nch], ALU.subtract, [k1, K_('gc')], [K_('edec')])
        self.tt('dve', beg[0:64, 0:nch], bcol[0:64, 0:nch], eg[0:64, 0:nch], ALU.mult, [K_('bcol'), K_('eg')], [K_('beg')])
        yield
        self.act(edec[0:64, 0:nch], edec[0:64, 0:nch], AF.Exp, [K_('edec')], [K_('edec')])
        if is_s:
            P.dma('sp', S, self.state_gdn[j, d, h], writes=[K_('S')])
        else:
            self.memset('pool', S, 0.0, [K_('S')])
        yield
        nbat = nch // NB
        border = range(nbat) if d == 0 else range(nbat - 1, -1, -1)
        blist = list(border)

        def load_kv(bi_):
            rr = slice(t0 + bi_ * NB * 64, t0 + (bi_ + 1) * NB * 64)
            P.dma(dq, ktb[0:64], self.kvn[rr, h * 128:(h + 1) * 128].rearrange("(c t) f -> t c f", t=64), writes=[K_('ktb')])
            P.dma(dq, vtb[0:64], self.kvn[rr, 512 + h * 128:512 + (h + 1) * 128].rearrange("(c t) f -> t c f", t=64), writes=[K_('vtb')])
        load_kv(blist[0])
        for bidx, bi in enumerate(blist):
            c0 = bi * NB
            cs = slice(c0, c0 + NB)
            self.tt('pool', Gm[0:64], bc_last(gcol[0:64, cs], 64), bc_mid(m_cum, NB), ALU.mult, [K_('gcol'), 'msk'], [K_('Gm')])
            yield
            for c in range(NB):
                self.mm(B0[0:64, c * 64:(c + 1) * 64], Gm[0:64, c, :], m_oth, True, True, [K_('Gm'), 'msk'], [k0])
            for c in range(NB):
                self.mm(B0[0:64, 256 + c * 64:256 + (c + 1) * 64], m_oth, Gm[0:64, c, :], True, True, [K_('Gm'), 'msk'], [k0])
            for c in range(NB):
                tk = slice((c0 + c) * 64, (c0 + c + 1) * 64)
                self.mm(B1[0:64, c * 64:(c + 1) * 64], kT[:, tk], kT[:, tk], True, True, [kkT], [k1])
            for c in range(NB):
                tk = slice((c0 + c) * 64, (c0 + c + 1) * 64)
                self.mm(B1[0:64, 256 + c * 64:256 + (c + 1) * 64], kT[:, tk], qT[:, tk], True, True, [kkT, kqT], [k1])
            yield
            self.act(Em[0:64], v3(B0), AF.Exp, [k0], [K_('Em')])
            self.act(ETm[0:64], v3(B0, 256), AF.Exp, [k0], [K_('ETm')])
            yield
            self.tt('dve', Em[0:64], Em[0:64], bc_mid(m_strict, NB), ALU.mult, [K_('Em'), 'msk'], [K_('Em')])
            self.tt('pool', ETm[0:64], ETm[0:64], bc_mid(m_incl, NB), ALU.mult, [K_('ETm'), 'msk'], [K_('ETm')])
            yield
            a0, b0 = Ak[0], Bk[0]
            self.tt('dve', a0[0:64], v3(B1), Em[0:64], ALU.mult, [k1, K_('Em')], [K_('A0')])
            self.tt('dve', qkT[0:64], v3(B1, 256), ETm[0:64], ALU.mult, [k1, K_('ETm')], [K_('qkT')])
            yield
            self.tt('pool', a0[0:64], a0[0:64], bc_last(bcol[0:64, cs], 64), ALU.mult, [K_('A0'), K_('bcol')], [K_('A0')])
            yield
            for c in range(NB):
                self.tr(B2[0:64, c * 64:(c + 1) * 64], a0[0:64, c, :], idb, [K_('A0'), 'ident'], [k2])
            yield
            self.cp('act', b0[0:64], v3(B2), [k2], [K_('B0')])
            self.tt('pool', Ao[0:64], a0[0:64], bc_mid(m_off, NB), ALU.mult, [K_('A0'), 'msk'], [K_('Ao')])
            yield
            self.tt('pool', a0[0:64], a0[0:64], bc_mid(m_bd, NB), ALU.mult, [K_('A0'), 'msk'], [K_('A0')])
            self.tt('pool', b0[0:64], b0[0:64], bc_mid(m_bd, NB), ALU.mult, [K_('B0'), 'msk'], [K_('B0')])
            yield
            self.tt('pool', Q[0:64], bc_mid(idb, NB), b0[0:64], ALU.subtract, ['ident', K_('B0')], [K_('Q')])
            self.tt('pool', Pm[0:64], bc_mid(idb, NB), a0[0:64], ALU.subtract, ['ident', K_('A0')], [K_('Pm')])
            yield
            for kk in range(1, 5):
                ao, bo_ = Ak[(kk - 1) % 2], Bk[(kk - 1) % 2]
                an, bn = Ak[kk % 2], Bk[kk % 2]
                ko, kn_ = K_(f'A{(kk - 1) % 2}'), K_(f'A{kk % 2}')
                lo, ln_ = K_(f'B{(kk - 1) % 2}'), K_(f'B{kk % 2}')
                for c in range(NB):
                    self.mm(B2[0:64, c * 64:(c + 1) * 64], ao[0:64, c, :], bo_[0:64, c, :], True, True, [ko, lo], [k2])
                for c in range(NB):
                    self.mm(B3[0:64, c * 64:(c + 1) * 64], bo_[0:64, c, :], ao[0:64, c, :], True, True, [ko, lo], [k3])
                yield
                self.cp('act', bn[0:64], v3(B2), [k2], [ln_])
                self.cp('act', an[0:64], v3(B3), [k3], [kn_])
                yield
                for c in range(NB):
                    self.mm(B2[0:64, 256 + c * 64:256 + (c + 1) * 64], an[0:64, c, :], Q[0:64, c, :], True, True, [kn_, K_('Q')], [k2])
                for c in range(NB):
                    self.mm(B3[0:64, 256 + c * 64:256 + (c + 1) * 64], bn[0:64, c, :], Pm[0:64, c, :], True, True, [ln_, K_('Pm')], [k3])
                yield
                self.tt('dve', Q[0:64], Q[0:64], v3(B2, 256), ALU.add, [K_('Q'), k2], [K_('Q')])
                self.tt('dve', Pm[0:64], Pm[0:64], v3(B3, 256), ALU.add, [K_('Pm'), k3], [K_('Pm')])
                yield
            for c in range(NB):
                self.mm(B2[0:64, c * 64:(c + 1) * 64], Ao[0:64, c, :], Q[0:64, c, :], True, True, [K_('Ao'), K_('Q')], [k2])
            yield
            self.cp('act', Yb[0:64], v3(B2), [k2], [K_('Yb')])
            self.tt('pool', vb[0:64], vtb[0:64], bc_last(bcol[0:64, cs], 128), ALU.mult, [K_('vtb'), K_('bcol')], [K_('vb')])
            self.tt('pool', kbg[0:64], ktb[0:64], bc_last(beg[0:64, cs], 128), ALU.mult, [K_('ktb'), K_('beg')], [K_('kbg')])
            self.tt('pool', kdec[0:64], ktb[0:64], bc_last(edec[0:64, cs], 128), ALU.mult, [K_('ktb'), K_('edec')], [K_('kdec')])
            if bidx + 1 < len(blist):
                load_kv(blist[bidx + 1])
            yield
            for c in range(NB):
                self.mm(B3[0:64, c * 64:(c + 1) * 64], Pm[0:64, c, :], Yb[0:64, c, :], True, True, [K_('Pm'), K_('Yb')], [k3])
            yield
            self.tt('dve', Q[0:64], Q[0:64], v3(B3), ALU.subtract, [K_('Q'), k3], [K_('Q')])
            yield
            for c in range(NB):
                self.mm(B0[0:64, c * 128:(c + 1) * 128], Q[0:64, c, :], vb[0:64, c, :], True, True, [K_('Q'), K_('vb')], [k0])
            for c in range(NB):
                self.mm(B1[:, c * 64:(c + 1) * 64], kbg[0:64, c, :], Q[0:64, c, :], True, True, [K_('Q'), K_('kbg')], [k1])
            yield
            self.cp('act', u[0:64], B0[0:64, :].rearrange("p (c i) -> p c i", c=NB), [k0], [K_('u')])
            self.cp('act', wT, B1[:, 0:NB * 64].rearrange("p (c i) -> p c i", c=NB), [k1], [K_('wT')])
            yield
            corder = range(NB) if d == 0 else range(NB - 1, -1, -1)
            for c in corder:
                ch = c0 + c
                tk = slice(ch * 64, (ch + 1) * 64)
                self.mm(B2[0:64, 0:128], wT[:, c, :], S, True, True, [K_('wT'), K_('S')], [k2])
                self.mm(B3[0:64, 0:128], qT[:, tk], S, True, True, [kqT, K_('S')], [k3])
                yield
                self.tt('dve', vnew[0:64], u[0:64, c, :], B2[0:64, 0:128], ALU.subtract, [K_('u'), k2], [K_('vnew')])
                self.act(tqs[0:64], B3[0:64, 0:128], AF.Copy, [k3, K_('eg')], [K_('tqs')], scale=eg[0:64, ch:ch + 1])
                yield
                self.mm(B2[0:64, 128:256], qkT[0:64, c, :], vnew[0:64], True, True, [K_('qkT'), K_('vnew')], [k2])
                self.mm(B3[:, 128:256], kdec[0:64, c, :], vnew[0:64], True, True, [K_('kdec'), K_('vnew')], [k3])
                yield
                self.tt('dve', tmp2[0:64], tqs[0:64], B2[0:64, 128:256], ALU.add, [K_('tqs'), k2], [K_('tmp2')])
                self.stt('dve', S, S, gtb[:, ch:ch + 1], B3[:, 128:256], ALU.mult, ALU.add, [K_('S'), K_('gtb'), k3], [K_('S')])
                P.dma(dq, og[d][t0 + ch * 64:t0 + (ch + 1) * 64, h * 128:(h + 1) * 128], tmp2[0:64], reads=[K_('tmp2')])
        if not is_s:
            P.dma('sp', self.o_gdn[si, j, d, h], S, reads=[K_('S')])

    for si, (t0, T, is_s) in enumerate(SEQS):
        nch = T // 64
        for c8 in range(0, nch, 4):
            rr = slice(t0 + c8 * 64, t0 + (c8 + 4) * 64)
            P.dma('sp', gbc[0:64, c8:c8 + 4, :], self.gb[rr, :].rearrange("(c t) f -> t c f", t=64), writes=['gbc'])
        for pair in ((0, 1), (2, 3)):
            gens = []
            for hs, h in enumerate(pair):
                P.dma('sp', qTs[hs][:, 0:T], self.qkn[h * 128:(h + 1) * 128, t0:t0 + T], writes=[f'qT{hs}'])
                P.dma('act', kTs[hs][:, 0:T], self.qkn[512 + h * 128:512 + (h + 1) * 128, t0:t0 + T], writes=[f'kT{hs}'])
                for d in range(2):
                    gens.append(chain(hs * 2 + d, hs, h, d, si, t0, T, is_s))
            while gens:
                for g_ in list(gens):
                    try:
                        next(g_)
                    except StopIteration:
                        gens.remove(g_)
    P.barrier()
    A.reset(self.const_end)
    gnw = A.alloc([128])
    self.load_bcast(gnw, self.gdn_norm[j, :], 'gnw')
    oa = [A.alloc([512]) for _ in range(2)]
    ob_ = [A.alloc([512]) for _ in range(2)]
    zz = [A.alloc([512]) for _ in range(2)]
    sq = A.alloc([512])
    ss4 = A.alloc([8])
    for t in range(36):
        s = t % 2
        rr = slice(t * 128, (t + 1) * 128)
        P.dma('sp', oa[s], self.og0[rr, :], writes=[f'oa{s}'])
        P.dma('act', ob_[s], self.og1[rr, :], writes=[f'ob{s}'])
        P.dma('sp', zz[s], self.projN[rr, 1536:2048], writes=[f'zz{s}'])
        self.tt('dve', oa[s], oa[s], ob_[s], ALU.add, [f'oa{s}', f'ob{s}'], [f'oa{s}'])
        self.act(zz[s], zz[s], AF.Silu, [f'zz{s}'], [f'zz{s}'])
        self.tt('pool', sq, oa[s], oa[s], ALU.mult, [f'oa{s}'], ['sq'])
        r4 = ss4[:, 4 * s:4 * s + 4]
        self.P.op('dve', lambda e, o=r4, i=sq.rearrange("p (h d) -> p h d", h=4): e.tensor_reduce(out=o, in_=i, axis=AX.X, op=ALU.add),
                  ['sq'], [f'ss4{s}'])
        self.act(r4, r4, AF.Sqrt, [f'ss4{s}'], [f'ss4{s}'], bias=EPS, scale=1.0 / 128)
        self.recip(r4, r4, [f'ss4{s}'], [f'ss4{s}'])
        o3 = oa[s].rearrange("p (h d) -> p h d", h=4)
        self.tt('dve', o3, o3, bc_last(r4, 128), ALU.mult, [f'oa{s}', f'ss4{s}'], [f'oa{s}'])
        self.tt('pool', o3, o3, bc_mid(gnw, 4), ALU.mult, [f'oa{s}', 'gnw'], [f'oa{s}'])
        self.tt('dve', ob_[s], oa[s], zz[s], ALU.mult, [f'oa{s}', f'zz{s}'], [f'ob{s}'])
        P.dma('sp', self.mixN[rr, :], ob_[s], reads=[f'ob{s}'])


def phase_even_mix(self, l):
    j = l // 2
    self.phase_begin()
    _gdn(self, l)
    if self.upto in ('G1', 'G2', 'G3'):
        return
    self.P.barrier()
    self.A.reset(self.const_end)
    _attn_dense(self, l, 2064, 2576, 2832, 4, 2, (self.cgk, self.cgv), (self.gqa_q_norm[j], self.gqa_k_norm[j]), True,
                self.o_gk, self.o_gv)


K.phase_even_mix = phase_even_mix


_CACHE = {}


def kernel(**inputs):
    if 'nc' not in _CACHE:
        kb = K(n_layers=4)
        _CACHE['nc'] = kb.build()
    nc = _CACHE['nc']
    in_maps = make_in_maps(inputs)
    res = run_bass_kernel_spmd(nc, in_maps, core_ids=list(range(8)))
    rs = res.results
    y = np.stack([r["yout"] for r in rs], 0)
    y_prompt = np.ascontiguousarray(y[:, :512].reshape(16, 256, 1024))
    y_sample = np.ascontiguousarray(y[:, 512:])
    cat = lambda n: np.concatenate([r[n] for r in rs], 0)
    return (y_prompt, y_sample, cat("o_gdn"), cat("o_gk"), cat("o_gv"), cat("o_ssd"), cat("o_nk"), cat("o_nv"))


def _ssd(self, l):
    j = l // 2
    A, P = self.A, self.P
    B = self.banks
    W = 768
    wb = [A.alloc([W]) for _ in range(3)]
    for k in range(3):
        self.load_bcast(wb[k], self.ssd_conv[j, k, 0:W], f'wb{k}')
    cbb = A.alloc([W])
    self.load_bcast(cbb, self.ssd_conv_b[j, 0:W], 'cbb')
    x3s = [A.alloc([3, W]) for _ in range(4)]
    xm = [x_[:, 0, :] for x_ in x3s]
    x0 = [x_[:, 1, :] for x_ in x3s]
    xp = [x_[:, 2, :] for x_ in x3s]
    nt = 0
    src = self.projN
    for (t0, T, is_s) in SEQS:
        for a in range(T // 128):
            s = nt % 4
            nt += 1
            r0 = t0 + a * 128
            first, last = (a == 0), (a == T // 128 - 1)
            if MERGE_CONV_LOADS and not first and not last:
                wd_ = 768
                src3 = bass.AP(src.tensor, (r0 - 1) * EV_IN + 512, [[EV_IN, 128], [EV_IN, 3], [1, wd_]])
                P.dma('sp' if nt % 2 else 'act', x3s[s], src3, writes=[f'x0{s}', f'xm{s}', f'xp{s}'])
            else:
                P.dma('sp', x0[s], src[r0:r0 + 128, 512:1280], writes=[f'x0{s}'])
                if first:
                    self.memset('pool', xm[s], 0.0, [f'xm{s}'])
                    P.dma('act', xm[s][1:128, :], src[r0:r0 + 127, 512:1280], writes=[f'xm{s}'])
                else:
                    P.dma('act', xm[s], src[r0 - 1:r0 + 127, 512:1280], writes=[f'xm{s}'])
                if last:
                    self.memset('pool', xp[s], 0.0, [f'xp{s}'])
                    P.dma('act', xp[s][0:127, :], src[r0 + 1:r0 + 128, 512:1280], writes=[f'xp{s}'])
                else:
                    P.dma('act', xp[s], src[r0 + 1:r0 + 129, 512:1280], writes=[f'xp{s}'])
            self.tt('dve', x0[s], x0[s], wb[1], ALU.mult, [f'x0{s}', 'wb1'], [f'x0{s}'])
            self.tt('pool', xm[s], xm[s], wb[0], ALU.mult, [f'xm{s}', 'wb0'], [f'xm{s}'])
            self.tt('pool', xp[s], xp[s], wb[2], ALU.mult, [f'xp{s}', 'wb2'], [f'xp{s}'])
            self.tt('dve', x0[s], x0[s], xm[s], ALU.add, [f'x0{s}', f'xm{s}'], [f'x0{s}'])
            self.tt('dve', x0[s], x0[s], xp[s], ALU.add, [f'x0{s}', f'xp{s}'], [f'x0{s}'])
            self.tt('dve', x0[s], x0[s], cbb, ALU.add, [f'x0{s}', 'cbb'], [f'x0{s}'])
            self.act(x0[s], x0[s], AF.Silu, [f'x0{s}'], [f'x0{s}'])
            P.dma('sp', self.kvn[r0:r0 + 128, 0:W], x0[s], reads=[f'x0{s}'])
    raw = A.alloc([36, 16])
    for a in range(36):
        P.dma('sp', raw[:, a, :], self.projN[a * 128:(a + 1) * 128, 1536:1552], writes=['raw'])
    dtb = A.alloc([16])
    nea = A.alloc([16])
    self.load_bcast(dtb, self.ssd_dt_bias[j, :], 'dtb')
    self.load_bcast(nea, self.ssd_a_log[j, :], 'nea')
    self.act(nea, nea, AF.Exp, ['nea'], ['nea'])
    sbt = A.alloc([36, 32])
    self.tt('dve', raw, raw, bc_mid(dtb, 36), ALU.add, ['raw', 'dtb'], ['raw'])
    self.act(raw, raw, AF.Exp, ['raw'], ['raw'])
    self.act(sbt[:, :, 0:16], raw, AF.Ln, ['raw'], ['sbt'], bias=1.0)
    self.tt('dve', raw, sbt[:, :, 0:16], bc_mid(nea, 36), ALU.mult, ['sbt', 'nea'], ['raw'])
    self.ts('dve', sbt[:, :, 16:32], raw, -1.0, None, ALU.mult, None, ['raw'], ['sbt'])
    for a in range(36):
        P.dma('sp', self.sb[a * 128:(a + 1) * 128, :], sbt[:, a, :], reads=['sbt'])
    P.barrier()
    A.reset(self.const_end)
    xfs = [A.alloc([NT]) for _ in range(2)]
    accs = [A.alloc([NT]) for _ in range(2)]
    cws = [A.alloc([4]) for _ in range(2)]
    for c in range(4):
        xf, acc, cw = xfs[c % 2], accs[c % 2], cws[c % 2]
        ch0 = 512 + c * 128
        P.dma('sp' if c % 2 == 0 else 'act', xf, self.projT[1024 + c * 128:1024 + (c + 1) * 128, :], writes=[f'xf{c % 2}'])
        P.dma('sp', cw[:, 0:3], self.ssd_conv[j, :, ch0:ch0 + 128].rearrange("k p -> p k"), writes=[f'cw{c % 2}'], allow_slow_non_contiguous=True)
        P.dma('sp', cw[:, 3:4], self.ssd_conv_b[j, ch0:ch0 + 128].rearrange("(p o) -> p o", o=1), writes=[f'cw{c % 2}'])
        self.ts('dve', acc, xf, cw[:, 1:2], cw[:, 3:4], ALU.mult, ALU.add, [f'xf{c % 2}', f'cw{c % 2}'], [f'acc{c % 2}'])
        for (t0, T, is_s) in SEQS:
            self.stt('dve', acc[:, t0 + 1:t0 + T], xf[:, t0:t0 + T - 1], cw[:, 0:1], acc[:, t0 + 1:t0 + T], ALU.mult, ALU.add,
                     [f'xf{c % 2}', f'cw{c % 2}', f'acc{c % 2}'], [f'acc{c % 2}'])
            self.stt('dve', acc[:, t0:t0 + T - 1], xf[:, t0 + 1:t0 + T], cw[:, 2:3], acc[:, t0:t0 + T - 1], ALU.mult, ALU.add,
                     [f'xf{c % 2}', f'cw{c % 2}', f'acc{c % 2}'], [f'acc{c % 2}'])
        self.act(acc, acc, AF.Silu, [f'acc{c % 2}'], [f'acc{c % 2}'])
        P.dma('sp' if c % 2 == 0 else 'act', self.qkn[c * 128:(c + 1) * 128, :], acc, reads=[f'acc{c % 2}'])
    P.barrier()
    A.reset(self.const_end)
    msk = A.alloc([4, 64])
    P.dma('sp', msk[0:64], self.c_masks[0:4].rearrange("m t i -> t m i"), writes=['msk'])
    snw = A.alloc([512])
    self.load_bcast(snw, self.ssd_norm[j, :], 'snw')
    dsk = A.alloc([8])
    self.load_bcast(dsk, self.ssd_d[j, :], 'dsk')
    bcT = A.alloc([2, 4096])
    ccT = A.alloc([2, 4096])
    sca = A.alloc([64, 32])
    id64 = self.ident[0:64, 0:64]
    X = []
    for d in range(2):
        x = {}
        for nm in ('lad', 'dtd', 'cs', 'ecs', 'edec', 'cdec'):
            x[nm] = A.alloc([64, 8])
        for nm in ('hT', 'Gm', 'LT', 'MT', 'xdt', 'xd2', 'yo', 'yy'):
            x[nm] = A.alloc([8, 64])
        x['xb'] = [A.alloc([768]) for _ in range(2)]
        x['h0'] = A.alloc([8, 128])
        X.append(x)
    ydst = [self.yf, self.yb]

    def chain(d, si, t0, T, is_s):
        x = X[d]
        K_ = lambda n: f'{n}_{d}'
        nch = T // 64
        B0, B1, B2, B3 = self.banks[4 * d:4 * d + 4]
        k0, k1, k2, k3 = [f'bk{4 * d + i}' for i in range(4)]
        m_cum, m_oth, m_incl = (msk[0:64, 0], msk[0:64, 1], msk[0:64, 0]) if d == 0 else (msk[0:64, 2], msk[0:64, 3], msk[0:64, 2])
        lad, dtd, cs, ecs, edec, cdec, hT = (x[n] for n in ('lad', 'dtd', 'cs', 'ecs', 'edec', 'cdec', 'hT'))
        Gm, LT, MT, xdt, xd2, yo, yy, h0 = (x[n] for n in ('Gm', 'LT', 'MT', 'xdt', 'xd2', 'yo', 'yy', 'h0'))
        self.cp('dve', dtd[0:64, 0:nch, :], sca[0:64, 0:nch, d * 8:d * 8 + 8], ['sca'], [K_('dtd')])
        self.cp('dve', lad[0:64, 0:nch, :], sca[0:64, 0:nch, 16 + d * 8:16 + d * 8 + 8], ['sca'], [K_('lad')])
        yield
        n8 = nch * 8
        fl = lambda a_, p_=64: a_[0:p_, 0:nch, :].rearrange("p c h -> p (c h)")
        self.mm(B0[0:64, 0:n8], m_cum, fl(lad), True, True, ['msk', K_('lad')], [k0])
        self.mm(B1[:, 0:n8], self.ones[0:64, :], fl(lad), True, True, ['ones', K_('lad')], [k1])
        yield
        self.cp('dve', fl(cs), B0[0:64, 0:n8], [k0], [K_('cs')])
        self.act(fl(ecs), B0[0:64, 0:n8], AF.Exp, [k0], [K_('ecs')])
        self.act(fl(cdec, 128), B1[:, 0:n8], AF.Exp, [k1], [K_('cdec')])
        yield
        self.tt('dve', fl(edec), B1[0:64, 0:n8], fl(cs), ALU.subtract, [k1, K_('cs')], [K_('edec')])
        yield
        self.act(fl(edec), fl(edec), AF.Exp, [K_('edec')], [K_('edec')])
        if is_s:
            for hh in range(8):
                P.dma('sp' if d == 0 else 'act', h0[0:64, hh, :], self.state_ssd[j, d, hh], writes=[K_('h0')])
            yield
            for hh in range(8):
                self.tr(B2[:, hh * 64:(hh + 1) * 64], h0[0:64, hh, :], id64, [K_('h0'), 'ident'], [k2])
            yield
            self.cp('dve', hT, B2.rearrange("p (h q) -> p h q", h=8), [k2], [K_('hT')])
        else:
            self.memset('pool', hT, 0.0, [K_('hT')])
        yield
        corder = range(nch) if d == 0 else range(nch - 1, -1, -1)
        nx = 0
        for c in corder:
            r0 = t0 + c * 64
            tk = slice(c * 64, (c + 1) * 64)
            s = nx % 2
            nx += 1
            xk = K_(f'xb{s}')
            xbs = x['xb'][s]
            P.dma('sp' if d == 0 else 'act', xbs[0:64, :], self.kvn[r0:r0 + 64, 0:768], writes=[xk])
            self.tt('dve', Gm[0:64], bc_last(lad[0:64, c, :], 64), bc_mid(m_cum, 8), ALU.mult, [K_('lad'), 'msk'], [K_('Gm')])
            yield
            x3 = xbs[0:64, 0:512].rearrange("p (h q) -> p h q", h=8)
            self.mm(B0[0:64, :], m_oth, Gm[0:64].rearrange("p h i -> p (h i)"), True, True, ['msk', K_('Gm')], [k0])
            for g in range(2):
                self.mm(B1[0:64, g * 64:(g + 1) * 64], bcT[:, g, tk], ccT[:, g, tk], True, True, ['bcT', 'ccT'], [k1])
            for g in range(2):
                self.mm(B2[0:64, g * 256:(g + 1) * 256], ccT[:, g, tk], hT[:, g * 4:(g + 1) * 4, :].rearrange("p h q -> p (h q)"),
                        True, True, ['ccT', K_('hT')], [k2])
            self.tt('pool', xdt[0:64], x3, bc_last(dtd[0:64, c, :], 64), ALU.mult, [xk, K_('dtd')], [K_('xdt')])
            yield
            self.act(LT[0:64], B0[0:64, :].rearrange("p (h i) -> p h i", h=8), AF.Exp, [k0], [K_('LT')])
            self.tt('dve', yo[0:64], B2[0:64, :].rearrange("p (h q) -> p h q", h=8), bc_last(ecs[0:64, c, :], 64), ALU.mult,
                    [k2, K_('ecs')], [K_('yo')])
            self.tt('pool', xd2[0:64], xdt[0:64], bc_last(edec[0:64, c, :], 64), ALU.mult, [K_('xdt'), K_('edec')], [K_('xd2')])
            yield
            self.tt('pool', LT[0:64], LT[0:64], bc_mid(m_incl, 8), ALU.mult, [K_('LT'), 'msk'], [K_('LT')])
            for g in range(2):
                self.mm(B3[:, g * 256:(g + 1) * 256], xbs[0:64, 512 + g * 128:512 + (g + 1) * 128],
                        xd2[0:64, g * 4:(g + 1) * 4, :].rearrange("p h q -> p (h q)"), True, True, [xk, K_('xd2')], [k3])
            yield
            for g in range(2):
                self.tt('dve', MT[0:64, g * 4:(g + 1) * 4, :], LT[0:64, g * 4:(g + 1) * 4, :], bc_mid(B1[0:64, g * 64:(g + 1) * 64], 4),
                        ALU.mult, [K_('LT'), k1], [K_('MT')])
            self.tt('pool', hT, hT, bc_last(cdec[:, c, :], 64), ALU.mult, [K_('hT'), K_('cdec')], [K_('hT')])
            yield
            for hh in range(8):
                self.mm(B2[0:64, hh * 64:(hh + 1) * 64], MT[0:64, hh, :], xdt[0:64, hh, :], True, True, [K_('MT'), K_('xdt')], [k2])
            self.tt('dve', hT, hT, B3.rearrange("p (h q) -> p h q", h=8), ALU.add, [K_('hT'), k3], [K_('hT')])
            yield
            self.tt('dve', yy[0:64], yo[0:64], B2[0:64, :].rearrange("p (h q) -> p h q", h=8), ALU.add, [K_('yo'), k2], [K_('yy')])
            yield
            P.dma('sp' if d == 0 else 'act', ydst[d][r0:r0 + 64, :], yy[0:64].rearrange("p h q -> p (h q)"), reads=[K_('yy')],
                  writes=[f'ydram{d}'])
        if not is_s:
            hs = h0
            for hh in range(8):
                self.tr(B0[0:64, (hh % 4) * 128:(hh % 4 + 1) * 128] if hh < 4 else B1[0:64, (hh % 4) * 128:(hh % 4 + 1) * 128],
                        hT[:, hh, :], self.ident, [K_('hT'), 'ident'], [k0 if hh < 4 else k1])
            yield
            self.cp('dve', hs[0:64, 0:4, :], B0[0:64, :].rearrange("p (h n) -> p h n", h=4), [k0], [K_('h0')])
            self.cp('act', hs[0:64, 4:8, :], B1[0:64, :].rearrange("p (h n) -> p h n", h=4), [k1], [K_('h0')])
            yield
            P.dma('sp', self.o_ssd[si, j, d].rearrange("h p n -> p h n"), hs[0:64], reads=[K_('h0')])

    for si, (t0, T, is_s) in enumerate(SEQS):
        nch = T // 64
        for g in range(2):
            P.dma('sp', bcT[:, g, 0:T], self.qkn[g * 128:(g + 1) * 128, t0:t0 + T], writes=['bcT'])
            P.dma('act', ccT[:, g, 0:T], self.qkn[256 + g * 128:256 + (g + 1) * 128, t0:t0 + T], writes=['ccT'])
        for c4 in range(0, nch, 4):
            P.dma('sp', sca[0:64, c4:c4 + 4, :], self.sb[t0 + c4 * 64:t0 + (c4 + 4) * 64, :].rearrange("(c t) f -> t c f", t=64), writes=['sca'])
        gens = [chain(0, si, t0, T, is_s), chain(1, si, t0, T, is_s)]
        while gens:
            for g_ in list(gens):
                try:
                    next(g_)
                except StopIteration:
                    gens.remove(g_)
    P.barrier()
    A.reset(self.const_end)
    snw = A.alloc([512])
    self.load_bcast(snw, self.ssd_norm[j, :], 'snw')
    dsk = A.alloc([8])
    self.load_bcast(dsk, self.ssd_d[j, :], 'dsk')
    ya = [A.alloc([512]) for _ in range(2)]
    yb_ = [A.alloc([512]) for _ in range(2)]
    xx = [A.alloc([512]) for _ in range(2)]
    zz = [A.alloc([512]) for _ in range(2)]
    sq = A.alloc([512])
    st = A.alloc([4])
    for t in range(36):
        s = t % 2
        rr = slice(t * 128, (t + 1) * 128)
        P.dma('sp', ya[s], self.yf[rr, :], writes=[f'ya{s}'])
        P.dma('act', yb_[s], self.yb[rr, :], writes=[f'yb{s}'])
        P.dma('sp', xx[s], self.kvn[rr, 0:512], writes=[f'xx{s}'])
        P.dma('act', zz[s], self.projN[rr, 0:512], writes=[f'zz{s}'])
        self.tt('dve', ya[s], ya[s], yb_[s], ALU.add, [f'ya{s}', f'yb{s}'], [f'ya{s}'])
        x3 = xx[s].rearrange("p (h q) -> p h q", h=8)
        self.tt('pool', x3, x3, bc_last(dsk, 64), ALU.mult, [f'xx{s}', 'dsk'], [f'xx{s}'])
        self.act(zz[s], zz[s], AF.Silu, [f'zz{s}'], [f'zz{s}'])
        self.tt('dve', ya[s], ya[s], xx[s], ALU.add, [f'ya{s}', f'xx{s}'], [f'ya{s}'])
        self.tt('dve', ya[s], ya[s], zz[s], ALU.mult, [f'ya{s}', f'zz{s}'], [f'ya{s}'])
        self.act(sq, ya[s], AF.Square, [f'ya{s}'], ['sq', f'ssq{s}'], accum_out=st[:, s:s + 1])
        self.act(st[:, 2 + s:3 + s], st[:, s:s + 1], AF.Sqrt, [f'ssq{s}'], [f'rsq{s}'], bias=EPS, scale=1.0 / 512)
        self.recip(st[:, 2 + s:3 + s], st[:, 2 + s:3 + s], [f'rsq{s}'], [f'rsq{s}'])
        self.stt('dve', yb_[s], ya[s], st[:, 2 + s:3 + s], snw, ALU.mult, ALU.mult, [f'ya{s}', f'rsq{s}', 'snw'], [f'yb{s}'])
        P.dma('sp', self.mixN[rr, :], yb_[s], reads=[f'yb{s}'])


def _na_sample(self, l):
    j = l // 2
    A, P = self.A, self.P
    B = self.banks
    t0 = 512
    scale = 128 ** -0.5
    BIG = 30000.0
    kdT = A.alloc([4, 4096], BF16)
    qdT = A.alloc([4, 4096], BF16)
    Ve = A.alloc([32, 512], BF16)
    Vo = A.alloc([31, 512], BF16)
    kcT = A.alloc([4, 512], BF16)
    Vc = A.alloc([4, 512], BF16)
    Bt = A.alloc([60, 64])
    nm = A.alloc([64])
    ngm = A.alloc([64])
    Z = A.alloc([160])
    ctk = A.alloc([4, 128])
    for h in range(4):
        P.dma('pool', kdT[:, h, :], self.projT[2064 + h * 128:2064 + (h + 1) * 128, t0:t0 + 4096], writes=['kdT'])
        P.dma('pool', qdT[:, h, :], self.projT[1552 + h * 128:1552 + (h + 1) * 128, t0:t0 + 4096], writes=['qdT'])
    for a4 in range(0, 32, 4):
        P.dma('pool', Ve[:, a4:a4 + 4, :], self.projN[t0 + a4 * 128:t0 + (a4 + 4) * 128, 2576:3088].rearrange("(a p) f -> p a f", p=128), writes=['Ve'])
    for a4 in range(0, 31, 4):
        n = min(4, 31 - a4)
        P.dma('pool', Vo[:, a4:a4 + n, :], self.projN[t0 + 64 + a4 * 128:t0 + 64 + (a4 + n) * 128, 2576:3088].rearrange("(a p) f -> p a f", p=128), writes=['Vo'])
    P.dma('pool', Vc, self.cnv[j].rearrange("(a p) h d -> p a (h d)", p=128), writes=['Vc'])
    for h in range(4):
        P.dma('sp', ctk, self.cnk[j, :, h, :].rearrange("(a p) d -> p a d", p=128), writes=['ctk'])
        for a in range(4):
            self.tr(B[0][:, a * 128:(a + 1) * 128], ctk[:, a, :], self.ident, ['ctk', 'ident'], ['b0'])
        self.cp('act', kcT[:, h, :], B[0], ['b0'], ['kcT'])
    self.memset('pool', Z[0:60, :], 0.0, ['Z'])
    P.dma('sp', Z[0:60, 64:95], self.na_rpb[j], writes=['Z'])
    tz = P.dma('sp', self.rpbpad, Z[0:60, :], reads=['Z'], writes=['rpbpad'])
    for q in range(64):
        src = bass.AP(self.rpbpad.tensor, 79 - q, [[0, 1], [160, 60], [1, 64]])
        P.dma('sp' if q % 2 else 'act', Bt[q:q + 1, :, :], src, reads=['rpbpad'], writes=['Bt'])
    P.dma('sp', nm[0:64, :], self.c_namask, writes=['nm'])
    self.ts('dve', ngm[0:64, :], nm[0:64, :], -1.0, BIG, ALU.add, ALU.mult, ['nm'], ['ngm'])
    self.tt('dve', Bt[0:64], Bt[0:64], bc_mid(nm[0:64, :], 60), ALU.mult, ['Bt', 'nm'], ['Bt'])
    self.tt('dve', Bt[0:64], Bt[0:64], bc_mid(ngm[0:64, :], 60), ALU.add, ['Bt', 'ngm'], ['Bt'])
    idb64 = self.identb[0:64, 0:64]
    XN = []
    for ch in range(2):
        XN.append(dict(ssb=A.alloc([1024]), pf=A.alloc([1024]), pb=A.alloc([1024], BF16), pT=[A.alloc([512], BF16) for _ in range(2)],
                       st=A.alloc([4]), osb=A.alloc([512])))

    def chain(cn):
        x = XN[cn]
        K_ = lambda n: f'{n}_{cn}'
        BS, BC, BT, BO = B[4 * cn:4 * cn + 4]
        kS, kC, kT, kO = [f'bk{4 * cn + i}' for i in range(4)]
        ptb = BT.bitcast(BF16)
        ssb, pf, pb, st, osb = x['ssb'], x['pf'], x['pb'], x['st'], x['osb']
        heads = (2 * cn, 2 * cn + 1)
        npt = 0
        for r in range(64):
            r0 = min(max(r - 4, 0), 56)
            dr0 = r0 - r + 7
            Vx, ti0, vk = (Ve, r0 // 2, 'Ve') if r0 % 2 == 0 else (Vo, (r0 - 1) // 2, 'Vo')
            for hi, h in enumerate(heads):
                qs = qdT[:, h, r * 64:(r + 1) * 64]
                self.mm(BS[0:64, :], qs, kdT[:, h, r0 * 64:r0 * 64 + 512], True, True, ['qdT', 'kdT'], [kS])
                self.mm(BC[0:64, :], qs, kcT[:, h, :], True, True, ['qdT', 'kcT'], [kC])
                yield
                self.stt('dve', ssb[0:64, 0:512].rearrange("p (a k) -> p a k", a=8), BS[0:64, :].rearrange("p (a k) -> p a k", a=8), scale,
                         Bt[0:64, h * 15 + dr0:h * 15 + dr0 + 8, :], ALU.mult, ALU.add, [kS, 'Bt'], [K_('ssb')])
                self.act(ssb[0:64, 512:1024], BC[0:64, :], AF.Copy, [kC], [K_('ssb')], scale=scale)
                yield
                self.P.op('dve', lambda e, o=st[0:64, 0:1], i=ssb[0:64, :]: e.tensor_reduce(out=o, in_=i, axis=AX.X, op=ALU.max),
                          [K_('ssb')], [K_('mx')])
                yield
                self.ts('dve', st[0:64, 1:2], st[0:64, 0:1], -1.0, None, ALU.mult, None, [K_('mx')], [K_('nmx')])
                yield
                self.act(pf[0:64, :], ssb[0:64, :], AF.Exp, [K_('ssb'), K_('nmx')], [K_('pf'), K_('sm')], bias=st[0:64, 1:2], accum_out=st[0:64, 2:3])
                yield
                self.recip(st[0:64, 3:4], st[0:64, 2:3], [K_('sm')], [K_('rsm')])
                yield
                self.ts('dve', pb[0:64, :], pf[0:64, :], st[0:64, 3:4], None, ALU.mult, None, [K_('pf'), K_('rsm')], [K_('pb')])
                yield
                for kc in range(8):
                    self.tr(ptb[:, kc * 64:(kc + 1) * 64], pb[0:64, kc * 128:(kc + 1) * 128], idb64, [K_('pb'), 'identb'], [kT])
                yield
                pt_ = x['pT'][npt % 2]
                pk = K_(f'pT{npt % 2}')
                npt += 1
                self.cp('act', pt_, ptb[:, 0:512], [kT], [pk])
                yield
                oc = hi * 256 + (r % 4) * 64
                for jj in range(8):
                    lhs = Vx[:, ti0 + jj, h * 128:(h + 1) * 128] if jj < 4 else Vc[:, jj - 4, h * 128:(h + 1) * 128]
                    self.mm(BO[:, oc:oc + 64], lhs, pt_[:, jj * 64:(jj + 1) * 64], jj == 0, jj == 7, [vk, 'Vc', pk], [kO])
                yield
            if r % 4 == 3:
                self.cp('dve' if cn else 'act', osb, BO, [kO], [K_('osb')])
                yield
                for hi, h in enumerate(heads):
                    P.dma('sp' if cn else 'act', self.mixT[h * 128:(h + 1) * 128, t0 + (r - 3) * 64:t0 + (r + 1) * 64],
                          osb[:, hi * 256:(hi + 1) * 256], reads=[K_('osb')])

    gens = [chain(0), chain(1)]
    while gens:
        for g_ in list(gens):
            try:
                next(g_)
            except StopIteration:
                gens.remove(g_)


def phase_odd_mix(self, l):
    j = l // 2
    self.phase_begin()
    _ssd(self, l)
    if self.upto == 'S1':
        return
    self.P.barrier()
    self.A.reset(self.const_end)
    _attn_dense(self, l, 1552, 2064, 2576, 4, 4, None, None, False, self.o_nk, self.o_nv, only_prompts=True)
    self.P.barrier()
    self.A.reset(self.const_end)
    _na_sample(self, l)


K.phase_odd_mix = phase_odd_mix
```

```python
import numpy as np
import concourse.bass as bass
import concourse.mybir as mybir
from concourse.bass_utils import run_bass_kernel_spmd

F32 = mybir.dt.float32
BF16 = mybir.dt.bfloat16
AF = mybir.ActivationFunctionType
ALU = mybir.AluOpType
AX = mybir.AxisListType

ENGS = ('pe', 'act', 'dve', 'pool', 'sp')
SEM_ROT = 20000
NDSEM = 12

D = 1024
NT = 4608
EPS = 1e-6
SEQS = [(0, 256, False), (256, 256, False), (512, 4096, True)]
EV_IN = 3088


class Prog:
    def __init__(self, nc, same_engine_sync=True):
        self.nc = nc
        self.same = same_engine_sync
        self.stream = {e: [] for e in ENGS}
        self.nsem = 0
        self.esem = {}
        self.ecnt = {}
        self.allsems = []
        for e in ENGS:
            self._new_esem(e)
        self.known = {e: {} for e in ENGS}
        self.res = {}
        self.dsems = {}
        self.dpos = {}
        self.n_ops = 0

    def _alloc_sem(self, name):
        self.nsem += 1
        s = self.nc.alloc_semaphore(name=f"{name}_{self.nsem}")
        return s

    def _new_esem(self, e):
        self.esem[e] = self._alloc_sem(f"e_{e}")
        self.ecnt[e] = 0

    def _deps(self, eng, reads, writes):
        deps = {}

        def add(tok):
            if tok is None:
                return
            sem, val, src = tok
            if src == eng and (eng == 'pe' or not self.same):
                return
            k = id(sem)
            if k not in deps or deps[k][1] < val:
                deps[k] = (sem, val)
        for r in reads:
            st = self.res.get(r)
            if st:
                add(st[0])
        for w in writes:
            st = self.res.get(w)
            if st:
                add(st[0])
                for t in st[1]:
                    add(t)
        out = []
        kn = self.known[eng]
        for k, (sem, val) in deps.items():
            if kn.get(k, 0) >= val:
                continue
            kn[k] = val
            out.append((sem, val))
        return out

    def _commit(self, tok, reads, writes):
        for w in writes:
            self.res[w] = [tok, []]
        for r in reads:
            st = self.res.setdefault(r, [None, []])
            st[1].append(tok)
            if len(st[1]) > 48:
                best = {}
                for t in st[1]:
                    k = id(t[0])
                    if k not in best or best[k][1] < t[1]:
                        best[k] = t
                st[1] = list(best.values())

    def op(self, eng, fn, reads=(), writes=()):
        waits = self._deps(eng, reads, writes)
        if self.ecnt[eng] >= SEM_ROT:
            self._new_esem(eng)
        sem = self.esem[eng]
        self.ecnt[eng] += 1
        tok = (sem, self.ecnt[eng], eng)
        self.stream[eng].append((waits, fn, sem, 1))
        self._commit(tok, reads, writes)
        self.n_ops += 1
        return tok

    def dma(self, q, out, in_, reads=(), writes=(), **kw):
        waits = self._deps(q, reads, writes)
        if q not in self.dsems:
            self.dsems[q] = [[self._alloc_sem(f"d_{q}"), 0] for _ in range(NDSEM)]
            self.dpos[q] = 0
        slot = self.dsems[q][self.dpos[q] % NDSEM]
        self.dpos[q] += 1
        sem = slot[0]
        kn = self.known[q]
        if slot[1] > 0 and kn.get(id(sem), 0) < slot[1]:
            waits.append((sem, slot[1]))
            kn[id(sem)] = slot[1]
        slot[1] += 16
        tok = (sem, slot[1], 'dma_' + q)
        self.stream[q].append((waits, lambda e, o=out, i=in_, k=kw: e.dma_start(out=o, in_=i, **k), sem, 16))
        self._commit(tok, reads, writes)
        self.n_ops += 1
        return tok

    def barrier(self):
        pts = []
        for e in ENGS:
            if self.ecnt[e] > 0:
                pts.append((self.esem[e], self.ecnt[e]))
        for q in self.dsems:
            for s in self.dsems[q]:
                if s[1] > 0:
                    pts.append((s[0], s[1]))
        for e in ENGS:
            kn = self.known[e]
            w = []
            for (s, v) in pts:
                if kn.get(id(s), 0) < v:
                    kn[id(s)] = v
                    w.append((s, v))
            if w:
                self.stream[e].append((w, None, None, 0))
        self.res = {}

    def emit(self):
        nc = self.nc
        with nc.Block() as block:
            def run(e, name):
                for waits, fn, sem, inc in self.stream[name]:
                    for (s, v) in waits:
                        e.wait_ge(s, v)
                    if fn is not None:
                        fn(e).then_inc(sem, inc)

            @block.tensor
            def _(e):
                run(e, 'pe')

            @block.scalar
            def _(e):
                run(e, 'act')

            @block.vector
            def _(e):
                run(e, 'dve')

            @block.gpsimd
            def _(e):
                run(e, 'pool')

            @block.sync
            def _(e):
                run(e, 'sp')


class Arena:
    def __init__(self, ap, nwords):
        self.ap = ap
        self.n = nwords
        self.off = 0
        self.uid = 0

    def reset(self, off=0):
        self.off = off

    def alloc(self, shape, dtype=F32):
        n = int(np.prod(shape))
        words = n if dtype == F32 else (n + 1) // 2
        words = (words + 7) // 8 * 8
        assert self.off + words <= self.n, f"arena overflow {self.off}+{words}>{self.n}"
        a = self.ap[:, self.off:self.off + words]
        self.off += words
        if dtype != F32:
            a = a.bitcast(dtype)
        a = a[:, 0:n]
        if len(shape) == 2:
            a = a.rearrange("p (a b) -> p a b", a=shape[0])
        elif len(shape) == 3:
            a = a.rearrange("p (a b c) -> p a b c", a=shape[0], b=shape[1])
        return a


def bc_last(a, n):
    return bass.AP(a.tensor, a.offset, [list(x) for x in a.ap] + [[0, n]])


def bc_mid(a, n):
    return bass.AP(a.tensor, a.offset, [list(a.ap[0]), [0, n]] + [list(x) for x in a.ap[1:]])


class K:
    def __init__(self, n_layers=4, upto=None, taps=()):
        self.n_layers = n_layers
        self.upto = upto
        self.taps = taps
        nc = bass.Bass("TRN2", target_bir_lowering=False)
        self.nc = nc
        self.P = Prog(nc)
        self.uid = 0
        di = lambda name, shape: nc.dram_tensor(name, list(shape), F32, kind="ExternalInput").ap()
        do = lambda name, shape: nc.dram_tensor(name, list(shape), F32, kind="ExternalOutput").ap()
        ds = lambda name, shape: nc.dram_tensor(name, list(shape), F32, kind="Internal").ap()
        self.xin = di("xin", [NT, D])
        self.cvec = di("cvec", [2, D])
        self.state_gdn = di("state_gdn", [2, 2, 4, 128, 128])
        self.cgk = di("cache_gqa_k", [2, 512, 2, 128])
        self.cgv = di("cache_gqa_v", [2, 512, 2, 128])
        self.state_ssd = di("state_ssd", [2, 2, 8, 64, 128])
        self.cnk = di("cache_na_k", [2, 512, 4, 128])
        self.cnv = di("cache_na_v", [2, 512, 4, 128])
        self.ada_w = di("ada_w", [4, D, 6 * D])
        self.ada_b = di("ada_b", [4, 6 * D])
        self.norms = {n: di(n, [4, D]) for n in ("norm_mix_pre", "norm_mix_post", "norm_mlp_pre", "norm_mlp_post")}
        self.mlp_w1 = di("mlp_w1", [4, D, 4 * D])
        self.mlp_w2 = di("mlp_w2", [4, 4 * D, D])
        self.ev_w_in = di("ev_w_in", [2, D, EV_IN])
        self.ev_w_out = di("ev_w_out", [2, D, D])
        self.gdn_conv = di("gdn_conv", [2, 3, 1536])
        self.gdn_a_log = di("gdn_a_log", [2, 8])
        self.gdn_dt_bias = di("gdn_dt_bias", [2, 8])
        self.gdn_norm = di("gdn_norm", [2, 128])
        self.gqa_q_norm = di("gqa_q_norm", [2, 128])
        self.gqa_k_norm = di("gqa_k_norm", [2, 128])
        self.od_w_in = di("od_w_in", [2, D, EV_IN])
        self.od_w_out = di("od_w_out", [2, D, D])
        self.ssd_conv = di("ssd_conv", [2, 3, 1024])
        self.ssd_conv_b = di("ssd_conv_b", [2, 1024])
        self.ssd_a_log = di("ssd_a_log", [2, 16])
        self.ssd_dt_bias = di("ssd_dt_bias", [2, 16])
        self.ssd_d = di("ssd_d", [2, 8])
        self.ssd_norm = di("ssd_norm", [2, 512])
        self.na_rpb = di("na_rpb", [2, 60, 31])
        self.c_ident = di("c_ident", [128, 128])
        self.c_masks = di("c_masks", [8, 64, 64])
        self.c_rope = di("c_rope", [2, 128, 4096])
        self.c_perm = di("c_perm", [128, 128])
        self.c_namask = di("c_namask", [64, 64])
        self.yout = do("yout", [NT, D])
        self.o_gdn = do("o_gdn", [2, 2, 2, 4, 128, 128])
        self.o_gk = do("o_gk", [2, 2, 256, 2, 128])
        self.o_gv = do("o_gv", [2, 2, 256, 2, 128])
        self.o_ssd = do("o_ssd", [2, 2, 2, 8, 64, 128])
        self.o_nk = do("o_nk", [2, 2, 256, 4, 128])
        self.o_nv = do("o_nv", [2, 2, 256, 4, 128])
        self.xres = ds("xres", [NT, D])
        self.modrow = ds("modrow", [4, 2, 6 * D])
        self.projT = ds("projT", [3200, NT])
        self.projN = ds("projN", [NT, EV_IN])
        self.mixN = ds("mixN", [NT, 512])
        self.mixT = ds("mixT", [512, NT])
        self.qkn = ds("qkn", [1024, NT])
        self.kvn = ds("kvn", [NT, 1024])
        self.rpbpad = ds("rpbpad", [60, 160])
        self.gb = ds("gb", [NT, 16])
        self.sb = ds("sb", [NT, 32])
        self.yf = ds("yf", [NT, 512])
        self.yb = ds("yb", [NT, 512])
        self.og0 = ds("og0", [NT, 512])
        self.og1 = ds("og1", [NT, 512])
        self.tapbufs = {}
        for (name, shape) in taps:
            self.tapbufs[name] = do("tap_" + name, shape)
        self.NW = 53000
        self.arena_t = nc.alloc_sbuf_tensor("arena", [128, self.NW], F32)
        self.A = Arena(self.arena_t.ap() if hasattr(self.arena_t, 'ap') else self.arena_t[:, :], self.NW)
        self.banks = []
        for i in range(8):
            t = nc.alloc_psum_tensor(f"bank{i}", [128, 512], F32)
            self.banks.append(t.ap() if hasattr(t, 'ap') else t[:, :])

    def bk(self, w, *aps):
        w = list(w)
        for a in aps:
            nm = getattr(getattr(a, 'tensor', None), 'name', '')
            if isinstance(nm, str) and nm.startswith('bank'):
                k = 'X' + nm
                if k not in w:
                    w.append(k)
        return w

    def mm(self, out, lhsT, rhs, start, stop, r, w):
        return self.P.op('pe', lambda e: e.matmul(out, lhsT=lhsT, rhs=rhs, start=start, stop=stop), r, self.bk(w, out))

    def tr(self, out, in_, ident, r, w):
        return self.P.op('pe', lambda e: e.transpose(out, in_, ident), r, self.bk(w, out))

    def act(self, out, in_, func, r, w, **kw):
        return self.P.op('act', lambda e: e.activation(out=out, in_=in_, func=func, **kw), r, self.bk(w, out, in_))

    def ts(self, eng, out, in0, s1, s2, op0, op1, r, w):
        w = self.bk(w, out, in0)
        if op1 is None:
            return self.P.op(eng, lambda e: e.tensor_scalar(out=out, in0=in0, scalar1=s1, scalar2=None, op0=op0), r, w)
        return self.P.op(eng, lambda e: e.tensor_scalar(out=out, in0=in0, scalar1=s1, scalar2=s2, op0=op0, op1=op1), r, w)

    def tt(self, eng, out, in0, in1, op, r, w):
        return self.P.op(eng, lambda e: e.tensor_tensor(out=out, in0=in0, in1=in1, op=op), r, self.bk(w, out, in0, in1))

    def stt(self, eng, out, in0, scalar, in1, op0, op1, r, w):
        return self.P.op(eng, lambda e: e.scalar_tensor_tensor(out=out, in0=in0, scalar=scalar, in1=in1, op0=op0, op1=op1), r,
                         self.bk(w, out, in0, in1))

    def recip(self, out, in_, r, w):
        return self.P.op('dve', lambda e: e.reciprocal(out=out, in_=in_), r, self.bk(w, out, in_))

    def cp(self, eng, out, in_, r, w):
        if eng == 'act':
            return self.act(out, in_, AF.Copy, r, w)
        return self.P.op(eng, lambda e: e.tensor_copy(out=out, in_=in_), r, self.bk(w, out, in_))

    def memset(self, eng, ap, val, w):
        return self.P.op(eng, lambda e: e.memset(ap, val), (), w)

    def key(self, name):
        self.uid += 1
        return f"{name}#{self.uid}"

    def tap(self, name, src_ap, reads):
        if name in self.tapbufs:
            self.P.dma('sp', self.tapbufs[name], src_ap, reads=reads)

    def phase_begin(self):
        self.P.barrier()
        A = self.A
        A.reset()
        self.ident = A.alloc([128])
        self.identb = A.alloc([128], BF16)
        self.P.dma('sp', self.ident, self.c_ident, writes=['ident'])
        self.cp('dve', self.identb, self.ident, ['ident'], ['identb'])
        self.ones = A.alloc([128])
        self.memset('pool', self.ones, 1.0, ['ones'])
        self.onesb = A.alloc([128], BF16)
        self.memset('pool', self.onesb, 1.0, ['onesb'])
        self.const_end = A.off

    def load_bcast(self, dst, row_ap, key, q='sp'):
        self.P.dma(q, dst, row_ap.partition_broadcast(128), writes=[key])

    def rstd_from_ss(self, rstd, ss, n, r, w):
        self.act(rstd, ss, AF.Sqrt, r, w, bias=EPS, scale=1.0 / n)
        self.P.op('dve', lambda e: e.reciprocal(out=rstd, in_=rstd), w, w)

    def phase_ada(self):
        self.phase_begin()
        A, P = self.A, self.P
        cT = A.alloc([2, 8])
        for v in range(2):
            P.dma('sp', cT[:, v, :], self.cvec[v, :].rearrange("(k p) -> p k", p=128), writes=['cT'],
                  allow_slow_non_contiguous=True)
        sc = A.alloc([2, 8])
        self.act(sc, cT, AF.Silu, ['cT'], ['sc'])
        L = A.alloc([2, 8, 128])
        for v in range(2):
            self.cp('dve', L[:, v, :, :], bc_last(sc[:, v, :], 128), ['sc'], ['L'])
        wbuf = [A.alloc([1536]) for _ in range(3)]
        bb = A.alloc([1536])
        ob = [A.alloc([1536]) for _ in range(2)]
        for l in range(self.n_layers):
            for g in range(4):
                c0 = g * 1536
                self.load_bcast(bb, self.ada_b[l, c0:c0 + 1536], 'bb', q='act')
                for k in range(8):
                    wb = wbuf[k % 3]
                    wk = f'wbuf{k % 3}'
                    P.dma('sp', wb, self.ada_w[l, k * 128:(k + 1) * 128, c0:c0 + 1536], writes=[wk])
                    for v in range(2):
                        for n in range(3):
                            self.mm(self.banks[v * 3 + n], L[:, v, k, :], wb[:, n * 512:(n + 1) * 512],
                                    k == 0, k == 7, [wk, 'L'], [f'bank{v * 3 + n}'])
                for v in range(2):
                    for n in range(3):
                        self.tt('dve', ob[v][:, n * 512:(n + 1) * 512], self.banks[v * 3 + n], bb[:, n * 512:(n + 1) * 512],
                                ALU.add, [f'bank{v * 3 + n}', 'bb'], [f'ob{v}'])
                    P.dma('act', self.modrow[l, v:v + 1, c0:c0 + 1536], ob[v][0:1, :], reads=[f'ob{v}'])

    def mod_vec(self, dst, l, v, idx, key, plus1=False):
        self.load_bcast(dst, self.modrow[l, v, idx * D:(idx + 1) * D], key, q='act')
        if plus1:
            self.ts('dve', dst, dst, 1.0, None, ALU.add, None, [key], [key])

    def phase_A(self, l, xsrc):
        even = (l % 2 == 0)
        j = l // 2
        w_in = self.ev_w_in[j] if even else self.od_w_in[j]
        if even:
            fm = [(c, 128) for c in range(0, 1024, 128)] + [(c, 128) for c in range(2064, 2832, 128)]
            tm = [(512, 512), (1024, 512), (1536, 512), (2048, 16), (2832, 256)]
        else:
            fm = [(c, 128) for c in range(1024, 1536, 128)] + [(c, 128) for c in range(1552, 2576, 128)]
            tm = [(0, 512), (512, 512), (1024, 256), (1536, 16), (2576, 512)]
        self.phase_begin()
        A, P = self.A, self.P
        W = A.alloc([8, EV_IN], BF16)
        for k in range(8):
            P.dma('pool', W[:, k, :], w_in[k * 128:(k + 1) * 128, :], writes=[f'W{k}'])
        Wk = [f'W{k}' for k in range(8)]
        tmpv = A.alloc([D])
        self.load_bcast(tmpv, self.norms["norm_mix_pre"][l, :], 'tmpv')
        A1 = []
        SH = []
        for v in range(2):
            a1 = A.alloc([D])
            sh = A.alloc([D])
            self.mod_vec(sh, l, v, 0, f'sh{v}')
            self.mod_vec(a1, l, v, 1, f'a1{v}', plus1=True)
            self.tt('dve', a1, a1, tmpv, ALU.mult, [f'a1{v}', 'tmpv'], [f'a1{v}'])
            A1.append(a1)
            SH.append(sh)
        xt = [A.alloc([D]) for _ in range(2)]
        junk = A.alloc([D], BF16)
        tmpf = A.alloc([D])
        hm = [A.alloc([D], BF16) for _ in range(2)]
        st = A.alloc([4])
        hmT = [A.alloc([8, 512], BF16) for _ in range(2)]
        stage = [A.alloc([512]) for _ in range(4)]
        ptr = [self.banks[0].bitcast(BF16), self.banks[1].bitcast(BF16)]
        nst = 0
        nps = 0
        for g in range(9):
            v = 0 if g == 0 else 1
            hT = hmT[g % 2]
            hk = f'hmT{g % 2}'
            for ti in range(4):
                t = g * 4 + ti
                s = t % 2
                P.dma('sp', xt[s], xsrc[t * 128:(t + 1) * 128, :], writes=[f'xt{s}'])
                self.act(junk, xt[s], AF.Square, [f'xt{s}'], ['junk', f'ss{s}'], accum_out=st[:, s:s + 1])
                self.rstd_from_ss(st[:, 2 + s:3 + s], st[:, s:s + 1], D, [f'ss{s}'], [f'rs{s}'])
                self.stt('dve', tmpf, xt[s], st[:, 2 + s:3 + s], A1[v], ALU.mult, ALU.mult,
                         [f'xt{s}', f'rs{s}', f'a1{v}'], ['tmpf'])
                self.tt('dve', hm[s], tmpf, SH[v], ALU.add, ['tmpf', f'sh{v}'], [f'hm{s}'])
                for k in range(8):
                    self.tr(ptr[s][:, k * 128:(k + 1) * 128], hm[s][:, k * 128:(k + 1) * 128], self.identb,
                            [f'hm{s}', 'identb'], [f'ptr{s}'])
                self.cp('act', hT[:, :, ti * 128:(ti + 1) * 128], ptr[s].rearrange("p (k t) -> p k t", k=8),
                        [f'ptr{s}'], [hk])
            for (c0, wd) in fm:
                b = 2 + nps % 3
                nps += 1
                for k in range(8):
                    self.mm(self.banks[b][0:wd, :], W[:, k, c0:c0 + wd], hT[:, k, :], k == 0, k == 7,
                            [Wk[k], hk], [f'bank{b}'])
                sg = nst % 4
                nst += 1
                self.cp('act' if nst % 2 else 'dve', stage[sg][0:wd, :], self.banks[b][0:wd, :], [f'bank{b}'], [f'stage{sg}'])
                P.dma('sp', self.projT[c0:c0 + wd, g * 512:(g + 1) * 512], stage[sg][0:wd, :], reads=[f'stage{sg}'])
            for ti in range(4):
                t = g * 4 + ti
                for (c0, wd) in tm:
                    b = 5 + nps % 3
                    nps += 1
                    for k in range(8):
                        self.mm(self.banks[b][:, 0:wd], hT[:, k, ti * 128:(ti + 1) * 128], W[:, k, c0:c0 + wd],
                                k == 0, k == 7, [Wk[k], hk], [f'bank{b}'])
                    sg = nst % 4
                    nst += 1
                    self.cp('act' if nst % 2 else 'dve', stage[sg][:, 0:wd], self.banks[b][:, 0:wd], [f'bank{b}'], [f'stage{sg}'])
                    P.dma('sp', self.projN[t * 128:(t + 1) * 128, c0:c0 + wd], stage[sg][:, 0:wd], reads=[f'stage{sg}'])

    def phase_C0(self, l, xsrc):
        even = (l % 2 == 0)
        j = l // 2
        w_out = self.ev_w_out[j] if even else self.od_w_out[j]
        self.phase_begin()
        A, P = self.A, self.P
        W = A.alloc([8, D], BF16)
        for k in range(8):
            P.dma('pool', W[:, k, :], w_out[k * 128:(k + 1) * 128, :], writes=[f'W{k}'])
        tmpv = A.alloc([D])
        self.load_bcast(tmpv, self.norms["norm_mix_post"][l, :], 'tmpv')
        G1 = []
        for v in range(2):
            g1 = A.alloc([D])
            self.mod_vec(g1, l, v, 2, f'g1{v}')
            self.tt('dve', g1, g1, tmpv, ALU.mult, [f'g1{v}', 'tmpv'], [f'g1{v}'])
            G1.append(g1)
        xt = [A.alloc([D]) for _ in range(2)]
        mn = [A.alloc([512], BF16) for _ in range(2)]
        mT = [A.alloc([4, 512], BF16) for _ in range(2)]
        mTn = [A.alloc([4, 128], BF16) for _ in range(2)]
        junk = A.alloc([D], BF16)
        tmpf = A.alloc([D])
        st = A.alloc([8])
        ptr = [self.banks[0].bitcast(BF16), self.banks[1].bitcast(BF16)]
        for g in range(9):
            v = 0 if g == 0 else 1
            gs = g % 2
            for k in range(4):
                P.dma('pool', mT[gs][:, k, :], self.mixT[k * 128:(k + 1) * 128, g * 512:(g + 1) * 512], writes=[f'mT{gs}'])
            for ti in range(4):
                t = g * 4 + ti
                s = t % 2
                P.dma('sp', xt[s], xsrc[t * 128:(t + 1) * 128, :], writes=[f'xt{s}'])
                P.dma('pool', mn[s], self.mixN[t * 128:(t + 1) * 128, :], writes=[f'mn{s}'])
                for k in range(4):
                    self.tr(ptr[s][:, k * 128:(k + 1) * 128], mn[s][:, k * 128:(k + 1) * 128], self.identb,
                            [f'mn{s}', 'identb'], [f'ptr{s}'])
                self.cp('act', mTn[s], ptr[s][:, 0:512].rearrange("p (k t) -> p k t", k=4), [f'ptr{s}'], [f'mTn{s}'])
                for h in range(2):
                    b = 2 + 2 * s + h
                    for k in range(8):
                        lhs = mTn[s][:, k, :] if k < 4 else mT[gs][:, k - 4, ti * 128:(ti + 1) * 128]
                        self.mm(self.banks[b], lhs, W[:, k, h * 512:(h + 1) * 512], k == 0, k == 7,
                                [f'W{k}', f'mTn{s}', f'mT{gs}'], [f'bank{b}'])
                    self.act(junk[:, 0:512], self.banks[b], AF.Square, [f'bank{b}'], ['junk', f'ss{s}{h}'],
                             accum_out=st[:, 2 * s + h:2 * s + h + 1])
                self.tt('dve', st[:, 4 + s:5 + s], st[:, 2 * s:2 * s + 1], st[:, 2 * s + 1:2 * s + 2], ALU.add,
                        [f'ss{s}0', f'ss{s}1'], [f'sst{s}'])
                self.rstd_from_ss(st[:, 6 + s:7 + s], st[:, 4 + s:5 + s], D, [f'sst{s}'], [f'rs{s}'])
                for h in range(2):
                    b = 2 + 2 * s + h
                    self.stt('dve', tmpf[:, h * 512:(h + 1) * 512], self.banks[b], st[:, 6 + s:7 + s],
                             G1[v][:, h * 512:(h + 1) * 512], ALU.mult, ALU.mult, [f'bank{b}', f'rs{s}', f'g1{v}'], ['tmpf'])
                self.tt('pool', xt[s], xt[s], tmpf, ALU.add, [f'xt{s}', 'tmpf'], [f'xt{s}'])
                P.dma('sp', self.xres[t * 128:(t + 1) * 128, :], xt[s], reads=[f'xt{s}'])

    def phase_C1(self, l, xdst):
        self.phase_begin()
        A, P = self.A, self.P
        W1 = A.alloc([8, 4 * D], BF16)
        W2 = A.alloc([32, D], BF16)
        for k in range(8):
            P.dma('pool', W1[:, k, :], self.mlp_w1[l, k * 128:(k + 1) * 128, :], writes=[f'W1{k}'])
        for k in range(32):
            P.dma('pool', W2[:, k, :], self.mlp_w2[l, k * 128:(k + 1) * 128, :], writes=[f'W2{k}'])
        W1k = [f'W1{k}' for k in range(8)]
        cur = {}
        cur['sh'] = A.alloc([D])
        cur['a2'] = A.alloc([D])
        cur['g2'] = A.alloc([D])
        xt = [A.alloc([D]) for _ in range(2)]
        junk = A.alloc([D], BF16)
        tmpf = A.alloc([D])
        hm = [A.alloc([D], BF16) for _ in range(2)]
        st = A.alloc([16])
        hmT = A.alloc([8, 512], BF16)
        h1T = A.alloc([32, 512], BF16)
        r1 = [A.alloc([512]) for _ in range(2)]
        ptr = [self.banks[0].bitcast(BF16), self.banks[1].bitcast(BF16)]

        def load_vecs(v):
            if cur.get('v') == v:
                return
            cur['v'] = v
            self.mod_vec(cur['sh'], l, v, 3, 'sh2')
            self.mod_vec(cur['a2'], l, v, 4, 'a2', plus1=True)
            self.load_bcast(tmpf, self.norms["norm_mlp_pre"][l, :], 'tmpf')
            self.tt('dve', cur['a2'], cur['a2'], tmpf, ALU.mult, ['a2', 'tmpf'], ['a2'])
            self.mod_vec(cur['g2'], l, v, 5, 'g2')
            self.load_bcast(tmpf, self.norms["norm_mlp_post"][l, :], 'tmpf')
            self.tt('dve', cur['g2'], cur['g2'], tmpf, ALU.mult, ['g2', 'tmpf'], ['g2'])

        nx = 0
        for g in range(9):
            v = 0 if g == 0 else 1
            load_vecs(v)
            for ti in range(4):
                t = g * 4 + ti
                s = nx % 2
                nx += 1
                P.dma('sp', xt[s], self.xres[t * 128:(t + 1) * 128, :], writes=[f'xt{s}'])
                self.act(junk, xt[s], AF.Square, [f'xt{s}'], ['junk', f'ss{s}'], accum_out=st[:, s:s + 1])
                self.rstd_from_ss(st[:, 2 + s:3 + s], st[:, s:s + 1], D, [f'ss{s}'], [f'rs{s}'])
                self.stt('dve', tmpf, xt[s], st[:, 2 + s:3 + s], cur['a2'], ALU.mult, ALU.mult,
                         [f'xt{s}', f'rs{s}', 'a2'], ['tmpf'])
                self.tt('dve', hm[s], tmpf, cur['sh'], ALU.add, ['tmpf', 'sh2'], [f'hm{s}'])
                for k in range(8):
                    self.tr(ptr[s][:, k * 128:(k + 1) * 128], hm[s][:, k * 128:(k + 1) * 128], self.identb,
                            [f'hm{s}', 'identb'], [f'ptr{s}'])
                self.cp('act', hmT[:, :, ti * 128:(ti + 1) * 128], ptr[s].rearrange("p (k t) -> p k t", k=8),
                        [f'ptr{s}'], ['hmT'])
            for f in range(32):
                b = 2 + f % 2
                for k in range(8):
                    self.mm(self.banks[b], W1[:, k, f * 128:(f + 1) * 128], hmT[:, k, :], k == 0, k == 7,
                            [W1k[k], 'hmT'], [f'bank{b}'])
                rs = f % 2
                self.act(r1[rs], self.banks[b], AF.Relu, [f'bank{b}'], [f'r1{rs}'])
                self.tt('dve' if f % 4 < 3 else 'pool', h1T[:, f, :], r1[rs], r1[rs], ALU.mult, [f'r1{rs}'], [f'h1T{f}'])
            for ti in range(4):
                t = g * 4 + ti
                s = nx % 2
                nx += 1
                P.dma('sp', xt[s], self.xres[t * 128:(t + 1) * 128, :], writes=[f'xt{s}'])
                for h in range(2):
                    b = 4 + 2 * s + h
                    for f in range(32):
                        self.mm(self.banks[b], h1T[:, f, ti * 128:(ti + 1) * 128], W2[:, f, h * 512:(h + 1) * 512],
                                f == 0, f == 31, [f'W2{f}', f'h1T{f}'], [f'bank{b}'])
                    self.act(junk[:, 0:512], self.banks[b], AF.Square, [f'bank{b}'], ['junk', f'q{s}{h}'],
                             accum_out=st[:, 4 + 2 * s + h:5 + 2 * s + h])
                self.tt('dve', st[:, 8 + s:9 + s], st[:, 4 + 2 * s:5 + 2 * s], st[:, 5 + 2 * s:6 + 2 * s], ALU.add,
                        [f'q{s}0', f'q{s}1'], [f'qt{s}'])
                self.rstd_from_ss(st[:, 10 + s:11 + s], st[:, 8 + s:9 + s], D, [f'qt{s}'], [f'qr{s}'])
                for h in range(2):
                    b = 4 + 2 * s + h
                    self.stt('dve', tmpf[:, h * 512:(h + 1) * 512], self.banks[b], st[:, 10 + s:11 + s],
                             cur['g2'][:, h * 512:(h + 1) * 512], ALU.mult, ALU.mult, [f'bank{b}', f'qr{s}', 'g2'], ['tmpf'])
                self.tt('pool', xt[s], xt[s], tmpf, ALU.add, [f'xt{s}', 'tmpf'], [f'xt{s}'])
                P.dma('sp', xdst[t * 128:(t + 1) * 128, :], xt[s], reads=[f'xt{s}'])

    def build(self):
        P = self.P
        self.phase_ada()
        if self.upto == 'ada':
            return self.finish()
        if self.upto in ('T1', 'S1'):
            self.phase_A(1, self.xin)
            self.phase_odd_mix(1)
            return self.finish()
        for l in range(self.n_layers):
            xsrc = self.xin if l == 0 else self.xres
            self.phase_A(l, xsrc)
            if self.upto == f'A{l}':
                return self.finish()
            if l % 2 == 0:
                self.phase_even_mix(l)
            else:
                self.phase_odd_mix(l)
            if self.upto in (f'M{l}', 'G1', 'G2', 'G3', 'S1'):
                return self.finish()
            self.phase_C0(l, xsrc)
            last = (l == self.n_layers - 1)
            self.phase_C1(l, self.yout if last else self.xres)
        return self.finish()

    def finish(self):
        P = self.P
        P.barrier()
        for name, buf in self.tapbufs.items():
            src = getattr(self, name)
            P.dma('sp', buf, src)
        P.barrier()
        P.emit()
        return self.nc

    def phase_even_mix(self, l):
        raise NotImplementedError

    def phase_odd_mix(self, l):
        raise NotImplementedError


def host_consts():
    ident = np.eye(128, dtype=np.float32)
    t = np.arange(64)
    masks = np.zeros((8, 64, 64), np.float32)
    masks[0] = (t[:, None] <= t[None, :])
    masks[1] = (t[:, None] > t[None, :])
    masks[2] = (t[:, None] >= t[None, :])
    masks[3] = (t[:, None] < t[None, :])
    masks[4] = ((t[:, None] // 32) == (t[None, :] // 32))
    masks[5] = 1.0 - masks[4]
    half = 64
    inv_freq = (10000.0 ** (-np.arange(0, half, 2, dtype=np.float32) / half)).astype(np.float32)
    tok = np.arange(4096)
    row = (tok // 64).astype(np.float32)
    col = (tok % 64).astype(np.float32)
    ang = np.concatenate([row[None, :] * inv_freq[:, None], row[None, :] * inv_freq[:, None],
                          col[None, :] * inv_freq[:, None], col[None, :] * inv_freq[:, None]], 0)
    C = np.cos(ang).astype(np.float32)
    S = np.sin(ang).astype(np.float32)
    sign = np.ones((128, 1), np.float32)
    sign[0:32] = -1.0
    sign[64:96] = -1.0
    rope = np.stack([C, S * sign]).astype(np.float32)
    perm = np.zeros((128, 128), np.float32)
    for p in range(128):
        blk = p // 64 * 64
        q = p - blk
        partner = blk + (q + 32) % 64
        perm[partner, p] = 1.0
    q = np.arange(64)
    kc = np.arange(64)
    cs = np.clip(q - 8, 0, 48)
    namask = ((kc[None, :] >= cs[:, None]) & (kc[None, :] < cs[:, None] + 16)).astype(np.float32)
    return dict(c_ident=ident, c_masks=masks, c_rope=rope, c_perm=perm, c_namask=namask)


def make_in_maps(inp):
    f = lambda a: np.ascontiguousarray(a, dtype=np.float32)
    consts = host_consts()
    shared = {}
    for n in ("ada_w", "ada_b", "norm_mix_pre", "norm_mix_post", "norm_mlp_pre", "norm_mlp_post", "mlp_w1", "mlp_w2",
              "ev_w_in", "ev_w_out", "gdn_conv", "gdn_norm", "gqa_q_norm", "gqa_k_norm", "od_w_in", "od_w_out",
              "ssd_conv", "ssd_conv_b", "ssd_d", "ssd_norm"):
        shared[n] = f(inp[n])
    shared["gdn_a_log"] = f(inp["gdn_a_log"]).reshape(2, 8)
    shared["gdn_dt_bias"] = f(inp["gdn_dt_bias"]).reshape(2, 8)
    shared["ssd_a_log"] = f(inp["ssd_a_log"]).reshape(2, 16)
    shared["ssd_dt_bias"] = f(inp["ssd_dt_bias"]).reshape(2, 16)
    shared["na_rpb"] = f(inp["na_rpb"]).reshape(2, 60, 31)
    shared.update(consts)
    maps = []
    for c in range(8):
        m = dict(shared)
        m["xin"] = f(np.concatenate([inp["x_prompt"][2 * c], inp["x_prompt"][2 * c + 1], inp["x_sample"][c]], 0))
        m["cvec"] = f(np.stack([inp["c_ctx"], inp["c"][c]], 0))
        m["state_gdn"] = f(inp["state_gdn"][c])
        m["cache_gqa_k"] = f(inp["cache_gqa_k"][c])
        m["cache_gqa_v"] = f(inp["cache_gqa_v"][c])
        m["state_ssd"] = f(inp["state_ssd"][c])
        m["cache_na_k"] = f(inp["cache_na_k"][c])
        m["cache_na_v"] = f(inp["cache_na_v"][c])
        maps.append(m)
    return maps


def _prep_qk(self, x, T, wcol, rope, out, tmp, rs, tables, pos0, kx, ko, bank):
    step = 512
    for c0 in range(0, T, step):
        w = min(step, T - c0)
        xs = x[:, c0:c0 + w]
        if wcol is not None:
            self.act(tmp[:, 0:w], xs, AF.Square, [kx], ['pq_tmp'])
            self.mm(self.banks[bank][:, 0:w], self.ones, tmp[:, 0:w], True, True, ['ones', 'pq_tmp'], [f'bank{bank}'])
            self.act(rs[:, 0:w], self.banks[bank][:, 0:w], AF.Sqrt, [f'bank{bank}'], ['pq_rs'], bias=EPS, scale=1.0 / 128)
            self.P.op('dve', lambda e, a=rs[:, 0:w]: e.reciprocal(out=a, in_=a), ['pq_rs'], ['pq_rs'])
            self.stt('dve', xs, xs, wcol, rs[:, 0:w], ALU.mult, ALU.mult, [kx, 'pq_rs', 'pq_w'], [kx])
        if rope:
            C, S, perm = tables
            self.mm(self.banks[bank][:, 0:w], perm, xs, True, True, ['perm', kx], [f'bank{bank}'])
            self.tt('dve', tmp[:, 0:w], self.banks[bank][:, 0:w], S[:, pos0 + c0:pos0 + c0 + w], ALU.mult,
                    [f'bank{bank}', 'ropeS'], ['pq_tmp'])
            self.tt('pool', rs[:, 0:w], xs, C[:, pos0 + c0:pos0 + c0 + w], ALU.mult, [kx, 'ropeC'], ['pq_rs'])
            self.tt('dve', out[:, c0:c0 + w], tmp[:, 0:w], rs[:, 0:w], ALU.add, ['pq_tmp', 'pq_rs'], [ko])
        else:
            self.cp('dve', out[:, c0:c0 + w], xs, [kx], [ko])


def _attn_dense(self, l, qrow0, krow0, vcol0, nq, nkv, sample_ctx, norm, rope, kout, vout, only_prompts=False):
    j = l // 2
    A, P = self.A, self.P
    rep = nq // nkv
    scale = 128 ** -0.5
    TKM = 4096 + 512
    kTb = A.alloc([TKM], BF16)
    Vb = A.alloc([36, 128], BF16)
    qTb = A.alloc([4096], BF16)
    xk = A.alloc([4096])
    xq = A.alloc([4096])
    tmp = A.alloc([512])
    rs = A.alloc([512])
    pt = [A.alloc([512], BF16) for _ in range(3)]
    rden = A.alloc([512])
    osb = [A.alloc([512]) for _ in range(2)]
    ctk = A.alloc([4, 128])
    kst = A.alloc([2, 128])
    tables = None
    if rope:
        C = A.alloc([4096])
        S = A.alloc([4096])
        perm = A.alloc([128])
        P.dma('sp', C, self.c_rope[0], writes=['ropeC'])
        P.dma('sp', S, self.c_rope[1], writes=['ropeS'])
        P.dma('sp', perm, self.c_perm, writes=['perm'])
        tables = (C, S, perm)
    wq = wk = None
    if norm is not None:
        wq = A.alloc([1])
        wk = A.alloc([1])
        P.dma('sp', wq, norm[0].rearrange("(p o) -> p o", o=1), writes=['pq_w'])
        P.dma('sp', wk, norm[1].rearrange("(p o) -> p o", o=1), writes=['pq_w'])
    no = 0
    npt = 0
    for si, (t0, T, is_s) in enumerate(SEQS):
        if only_prompts and is_s:
            continue
        Tk = T + (512 if (is_s and sample_ctx is not None) else 0)
        nkc = Tk // 128
        for g in range(nkv):
            P.dma('sp', xk[:, 0:T], self.projT[krow0 + g * 128:krow0 + (g + 1) * 128, t0:t0 + T], writes=['xk'])
            _prep_qk(self, xk, T, wk, rope and is_s, kTb, tmp, rs, tables, 0, 'xk', 'kTb', 6)
            if not is_s and kout is not None:
                for a in range(T // 128):
                    self.tr(self.banks[7][:, 0:128], xk[:, a * 128:(a + 1) * 128], self.ident, ['xk', 'ident'], ['bank7'])
                    self.cp('act', kst[:, a, :], self.banks[7][:, 0:128], ['bank7'], ['kst'])
                P.dma('sp', kout[si, j, :, g, :].rearrange("(a p) d -> p a d", p=128), kst, reads=['kst'])
            if Tk > T:
                ck, cv = sample_ctx
                P.dma('sp', ctk, ck[j, :, g, :].rearrange("(a p) d -> p a d", p=128), writes=['ctk'])
                for a in range(4):
                    self.tr(self.banks[7][:, 0:128], ctk[:, a, :], self.ident, ['ctk', 'ident'], ['bank7'])
                    self.cp('act', kTb[:, T + a * 128:T + (a + 1) * 128], self.banks[7][:, 0:128], ['bank7'], ['kTb'])
                P.dma('pool', Vb[:, T // 128:T // 128 + 4, :], cv[j, :, g, :].rearrange("(a p) d -> p a d", p=128), writes=['Vb'])
            P.dma('pool', Vb[:, 0:T // 128, :],
                  self.projN[t0:t0 + T, vcol0 + g * 128:vcol0 + (g + 1) * 128].rearrange("(a p) d -> p a d", p=128), writes=['Vb'])
            if not is_s and vout is not None:
                P.dma('act', vout[si, j, :, g, :], self.projN[t0:t0 + T, vcol0 + g * 128:vcol0 + (g + 1) * 128])
            for r_ in range(rep):
                h = g * rep + r_
                P.dma('sp', xq[:, 0:T], self.projT[qrow0 + h * 128:qrow0 + (h + 1) * 128, t0:t0 + T], writes=['xq'])
                _prep_qk(self, xq, T, wq, rope and is_s, qTb, tmp, rs, tables, 0, 'xq', 'qTb', 6)
                QW = min(512, T)
                for qc in range(T // QW):
                    q0 = qc * QW
                    bo = 2 + 2 * (no % 2)
                    def smm(kc_, bs_):
                        self.mm(self.banks[bs_][:, 0:QW], kTb[:, kc_ * 128:(kc_ + 1) * 128], qTb[:, q0:q0 + QW], True, True,
                                ['kTb', 'qTb'], [f'bank{bs_}'])
                    smm(0, npt % 2)
                    for kc in range(nkc):
                        bs = npt % 2
                        p_ = pt[npt % 3]
                        pk = f'pt{npt % 3}'
                        npt += 1
                        if kc + 1 < nkc:
                            smm(kc + 1, npt % 2)
                        self.act(p_[:, 0:QW], self.banks[bs][:, 0:QW], AF.Exp, [f'bank{bs}'], [pk], scale=scale)
                        self.mm(self.banks[bo][:, 0:QW], Vb[:, kc, :], p_[:, 0:QW], kc == 0, kc == nkc - 1, ['Vb', pk], [f'bank{bo}'])
                        self.mm(self.banks[bo + 1][:, 0:QW], self.onesb, p_[:, 0:QW], kc == 0, kc == nkc - 1, ['onesb', pk], [f'bank{bo + 1}'])
                    self.recip(rden[:, 0:QW], self.banks[bo + 1][:, 0:QW], [f'bank{bo + 1}'], ['rden'])
                    ob = osb[no % 2]
                    okk = f'osb{no % 2}'
                    no += 1
                    self.tt('dve', ob[:, 0:QW], self.banks[bo][:, 0:QW], rden[:, 0:QW], ALU.mult, [f'bank{bo}', 'rden'], [okk])
                    P.dma('sp', self.mixT[h * 128:(h + 1) * 128, t0 + q0:t0 + q0 + QW], ob[:, 0:QW], reads=[okk])


K._attn_dense = _attn_dense


def _gdn(self, l):
    j = l // 2
    A, P = self.A, self.P
    NB = 4
    wb = [A.alloc([1024]) for _ in range(3)]
    for k in range(3):
        self.load_bcast(wb[k], self.gdn_conv[j, k, 512:1536], f'wb{k}')
    xm = [A.alloc([1024]) for _ in range(2)]
    x0 = [A.alloc([1024]) for _ in range(2)]
    xp = [A.alloc([1024]) for _ in range(2)]
    sq = A.alloc([512])
    ss4 = A.alloc([4])
    nt = 0
    for (t0, T, is_s) in SEQS:
        for a in range(T // 128):
            s = nt % 2
            nt += 1
            r0 = t0 + a * 128
            first, last = (a == 0), (a == T // 128 - 1)
            src = self.projN
            P.dma('sp', x0[s], src[r0:r0 + 128, 512:1536], writes=[f'x0{s}'])
            if first:
                self.memset('pool', xm[s], 0.0, [f'xm{s}'])
                P.dma('act', xm[s][1:128, :], src[r0:r0 + 127, 512:1536], writes=[f'xm{s}'])
            else:
                P.dma('act', xm[s], src[r0 - 1:r0 + 127, 512:1536], writes=[f'xm{s}'])
            if last:
                self.memset('pool', xp[s], 0.0, [f'xp{s}'])
                P.dma('act', xp[s][0:127, :], src[r0 + 1:r0 + 128, 512:1536], writes=[f'xp{s}'])
            else:
                P.dma('act', xp[s], src[r0 + 1:r0 + 129, 512:1536], writes=[f'xp{s}'])
            self.tt('dve', x0[s], x0[s], wb[1], ALU.mult, [f'x0{s}', 'wb1'], [f'x0{s}'])
            self.tt('pool', xm[s], xm[s], wb[0], ALU.mult, [f'xm{s}', 'wb0'], [f'xm{s}'])
            self.tt('pool', xp[s], xp[s], wb[2], ALU.mult, [f'xp{s}', 'wb2'], [f'xp{s}'])
            self.tt('dve', x0[s], x0[s], xm[s], ALU.add, [f'x0{s}', f'xm{s}'], [f'x0{s}'])
            self.tt('dve', x0[s], x0[s], xp[s], ALU.add, [f'x0{s}', f'xp{s}'], [f'x0{s}'])
            self.act(x0[s], x0[s], AF.Silu, [f'x0{s}'], [f'x0{s}'])
            self.tt('dve', sq, x0[s][:, 0:512], x0[s][:, 0:512], ALU.mult, [f'x0{s}'], ['sq'])
            self.P.op('dve', lambda e, o=ss4, i=sq.rearrange("p (h d) -> p h d", h=4): e.tensor_reduce(out=o, in_=i, axis=AX.X, op=ALU.add),
                      ['sq'], ['ss4'])
            self.act(ss4, ss4, AF.Sqrt, ['ss4'], ['ss4'], bias=EPS, scale=1.0)
            self.P.op('dve', lambda e, o=ss4: e.reciprocal(out=o, in_=o), ['ss4'], ['ss4'])
            kv = x0[s][:, 0:512].rearrange("p (h d) -> p h d", h=4)
            self.tt('dve', kv, kv, bc_last(ss4, 128), ALU.mult, [f'x0{s}', 'ss4'], [f'x0{s}'])
            P.dma('sp', self.kvn[r0:r0 + 128, :], x0[s], reads=[f'x0{s}'])
    raw = A.alloc([36, 16])
    for a in range(36):
        P.dma('sp', raw[:, a, :], self.projN[a * 128:(a + 1) * 128, 2048:2064], writes=['raw'])
    dtb = A.alloc([8])
    nea = A.alloc([8])
    self.load_bcast(dtb, self.gdn_dt_bias[j, :], 'dtb')
    self.load_bcast(nea, self.gdn_a_log[j, :], 'nea')
    self.act(nea, nea, AF.Exp, ['nea'], ['nea'])
    gbt = A.alloc([36, 16])
    self.act(gbt[:, :, 0:8], raw[:, :, 0:8], AF.Sigmoid, ['raw'], ['gbt'])
    self.tt('dve', raw[:, :, 8:16], raw[:, :, 8:16], bc_mid(dtb, 36), ALU.add, ['raw', 'dtb'], ['raw'])
    self.act(raw[:, :, 8:16], raw[:, :, 8:16], AF.Exp, ['raw'], ['raw'])
    self.act(raw[:, :, 8:16], raw[:, :, 8:16], AF.Ln, ['raw'], ['raw'], bias=1.0)
    self.tt('dve', raw[:, :, 8:16], raw[:, :, 8:16], bc_mid(nea, 36), ALU.mult, ['raw', 'nea'], ['raw'])
    self.ts('dve', gbt[:, :, 8:16], raw[:, :, 8:16], -1.0, None, ALU.mult, None, ['raw'], ['gbt'])
    for a in range(36):
        P.dma('sp', self.gb[a * 128:(a + 1) * 128, :], gbt[:, a, :], reads=['gbt'])
    if self.upto == 'G1':
        return
    P.barrier()
    A.reset(self.const_end)
    xf = A.alloc([NT])
    acc = A.alloc([NT])
    cw = A.alloc([3])
    tq = A.alloc([512])
    rq = A.alloc([512])
    for c in range(8):
        P.dma('sp', xf, self.projT[c * 128:(c + 1) * 128, :], writes=['xf'])
        P.dma('sp', cw, self.gdn_conv[j, :, c * 128:(c + 1) * 128].rearrange("k p -> p k"), writes=['cw'], allow_slow_non_contiguous=True)
        self.ts('dve', acc, xf, cw[:, 1:2], None, ALU.mult, None, ['xf', 'cw'], ['acc'])
        for (t0, T, is_s) in SEQS:
            self.stt('dve', acc[:, t0 + 1:t0 + T], xf[:, t0:t0 + T - 1], cw[:, 0:1], acc[:, t0 + 1:t0 + T], ALU.mult, ALU.add,
                     ['xf', 'cw', 'acc'], ['acc'])
            self.stt('dve', acc[:, t0:t0 + T - 1], xf[:, t0 + 1:t0 + T], cw[:, 2:3], acc[:, t0:t0 + T - 1], ALU.mult, ALU.add,
                     ['xf', 'cw', 'acc'], ['acc'])
        self.act(acc, acc, AF.Silu, ['acc'], ['acc'])
        for g in range(9):
            sl = acc[:, g * 512:(g + 1) * 512]
            self.act(tq, sl, AF.Square, ['acc'], ['tq'])
            self.mm(self.banks[0], self.ones, tq, True, True, ['ones', 'tq'], ['bank0'])
            self.act(rq, self.banks[0], AF.Sqrt, ['bank0'], ['rq'], bias=EPS, scale=1.0)
            self.P.op('dve', lambda e, o=rq: e.reciprocal(out=o, in_=o), ['rq'], ['rq'])
            self.stt('dve', sl, sl, (128 ** -0.5) if c < 4 else 1.0, rq, ALU.mult, ALU.mult, ['acc', 'rq'], ['acc'])
        P.dma('sp', self.qkn[c * 128:(c + 1) * 128, :], acc, reads=['acc'])
    if self.upto == 'G2':
        return
    P.barrier()
    A.reset(self.const_end)
    msk = A.alloc([6, 64])
    P.dma('sp', msk[0:64], self.c_masks[0:6].rearrange("m t i -> t m i"), writes=['msk'])
    m_bd, m_off = msk[0:64, 4], msk[0:64, 5]
    qTs = [A.alloc([4096]) for _ in range(2)]
    kTs = [A.alloc([4096]) for _ in range(2)]
    gbc = A.alloc([64, 16])
    X = []
    for ci_ in range(4):
        x = {}
        for nm in ('gcol', 'bcol', 'gc', 'eg', 'edec', 'beg', 'gtb'):
            x[nm] = A.alloc([64])
        x['S'] = A.alloc([128])
        for nm in ('Gm', 'Em', 'ETm', 'A0', 'A1', 'Q', 'Pm', 'qkT', 'wT'):
            x[nm] = A.alloc([NB, 64])
        x['BQ0'] = A.alloc([NB, 128])
        x['BQ1'] = A.alloc([NB, 128])
        x['Ao'] = x['Em']
        x['Yb'] = x['Gm']
        for nm in ('vb', 'kbg', 'kdec', 'u', 'ktb', 'vtb'):
            x[nm] = A.alloc([NB, 128])
        for nm in ('vnew', 'tqs', 'tmp2'):
            x[nm] = A.alloc([128])
        X.append(x)
    og = [self.og0, self.og1]
    idb = self.ident[0:64, 0:64]
    v3 = lambda bk, off=0: bk[0:64, off:off + NB * 64].rearrange("p (c i) -> p c i", c=NB)

    def chain(ci_, hs, h, d, si, t0, T, is_s):
        x = X[ci_]
        K_ = lambda n: f"{ {'Ao': 'Em', 'Yb': 'Gm'}.get(n, n) }_{ci_}"
        qT, kT, kqT, kkT = qTs[hs], kTs[hs], f'qT{hs}', f'kT{hs}'
        ktb, vtb = x['ktb'], x['vtb']
        dq = 'sp' if ci_ % 2 == 0 else 'act'
        nch = T // 64
        ci = d * 4 + h
        B0 = B2 = self.banks[2 * ci_]
        B1 = B3 = self.banks[2 * ci_ + 1]
        k0 = k2 = f'bk{2 * ci_}'
        k1 = k3 = f'bk{2 * ci_ + 1}'
        m_cum, m_oth, m_strict, m_incl = (msk[0:64, 0], msk[0:64, 1], msk[0:64, 1], msk[0:64, 0]) if d == 0 else \
                                         (msk[0:64, 2], msk[0:64, 3], msk[0:64, 3], msk[0:64, 2])
        gcol, bcol, gc, eg, edec, beg, gtb, S = (x[n] for n in ('gcol', 'bcol', 'gc', 'eg', 'edec', 'beg', 'gtb', 'S'))
        Gm, Em, ETm, Q, Pm, Ao, Yb, qkT, wT = (x[n] for n in ('Gm', 'Em', 'ETm', 'Q', 'Pm', 'Ao', 'Yb', 'qkT', 'wT'))
        vb, kbg, kdec, u, vnew, tqs, tmp2 = (x[n] for n in ('vb', 'kbg', 'kdec', 'u', 'vnew', 'tqs', 'tmp2'))
        Ak = [x['A0'], x['A1']]
        self.cp('dve', bcol[0:64, 0:nch], gbc[0:64, 0:nch, ci], ['gbc'], [K_('bcol')])
        self.cp('dve', gcol[0:64, 0:nch], gbc[0:64, 0:nch, 8 + ci], ['gbc'], [K_('gcol')])
        yield
        self.mm(B0[0:64, 0:nch], m_cum, gcol[0:64, 0:nch], True, True, ['msk', K_('gcol')], [k0])
        self.mm(B1[:, 0:nch], self.ones[0:64, :], gcol[0:64, 0:nch], True, True, ['ones', K_('gcol')], [k1])
        yield
        self.cp('dve', gc[0:64, 0:nch], B0[0:64, 0:nch], [k0], [K_('gc')])
        self.act(eg[0:64, 0:nch], B0[0:64, 0:nch], AF.Exp, [k0], [K_('eg')])
        self.act(gtb[:, 0:nch], B1[:, 0:nch], AF.Exp, [k1], [K_('gtb')])
        yield
        self.tt('dve', edec[0:64, 0:nch], B1[0:64, 0:nch], gc[0:64, 0:nch], ALU.subtract, [k1, K_('gc')], [K_('edec')])
        self.tt('dve', beg[0:64, 0:nch], bcol[0:64, 0:nch], eg[0:64, 0:nch], ALU.mult, [K_('bcol'), K_('eg')], [K_('beg')])
        yield
        self.act(edec[0:64, 0:nch], edec[0:64, 0:nch], AF.Exp, [K_('edec')], [K_('edec')])
        if is_s:
            P.dma('sp', S, self.state_gdn[j, d, h], writes=[K_('S')])
        else:
            self.memset('pool', S, 0.0, [K_('S')])
        yield
        nbat = nch // NB
        border = range(nbat) if d == 0 else range(nbat - 1, -1, -1)
        blist = list(border)

        def load_kv(bi_):
            rr = slice(t0 + bi_ * NB * 64, t0 + (bi_ + 1) * NB * 64)
            P.dma(dq, ktb[0:64], self.kvn[rr, h * 128:(h + 1) * 128].rearrange("(c t) f -> t c f", t=64), writes=[K_('ktb')])
            P.dma(dq, vtb[0:64], self.kvn[rr, 512 + h * 128:512 + (h + 1) * 128].rearrange("(c t) f -> t c f", t=64), writes=[K_('vtb')])
        load_kv(blist[0])
        for bidx, bi in enumerate(blist):
            c0 = bi * NB
            cs = slice(c0, c0 + NB)
            self.tt('pool', Gm[0:64], bc_last(gcol[0:64, cs], 64), bc_mid(m_cum, NB), ALU.mult, [K_('gcol'), 'msk'], [K_('Gm')])
            yield
            for c in range(NB):
                self.mm(B0[0:64, c * 64:(c + 1) * 64], Gm[0:64, c, :], m_oth, True, True, [K_('Gm'), 'msk'], [k0])
            self.mm(B0[0:64, 256:512], m_oth, Gm[0:64].rearrange("p c i -> p (c i)"), True, True, [K_('Gm'), 'msk'], [k0])
            for c in range(NB):
                tk = slice((c0 + c) * 64, (c0 + c + 1) * 64)
                qa_, ka_ = qT[:, tk], kT[:, tk]
                qk_rhs = bass.AP(qa_.tensor, qa_.offset, [list(qa_.ap[0]), [ka_.offset - qa_.offset, 2], [1, 64]])
                self.mm(B1[0:64, c * 128:(c + 1) * 128], kT[:, tk], qk_rhs, True, True, [kkT, kqT], [k1])
            yield
            self.act(Em[0:64], v3(B0), AF.Exp, [k0], [K_('Em')])
            self.act(ETm[0:64], v3(B0, 256), AF.Exp, [k0], [K_('ETm')])
            yield
            self.tt('dve', Em[0:64], Em[0:64], bc_mid(m_strict, NB), ALU.mult, [K_('Em'), 'msk'], [K_('Em')])
            self.tt('pool', ETm[0:64], ETm[0:64], bc_mid(m_incl, NB), ALU.mult, [K_('ETm'), 'msk'], [K_('ETm')])
            yield
            a0 = Ak[0]
            gq = B1[0:64, :].rearrange("p (c i) -> p c i", c=NB)
            self.tt('dve', a0[0:64], gq[:, :, 64:128], Em[0:64], ALU.mult, [k1, K_('Em')], [K_('A0')])
            self.tt('dve', qkT[0:64], gq[:, :, 0:64], ETm[0:64], ALU.mult, [k1, K_('ETm')], [K_('qkT')])
            yield
            self.tt('pool', a0[0:64], a0[0:64], bc_last(bcol[0:64, cs], 64), ALU.mult, [K_('A0'), K_('bcol')], [K_('A0')])
            yield
            for c in range(NB):
                self.tr(B2[0:64, c * 64:(c + 1) * 64], a0[0:64, c, :], idb, [K_('A0'), 'ident'], [k2])
            yield
            BQ = [x['BQ0'], x['BQ1']]
            kBQ = [K_('BQ0'), K_('BQ1')]
            self.cp('act', BQ[0][0:64, :, 0:64], v3(B2), [k2], [kBQ[0]])
            self.tt('pool', Ao[0:64], a0[0:64], bc_mid(m_off, NB), ALU.mult, [K_('A0'), 'msk'], [K_('Ao')])
            yield
            self.tt('pool', a0[0:64], a0[0:64], bc_mid(m_bd, NB), ALU.mult, [K_('A0'), 'msk'], [K_('A0')])
            self.tt('pool', BQ[0][0:64, :, 0:64], BQ[0][0:64, :, 0:64], bc_mid(m_bd, NB), ALU.mult, [kBQ[0], 'msk'], [kBQ[0]])
            yield
            self.tt('pool', BQ[1][0:64, :, 64:128], bc_mid(idb, NB), BQ[0][0:64, :, 0:64], ALU.subtract, ['ident', kBQ[0]], [kBQ[1]])
            for c in range(NB):
                self.mm(B2[0:64, c * 64:(c + 1) * 64], a0[0:64, c, :], BQ[0][0:64, c, 0:64], True, True, [K_('A0'), kBQ[0]], [k2])
            for c in range(NB):
                self.mm(B3[0:64, c * 64:(c + 1) * 64], BQ[0][0:64, c, 0:64], a0[0:64, c, :], True, True, [K_('A0'), kBQ[0]], [k3])
            yield
            self.cp('act', BQ[1][0:64, :, 0:64], v3(B2), [k2], [kBQ[1]])
            self.cp('act', Ak[1][0:64], v3(B3), [k3], [K_('A1')])
            yield
            for kk in range(1, 4):
                cur, nxt = BQ[kk % 2], BQ[(kk + 1) % 2]
                kc_, kn2 = kBQ[kk % 2], kBQ[(kk + 1) % 2]
                ak, an = Ak[kk % 2], Ak[(kk + 1) % 2]
                ka, kan = K_(f'A{kk % 2}'), K_(f'A{(kk + 1) % 2}')
                for c in range(NB):
                    self.mm(B2[0:64, c * 128:(c + 1) * 128], ak[0:64, c, :], cur[0:64, c, :], True, True, [ka, kc_], [k2])
                for c in range(NB):
                    self.mm(B3[0:64, c * 64:(c + 1) * 64], cur[0:64, c, 0:64], ak[0:64, c, :], True, True, [ka, kc_], [k3])
                yield
                p4 = B2[0:64, :].rearrange("p (c i) -> p c i", c=NB)
                self.cp('act', nxt[0:64, :, 0:64], p4[:, :, 0:64], [k2], [kn2])
                self.tt('dve', nxt[0:64, :, 64:128], cur[0:64, :, 64:128], p4[:, :, 64:128], ALU.add, [kc_, k2], [kn2])
                self.cp('act', an[0:64], v3(B3), [k3], [kan])
                yield
            for c in range(NB):
                self.mm(B2[0:64, c * 64:(c + 1) * 64], Ak[0][0:64, c, :], BQ[0][0:64, c, 64:128], True, True, [K_('A0'), kBQ[0]], [k2])
            yield
            self.tt('dve', Q[0:64], BQ[0][0:64, :, 64:128], v3(B2), ALU.add, [kBQ[0], k2], [K_('Q')])
            yield
            for c in range(NB):
                self.tr(B3[0:64, c * 64:(c + 1) * 64], Q[0:64, c, :], idb, [K_('Q'), 'ident'], [k3])
            yield
            self.cp('act', Pm[0:64], v3(B3), [k3], [K_('Pm')])
            for c in range(NB):
                self.mm(B2[0:64, c * 64:(c + 1) * 64], Ao[0:64, c, :], Q[0:64, c, :], True, True, [K_('Ao'), K_('Q')], [k2])
            yield
            self.cp('act', Yb[0:64], v3(B2), [k2], [K_('Yb')])
            self.tt('pool', vb[0:64], vtb[0:64], bc_last(bcol[0:64, cs], 128), ALU.mult, [K_('vtb'), K_('bcol')], [K_('vb')])
            self.tt('pool', kbg[0:64], ktb[0:64], bc_last(beg[0:64, cs], 128), ALU.mult, [K_('ktb'), K_('beg')], [K_('kbg')])
            self.tt('pool', kdec[0:64], ktb[0:64], bc_last(edec[0:64, cs], 128), ALU.mult, [K_('ktb'), K_('edec')], [K_('kdec')])
            if bidx + 1 < len(blist):
                load_kv(blist[bidx + 1])
            yield
            for c in range(NB):
                self.mm(B3[0:64, c * 64:(c + 1) * 64], Pm[0:64, c, :], Yb[0:64, c, :], True, True, [K_('Pm'), K_('Yb')], [k3])
            yield
            self.tt('dve', Q[0:64], Q[0:64], v3(B3), ALU.subtract, [K_('Q'), k3], [K_('Q')])
            yield
            for c in range(NB):
                self.mm(B0[0:64, c * 128:(c + 1) * 128], Q[0:64, c, :], vb[0:64, c, :], True, True, [K_('Q'), K_('vb')], [k0])
            for c in range(NB):
                self.mm(B1[:, c * 64:(c + 1) * 64], kbg[0:64, c, :], Q[0:64, c, :], True, True, [K_('Q'), K_('kbg')], [k1])
            yield
            self.cp('act', u[0:64], B0[0:64, :].rearrange("p (c i) -> p c i", c=NB), [k0], [K_('u')])
            self.cp('act', wT, B1[:, 0:NB * 64].rearrange("p (c i) -> p c i", c=NB), [k1], [K_('wT')])
            yield
            corder = range(NB) if d == 0 else range(NB - 1, -1, -1)
            for c in corder:
                ch = c0 + c
                tk = slice(ch * 64, (ch + 1) * 64)
                self.mm(B2[0:64, 0:128], wT[:, c, :], S, True, True, [K_('wT'), K_('S')], [k2])
                self.mm(B3[0:64, 0:128], qT[:, tk], S, True, True, [kqT, K_('S')], [k3])
                yield
                self.tt('dve', vnew[0:64], u[0:64, c, :], B2[0:64, 0:128], ALU.subtract, [K_('u'), k2], [K_('vnew')])
                self.act(tqs[0:64], B3[0:64, 0:128], AF.Copy, [k3, K_('eg')], [K_('tqs')], scale=eg[0:64, ch:ch + 1])
                yield
                self.mm(B2[0:64, 128:256], qkT[0:64, c, :], vnew[0:64], True, True, [K_('qkT'), K_('vnew')], [k2])
                self.mm(B3[:, 128:256], kdec[0:64, c, :], vnew[0:64], True, True, [K_('kdec'), K_('vnew')], [k3])
                yield
                self.tt('dve', tmp2[0:64], tqs[0:64], B2[0:64, 128:256], ALU.add, [K_('tqs'), k2], [K_('tmp2')])
                self.stt('dve', S, S, gtb[:, ch:ch + 1], B3[:, 128:256], ALU.mult, ALU.add, [K_('S'), K_('gtb'), k3], [K_('S')])
                P.dma(dq, og[d][t0 + ch * 64:t0 + (ch + 1) * 64, h * 128:(h + 1) * 128], tmp2[0:64], reads=[K_('tmp2')])
        if not is_s:
            P.dma('sp', self.o_gdn[si, j, d, h], S, reads=[K_('S')])

    for si, (t0, T, is_s) in enumerate(SEQS):
        nch = T // 64
        for c8 in range(0, nch, 4):
            rr = slice(t0 + c8 * 64, t0 + (c8 + 4) * 64)
            P.dma('sp', gbc[0:64, c8:c8 + 4, :], self.gb[rr, :].rearrange("(c t) f -> t c f", t=64), writes=['gbc'])
        for pair in ((0, 1), (2, 3)):
            gens = []
            for hs, h in enumerate(pair):
                P.dma('sp', qTs[hs][:, 0:T], self.qkn[h * 128:(h + 1) * 128, t0:t0 + T], writes=[f'qT{hs}'])
                P.dma('act', kTs[hs][:, 0:T], self.qkn[512 + h * 128:512 + (h + 1) * 128, t0:t0 + T], writes=[f'kT{hs}'])
                for d in range(2):
                    gens.append(chain(hs * 2 + d, hs, h, d, si, t0, T, is_s))
            while gens:
                for g_ in list(gens):
                    try:
                        next(g_)
                    except StopIteration:
                        gens.remove(g_)
    P.barrier()
    A.reset(self.const_end)
    gnw = A.alloc([128])
    self.load_bcast(gnw, self.gdn_norm[j, :], 'gnw')
    oa = [A.alloc([512]) for _ in range(2)]
    ob_ = [A.alloc([512]) for _ in range(2)]
    zz = [A.alloc([512]) for _ in range(2)]
    sq = A.alloc([512])
    ss4 = A.alloc([8])
    for t in range(36):
        s = t % 2
        rr = slice(t * 128, (t + 1) * 128)
        P.dma('sp', oa[s], self.og0[rr, :], writes=[f'oa{s}'])
        P.dma('act', ob_[s], self.og1[rr, :], writes=[f'ob{s}'])
        P.dma('sp', zz[s], self.projN[rr, 1536:2048], writes=[f'zz{s}'])
        self.tt('dve', oa[s], oa[s], ob_[s], ALU.add, [f'oa{s}', f'ob{s}'], [f'oa{s}'])
        self.act(zz[s], zz[s], AF.Silu, [f'zz{s}'], [f'zz{s}'])
        self.tt('pool', sq, oa[s], oa[s], ALU.mult, [f'oa{s}'], ['sq'])
        r4 = ss4[:, 4 * s:4 * s + 4]
        self.P.op('dve', lambda e, o=r4, i=sq.rearrange("p (h d) -> p h d", h=4): e.tensor_reduce(out=o, in_=i, axis=AX.X, op=ALU.add),
                  ['sq'], [f'ss4{s}'])
        self.act(r4, r4, AF.Sqrt, [f'ss4{s}'], [f'ss4{s}'], bias=EPS, scale=1.0 / 128)
        self.recip(r4, r4, [f'ss4{s}'], [f'ss4{s}'])
        o3 = oa[s].rearrange("p (h d) -> p h d", h=4)
        self.tt('dve', o3, o3, bc_last(r4, 128), ALU.mult, [f'oa{s}', f'ss4{s}'], [f'oa{s}'])
        self.tt('pool', o3, o3, bc_mid(gnw, 4), ALU.mult, [f'oa{s}', 'gnw'], [f'oa{s}'])
        self.tt('dve', ob_[s], oa[s], zz[s], ALU.mult, [f'oa{s}', f'zz{s}'], [f'ob{s}'])
        P.dma('sp', self.mixN[rr, :], ob_[s], reads=[f'ob{s}'])


def phase_even_mix(self, l):
    j = l // 2
    self.phase_begin()
    _gdn(self, l)
    if self.upto in ('G1', 'G2', 'G3'):
        return
    self.P.barrier()
    self.A.reset(self.const_end)
    _attn_dense(self, l, 2064, 2576, 2832, 4, 2, (self.cgk, self.cgv), (self.gqa_q_norm[j], self.gqa_k_norm[j]), True,
                self.o_gk, self.o_gv)


K.phase_even_mix = phase_even_mix


_CACHE = {}


def kernel(**inputs):
    if 'nc' not in _CACHE:
        kb = K(n_layers=4)
        _CACHE['nc'] = kb.build()
    nc = _CACHE['nc']
    in_maps = make_in_maps(inputs)
    res = run_bass_kernel_spmd(nc, in_maps, core_ids=list(range(8)))
    rs = res.results
    y = np.stack([r["yout"] for r in rs], 0)
    y_prompt = np.ascontiguousarray(y[:, :512].reshape(16, 256, 1024))
    y_sample = np.ascontiguousarray(y[:, 512:])
    cat = lambda n: np.concatenate([r[n] for r in rs], 0)
    return (y_prompt, y_sample, cat("o_gdn"), cat("o_gk"), cat("o_gv"), cat("o_ssd"), cat("o_nk"), cat("o_nv"))


def _ssd(self, l):
    j = l // 2
    A, P = self.A, self.P
    B = self.banks
    W = 768
    wb = [A.alloc([W]) for _ in range(3)]
    for k in range(3):
        self.load_bcast(wb[k], self.ssd_conv[j, k, 0:W], f'wb{k}')
    cbb = A.alloc([W])
    self.load_bcast(cbb, self.ssd_conv_b[j, 0:W], 'cbb')
    xm = [A.alloc([W]) for _ in range(2)]
    x0 = [A.alloc([W]) for _ in range(2)]
    xp = [A.alloc([W]) for _ in range(2)]
    nt = 0
    src = self.projN
    for (t0, T, is_s) in SEQS:
        for a in range(T // 128):
            s = nt % 2
            nt += 1
            r0 = t0 + a * 128
            first, last = (a == 0), (a == T // 128 - 1)
            P.dma('sp', x0[s], src[r0:r0 + 128, 512:1280], writes=[f'x0{s}'])
            if first:
                self.memset('pool', xm[s], 0.0, [f'xm{s}'])
                P.dma('act', xm[s][1:128, :], src[r0:r0 + 127, 512:1280], writes=[f'xm{s}'])
            else:
                P.dma('act', xm[s], src[r0 - 1:r0 + 127, 512:1280], writes=[f'xm{s}'])
            if last:
                self.memset('pool', xp[s], 0.0, [f'xp{s}'])
                P.dma('act', xp[s][0:127, :], src[r0 + 1:r0 + 128, 512:1280], writes=[f'xp{s}'])
            else:
                P.dma('act', xp[s], src[r0 + 1:r0 + 129, 512:1280], writes=[f'xp{s}'])
            self.tt('dve', x0[s], x0[s], wb[1], ALU.mult, [f'x0{s}', 'wb1'], [f'x0{s}'])
            self.tt('pool', xm[s], xm[s], wb[0], ALU.mult, [f'xm{s}', 'wb0'], [f'xm{s}'])
            self.tt('pool', xp[s], xp[s], wb[2], ALU.mult, [f'xp{s}', 'wb2'], [f'xp{s}'])
            self.tt('dve', x0[s], x0[s], xm[s], ALU.add, [f'x0{s}', f'xm{s}'], [f'x0{s}'])
            self.tt('dve', x0[s], x0[s], xp[s], ALU.add, [f'x0{s}', f'xp{s}'], [f'x0{s}'])
            self.tt('dve', x0[s], x0[s], cbb, ALU.add, [f'x0{s}', 'cbb'], [f'x0{s}'])
            self.act(x0[s], x0[s], AF.Silu, [f'x0{s}'], [f'x0{s}'])
            P.dma('sp', self.kvn[r0:r0 + 128, 0:W], x0[s], reads=[f'x0{s}'])
    raw = A.alloc([36, 16])
    for a in range(36):
        P.dma('sp', raw[:, a, :], self.projN[a * 128:(a + 1) * 128, 1536:1552], writes=['raw'])
    dtb = A.alloc([16])
    nea = A.alloc([16])
    self.load_bcast(dtb, self.ssd_dt_bias[j, :], 'dtb')
    self.load_bcast(nea, self.ssd_a_log[j, :], 'nea')
    self.act(nea, nea, AF.Exp, ['nea'], ['nea'])
    sbt = A.alloc([36, 32])
    self.tt('dve', raw, raw, bc_mid(dtb, 36), ALU.add, ['raw', 'dtb'], ['raw'])
    self.act(raw, raw, AF.Exp, ['raw'], ['raw'])
    self.act(sbt[:, :, 0:16], raw, AF.Ln, ['raw'], ['sbt'], bias=1.0)
    self.tt('dve', raw, sbt[:, :, 0:16], bc_mid(nea, 36), ALU.mult, ['sbt', 'nea'], ['raw'])
    self.ts('dve', sbt[:, :, 16:32], raw, -1.0, None, ALU.mult, None, ['raw'], ['sbt'])
    for a in range(36):
        P.dma('sp', self.sb[a * 128:(a + 1) * 128, :], sbt[:, a, :], reads=['sbt'])
    P.barrier()
    A.reset(self.const_end)
    xf = A.alloc([NT])
    acc = A.alloc([NT])
    cw = A.alloc([4])
    for c in range(4):
        ch0 = 512 + c * 128
        P.dma('sp', xf, self.projT[1024 + c * 128:1024 + (c + 1) * 128, :], writes=['xf'])
        P.dma('sp', cw[:, 0:3], self.ssd_conv[j, :, ch0:ch0 + 128].rearrange("k p -> p k"), writes=['cw'], allow_slow_non_contiguous=True)
        P.dma('sp', cw[:, 3:4], self.ssd_conv_b[j, ch0:ch0 + 128].rearrange("(p o) -> p o", o=1), writes=['cw'])
        self.ts('dve', acc, xf, cw[:, 1:2], cw[:, 3:4], ALU.mult, ALU.add, ['xf', 'cw'], ['acc'])
        for (t0, T, is_s) in SEQS:
            self.stt('dve', acc[:, t0 + 1:t0 + T], xf[:, t0:t0 + T - 1], cw[:, 0:1], acc[:, t0 + 1:t0 + T], ALU.mult, ALU.add,
                     ['xf', 'cw', 'acc'], ['acc'])
            self.stt('dve', acc[:, t0:t0 + T - 1], xf[:, t0 + 1:t0 + T], cw[:, 2:3], acc[:, t0:t0 + T - 1], ALU.mult, ALU.add,
                     ['xf', 'cw', 'acc'], ['acc'])
        self.act(acc, acc, AF.Silu, ['acc'], ['acc'])
        P.dma('sp', self.qkn[c * 128:(c + 1) * 128, :], acc, reads=['acc'])
    P.barrier()
    A.reset(self.const_end)
    msk = A.alloc([4, 64])
    P.dma('sp', msk[0:64], self.c_masks[0:4].rearrange("m t i -> t m i"), writes=['msk'])
    snw = A.alloc([512])
    self.load_bcast(snw, self.ssd_norm[j, :], 'snw')
    dsk = A.alloc([8])
    self.load_bcast(dsk, self.ssd_d[j, :], 'dsk')
    bcT = A.alloc([2, 4096])
    ccT = A.alloc([2, 4096])
    sca = A.alloc([64, 32])
    id64 = self.ident[0:64, 0:64]
    X = []
    for d in range(2):
        x = {}
        for nm in ('lad', 'dtd', 'cs', 'ecs', 'edec', 'cdec'):
            x[nm] = A.alloc([64, 8])
        for nm in ('hT', 'Gm', 'LT', 'MT', 'xdt', 'xd2', 'yo', 'yy'):
            x[nm] = A.alloc([8, 64])
        x['xb'] = [A.alloc([768]) for _ in range(2)]
        x['h0'] = A.alloc([8, 128])
        X.append(x)
    ydst = [self.yf, self.yb]

    def chain(d, si, t0, T, is_s):
        x = X[d]
        K_ = lambda n: f'{n}_{d}'
        nch = T // 64
        B0, B1, B2, B3 = self.banks[4 * d:4 * d + 4]
        k0, k1, k2, k3 = [f'bk{4 * d + i}' for i in range(4)]
        m_cum, m_oth, m_incl = (msk[0:64, 0], msk[0:64, 1], msk[0:64, 0]) if d == 0 else (msk[0:64, 2], msk[0:64, 3], msk[0:64, 2])
        lad, dtd, cs, ecs, edec, cdec, hT = (x[n] for n in ('lad', 'dtd', 'cs', 'ecs', 'edec', 'cdec', 'hT'))
        Gm, LT, MT, xdt, xd2, yo, yy, h0 = (x[n] for n in ('Gm', 'LT', 'MT', 'xdt', 'xd2', 'yo', 'yy', 'h0'))
        self.cp('dve', dtd[0:64, 0:nch, :], sca[0:64, 0:nch, d * 8:d * 8 + 8], ['sca'], [K_('dtd')])
        self.cp('dve', lad[0:64, 0:nch, :], sca[0:64, 0:nch, 16 + d * 8:16 + d * 8 + 8], ['sca'], [K_('lad')])
        yield
        n8 = nch * 8
        fl = lambda a_, p_=64: a_[0:p_, 0:nch, :].rearrange("p c h -> p (c h)")
        self.mm(B0[0:64, 0:n8], m_cum, fl(lad), True, True, ['msk', K_('lad')], [k0])
        self.mm(B1[:, 0:n8], self.ones[0:64, :], fl(lad), True, True, ['ones', K_('lad')], [k1])
        yield
        self.cp('dve', fl(cs), B0[0:64, 0:n8], [k0], [K_('cs')])
        self.act(fl(ecs), B0[0:64, 0:n8], AF.Exp, [k0], [K_('ecs')])
        self.act(fl(cdec, 128), B1[:, 0:n8], AF.Exp, [k1], [K_('cdec')])
        yield
        self.tt('dve', fl(edec), B1[0:64, 0:n8], fl(cs), ALU.subtract, [k1, K_('cs')], [K_('edec')])
        yield
        self.act(fl(edec), fl(edec), AF.Exp, [K_('edec')], [K_('edec')])
        if is_s:
            for hh in range(8):
                P.dma('sp' if d == 0 else 'act', h0[0:64, hh, :], self.state_ssd[j, d, hh], writes=[K_('h0')])
            yield
            for hh in range(8):
                self.tr(B2[:, hh * 64:(hh + 1) * 64], h0[0:64, hh, :], id64, [K_('h0'), 'ident'], [k2])
            yield
            self.cp('dve', hT, B2.rearrange("p (h q) -> p h q", h=8), [k2], [K_('hT')])
        else:
            self.memset('pool', hT, 0.0, [K_('hT')])
        yield
        corder = range(nch) if d == 0 else range(nch - 1, -1, -1)
        nx = 0
        for c in corder:
            r0 = t0 + c * 64
            tk = slice(c * 64, (c + 1) * 64)
            s = nx % 2
            nx += 1
            xk = K_(f'xb{s}')
            xbs = x['xb'][s]
            P.dma('sp' if d == 0 else 'act', xbs[0:64, :], self.kvn[r0:r0 + 64, 0:768], writes=[xk])
            self.tt('dve', Gm[0:64], bc_last(lad[0:64, c, :], 64), bc_mid(m_cum, 8), ALU.mult, [K_('lad'), 'msk'], [K_('Gm')])
            yield
            x3 = xbs[0:64, 0:512].rearrange("p (h q) -> p h q", h=8)
            self.mm(B0[0:64, :], m_oth, Gm[0:64].rearrange("p h i -> p (h i)"), True, True, ['msk', K_('Gm')], [k0])
            for g in range(2):
                self.mm(B1[0:64, g * 64:(g + 1) * 64], bcT[:, g, tk], ccT[:, g, tk], True, True, ['bcT', 'ccT'], [k1])
            for g in range(2):
                self.mm(B2[0:64, g * 256:(g + 1) * 256], ccT[:, g, tk], hT[:, g * 4:(g + 1) * 4, :].rearrange("p h q -> p (h q)"),
                        True, True, ['ccT', K_('hT')], [k2])
            self.tt('pool', xdt[0:64], x3, bc_last(dtd[0:64, c, :], 64), ALU.mult, [xk, K_('dtd')], [K_('xdt')])
            yield
            self.act(LT[0:64], B0[0:64, :].rearrange("p (h i) -> p h i", h=8), AF.Exp, [k0], [K_('LT')])
            self.tt('dve', yo[0:64], B2[0:64, :].rearrange("p (h q) -> p h q", h=8), bc_last(ecs[0:64, c, :], 64), ALU.mult,
                    [k2, K_('ecs')], [K_('yo')])
            self.tt('pool', xd2[0:64], xdt[0:64], bc_last(edec[0:64, c, :], 64), ALU.mult, [K_('xdt'), K_('edec')], [K_('xd2')])
            yield
            self.tt('pool', LT[0:64], LT[0:64], bc_mid(m_incl, 8), ALU.mult, [K_('LT'), 'msk'], [K_('LT')])
            for g in range(2):
                self.mm(B3[:, g * 256:(g + 1) * 256], xbs[0:64, 512 + g * 128:512 + (g + 1) * 128],
                        xd2[0:64, g * 4:(g + 1) * 4, :].rearrange("p h q -> p (h q)"), True, True, [xk, K_('xd2')], [k3])
            yield
            for g in range(2):
                self.tt('dve', MT[0:64, g * 4:(g + 1) * 4, :], LT[0:64, g * 4:(g + 1) * 4, :], bc_mid(B1[0:64, g * 64:(g + 1) * 64], 4),
                        ALU.mult, [K_('LT'), k1], [K_('MT')])
            self.tt('pool', hT, hT, bc_last(cdec[:, c, :], 64), ALU.mult, [K_('hT'), K_('cdec')], [K_('hT')])
            yield
            for hh in range(8):
                self.mm(B2[0:64, hh * 64:(hh + 1) * 64], MT[0:64, hh, :], xdt[0:64, hh, :], True, True, [K_('MT'), K_('xdt')], [k2])
            self.tt('dve', hT, hT, B3.rearrange("p (h q) -> p h q", h=8), ALU.add, [K_('hT'), k3], [K_('hT')])
            yield
            self.tt('dve', yy[0:64], yo[0:64], B2[0:64, :].rearrange("p (h q) -> p h q", h=8), ALU.add, [K_('yo'), k2], [K_('yy')])
            yield
            P.dma('sp' if d == 0 else 'act', ydst[d][r0:r0 + 64, :], yy[0:64].rearrange("p h q -> p (h q)"), reads=[K_('yy')],
                  writes=[f'ydram{d}'])
        if not is_s:
            hs = h0
            for hh in range(8):
                self.tr(B0[0:64, (hh % 4) * 128:(hh % 4 + 1) * 128] if hh < 4 else B1[0:64, (hh % 4) * 128:(hh % 4 + 1) * 128],
                        hT[:, hh, :], self.ident, [K_('hT'), 'ident'], [k0 if hh < 4 else k1])
            yield
            self.cp('dve', hs[0:64, 0:4, :], B0[0:64, :].rearrange("p (h n) -> p h n", h=4), [k0], [K_('h0')])
            self.cp('act', hs[0:64, 4:8, :], B1[0:64, :].rearrange("p (h n) -> p h n", h=4), [k1], [K_('h0')])
            yield
            P.dma('sp', self.o_ssd[si, j, d].rearrange("h p n -> p h n"), hs[0:64], reads=[K_('h0')])

    for si, (t0, T, is_s) in enumerate(SEQS):
        nch = T // 64
        for g in range(2):
            P.dma('sp', bcT[:, g, 0:T], self.qkn[g * 128:(g + 1) * 128, t0:t0 + T], writes=['bcT'])
            P.dma('act', ccT[:, g, 0:T], self.qkn[256 + g * 128:256 + (g + 1) * 128, t0:t0 + T], writes=['ccT'])
        for c4 in range(0, nch, 4):
            P.dma('sp', sca[0:64, c4:c4 + 4, :], self.sb[t0 + c4 * 64:t0 + (c4 + 4) * 64, :].rearrange("(c t) f -> t c f", t=64), writes=['sca'])
        gens = [chain(0, si, t0, T, is_s), chain(1, si, t0, T, is_s)]
        while gens:
            for g_ in list(gens):
                try:
                    next(g_)
                except StopIteration:
                    gens.remove(g_)
    P.barrier()
    A.reset(self.const_end)
    snw = A.alloc([512])
    self.load_bcast(snw, self.ssd_norm[j, :], 'snw')
    dsk = A.alloc([8])
    self.load_bcast(dsk, self.ssd_d[j, :], 'dsk')
    ya = [A.alloc([512]) for _ in range(2)]
    yb_ = [A.alloc([512]) for _ in range(2)]
    xx = [A.alloc([512]) for _ in range(2)]
    zz = [A.alloc([512]) for _ in range(2)]
    sq = A.alloc([512])
    st = A.alloc([4])
    for t in range(36):
        s = t % 2
        rr = slice(t * 128, (t + 1) * 128)
        P.dma('sp', ya[s], self.yf[rr, :], writes=[f'ya{s}'])
        P.dma('act', yb_[s], self.yb[rr, :], writes=[f'yb{s}'])
        P.dma('sp', xx[s], self.kvn[rr, 0:512], writes=[f'xx{s}'])
        P.dma('act', zz[s], self.projN[rr, 0:512], writes=[f'zz{s}'])
        self.tt('dve', ya[s], ya[s], yb_[s], ALU.add, [f'ya{s}', f'yb{s}'], [f'ya{s}'])
        x3 = xx[s].rearrange("p (h q) -> p h q", h=8)
        self.tt('pool', x3, x3, bc_last(dsk, 64), ALU.mult, [f'xx{s}', 'dsk'], [f'xx{s}'])
        self.act(zz[s], zz[s], AF.Silu, [f'zz{s}'], [f'zz{s}'])
        self.tt('dve', ya[s], ya[s], xx[s], ALU.add, [f'ya{s}', f'xx{s}'], [f'ya{s}'])
        self.tt('dve', ya[s], ya[s], zz[s], ALU.mult, [f'ya{s}', f'zz{s}'], [f'ya{s}'])
        self.act(sq, ya[s], AF.Square, [f'ya{s}'], ['sq', f'ssq{s}'], accum_out=st[:, s:s + 1])
        self.act(st[:, 2 + s:3 + s], st[:, s:s + 1], AF.Sqrt, [f'ssq{s}'], [f'rsq{s}'], bias=EPS, scale=1.0 / 512)
        self.recip(st[:, 2 + s:3 + s], st[:, 2 + s:3 + s], [f'rsq{s}'], [f'rsq{s}'])
        self.stt('dve', yb_[s], ya[s], st[:, 2 + s:3 + s], snw, ALU.mult, ALU.mult, [f'ya{s}', f'rsq{s}', 'snw'], [f'yb{s}'])
        P.dma('sp', self.mixN[rr, :], yb_[s], reads=[f'yb{s}'])


def _na_sample(self, l):
    j = l // 2
    A, P = self.A, self.P
    B = self.banks
    t0 = 512
    scale = 128 ** -0.5
    BIG = 30000.0
    kdT = A.alloc([4, 4096], BF16)
    qdT = A.alloc([4, 4096], BF16)
    Ve = A.alloc([32, 512], BF16)
    Vo = A.alloc([31, 512], BF16)
    kcT = A.alloc([4, 512], BF16)
    Vc = A.alloc([4, 512], BF16)
    Bt = A.alloc([60, 64])
    nm = A.alloc([64])
    ngm = A.alloc([64])
    Z = A.alloc([160])
    ctk = A.alloc([4, 128])
    for h in range(4):
        P.dma('pool', kdT[:, h, :], self.projT[2064 + h * 128:2064 + (h + 1) * 128, t0:t0 + 4096], writes=['kdT'])
        P.dma('pool', qdT[:, h, :], self.projT[1552 + h * 128:1552 + (h + 1) * 128, t0:t0 + 4096], writes=['qdT'])
    for a4 in range(0, 32, 4):
        P.dma('pool', Ve[:, a4:a4 + 4, :], self.projN[t0 + a4 * 128:t0 + (a4 + 4) * 128, 2576:3088].rearrange("(a p) f -> p a f", p=128), writes=['Ve'])
    for a4 in range(0, 31, 4):
        n = min(4, 31 - a4)
        P.dma('pool', Vo[:, a4:a4 + n, :], self.projN[t0 + 64 + a4 * 128:t0 + 64 + (a4 + n) * 128, 2576:3088].rearrange("(a p) f -> p a f", p=128), writes=['Vo'])
    P.dma('pool', Vc, self.cnv[j].rearrange("(a p) h d -> p a (h d)", p=128), writes=['Vc'])
    for h in range(4):
        P.dma('sp', ctk, self.cnk[j, :, h, :].rearrange("(a p) d -> p a d", p=128), writes=['ctk'])
        for a in range(4):
            self.tr(B[0][:, a * 128:(a + 1) * 128], ctk[:, a, :], self.ident, ['ctk', 'ident'], ['b0'])
        self.cp('act', kcT[:, h, :], B[0], ['b0'], ['kcT'])
    self.memset('pool', Z[0:60, :], 0.0, ['Z'])
    P.dma('sp', Z[0:60, 64:95], self.na_rpb[j], writes=['Z'])
    tz = P.dma('sp', self.rpbpad, Z[0:60, :], reads=['Z'], writes=['rpbpad'])
    for q in range(64):
        src = bass.AP(self.rpbpad.tensor, 79 - q, [[0, 1], [160, 60], [1, 64]])
        P.dma('sp' if q % 2 else 'act', Bt[q:q + 1, :, :], src, reads=['rpbpad'], writes=['Bt'])
    P.dma('sp', nm[0:64, :], self.c_namask, writes=['nm'])
    self.ts('dve', ngm[0:64, :], nm[0:64, :], -1.0, BIG, ALU.add, ALU.mult, ['nm'], ['ngm'])
    self.tt('dve', Bt[0:64], Bt[0:64], bc_mid(nm[0:64, :], 60), ALU.mult, ['Bt', 'nm'], ['Bt'])
    self.tt('dve', Bt[0:64], Bt[0:64], bc_mid(ngm[0:64, :], 60), ALU.add, ['Bt', 'ngm'], ['Bt'])
    idb64 = self.identb[0:64, 0:64]
    XN = []
    for ch in range(2):
        XN.append(dict(ssb=A.alloc([1024]), pf=A.alloc([1024]), pb=A.alloc([1024], BF16), pT=[A.alloc([512], BF16) for _ in range(2)],
                       st=A.alloc([4]), osb=A.alloc([512])))

    def chain(cn):
        x = XN[cn]
        K_ = lambda n: f'{n}_{cn}'
        BS, BC, BT, BO = B[4 * cn:4 * cn + 4]
        kS, kC, kT, kO = [f'bk{4 * cn + i}' for i in range(4)]
        ptb = BT.bitcast(BF16)
        ssb, pf, pb, st, osb = x['ssb'], x['pf'], x['pb'], x['st'], x['osb']
        heads = (2 * cn, 2 * cn + 1)
        npt = 0
        for r in range(64):
            r0 = min(max(r - 4, 0), 56)
            dr0 = r0 - r + 7
            Vx, ti0, vk = (Ve, r0 // 2, 'Ve') if r0 % 2 == 0 else (Vo, (r0 - 1) // 2, 'Vo')
            for hi, h in enumerate(heads):
                qs = qdT[:, h, r * 64:(r + 1) * 64]
                self.mm(BS[0:64, :], qs, kdT[:, h, r0 * 64:r0 * 64 + 512], True, True, ['qdT', 'kdT'], [kS])
                self.mm(BC[0:64, :], qs, kcT[:, h, :], True, True, ['qdT', 'kcT'], [kC])
                yield
                self.stt('dve', ssb[0:64, 0:512].rearrange("p (a k) -> p a k", a=8), BS[0:64, :].rearrange("p (a k) -> p a k", a=8), scale,
                         Bt[0:64, h * 15 + dr0:h * 15 + dr0 + 8, :], ALU.mult, ALU.add, [kS, 'Bt'], [K_('ssb')])
                self.act(ssb[0:64, 512:1024], BC[0:64, :], AF.Copy, [kC], [K_('ssb')], scale=scale)
                yield
                self.P.op('dve', lambda e, o=st[0:64, 0:1], i=ssb[0:64, :]: e.tensor_reduce(out=o, in_=i, axis=AX.X, op=ALU.max),
                          [K_('ssb')], [K_('mx')])
                yield
                self.ts('dve', st[0:64, 1:2], st[0:64, 0:1], -1.0, None, ALU.mult, None, [K_('mx')], [K_('nmx')])
                yield
                self.act(pf[0:64, :], ssb[0:64, :], AF.Exp, [K_('ssb'), K_('nmx')], [K_('pf'), K_('sm')], bias=st[0:64, 1:2], accum_out=st[0:64, 2:3])
                yield
                self.recip(st[0:64, 3:4], st[0:64, 2:3], [K_('sm')], [K_('rsm')])
                yield
                self.ts('dve', pb[0:64, :], pf[0:64, :], st[0:64, 3:4], None, ALU.mult, None, [K_('pf'), K_('rsm')], [K_('pb')])
                yield
                for kc in range(8):
                    self.tr(ptb[:, kc * 64:(kc + 1) * 64], pb[0:64, kc * 128:(kc + 1) * 128], idb64, [K_('pb'), 'identb'], [kT])
                yield
                pt_ = x['pT'][npt % 2]
                pk = K_(f'pT{npt % 2}')
                npt += 1
                self.cp('act', pt_, ptb[:, 0:512], [kT], [pk])
                yield
                oc = hi * 256 + (r % 4) * 64
                for jj in range(8):
                    lhs = Vx[:, ti0 + jj, h * 128:(h + 1) * 128] if jj < 4 else Vc[:, jj - 4, h * 128:(h + 1) * 128]
                    self.mm(BO[:, oc:oc + 64], lhs, pt_[:, jj * 64:(jj + 1) * 64], jj == 0, jj == 7, [vk, 'Vc', pk], [kO])
                yield
            if r % 4 == 3:
                self.cp('dve' if cn else 'act', osb, BO, [kO], [K_('osb')])
                yield
                for hi, h in enumerate(heads):
                    P.dma('sp' if cn else 'act', self.mixT[h * 128:(h + 1) * 128, t0 + (r - 3) * 64:t0 + (r + 1) * 64],
                          osb[:, hi * 256:(hi + 1) * 256], reads=[K_('osb')])

    gens = [chain(0), chain(1)]
    while gens:
        for g_ in list(gens):
            try:
                next(g_)
            except StopIteration:
                gens.remove(g_)


def phase_odd_mix(self, l):
    j = l // 2
    self.phase_begin()
    _ssd(self, l)
    if self.upto == 'S1':
        return
    self.P.barrier()
    self.A.reset(self.const_end)
    _attn_dense(self, l, 1552, 2064, 2576, 4, 4, None, None, False, self.o_nk, self.o_nv, only_prompts=True)
    self.P.barrier()
    self.A.reset(self.const_end)
    _na_sample(self, l)


K.phase_odd_mix = phase_odd_mix
```

```python
import numpy as np
import concourse.bass as bass
import concourse.mybir as mybir
from concourse.bass_utils import run_bass_kernel_spmd

F32 = mybir.dt.float32
BF16 = mybir.dt.bfloat16
AF = mybir.ActivationFunctionType
ALU = mybir.AluOpType
AX = mybir.AxisListType

ENGS = ('pe', 'act', 'dve', 'pool', 'sp')
SEM_ROT = 20000
NDSEM = 12

D = 1024
NT = 4608
EPS = 1e-6
SEQS = [(0, 256, False), (256, 256, False), (512, 4096, True)]
EV_IN = 3088


class Prog:
    def __init__(self, nc, same_engine_sync=True):
        self.nc = nc
        self.same = same_engine_sync
        self.stream = {e: [] for e in ENGS}
        self.nsem = 0
        self.esem = {}
        self.ecnt = {}
        self.allsems = []
        for e in ENGS:
            self._new_esem(e)
        self.known = {e: {} for e in ENGS}
        self.res = {}
        self.dsems = {}
        self.dpos = {}
        self.n_ops = 0

    def _alloc_sem(self, name):
        self.nsem += 1
        s = self.nc.alloc_semaphore(name=f"{name}_{self.nsem}")
        return s

    def _new_esem(self, e):
        self.esem[e] = self._alloc_sem(f"e_{e}")
        self.ecnt[e] = 0

    def _deps(self, eng, reads, writes):
        deps = {}

        def add(tok):
            if tok is None:
                return
            sem, val, src = tok
            if src == eng and (eng == 'pe' or not self.same):
                return
            k = id(sem)
            if k not in deps or deps[k][1] < val:
                deps[k] = (sem, val)
        for r in reads:
            st = self.res.get(r)
            if st:
                add(st[0])
        for w in writes:
            st = self.res.get(w)
            if st:
                add(st[0])
                for t in st[1]:
                    add(t)
        out = []
        kn = self.known[eng]
        for k, (sem, val) in deps.items():
            if kn.get(k, 0) >= val:
                continue
            kn[k] = val
            out.append((sem, val))
        return out

    def _commit(self, tok, reads, writes):
        for w in writes:
            self.res[w] = [tok, []]
        for r in reads:
            st = self.res.setdefault(r, [None, []])
            st[1].append(tok)
            if len(st[1]) > 48:
                best = {}
                for t in st[1]:
                    k = id(t[0])
                    if k not in best or best[k][1] < t[1]:
                        best[k] = t
                st[1] = list(best.values())

    def op(self, eng, fn, reads=(), writes=()):
        waits = self._deps(eng, reads, writes)
        if self.ecnt[eng] >= SEM_ROT:
            self._new_esem(eng)
        sem = self.esem[eng]
        self.ecnt[eng] += 1
        tok = (sem, self.ecnt[eng], eng)
        self.stream[eng].append((waits, fn, sem, 1))
        self._commit(tok, reads, writes)
        self.n_ops += 1
        return tok

    def dma(self, q, out, in_, reads=(), writes=(), **kw):
        waits = self._deps(q, reads, writes)
        if q not in self.dsems:
            self.dsems[q] = [[self._alloc_sem(f"d_{q}"), 0] for _ in range(NDSEM)]
            self.dpos[q] = 0
        slot = self.dsems[q][self.dpos[q] % NDSEM]
        self.dpos[q] += 1
        sem = slot[0]
        kn = self.known[q]
        if slot[1] > 0 and kn.get(id(sem), 0) < slot[1]:
            waits.append((sem, slot[1]))
            kn[id(sem)] = slot[1]
        slot[1] += 16
        tok = (sem, slot[1], 'dma_' + q)
        self.stream[q].append((waits, lambda e, o=out, i=in_, k=kw: e.dma_start(out=o, in_=i, **k), sem, 16))
        self._commit(tok, reads, writes)
        self.n_ops += 1
        return tok

    def barrier(self):
        pts = []
        for e in ENGS:
            if self.ecnt[e] > 0:
                pts.append((self.esem[e], self.ecnt[e]))
        for q in self.dsems:
            for s in self.dsems[q]:
                if s[1] > 0:
                    pts.append((s[0], s[1]))
        for e in ENGS:
            kn = self.known[e]
            w = []
            for (s, v) in pts:
                if kn.get(id(s), 0) < v:
                    kn[id(s)] = v
                    w.append((s, v))
            if w:
                self.stream[e].append((w, None, None, 0))
        self.res = {}

    def emit(self):
        nc = self.nc
        with nc.Block() as block:
            def run(e, name):
                for waits, fn, sem, inc in self.stream[name]:
                    for (s, v) in waits:
                        e.wait_ge(s, v)
                    if fn is not None:
                        fn(e).then_inc(sem, inc)

            @block.tensor
            def _(e):
                run(e, 'pe')

            @block.scalar
            def _(e):
                run(e, 'act')

            @block.vector
            def _(e):
                run(e, 'dve')

            @block.gpsimd
            def _(e):
                run(e, 'pool')

            @block.sync
            def _(e):
                run(e, 'sp')


class Arena:
    def __init__(self, ap, nwords):
        self.ap = ap
        self.n = nwords
        self.off = 0
        self.uid = 0

    def reset(self, off=0):
        self.off = off

    def alloc(self, shape, dtype=F32):
        n = int(np.prod(shape))
        words = n if dtype == F32 else (n + 1) // 2
        words = (words + 7) // 8 * 8
        assert self.off + words <= self.n, f"arena overflow {self.off}+{words}>{self.n}"
        a = self.ap[:, self.off:self.off + words]
        self.off += words
        if dtype != F32:
            a = a.bitcast(dtype)
        a = a[:, 0:n]
        if len(shape) == 2:
            a = a.rearrange("p (a b) -> p a b", a=shape[0])
        elif len(shape) == 3:
            a = a.rearrange("p (a b c) -> p a b c", a=shape[0], b=shape[1])
        return a


def bc_last(a, n):
    return bass.AP(a.tensor, a.offset, [list(x) for x in a.ap] + [[0, n]])


def bc_mid(a, n):
    return bass.AP(a.tensor, a.offset, [list(a.ap[0]), [0, n]] + [list(x) for x in a.ap[1:]])


class K:
    def __init__(self, n_layers=4, upto=None, taps=()):
        self.n_layers = n_layers
        self.upto = upto
        self.taps = taps
        nc = bass.Bass("TRN2", target_bir_lowering=False)
        self.nc = nc
        self.P = Prog(nc)
        self.uid = 0
        di = lambda name, shape: nc.dram_tensor(name, list(shape), F32, kind="ExternalInput").ap()
        do = lambda name, shape: nc.dram_tensor(name, list(shape), F32, kind="ExternalOutput").ap()
        ds = lambda name, shape: nc.dram_tensor(name, list(shape), F32, kind="Internal").ap()
        self.xin = di("xin", [NT, D])
        self.cvec = di("cvec", [2, D])
        self.state_gdn = di("state_gdn", [2, 2, 4, 128, 128])
        self.cgk = di("cache_gqa_k", [2, 512, 2, 128])
        self.cgv = di("cache_gqa_v", [2, 512, 2, 128])
        self.state_ssd = di("state_ssd", [2, 2, 8, 64, 128])
        self.cnk = di("cache_na_k", [2, 512, 4, 128])
        self.cnv = di("cache_na_v", [2, 512, 4, 128])
        self.ada_w = di("ada_w", [4, D, 6 * D])
        self.ada_b = di("ada_b", [4, 6 * D])
        self.norms = {n: di(n, [4, D]) for n in ("norm_mix_pre", "norm_mix_post", "norm_mlp_pre", "norm_mlp_post")}
        self.mlp_w1 = di("mlp_w1", [4, D, 4 * D])
        self.mlp_w2 = di("mlp_w2", [4, 4 * D, D])
        self.ev_w_in = di("ev_w_in", [2, D, EV_IN])
        self.ev_w_out = di("ev_w_out", [2, D, D])
        self.gdn_conv = di("gdn_conv", [2, 3, 1536])
        self.gdn_a_log = di("gdn_a_log", [2, 8])
        self.gdn_dt_bias = di("gdn_dt_bias", [2, 8])
        self.gdn_norm = di("gdn_norm", [2, 128])
        self.gqa_q_norm = di("gqa_q_norm", [2, 128])
        self.gqa_k_norm = di("gqa_k_norm", [2, 128])
        self.od_w_in = di("od_w_in", [2, D, EV_IN])
        self.od_w_out = di("od_w_out", [2, D, D])
        self.ssd_conv = di("ssd_conv", [2, 3, 1024])
        self.ssd_conv_b = di("ssd_conv_b", [2, 1024])
        self.ssd_a_log = di("ssd_a_log", [2, 16])
        self.ssd_dt_bias = di("ssd_dt_bias", [2, 16])
        self.ssd_d = di("ssd_d", [2, 8])
        self.ssd_norm = di("ssd_norm", [2, 512])
        self.na_rpb = di("na_rpb", [2, 60, 31])
        self.c_ident = di("c_ident", [128, 128])
        self.c_masks = di("c_masks", [8, 64, 64])
        self.c_rope = di("c_rope", [2, 128, 4096])
        self.c_perm = di("c_perm", [128, 128])
        self.c_namask = di("c_namask", [64, 64])
        self.yout = do("yout", [NT, D])
        self.o_gdn = do("o_gdn", [2, 2, 2, 4, 128, 128])
        self.o_gk = do("o_gk", [2, 2, 256, 2, 128])
        self.o_gv = do("o_gv", [2, 2, 256, 2, 128])
        self.o_ssd = do("o_ssd", [2, 2, 2, 8, 64, 128])
        self.o_nk = do("o_nk", [2, 2, 256, 4, 128])
        self.o_nv = do("o_nv", [2, 2, 256, 4, 128])
        self.xres = ds("xres", [NT, D])
        self.modrow = ds("modrow", [4, 2, 6 * D])
        self.projT = ds("projT", [3200, NT])
        self.projN = ds("projN", [NT, EV_IN])
        self.mixN = ds("mixN", [NT, 512])
        self.mixT = ds("mixT", [512, NT])
        self.qkn = ds("qkn", [1024, NT])
        self.kvn = ds("kvn", [NT, 1024])
        self.rpbpad = ds("rpbpad", [60, 160])
        self.gb = ds("gb", [NT, 16])
        self.sb = ds("sb", [NT, 32])
        self.yf = ds("yf", [NT, 512])
        self.yb = ds("yb", [NT, 512])
        self.og0 = ds("og0", [NT, 512])
        self.og1 = ds("og1", [NT, 512])
        self.tapbufs = {}
        for (name, shape) in taps:
            self.tapbufs[name] = do("tap_" + name, shape)
        self.NW = 53000
        self.arena_t = nc.alloc_sbuf_tensor("arena", [128, self.NW], F32)
        self.A = Arena(self.arena_t.ap() if hasattr(self.arena_t, 'ap') else self.arena_t[:, :], self.NW)
        self.banks = []
        for i in range(8):
            t = nc.alloc_psum_tensor(f"bank{i}", [128, 512], F32)
            self.banks.append(t.ap() if hasattr(t, 'ap') else t[:, :])

    def bk(self, w, *aps):
        w = list(w)
        for a in aps:
            nm = getattr(getattr(a, 'tensor', None), 'name', '')
            if isinstance(nm, str) and nm.startswith('bank'):
                k = 'X' + nm
                if k not in w:
                    w.append(k)
        return w

    def mm(self, out, lhsT, rhs, start, stop, r, w):
        return self.P.op('pe', lambda e: e.matmul(out, lhsT=lhsT, rhs=rhs, start=start, stop=stop), r, self.bk(w, out))

    def tr(self, out, in_, ident, r, w):
        return self.P.op('pe', lambda e: e.transpose(out, in_, ident), r, self.bk(w, out))

    def act(self, out, in_, func, r, w, **kw):
        return self.P.op('act', lambda e: e.activation(out=out, in_=in_, func=func, **kw), r, self.bk(w, out, in_))

    def ts(self, eng, out, in0, s1, s2, op0, op1, r, w):
        w = self.bk(w, out, in0)
        if op1 is None:
            return self.P.op(eng, lambda e: e.tensor_scalar(out=out, in0=in0, scalar1=s1, scalar2=None, op0=op0), r, w)
        return self.P.op(eng, lambda e: e.tensor_scalar(out=out, in0=in0, scalar1=s1, scalar2=s2, op0=op0, op1=op1), r, w)

    def tt(self, eng, out, in0, in1, op, r, w):
        return self.P.op(eng, lambda e: e.tensor_tensor(out=out, in0=in0, in1=in1, op=op), r, self.bk(w, out, in0, in1))

    def stt(self, eng, out, in0, scalar, in1, op0, op1, r, w):
        return self.P.op(eng, lambda e: e.scalar_tensor_tensor(out=out, in0=in0, scalar=scalar, in1=in1, op0=op0, op1=op1), r,
                         self.bk(w, out, in0, in1))

    def recip(self, out, in_, r, w):
        return self.P.op('dve', lambda e: e.reciprocal(out=out, in_=in_), r, self.bk(w, out, in_))

    def cp(self, eng, out, in_, r, w):
        if eng == 'act':
            return self.act(out, in_, AF.Copy, r, w)
        return self.P.op(eng, lambda e: e.tensor_copy(out=out, in_=in_), r, self.bk(w, out, in_))

    def memset(self, eng, ap, val, w):
        return self.P.op(eng, lambda e: e.memset(ap, val), (), w)

    def key(self, name):
        self.uid += 1
        return f"{name}#{self.uid}"

    def tap(self, name, src_ap, reads):
        if name in self.tapbufs:
            self.P.dma('sp', self.tapbufs[name], src_ap, reads=reads)

    def phase_begin(self):
        self.P.barrier()
        A = self.A
        A.reset()
        self.ident = A.alloc([128])
        self.identb = A.alloc([128], BF16)
        self.P.dma('sp', self.ident, self.c_ident, writes=['ident'])
        self.cp('dve', self.identb, self.ident, ['ident'], ['identb'])
        self.ones = A.alloc([128])
        self.memset('pool', self.ones, 1.0, ['ones'])
        self.onesb = A.alloc([128], BF16)
        self.memset('pool', self.onesb, 1.0, ['onesb'])
        self.const_end = A.off

    def load_bcast(self, dst, row_ap, key, q='sp'):
        self.P.dma(q, dst, row_ap.partition_broadcast(128), writes=[key])

    def rstd_from_ss(self, rstd, ss, n, r, w):
        self.act(rstd, ss, AF.Sqrt, r, w, bias=EPS, scale=1.0 / n)
        self.P.op('dve', lambda e: e.reciprocal(out=rstd, in_=rstd), w, w)

    def phase_ada(self):
        self.phase_begin()
        A, P = self.A, self.P
        cT = A.alloc([2, 8])
        for v in range(2):
            P.dma('sp', cT[:, v, :], self.cvec[v, :].rearrange("(k p) -> p k", p=128), writes=['cT'],
                  allow_slow_non_contiguous=True)
        sc = A.alloc([2, 8])
        self.act(sc, cT, AF.Silu, ['cT'], ['sc'])
        L = A.alloc([2, 8, 128])
        for v in range(2):
            self.cp('dve', L[:, v, :, :], bc_last(sc[:, v, :], 128), ['sc'], ['L'])
        wbuf = [A.alloc([1536]) for _ in range(3)]
        bb = A.alloc([1536])
        ob = [A.alloc([1536]) for _ in range(2)]
        for l in range(self.n_layers):
            for g in range(4):
                c0 = g * 1536
                self.load_bcast(bb, self.ada_b[l, c0:c0 + 1536], 'bb', q='act')
                for k in range(8):
                    wb = wbuf[k % 3]
                    wk = f'wbuf{k % 3}'
                    P.dma('sp', wb, self.ada_w[l, k * 128:(k + 1) * 128, c0:c0 + 1536], writes=[wk])
                    for v in range(2):
                        for n in range(3):
                            self.mm(self.banks[v * 3 + n], L[:, v, k, :], wb[:, n * 512:(n + 1) * 512],
                                    k == 0, k == 7, [wk, 'L'], [f'bank{v * 3 + n}'])
                for v in range(2):
                    for n in range(3):
                        self.tt('dve', ob[v][:, n * 512:(n + 1) * 512], self.banks[v * 3 + n], bb[:, n * 512:(n + 1) * 512],
                                ALU.add, [f'bank{v * 3 + n}', 'bb'], [f'ob{v}'])
                    P.dma('act', self.modrow[l, v:v + 1, c0:c0 + 1536], ob[v][0:1, :], reads=[f'ob{v}'])

    def mod_vec(self, dst, l, v, idx, key, plus1=False):
        self.load_bcast(dst, self.modrow[l, v, idx * D:(idx + 1) * D], key, q='act')
        if plus1:
            self.ts('dve', dst, dst, 1.0, None, ALU.add, None, [key], [key])

    def phase_A(self, l, xsrc):
        even = (l % 2 == 0)
        j = l // 2
        w_in = self.ev_w_in[j] if even else self.od_w_in[j]
        if even:
            fm = [(c, 128) for c in range(0, 1024, 128)] + [(c, 128) for c in range(2064, 2832, 128)]
            tm = [(512, 512), (1024, 512), (1536, 512), (2048, 16), (2832, 256)]
        else:
            fm = [(c, 128) for c in range(1024, 1536, 128)] + [(c, 128) for c in range(1552, 2576, 128)]
            tm = [(0, 512), (512, 512), (1024, 256), (1536, 16), (2576, 512)]
        self.phase_begin()
        A, P = self.A, self.P
        W = A.alloc([8, EV_IN], BF16)
        for k in range(8):
            P.dma('pool', W[:, k, :], w_in[k * 128:(k + 1) * 128, :], writes=[f'W{k}'])
        Wk = [f'W{k}' for k in range(8)]
        tmpv = A.alloc([D])
        self.load_bcast(tmpv, self.norms["norm_mix_pre"][l, :], 'tmpv')
        A1 = []
        SH = []
        for v in range(2):
            a1 = A.alloc([D])
            sh = A.alloc([D])
            self.mod_vec(sh, l, v, 0, f'sh{v}')
            self.mod_vec(a1, l, v, 1, f'a1{v}', plus1=True)
            self.tt('dve', a1, a1, tmpv, ALU.mult, [f'a1{v}', 'tmpv'], [f'a1{v}'])
            A1.append(a1)
            SH.append(sh)
        xt = [A.alloc([D]) for _ in range(2)]
        junk = A.alloc([D], BF16)
        tmpf = A.alloc([D])
        hm = [A.alloc([D], BF16) for _ in range(2)]
        st = A.alloc([4])
        hmT = [A.alloc([8, 512], BF16) for _ in range(2)]
        stage = [A.alloc([512]) for _ in range(4)]
        ptr = [self.banks[0].bitcast(BF16), self.banks[1].bitcast(BF16)]
        nst = 0
        nps = 0
        for g in range(9):
            v = 0 if g == 0 else 1
            hT = hmT[g % 2]
            hk = f'hmT{g % 2}'
            for ti in range(4):
                t = g * 4 + ti
                s = t % 2
                P.dma('sp', xt[s], xsrc[t * 128:(t + 1) * 128, :], writes=[f'xt{s}'])
                self.act(junk, xt[s], AF.Square, [f'xt{s}'], ['junk', f'ss{s}'], accum_out=st[:, s:s + 1])
                self.rstd_from_ss(st[:, 2 + s:3 + s], st[:, s:s + 1], D, [f'ss{s}'], [f'rs{s}'])
                self.stt('dve', tmpf, xt[s], st[:, 2 + s:3 + s], A1[v], ALU.mult, ALU.mult,
                         [f'xt{s}', f'rs{s}', f'a1{v}'], ['tmpf'])
                self.tt('dve', hm[s], tmpf, SH[v], ALU.add, ['tmpf', f'sh{v}'], [f'hm{s}'])
                for k in range(8):
                    self.tr(ptr[s][:, k * 128:(k + 1) * 128], hm[s][:, k * 128:(k + 1) * 128], self.identb,
                            [f'hm{s}', 'identb'], [f'ptr{s}'])
                self.cp('act', hT[:, :, ti * 128:(ti + 1) * 128], ptr[s].rearrange("p (k t) -> p k t", k=8),
                        [f'ptr{s}'], [hk])
            for (c0, wd) in fm:
                b = 2 + nps % 3
                nps += 1
                for k in range(8):
                    self.mm(self.banks[b][0:wd, :], W[:, k, c0:c0 + wd], hT[:, k, :], k == 0, k == 7,
                            [Wk[k], hk], [f'bank{b}'])
                sg = nst % 4
                nst += 1
                self.cp('act' if nst % 2 else 'dve', stage[sg][0:wd, :], self.banks[b][0:wd, :], [f'bank{b}'], [f'stage{sg}'])
                P.dma('sp', self.projT[c0:c0 + wd, g * 512:(g + 1) * 512], stage[sg][0:wd, :], reads=[f'stage{sg}'])
            for ti in range(4):
                t = g * 4 + ti
                for (c0, wd) in tm:
                    b = 5 + nps % 3
                    nps += 1
                    for k in range(8):
                        self.mm(self.banks[b][:, 0:wd], hT[:, k, ti * 128:(ti + 1) * 128], W[:, k, c0:c0 + wd],
                                k == 0, k == 7, [Wk[k], hk], [f'bank{b}'])
                    sg = nst % 4
                    nst += 1
                    self.cp('act' if nst % 2 else 'dve', stage[sg][:, 0:wd], self.banks[b][:, 0:wd], [f'bank{b}'], [f'stage{sg}'])
                    P.dma('sp', self.projN[t * 128:(t + 1) * 128, c0:c0 + wd], stage[sg][:, 0:wd], reads=[f'stage{sg}'])

    def phase_C0(self, l, xsrc):
        even = (l % 2 == 0)
        j = l // 2
        w_out = self.ev_w_out[j] if even else self.od_w_out[j]
        self.phase_begin()
        A, P = self.A, self.P
        W = A.alloc([8, D], BF16)
        for k in range(8):
            P.dma('pool', W[:, k, :], w_out[k * 128:(k + 1) * 128, :], writes=[f'W{k}'])
        tmpv = A.alloc([D])
        self.load_bcast(tmpv, self.norms["norm_mix_post"][l, :], 'tmpv')
        G1 = []
        for v in range(2):
            g1 = A.alloc([D])
            self.mod_vec(g1, l, v, 2, f'g1{v}')
            self.tt('dve', g1, g1, tmpv, ALU.mult, [f'g1{v}', 'tmpv'], [f'g1{v}'])
            G1.append(g1)
        xt = [A.alloc([D]) for _ in range(2)]
        mn = [A.alloc([512], BF16) for _ in range(2)]
        mT = [A.alloc([4, 512], BF16) for _ in range(2)]
        mTn = [A.alloc([4, 128], BF16) for _ in range(2)]
        junk = A.alloc([D], BF16)
        tmpf = A.alloc([D])
        st = A.alloc([8])
        ptr = [self.banks[0].bitcast(BF16), self.banks[1].bitcast(BF16)]
        for g in range(9):
            v = 0 if g == 0 else 1
            gs = g % 2
            for k in range(4):
                P.dma('pool', mT[gs][:, k, :], self.mixT[k * 128:(k + 1) * 128, g * 512:(g + 1) * 512], writes=[f'mT{gs}'])
            for ti in range(4):
                t = g * 4 + ti
                s = t % 2
                P.dma('sp', xt[s], xsrc[t * 128:(t + 1) * 128, :], writes=[f'xt{s}'])
                P.dma('pool', mn[s], self.mixN[t * 128:(t + 1) * 128, :], writes=[f'mn{s}'])
                for k in range(4):
                    self.tr(ptr[s][:, k * 128:(k + 1) * 128], mn[s][:, k * 128:(k + 1) * 128], self.identb,
                            [f'mn{s}', 'identb'], [f'ptr{s}'])
                self.cp('act', mTn[s], ptr[s][:, 0:512].rearrange("p (k t) -> p k t", k=4), [f'ptr{s}'], [f'mTn{s}'])
                for h in range(2):
                    b = 2 + 2 * s + h
                    for k in range(8):
                        lhs = mTn[s][:, k, :] if k < 4 else mT[gs][:, k - 4, ti * 128:(ti + 1) * 128]
                        self.mm(self.banks[b], lhs, W[:, k, h * 512:(h + 1) * 512], k == 0, k == 7,
                                [f'W{k}', f'mTn{s}', f'mT{gs}'], [f'bank{b}'])
                    self.act(junk[:, 0:512], self.banks[b], AF.Square, [f'bank{b}'], ['junk', f'ss{s}{h}'],
                             accum_out=st[:, 2 * s + h:2 * s + h + 1])
                self.tt('dve', st[:, 4 + s:5 + s], st[:, 2 * s:2 * s + 1], st[:, 2 * s + 1:2 * s + 2], ALU.add,
                        [f'ss{s}0', f'ss{s}1'], [f'sst{s}'])
                self.rstd_from_ss(st[:, 6 + s:7 + s], st[:, 4 + s:5 + s], D, [f'sst{s}'], [f'rs{s}'])
                for h in range(2):
                    b = 2 + 2 * s + h
                    self.stt('dve', tmpf[:, h * 512:(h + 1) * 512], self.banks[b], st[:, 6 + s:7 + s],
                             G1[v][:, h * 512:(h + 1) * 512], ALU.mult, ALU.mult, [f'bank{b}', f'rs{s}', f'g1{v}'], ['tmpf'])
                self.tt('pool', xt[s], xt[s], tmpf, ALU.add, [f'xt{s}', 'tmpf'], [f'xt{s}'])
                P.dma('sp', self.xres[t * 128:(t + 1) * 128, :], xt[s], reads=[f'xt{s}'])

    def phase_C1(self, l, xdst):
        self.phase_begin()
        A, P = self.A, self.P
        W1 = A.alloc([8, 4 * D], BF16)
        W2 = A.alloc([32, D], BF16)
        for k in range(8):
            P.dma('pool', W1[:, k, :], self.mlp_w1[l, k * 128:(k + 1) * 128, :], writes=[f'W1{k}'])
        for k in range(32):
            P.dma('pool', W2[:, k, :], self.mlp_w2[l, k * 128:(k + 1) * 128, :], writes=[f'W2{k}'])
        W1k = [f'W1{k}' for k in range(8)]
        cur = {}
        cur['sh'] = A.alloc([D])
        cur['a2'] = A.alloc([D])
        cur['g2'] = A.alloc([D])
        xt = [A.alloc([D]) for _ in range(2)]
        junk = A.alloc([D], BF16)
        tmpf = A.alloc([D])
        hm = [A.alloc([D], BF16) for _ in range(2)]
        st = A.alloc([16])
        hmT = A.alloc([8, 512], BF16)
        h1T = A.alloc([32, 512], BF16)
        r1 = [A.alloc([512]) for _ in range(2)]
        ptr = [self.banks[0].bitcast(BF16), self.banks[1].bitcast(BF16)]

        def load_vecs(v):
            if cur.get('v') == v:
                return
            cur['v'] = v
            self.mod_vec(cur['sh'], l, v, 3, 'sh2')
            self.mod_vec(cur['a2'], l, v, 4, 'a2', plus1=True)
            self.load_bcast(tmpf, self.norms["norm_mlp_pre"][l, :], 'tmpf')
            self.tt('dve', cur['a2'], cur['a2'], tmpf, ALU.mult, ['a2', 'tmpf'], ['a2'])
            self.mod_vec(cur['g2'], l, v, 5, 'g2')
            self.load_bcast(tmpf, self.norms["norm_mlp_post"][l, :], 'tmpf')
            self.tt('dve', cur['g2'], cur['g2'], tmpf, ALU.mult, ['g2', 'tmpf'], ['g2'])

        nx = 0
        for g in range(9):
            v = 0 if g == 0 else 1
            load_vecs(v)
            for ti in range(4):
                t = g * 4 + ti
                s = nx % 2
                nx += 1
                P.dma('sp', xt[s], self.xres[t * 128:(t + 1) * 128, :], writes=[f'xt{s}'])
                self.act(junk, xt[s], AF.Square, [f'xt{s}'], ['junk', f'ss{s}'], accum_out=st[:, s:s + 1])
                self.rstd_from_ss(st[:, 2 + s:3 + s], st[:, s:s + 1], D, [f'ss{s}'], [f'rs{s}'])
                self.stt('dve', tmpf, xt[s], st[:, 2 + s:3 + s], cur['a2'], ALU.mult, ALU.mult,
                         [f'xt{s}', f'rs{s}', 'a2'], ['tmpf'])
                self.tt('dve', hm[s], tmpf, cur['sh'], ALU.add, ['tmpf', 'sh2'], [f'hm{s}'])
                for k in range(8):
                    self.tr(ptr[s][:, k * 128:(k + 1) * 128], hm[s][:, k * 128:(k + 1) * 128], self.identb,
                            [f'hm{s}', 'identb'], [f'ptr{s}'])
                self.cp('act', hmT[:, :, ti * 128:(ti + 1) * 128], ptr[s].rearrange("p (k t) -> p k t", k=8),
                        [f'ptr{s}'], ['hmT'])
            for f in range(32):
                b = 2 + f % 2
                for k in range(8):
                    self.mm(self.banks[b], W1[:, k, f * 128:(f + 1) * 128], hmT[:, k, :], k == 0, k == 7,
                            [W1k[k], 'hmT'], [f'bank{b}'])
                rs = f % 2
                self.act(r1[rs], self.banks[b], AF.Relu, [f'bank{b}'], [f'r1{rs}'])
                self.tt('dve' if f % 4 < 3 else 'pool', h1T[:, f, :], r1[rs], r1[rs], ALU.mult, [f'r1{rs}'], [f'h1T{f}'])
            for ti in range(4):
                t = g * 4 + ti
                s = nx % 2
                nx += 1
                P.dma('sp', xt[s], self.xres[t * 128:(t + 1) * 128, :], writes=[f'xt{s}'])
                for h in range(2):
                    b = 4 + 2 * s + h
                    for f in range(32):
                        self.mm(self.banks[b], h1T[:, f, ti * 128:(ti + 1) * 128], W2[:, f, h * 512:(h + 1) * 512],
                                f == 0, f == 31, [f'W2{f}', f'h1T{f}'], [f'bank{b}'])
                    self.act(junk[:, 0:512], self.banks[b], AF.Square, [f'bank{b}'], ['junk', f'q{s}{h}'],
                             accum_out=st[:, 4 + 2 * s + h:5 + 2 * s + h])
                self.tt('dve', st[:, 8 + s:9 + s], st[:, 4 + 2 * s:5 + 2 * s], st[:, 5 + 2 * s:6 + 2 * s], ALU.add,
                        [f'q{s}0', f'q{s}1'], [f'qt{s}'])
                self.rstd_from_ss(st[:, 10 + s:11 + s], st[:, 8 + s:9 + s], D, [f'qt{s}'], [f'qr{s}'])
                for h in range(2):
                    b = 4 + 2 * s + h
                    self.stt('dve', tmpf[:, h * 512:(h + 1) * 512], self.banks[b], st[:, 10 + s:11 + s],
                             cur['g2'][:, h * 512:(h + 1) * 512], ALU.mult, ALU.mult, [f'bank{b}', f'qr{s}', 'g2'], ['tmpf'])
                self.tt('pool', xt[s], xt[s], tmpf, ALU.add, [f'xt{s}', 'tmpf'], [f'xt{s}'])
                P.dma('sp', xdst[t * 128:(t + 1) * 128, :], xt[s], reads=[f'xt{s}'])

    def build(self):
        P = self.P
        self.phase_ada()
        if self.upto == 'ada':
            return self.finish()
        if self.upto in ('T1', 'S1'):
            self.phase_A(1, self.xin)
            self.phase_odd_mix(1)
            return self.finish()
        for l in range(self.n_layers):
            xsrc = self.xin if l == 0 else self.xres
            self.phase_A(l, xsrc)
            if self.upto == f'A{l}':
                return self.finish()
            if l % 2 == 0:
                self.phase_even_mix(l)
            else:
                self.phase_odd_mix(l)
            if self.upto in (f'M{l}', 'G1', 'G2', 'G3', 'S1'):
                return self.finish()
            self.phase_C0(l, xsrc)
            last = (l == self.n_layers - 1)
            self.phase_C1(l, self.yout if last else self.xres)
        return self.finish()

    def finish(self):
        P = self.P
        P.barrier()
        for name, buf in self.tapbufs.items():
            src = getattr(self, name)
            P.dma('sp', buf, src)
        P.barrier()
        P.emit()
        return self.nc

    def phase_even_mix(self, l):
        raise NotImplementedError

    def phase_odd_mix(self, l):
        raise NotImplementedError


def host_consts():
    ident = np.eye(128, dtype=np.float32)
    t = np.arange(64)
    masks = np.zeros((8, 64, 64), np.float32)
    masks[0] = (t[:, None] <= t[None, :])
    masks[1] = (t[:, None] > t[None, :])
    masks[2] = (t[:, None] >= t[None, :])
    masks[3] = (t[:, None] < t[None, :])
    masks[4] = ((t[:, None] // 32) == (t[None, :] // 32))
    masks[5] = 1.0 - masks[4]
    half = 64
    inv_freq = (10000.0 ** (-np.arange(0, half, 2, dtype=np.float32) / half)).astype(np.float32)
    tok = np.arange(4096)
    row = (tok // 64).astype(np.float32)
    col = (tok % 64).astype(np.float32)
    ang = np.concatenate([row[None, :] * inv_freq[:, None], row[None, :] * inv_freq[:, None],
                          col[None, :] * inv_freq[:, None], col[None, :] * inv_freq[:, None]], 0)
    C = np.cos(ang).astype(np.float32)
    S = np.sin(ang).astype(np.float32)
    sign = np.ones((128, 1), np.float32)
    sign[0:32] = -1.0
    sign[64:96] = -1.0
    rope = np.stack([C, S * sign]).astype(np.float32)
    perm = np.zeros((128, 128), np.float32)
    for p in range(128):
        blk = p // 64 * 64
        q = p - blk
        partner = blk + (q + 32) % 64
        perm[partner, p] = 1.0
    q = np.arange(64)
    kc = np.arange(64)
    cs = np.clip(q - 8, 0, 48)
    namask = ((kc[None, :] >= cs[:, None]) & (kc[None, :] < cs[:, None] + 16)).astype(np.float32)
    return dict(c_ident=ident, c_masks=masks, c_rope=rope, c_perm=perm, c_namask=namask)


def make_in_maps(inp):
    f = lambda a: np.ascontiguousarray(a, dtype=np.float32)
    consts = host_consts()
    shared = {}
    for n in ("ada_w", "ada_b", "norm_mix_pre", "norm_mix_post", "norm_mlp_pre", "norm_mlp_post", "mlp_w1", "mlp_w2",
              "ev_w_in", "ev_w_out", "gdn_conv", "gdn_norm", "gqa_q_norm", "gqa_k_norm", "od_w_in", "od_w_out",
              "ssd_conv", "ssd_conv_b", "ssd_d", "ssd_norm"):
        shared[n] = f(inp[n])
    shared["gdn_a_log"] = f(inp["gdn_a_log"]).reshape(2, 8)
    shared["gdn_dt_bias"] = f(inp["gdn_dt_bias"]).reshape(2, 8)
    shared["ssd_a_log"] = f(inp["ssd_a_log"]).reshape(2, 16)
    shared["ssd_dt_bias"] = f(inp["ssd_dt_bias"]).reshape(2, 16)
    shared["na_rpb"] = f(inp["na_rpb"]).reshape(2, 60, 31)
    shared.update(consts)
    maps = []
    for c in range(8):
        m = dict(shared)
        m["xin"] = f(np.concatenate([inp["x_prompt"][2 * c], inp["x_prompt"][2 * c + 1], inp["x_sample"][c]], 0))
        m["cvec"] = f(np.stack([inp["c_ctx"], inp["c"][c]], 0))
        m["state_gdn"] = f(inp["state_gdn"][c])
        m["cache_gqa_k"] = f(inp["cache_gqa_k"][c])
        m["cache_gqa_v"] = f(inp["cache_gqa_v"][c])
        m["state_ssd"] = f(inp["state_ssd"][c])
        m["cache_na_k"] = f(inp["cache_na_k"][c])
        m["cache_na_v"] = f(inp["cache_na_v"][c])
        maps.append(m)
    return maps


def _prep_qk(self, x, T, wcol, rope, out, tmp, rs, tables, pos0, kx, ko, bank):
    step = 512
    for c0 in range(0, T, step):
        w = min(step, T - c0)
        xs = x[:, c0:c0 + w]
        if wcol is not None:
            self.act(tmp[:, 0:w], xs, AF.Square, [kx], ['pq_tmp'])
            self.mm(self.banks[bank][:, 0:w], self.ones, tmp[:, 0:w], True, True, ['ones', 'pq_tmp'], [f'bank{bank}'])
            self.act(rs[:, 0:w], self.banks[bank][:, 0:w], AF.Sqrt, [f'bank{bank}'], ['pq_rs'], bias=EPS, scale=1.0 / 128)
            self.P.op('dve', lambda e, a=rs[:, 0:w]: e.reciprocal(out=a, in_=a), ['pq_rs'], ['pq_rs'])
            self.stt('dve', xs, xs, wcol, rs[:, 0:w], ALU.mult, ALU.mult, [kx, 'pq_rs', 'pq_w'], [kx])
        if rope:
            C, S, perm = tables
            self.mm(self.banks[bank][:, 0:w], perm, xs, True, True, ['perm', kx], [f'bank{bank}'])
            self.tt('dve', tmp[:, 0:w], self.banks[bank][:, 0:w], S[:, pos0 + c0:pos0 + c0 + w], ALU.mult,
                    [f'bank{bank}', 'ropeS'], ['pq_tmp'])
            self.tt('pool', rs[:, 0:w], xs, C[:, pos0 + c0:pos0 + c0 + w], ALU.mult, [kx, 'ropeC'], ['pq_rs'])
            self.tt('dve', out[:, c0:c0 + w], tmp[:, 0:w], rs[:, 0:w], ALU.add, ['pq_tmp', 'pq_rs'], [ko])
        else:
            self.cp('dve', out[:, c0:c0 + w], xs, [kx], [ko])


def _attn_dense(self, l, qrow0, krow0, vcol0, nq, nkv, sample_ctx, norm, rope, kout, vout, only_prompts=False):
    j = l // 2
    A, P = self.A, self.P
    rep = nq // nkv
    scale = 128 ** -0.5
    TKM = 4096 + 512
    kTb = A.alloc([TKM], BF16)
    Vb = A.alloc([36, 128], BF16)
    qTb = A.alloc([4096], BF16)
    xk = A.alloc([4096])
    xq = A.alloc([4096])
    tmp = A.alloc([512])
    rs = A.alloc([512])
    pt = [A.alloc([512], BF16) for _ in range(3)]
    rden = A.alloc([512])
    osb = [A.alloc([512]) for _ in range(2)]
    ctk = A.alloc([4, 128])
    kst = A.alloc([2, 128])
    tables = None
    if rope:
        C = A.alloc([4096])
        S = A.alloc([4096])
        perm = A.alloc([128])
        P.dma('sp', C, self.c_rope[0], writes=['ropeC'])
        P.dma('sp', S, self.c_rope[1], writes=['ropeS'])
        P.dma('sp', perm, self.c_perm, writes=['perm'])
        tables = (C, S, perm)
    wq = wk = None
    if norm is not None:
        wq = A.alloc([1])
        wk = A.alloc([1])
        P.dma('sp', wq, norm[0].rearrange("(p o) -> p o", o=1), writes=['pq_w'])
        P.dma('sp', wk, norm[1].rearrange("(p o) -> p o", o=1), writes=['pq_w'])
    no = 0
    npt = 0
    for si, (t0, T, is_s) in enumerate(SEQS):
        if only_prompts and is_s:
            continue
        Tk = T + (512 if (is_s and sample_ctx is not None) else 0)
        nkc = Tk // 128
        for g in range(nkv):
            P.dma('sp', xk[:, 0:T], self.projT[krow0 + g * 128:krow0 + (g + 1) * 128, t0:t0 + T], writes=['xk'])
            _prep_qk(self, xk, T, wk, rope and is_s, kTb, tmp, rs, tables, 0, 'xk', 'kTb', 6)
            if not is_s and kout is not None:
                for a in range(T // 128):
                    self.tr(self.banks[7][:, 0:128], xk[:, a * 128:(a + 1) * 128], self.ident, ['xk', 'ident'], ['bank7'])
                    self.cp('act', kst[:, a, :], self.banks[7][:, 0:128], ['bank7'], ['kst'])
                P.dma('sp', kout[si, j, :, g, :].rearrange("(a p) d -> p a d", p=128), kst, reads=['kst'])
            if Tk > T:
                ck, cv = sample_ctx
                P.dma('sp', ctk, ck[j, :, g, :].rearrange("(a p) d -> p a d", p=128), writes=['ctk'])
                for a in range(4):
                    self.tr(self.banks[7][:, 0:128], ctk[:, a, :], self.ident, ['ctk', 'ident'], ['bank7'])
                    self.cp('act', kTb[:, T + a * 128:T + (a + 1) * 128], self.banks[7][:, 0:128], ['bank7'], ['kTb'])
                P.dma('pool', Vb[:, T // 128:T // 128 + 4, :], cv[j, :, g, :].rearrange("(a p) d -> p a d", p=128), writes=['Vb'])
            P.dma('pool', Vb[:, 0:T // 128, :],
                  self.projN[t0:t0 + T, vcol0 + g * 128:vcol0 + (g + 1) * 128].rearrange("(a p) d -> p a d", p=128), writes=['Vb'])
            if not is_s and vout is not None:
                P.dma('act', vout[si, j, :, g, :], self.projN[t0:t0 + T, vcol0 + g * 128:vcol0 + (g + 1) * 128])
            for r_ in range(rep):
                h = g * rep + r_
                P.dma('sp', xq[:, 0:T], self.projT[qrow0 + h * 128:qrow0 + (h + 1) * 128, t0:t0 + T], writes=['xq'])
                _prep_qk(self, xq, T, wq, rope and is_s, qTb, tmp, rs, tables, 0, 'xq', 'qTb', 6)
                QW = min(512, T)
                for qc in range(T // QW):
                    q0 = qc * QW
                    bo = 2 + 2 * (no % 2)
                    def smm(kc_, bs_):
                        self.mm(self.banks[bs_][:, 0:QW], kTb[:, kc_ * 128:(kc_ + 1) * 128], qTb[:, q0:q0 + QW], True, True,
                                ['kTb', 'qTb'], [f'bank{bs_}'])
                    smm(0, npt % 2)
                    for kc in range(nkc):
                        bs = npt % 2
                        p_ = pt[npt % 3]
                        pk = f'pt{npt % 3}'
                        npt += 1
                        if kc + 1 < nkc:
                            smm(kc + 1, npt % 2)
                        self.act(p_[:, 0:QW], self.banks[bs][:, 0:QW], AF.Exp, [f'bank{bs}'], [pk], scale=scale)
                        self.mm(self.banks[bo][:, 0:QW], Vb[:, kc, :], p_[:, 0:QW], kc == 0, kc == nkc - 1, ['Vb', pk], [f'bank{bo}'])
                        self.mm(self.banks[bo + 1][:, 0:QW], self.onesb, p_[:, 0:QW], kc == 0, kc == nkc - 1, ['onesb', pk], [f'bank{bo + 1}'])
                    self.recip(rden[:, 0:QW], self.banks[bo + 1][:, 0:QW], [f'bank{bo + 1}'], ['rden'])
                    ob = osb[no % 2]
                    okk = f'osb{no % 2}'
                    no += 1
                    self.tt('dve', ob[:, 0:QW], self.banks[bo][:, 0:QW], rden[:, 0:QW], ALU.mult, [f'bank{bo}', 'rden'], [okk])
                    P.dma('sp', self.mixT[h * 128:(h + 1) * 128, t0 + q0:t0 + q0 + QW], ob[:, 0:QW], reads=[okk])


K._attn_dense = _attn_dense


def _gdn(self, l):
    j = l // 2
    A, P = self.A, self.P
    NB = 4
    wb = [A.alloc([1024]) for _ in range(3)]
    for k in range(3):
        self.load_bcast(wb[k], self.gdn_conv[j, k, 512:1536], f'wb{k}')
    xm = [A.alloc([1024]) for _ in range(2)]
    x0 = [A.alloc([1024]) for _ in range(2)]
    xp = [A.alloc([1024]) for _ in range(2)]
    sq = A.alloc([512])
    ss4 = A.alloc([4])
    nt = 0
    for (t0, T, is_s) in SEQS:
        for a in range(T // 128):
            s = nt % 2
            nt += 1
            r0 = t0 + a * 128
            first, last = (a == 0), (a == T // 128 - 1)
            src = self.projN
            P.dma('sp', x0[s], src[r0:r0 + 128, 512:1536], writes=[f'x0{s}'])
            if first:
                self.memset('pool', xm[s], 0.0, [f'xm{s}'])
                P.dma('act', xm[s][1:128, :], src[r0:r0 + 127, 512:1536], writes=[f'xm{s}'])
            else:
                P.dma('act', xm[s], src[r0 - 1:r0 + 127, 512:1536], writes=[f'xm{s}'])
            if last:
                self.memset('pool', xp[s], 0.0, [f'xp{s}'])
                P.dma('act', xp[s][0:127, :], src[r0 + 1:r0 + 128, 512:1536], writes=[f'xp{s}'])
            else:
                P.dma('act', xp[s], src[r0 + 1:r0 + 129, 512:1536], writes=[f'xp{s}'])
            self.tt('dve', x0[s], x0[s], wb[1], ALU.mult, [f'x0{s}', 'wb1'], [f'x0{s}'])
            self.tt('pool', xm[s], xm[s], wb[0], ALU.mult, [f'xm{s}', 'wb0'], [f'xm{s}'])
            self.tt('pool', xp[s], xp[s], wb[2], ALU.mult, [f'xp{s}', 'wb2'], [f'xp{s}'])
            self.tt('dve', x0[s], x0[s], xm[s], ALU.add, [f'x0{s}', f'xm{s}'], [f'x0{s}'])
            self.tt('dve', x0[s], x0[s], xp[s], ALU.add, [f'x0{s}', f'xp{s}'], [f'x0{s}'])
            self.act(x0[s], x0[s], AF.Silu, [f'x0{s}'], [f'x0{s}'])
            self.tt('dve', sq, x0[s][:, 0:512], x0[s][:, 0:512], ALU.mult, [f'x0{s}'], ['sq'])
            self.P.op('dve', lambda e, o=ss4, i=sq.rearrange("p (h d) -> p h d", h=4): e.tensor_reduce(out=o, in_=i, axis=AX.X, op=ALU.add),
                      ['sq'], ['ss4'])
            self.act(ss4, ss4, AF.Sqrt, ['ss4'], ['ss4'], bias=EPS, scale=1.0)
            self.P.op('dve', lambda e, o=ss4: e.reciprocal(out=o, in_=o), ['ss4'], ['ss4'])
            kv = x0[s][:, 0:512].rearrange("p (h d) -> p h d", h=4)
            self.tt('dve', kv, kv, bc_last(ss4, 128), ALU.mult, [f'x0{s}', 'ss4'], [f'x0{s}'])
            P.dma('sp', self.kvn[r0:r0 + 128, :], x0[s], reads=[f'x0{s}'])
    raw = A.alloc([36, 16])
    for a in range(36):
        P.dma('sp', raw[:, a, :], self.projN[a * 128:(a + 1) * 128, 2048:2064], writes=['raw'])
    dtb = A.alloc([8])
    nea = A.alloc([8])
    self.load_bcast(dtb, self.gdn_dt_bias[j, :], 'dtb')
    self.load_bcast(nea, self.gdn_a_log[j, :], 'nea')
    self.act(nea, nea, AF.Exp, ['nea'], ['nea'])
    gbt = A.alloc([36, 16])
    self.act(gbt[:, :, 0:8], raw[:, :, 0:8], AF.Sigmoid, ['raw'], ['gbt'])
    self.tt('dve', raw[:, :, 8:16], raw[:, :, 8:16], bc_mid(dtb, 36), ALU.add, ['raw', 'dtb'], ['raw'])
    self.act(raw[:, :, 8:16], raw[:, :, 8:16], AF.Exp, ['raw'], ['raw'])
    self.act(raw[:, :, 8:16], raw[:, :, 8:16], AF.Ln, ['raw'], ['raw'], bias=1.0)
    self.tt('dve', raw[:, :, 8:16], raw[:, :, 8:16], bc_mid(nea, 36), ALU.mult, ['raw', 'nea'], ['raw'])
    self.ts('dve', gbt[:, :, 8:16], raw[:, :, 8:16], -1.0, None, ALU.mult, None, ['raw'], ['gbt'])
    for a in range(36):
        P.dma('sp', self.gb[a * 128:(a + 1) * 128, :], gbt[:, a, :], reads=['gbt'])
    if self.upto == 'G1':
        return
    P.barrier()
    A.reset(self.const_end)
    xf = A.alloc([NT])
    acc = A.alloc([NT])
    cw = A.alloc([3])
    tq = A.alloc([512])
    rq = A.alloc([512])
    for c in range(8):
        P.dma('sp', xf, self.projT[c * 128:(c + 1) * 128, :], writes=['xf'])
        P.dma('sp', cw, self.gdn_conv[j, :, c * 128:(c + 1) * 128].rearrange("k p -> p k"), writes=['cw'], allow_slow_non_contiguous=True)
        self.ts('dve', acc, xf, cw[:, 1:2], None, ALU.mult, None, ['xf', 'cw'], ['acc'])
        for (t0, T, is_s) in SEQS:
            self.stt('dve', acc[:, t0 + 1:t0 + T], xf[:, t0:t0 + T - 1], cw[:, 0:1], acc[:, t0 + 1:t0 + T], ALU.mult, ALU.add,
                     ['xf', 'cw', 'acc'], ['acc'])
            self.stt('dve', acc[:, t0:t0 + T - 1], xf[:, t0 + 1:t0 + T], cw[:, 2:3], acc[:, t0:t0 + T - 1], ALU.mult, ALU.add,
                     ['xf', 'cw', 'acc'], ['acc'])
        self.act(acc, acc, AF.Silu, ['acc'], ['acc'])
        for g in range(9):
            sl = acc[:, g * 512:(g + 1) * 512]
            self.act(tq, sl, AF.Square, ['acc'], ['tq'])
            self.mm(self.banks[0], self.ones, tq, True, True, ['ones', 'tq'], ['bank0'])
            self.act(rq, self.banks[0], AF.Sqrt, ['bank0'], ['rq'], bias=EPS, scale=1.0)
            self.P.op('dve', lambda e, o=rq: e.reciprocal(out=o, in_=o), ['rq'], ['rq'])
            self.stt('dve', sl, sl, (128 ** -0.5) if c < 4 else 1.0, rq, ALU.mult, ALU.mult, ['acc', 'rq'], ['acc'])
        P.dma('sp', self.qkn[c * 128:(c + 1) * 128, :], acc, reads=['acc'])
    if self.upto == 'G2':
        return
    P.barrier()
    A.reset(self.const_end)
    msk = A.alloc([6, 64])
    P.dma('sp', msk[0:64], self.c_masks[0:6].rearrange("m t i -> t m i"), writes=['msk'])
    m_bd, m_off = msk[0:64, 4], msk[0:64, 5]
    qTs = [A.alloc([4096]) for _ in range(2)]
    kTs = [A.alloc([4096]) for _ in range(2)]
    gbc = A.alloc([64, 16])
    X = []
    for ci_ in range(4):
        x = {}
        for nm in ('gcol', 'bcol', 'gc', 'eg', 'edec', 'beg', 'gtb'):
            x[nm] = A.alloc([64])
        x['S'] = A.alloc([128])
        for nm in ('Gm', 'Em', 'ETm', 'A0', 'A1', 'Q', 'Pm', 'qkT', 'wT'):
            x[nm] = A.alloc([NB, 64])
        x['BQ0'] = A.alloc([NB, 128])
        x['BQ1'] = A.alloc([NB, 128])
        x['Ao'] = x['Em']
        x['Yb'] = x['Gm']
        for nm in ('vb', 'kbg', 'kdec', 'u', 'ktb', 'vtb'):
            x[nm] = A.alloc([NB, 128])
        for nm in ('vnew', 'tqs', 'tmp2'):
            x[nm] = A.alloc([128])
        X.append(x)
    og = [self.og0, self.og1]
    idb = self.ident[0:64, 0:64]
    v3 = lambda bk, off=0: bk[0:64, off:off + NB * 64].rearrange("p (c i) -> p c i", c=NB)

    def chain(ci_, hs, h, d, si, t0, T, is_s):
        x = X[ci_]
        K_ = lambda n: f"{ {'Ao': 'Em', 'Yb': 'Gm'}.get(n, n) }_{ci_}"
        qT, kT, kqT, kkT = qTs[hs], kTs[hs], f'qT{hs}', f'kT{hs}'
        ktb, vtb = x['ktb'], x['vtb']
        dq = 'sp' if ci_ % 2 == 0 else 'act'
        nch = T // 64
        ci = d * 4 + h
        B0 = B2 = self.banks[2 * ci_]
        B1 = B3 = self.banks[2 * ci_ + 1]
        k0 = k2 = f'bk{2 * ci_}'
        k1 = k3 = f'bk{2 * ci_ + 1}'
        m_cum, m_oth, m_strict, m_incl = (msk[0:64, 0], msk[0:64, 1], msk[0:64, 1], msk[0:64, 0]) if d == 0 else \
                                         (msk[0:64, 2], msk[0:64, 3], msk[0:64, 3], msk[0:64, 2])
        gcol, bcol, gc, eg, edec, beg, gtb, S = (x[n] for n in ('gcol', 'bcol', 'gc', 'eg', 'edec', 'beg', 'gtb', 'S'))
        Gm, Em, ETm, Q, Pm, Ao, Yb, qkT, wT = (x[n] for n in ('Gm', 'Em', 'ETm', 'Q', 'Pm', 'Ao', 'Yb', 'qkT', 'wT'))
        vb, kbg, kdec, u, vnew, tqs, tmp2 = (x[n] for n in ('vb', 'kbg', 'kdec', 'u', 'vnew', 'tqs', 'tmp2'))
        Ak = [x['A0'], x['A1']]
        self.cp('dve', bcol[0:64, 0:nch], gbc[0:64, 0:nch, ci], ['gbc'], [K_('bcol')])
        self.cp('dve', gcol[0:64, 0:nch], gbc[0:64, 0:nch, 8 + ci], ['gbc'], [K_('gcol')])
        yield
        self.mm(B0[0:64, 0:nch], m_cum, gcol[0:64, 0:nch], True, True, ['msk', K_('gcol')], [k0])
        self.mm(B1[:, 0:nch], self.ones[0:64, :], gcol[0:64, 0:nch], True, True, ['ones', K_('gcol')], [k1])
        yield
        self.cp('dve', gc[0:64, 0:nch], B0[0:64, 0:nch], [k0], [K_('gc')])
        self.act(eg[0:64, 0:nch], B0[0:64, 0:nch], AF.Exp, [k0], [K_('eg')])
        self.act(gtb[:, 0:nch], B1[:, 0:nch], AF.Exp, [k1], [K_('gtb')])
        yield
        self.tt('dve', edec[0:64, 0:nch], B1[0:64, 0:nch], gc[0:64, 0:nch], ALU.subtract, [k1, K_('gc')], [K_('edec')])
        self.tt('dve', beg[0:64, 0:nch], bcol[0:64, 0:nch], eg[0:64, 0:nch], ALU.mult, [K_('bcol'), K_('eg')], [K_('beg')])
        yield
        self.act(edec[0:64, 0:nch], edec[0:64, 0:nch], AF.Exp, [K_('edec')], [K_('edec')])
        if is_s:
            P.dma('sp', S, self.state_gdn[j, d, h], writes=[K_('S')])
        else:
            self.memset('pool', S, 0.0, [K_('S')])
        yield
        nbat = nch // NB
        border = range(nbat) if d == 0 else range(nbat - 1, -1, -1)
        blist = list(border)

        def load_kv(bi_):
            rr = slice(t0 + bi_ * NB * 64, t0 + (bi_ + 1) * NB * 64)
            P.dma(dq, ktb[0:64], self.kvn[rr, h * 128:(h + 1) * 128].rearrange("(c t) f -> t c f", t=64), writes=[K_('ktb')])
            P.dma(dq, vtb[0:64], self.kvn[rr, 512 + h * 128:512 + (h + 1) * 128].rearrange("(c t) f -> t c f", t=64), writes=[K_('vtb')])
        load_kv(blist[0])
        for bidx, bi in enumerate(blist):
            c0 = bi * NB
            cs = slice(c0, c0 + NB)
            self.tt('pool', Gm[0:64], bc_last(gcol[0:64, cs], 64), bc_mid(m_cum, NB), ALU.mult, [K_('gcol'), 'msk'], [K_('Gm')])
            yield
            for c in range(NB):
                self.mm(B0[0:64, c * 64:(c + 1) * 64], Gm[0:64, c, :], m_oth, True, True, [K_('Gm'), 'msk'], [k0])
            self.mm(B0[0:64, 256:512], m_oth, Gm[0:64].rearrange("p c i -> p (c i)"), True, True, [K_('Gm'), 'msk'], [k0])
            for c in range(NB):
                tk = slice((c0 + c) * 64, (c0 + c + 1) * 64)
                qa_, ka_ = qT[:, tk], kT[:, tk]
                qk_rhs = bass.AP(qa_.tensor, qa_.offset, [list(qa_.ap[0]), [ka_.offset - qa_.offset, 2], [1, 64]])
                self.mm(B1[0:64, c * 128:(c + 1) * 128], kT[:, tk], qk_rhs, True, True, [kkT, kqT], [k1])
            yield
            self.act(Em[0:64], v3(B0), AF.Exp, [k0], [K_('Em')])
            self.act(ETm[0:64], v3(B0, 256), AF.Exp, [k0], [K_('ETm')])
            yield
            self.tt('dve', Em[0:64], Em[0:64], bc_mid(m_strict, NB), ALU.mult, [K_('Em'), 'msk'], [K_('Em')])
            self.tt('pool', ETm[0:64], ETm[0:64], bc_mid(m_incl, NB), ALU.mult, [K_('ETm'), 'msk'], [K_('ETm')])
            yield
            a0 = Ak[0]
            gq = B1[0:64, :].rearrange("p (c i) -> p c i", c=NB)
            self.tt('dve', a0[0:64], gq[:, :, 64:128], Em[0:64], ALU.mult, [k1, K_('Em')], [K_('A0')])
            self.tt('dve', qkT[0:64], gq[:, :, 0:64], ETm[0:64], ALU.mult, [k1, K_('ETm')], [K_('qkT')])
            yield
            self.tt('pool', a0[0:64], a0[0:64], bc_last(bcol[0:64, cs], 64), ALU.mult, [K_('A0'), K_('bcol')], [K_('A0')])
            yield
            for c in range(NB):
                self.tr(B2[0:64, c * 64:(c + 1) * 64], a0[0:64, c, :], idb, [K_('A0'), 'ident'], [k2])
            yield
            BQ = [x['BQ0'], x['BQ1']]
            kBQ = [K_('BQ0'), K_('BQ1')]
            self.cp('act', BQ[0][0:64, :, 0:64], v3(B2), [k2], [kBQ[0]])
            self.tt('pool', Ao[0:64], a0[0:64], bc_mid(m_off, NB), ALU.mult, [K_('A0'), 'msk'], [K_('Ao')])
            yield
            self.tt('pool', a0[0:64], a0[0:64], bc_mid(m_bd, NB), ALU.mult, [K_('A0'), 'msk'], [K_('A0')])
            self.tt('pool', BQ[0][0:64, :, 0:64], BQ[0][0:64, :, 0:64], bc_mid(m_bd, NB), ALU.mult, [kBQ[0], 'msk'], [kBQ[0]])
            yield
            self.tt('pool', BQ[1][0:64, :, 64:128], bc_mid(idb, NB), BQ[0][0:64, :, 0:64], ALU.subtract, ['ident', kBQ[0]], [kBQ[1]])
            for c in range(NB):
                self.mm(B2[0:64, c * 64:(c + 1) * 64], a0[0:64, c, :], BQ[0][0:64, c, 0:64], True, True, [K_('A0'), kBQ[0]], [k2])
            for c in range(NB):
                self.mm(B3[0:64, c * 64:(c + 1) * 64], BQ[0][0:64, c, 0:64], a0[0:64, c, :], True, True, [K_('A0'), kBQ[0]], [k3])
            yield
            self.cp('act', BQ[1][0:64, :, 0:64], v3(B2), [k2], [kBQ[1]])
            self.cp('act', Ak[1][0:64], v3(B3), [k3], [K_('A1')])
            yield
            for kk in range(1, 4):
                cur, nxt = BQ[kk % 2], BQ[(kk + 1) % 2]
                kc_, kn2 = kBQ[kk % 2], kBQ[(kk + 1) % 2]
                ak, an = Ak[kk % 2], Ak[(kk + 1) % 2]
                ka, kan = K_(f'A{kk % 2}'), K_(f'A{(kk + 1) % 2}')
                for c in range(NB):
                    self.mm(B2[0:64, c * 128:(c + 1) * 128], ak[0:64, c, :], cur[0:64, c, :], True, True, [ka, kc_], [k2])
                for c in range(NB):
                    self.mm(B3[0:64, c * 64:(c + 1) * 64], cur[0:64, c, 0:64], ak[0:64, c, :], True, True, [ka, kc_], [k3])
                yield
                p4 = B2[0:64, :].rearrange("p (c i) -> p c i", c=NB)
                self.cp('act', nxt[0:64, :, 0:64], p4[:, :, 0:64], [k2], [kn2])
                self.tt('dve', nxt[0:64, :, 64:128], cur[0:64, :, 64:128], p4[:, :, 64:128], ALU.add, [kc_, k2], [kn2])
                self.cp('act', an[0:64], v3(B3), [k3], [kan])
                yield
            for c in range(NB):
                self.mm(B2[0:64, c * 64:(c + 1) * 64], Ak[0][0:64, c, :], BQ[0][0:64, c, 64:128], True, True, [K_('A0'), kBQ[0]], [k2])
            yield
            self.tt('dve', Q[0:64], BQ[0][0:64, :, 64:128], v3(B2), ALU.add, [kBQ[0], k2], [K_('Q')])
            yield
            for c in range(NB):
                self.tr(B3[0:64, c * 64:(c + 1) * 64], Q[0:64, c, :], idb, [K_('Q'), 'ident'], [k3])
            yield
            self.cp('act', Pm[0:64], v3(B3), [k3], [K_('Pm')])
            for c in range(NB):
                self.mm(B2[0:64, c * 64:(c + 1) * 64], Ao[0:64, c, :], Q[0:64, c, :], True, True, [K_('Ao'), K_('Q')], [k2])
            yield
            self.cp('act', Yb[0:64], v3(B2), [k2], [K_('Yb')])
            self.tt('pool', vb[0:64], vtb[0:64], bc_last(bcol[0:64, cs], 128), ALU.mult, [K_('vtb'), K_('bcol')], [K_('vb')])
            self.tt('pool', kbg[0:64], ktb[0:64], bc_last(beg[0:64, cs], 128), ALU.mult, [K_('ktb'), K_('beg')], [K_('kbg')])
            self.tt('pool', kdec[0:64], ktb[0:64], bc_last(edec[0:64, cs], 128), ALU.mult, [K_('ktb'), K_('edec')], [K_('kdec')])
            if bidx + 1 < len(blist):
                load_kv(blist[bidx + 1])
            yield
            for c in range(NB):
                self.mm(B3[0:64, c * 64:(c + 1) * 64], Pm[0:64, c, :], Yb[0:64, c, :], True, True, [K_('Pm'), K_('Yb')], [k3])
            yield
            self.tt('dve', Q[0:64], Q[0:64], v3(B3), ALU.subtract, [K_('Q'), k3], [K_('Q')])
            yield
            for c in range(NB):
                self.mm(B0[0:64, c * 128:(c + 1) * 128], Q[0:64, c, :], vb[0:64, c, :], True, True, [K_('Q'), K_('vb')], [k0])
            for c in range(NB):
                self.mm(B1[:, c * 64:(c + 1) * 64], kbg[0:64, c, :], Q[0:64, c, :], True, True, [K_('Q'), K_('kbg')], [k1])
            yield
            self.cp('act', u[0:64], B0[0:64, :].rearrange("p (c i) -> p c i", c=NB), [k0], [K_('u')])
            self.cp('act', wT, B1[:, 0:NB * 64].rearrange("p (c i) -> p c i", c=NB), [k1], [K_('wT')])
            yield
            corder = range(NB) if d == 0 else range(NB - 1, -1, -1)
            for c in corder:
                ch = c0 + c
                tk = slice(ch * 64, (ch + 1) * 64)
                self.mm(B2[0:64, 0:128], wT[:, c, :], S, True, True, [K_('wT'), K_('S')], [k2])
                self.mm(B3[0:64, 0:128], qT[:, tk], S, True, True, [kqT, K_('S')], [k3])
                yield
                self.tt('dve', vnew[0:64], u[0:64, c, :], B2[0:64, 0:128], ALU.subtract, [K_('u'), k2], [K_('vnew')])
                self.act(tqs[0:64], B3[0:64, 0:128], AF.Copy, [k3, K_('eg')], [K_('tqs')], scale=eg[0:64, ch:ch + 1])
                yield
                self.mm(B2[0:64, 128:256], qkT[0:64, c, :], vnew[0:64], True, True, [K_('qkT'), K_('vnew')], [k2])
                self.mm(B3[:, 128:256], kdec[0:64, c, :], vnew[0:64], True, True, [K_('kdec'), K_('vnew')], [k3])
                yield
                self.tt('dve', tmp2[0:64], tqs[0:64], B2[0:64, 128:256], ALU.add, [K_('tqs'), k2], [K_('tmp2')])
                self.stt('dve', S, S, gtb[:, ch:ch + 1], B3[:, 128:256], ALU.mult, ALU.add, [K_('S'), K_('gtb'), k3], [K_('S')])
                P.dma(dq, og[d][t0 + ch * 64:t0 + (ch + 1) * 64, h * 128:(h + 1) * 128], tmp2[0:64], reads=[K_('tmp2')])
        if not is_s:
            P.dma('sp', self.o_gdn[si, j, d, h], S, reads=[K_('S')])

    for si, (t0, T, is_s) in enumerate(SEQS):
        nch = T // 64
        for c8 in range(0, nch, 4):
            rr = slice(t0 + c8 * 64, t0 + (c8 + 4) * 64)
            P.dma('sp', gbc[0:64, c8:c8 + 4, :], self.gb[rr, :].rearrange("(c t) f -> t c f", t=64), writes=['gbc'])
        for pair in ((0, 1), (2, 3)):
            gens = []
            for hs, h in enumerate(pair):
                P.dma('sp', qTs[hs][:, 0:T], self.qkn[h * 128:(h + 1) * 128, t0:t0 + T], writes=[f'qT{hs}'])
                P.dma('act', kTs[hs][:, 0:T], self.qkn[512 + h * 128:512 + (h + 1) * 128, t0:t0 + T], writes=[f'kT{hs}'])
                for d in range(2):
                    gens.append(chain(hs * 2 + d, hs, h, d, si, t0, T, is_s))
            while gens:
                for g_ in list(gens):
                    try:
                        next(g_)
                    except StopIteration:
                        gens.remove(g_)
    P.barrier()
    A.reset(self.const_end)
    gnw = A.alloc([128])
    self.load_bcast(gnw, self.gdn_norm[j, :], 'gnw')
    oa = [A.alloc([512]) for _ in range(2)]
    ob_ = [A.alloc([512]) for _ in range(2)]
    zz = [A.alloc([512]) for _ in range(2)]
    sq = A.alloc([512])
    ss4 = A.alloc([8])
    for t in range(36):
        s = t % 2
        rr = slice(t * 128, (t + 1) * 128)
        P.dma('sp', oa[s], self.og0[rr, :], writes=[f'oa{s}'])
        P.dma('act', ob_[s], self.og1[rr, :], writes=[f'ob{s}'])
        P.dma('sp', zz[s], self.projN[rr, 1536:2048], writes=[f'zz{s}'])
        self.tt('dve', oa[s], oa[s], ob_[s], ALU.add, [f'oa{s}', f'ob{s}'], [f'oa{s}'])
        self.act(zz[s], zz[s], AF.Silu, [f'zz{s}'], [f'zz{s}'])
        self.tt('pool', sq, oa[s], oa[s], ALU.mult, [f'oa{s}'], ['sq'])
        r4 = ss4[:, 4 * s:4 * s + 4]
        self.P.op('dve', lambda e, o=r4, i=sq.rearrange("p (h d) -> p h d", h=4): e.tensor_reduce(out=o, in_=i, axis=AX.X, op=ALU.add),
                  ['sq'], [f'ss4{s}'])
        self.act(r4, r4, AF.Sqrt, [f'ss4{s}'], [f'ss4{s}'], bias=EPS, scale=1.0 / 128)
        self.recip(r4, r4, [f'ss4{s}'], [f'ss4{s}'])
        o3 = oa[s].rearrange("p (h d) -> p h d", h=4)
        self.tt('dve', o3, o3, bc_last(r4, 128), ALU.mult, [f'oa{s}', f'ss4{s}'], [f'oa{s}'])
        self.tt('pool', o3, o3, bc_mid(gnw, 4), ALU.mult, [f'oa{s}', 'gnw'], [f'oa{s}'])
        self.tt('dve', ob_[s], oa[s], zz[s], ALU.mult, [f'oa{s}', f'zz{s}'], [f'ob{s}'])
        P.dma('sp', self.mixN[rr, :], ob_[s], reads=[f'ob{s}'])


def phase_even_mix(self, l):
    j = l // 2
    self.phase_begin()
    _gdn(self, l)
    if self.upto in ('G1', 'G2', 'G3'):
        return
    self.P.barrier()
    self.A.reset(self.const_end)
    _attn_dense(self, l, 2064, 2576, 2832, 4, 2, (self.cgk, self.cgv), (self.gqa_q_norm[j], self.gqa_k_norm[j]), True,
                self.o_gk, self.o_gv)


K.phase_even_mix = phase_even_mix


_CACHE = {}


def kernel(**inputs):
    if 'nc' not in _CACHE:
        kb = K(n_layers=4)
        _CACHE['nc'] = kb.build()
    nc = _CACHE['nc']
    in_maps = make_in_maps(inputs)
    res = run_bass_kernel_spmd(nc, in_maps, core_ids=list(range(8)))
    rs = res.results
    y = np.stack([r["yout"] for r in rs], 0)
    y_prompt = np.ascontiguousarray(y[:, :512].reshape(16, 256, 1024))
    y_sample = np.ascontiguousarray(y[:, 512:])
    cat = lambda n: np.concatenate([r[n] for r in rs], 0)
    return (y_prompt, y_sample, cat("o_gdn"), cat("o_gk"), cat("o_gv"), cat("o_ssd"), cat("o_nk"), cat("o_nv"))


def _ssd(self, l):
    j = l // 2
    A, P = self.A, self.P
    B = self.banks
    W = 768
    wb = [A.alloc([W]) for _ in range(3)]
    for k in range(3):
        self.load_bcast(wb[k], self.ssd_conv[j, k, 0:W], f'wb{k}')
    cbb = A.alloc([W])
    self.load_bcast(cbb, self.ssd_conv_b[j, 0:W], 'cbb')
    xm = [A.alloc([W]) for _ in range(2)]
    x0 = [A.alloc([W]) for _ in range(2)]
    xp = [A.alloc([W]) for _ in range(2)]
    nt = 0
    src = self.projN
    for (t0, T, is_s) in SEQS:
        for a in range(T // 128):
            s = nt % 2
            nt += 1
            r0 = t0 + a * 128
            first, last = (a == 0), (a == T // 128 - 1)
            P.dma('sp', x0[s], src[r0:r0 + 128, 512:1280], writes=[f'x0{s}'])
            if first:
                self.memset('pool', xm[s], 0.0, [f'xm{s}'])
                P.dma('act', xm[s][1:128, :], src[r0:r0 + 127, 512:1280], writes=[f'xm{s}'])
            else:
                P.dma('act', xm[s], src[r0 - 1:r0 + 127, 512:1280], writes=[f'xm{s}'])
            if last:
                self.memset('pool', xp[s], 0.0, [f'xp{s}'])
                P.dma('act', xp[s][0:127, :], src[r0 + 1:r0 + 128, 512:1280], writes=[f'xp{s}'])
            else:
                P.dma('act', xp[s], src[r0 + 1:r0 + 129, 512:1280], writes=[f'xp{s}'])
            self.tt('dve', x0[s], x0[s], wb[1], ALU.mult, [f'x0{s}', 'wb1'], [f'x0{s}'])
            self.tt('pool', xm[s], xm[s], wb[0], ALU.mult, [f'xm{s}', 'wb0'], [f'xm{s}'])
            self.tt('pool', xp[s], xp[s], wb[2], ALU.mult, [f'xp{s}', 'wb2'], [f'xp{s}'])
            self.tt('dve', x0[s], x0[s], xm[s], ALU.add, [f'x0{s}', f'xm{s}'], [f'x0{s}'])
            self.tt('dve', x0[s], x0[s], xp[s], ALU.add, [f'x0{s}', f'xp{s}'], [f'x0{s}'])
            self.tt('dve', x0[s], x0[s], cbb, ALU.add, [f'x0{s}', 'cbb'], [f'x0{s}'])
            self.act(x0[s], x0[s], AF.Silu, [f'x0{s}'], [f'x0{s}'])
            P.dma('sp', self.kvn[r0:r0 + 128, 0:W], x0[s], reads=[f'x0{s}'])
    raw = A.alloc([36, 16])
    for a in range(36):
        P.dma('sp', raw[:, a, :], self.projN[a * 128:(a + 1) * 128, 1536:1552], writes=['raw'])
    dtb = A.alloc([16])
    nea = A.alloc([16])
    self.load_bcast(dtb, self.ssd_dt_bias[j, :], 'dtb')
    self.load_bcast(nea, self.ssd_a_log[j, :], 'nea')
    self.act(nea, nea, AF.Exp, ['nea'], ['nea'])
    sbt = A.alloc([36, 32])
    self.tt('dve', raw, raw, bc_mid(dtb, 36), ALU.add, ['raw', 'dtb'], ['raw'])
    self.act(raw, raw, AF.Exp, ['raw'], ['raw'])
    self.act(sbt[:, :, 0:16], raw, AF.Ln, ['raw'], ['sbt'], bias=1.0)
    self.tt('dve', raw, sbt[:, :, 0:16], bc_mid(nea, 36), ALU.mult, ['sbt', 'nea'], ['raw'])
    self.ts('dve', sbt[:, :, 16:32], raw, -1.0, None, ALU.mult, None, ['raw'], ['sbt'])
    for a in range(36):
        P.dma('sp', self.sb[a * 128:(a + 1) * 128, :], sbt[:, a, :], reads=['sbt'])
    P.barrier()
    A.reset(self.const_end)
    xf = A.alloc([NT])
    acc = A.alloc([NT])
    cw = A.alloc([4])
    for c in range(4):
        ch0 = 512 + c * 128
        P.dma('sp', xf, self.projT[1024 + c * 128:1024 + (c + 1) * 128, :], writes=['xf'])
        P.dma('sp', cw[:, 0:3], self.ssd_conv[j, :, ch0:ch0 + 128].rearrange("k p -> p k"), writes=['cw'], allow_slow_non_contiguous=True)
        P.dma('sp', cw[:, 3:4], self.ssd_conv_b[j, ch0:ch0 + 128].rearrange("(p o) -> p o", o=1), writes=['cw'])
        self.ts('dve', acc, xf, cw[:, 1:2], cw[:, 3:4], ALU.mult, ALU.add, ['xf', 'cw'], ['acc'])
        for (t0, T, is_s) in SEQS:
            self.stt('dve', acc[:, t0 + 1:t0 + T], xf[:, t0:t0 + T - 1], cw[:, 0:1], acc[:, t0 + 1:t0 + T], ALU.mult, ALU.add,
                     ['xf', 'cw', 'acc'], ['acc'])
            self.stt('dve', acc[:, t0:t0 + T - 1], xf[:, t0 + 1:t0 + T], cw[:, 2:3], acc[:, t0:t0 + T - 1], ALU.mult, ALU.add,
                     ['xf', 'cw', 'acc'], ['acc'])
        self.act(acc, acc, AF.Silu, ['acc'], ['acc'])
        P.dma('sp', self.qkn[c * 128:(c + 1) * 128, :], acc, reads=['acc'])
    P.barrier()
    A.reset(self.const_end)
    msk = A.alloc([4, 64])
    P.dma('sp', msk[0:64], self.c_masks[0:4].rearrange("m t i -> t m i"), writes=['msk'])
    snw = A.alloc([512])
    self.load_bcast(snw, self.ssd_norm[j, :], 'snw')
    dsk = A.alloc([8])
    self.load_bcast(dsk, self.ssd_d[j, :], 'dsk')
    bcT = A.alloc([2, 4096])
    ccT = A.alloc([2, 4096])
    sca = A.alloc([64, 32])
    id64 = self.ident[0:64, 0:64]
    X = []
    for d in range(2):
        x = {}
        for nm in ('lad', 'dtd', 'cs', 'ecs', 'edec', 'cdec'):
            x[nm] = A.alloc([64, 8])
        for nm in ('hT', 'Gm', 'LT', 'MT', 'xdt', 'xd2', 'yo', 'yy'):
            x[nm] = A.alloc([8, 64])
        x['xb'] = [A.alloc([768]) for _ in range(2)]
        x['h0'] = A.alloc([8, 128])
        X.append(x)
    ydst = [self.yf, self.yb]

    def chain(d, si, t0, T, is_s):
        x = X[d]
        K_ = lambda n: f'{n}_{d}'
        nch = T // 64
        B0, B1, B2, B3 = self.banks[4 * d:4 * d + 4]
        k0, k1, k2, k3 = [f'bk{4 * d + i}' for i in range(4)]
        m_cum, m_oth, m_incl = (msk[0:64, 0], msk[0:64, 1], msk[0:64, 0]) if d == 0 else (msk[0:64, 2], msk[0:64, 3], msk[0:64, 2])
        lad, dtd, cs, ecs, edec, cdec, hT = (x[n] for n in ('lad', 'dtd', 'cs', 'ecs', 'edec', 'cdec', 'hT'))
        Gm, LT, MT, xdt, xd2, yo, yy, h0 = (x[n] for n in ('Gm', 'LT', 'MT', 'xdt', 'xd2', 'yo', 'yy', 'h0'))
        self.cp('dve', dtd[0:64, 0:nch, :], sca[0:64, 0:nch, d * 8:d * 8 + 8], ['sca'], [K_('dtd')])
        self.cp('dve', lad[0:64, 0:nch, :], sca[0:64, 0:nch, 16 + d * 8:16 + d * 8 + 8], ['sca'], [K_('lad')])
        yield
        n8 = nch * 8
        fl = lambda a_, p_=64: a_[0:p_, 0:nch, :].rearrange("p c h -> p (c h)")
        self.mm(B0[0:64, 0:n8], m_cum, fl(lad), True, True, ['msk', K_('lad')], [k0])
        self.mm(B1[:, 0:n8], self.ones[0:64, :], fl(lad), True, True, ['ones', K_('lad')], [k1])
        yield
        self.cp('dve', fl(cs), B0[0:64, 0:n8], [k0], [K_('cs')])
        self.act(fl(ecs), B0[0:64, 0:n8], AF.Exp, [k0], [K_('ecs')])
        self.act(fl(cdec, 128), B1[:, 0:n8], AF.Exp, [k1], [K_('cdec')])
        yield
        self.tt('dve', fl(edec), B1[0:64, 0:n8], fl(cs), ALU.subtract, [k1, K_('cs')], [K_('edec')])
        yield
        self.act(fl(edec), fl(edec), AF.Exp, [K_('edec')], [K_('edec')])
        if is_s:
            for hh in range(8):
                P.dma('sp' if d == 0 else 'act', h0[0:64, hh, :], self.state_ssd[j, d, hh], writes=[K_('h0')])
            yield
            for hh in range(8):
                self.tr(B2[:, hh * 64:(hh + 1) * 64], h0[0:64, hh, :], id64, [K_('h0'), 'ident'], [k2])
            yield
            self.cp('dve', hT, B2.rearrange("p (h q) -> p h q", h=8), [k2], [K_('hT')])
        else:
            self.memset('pool', hT, 0.0, [K_('hT')])
        yield
        corder = range(nch) if d == 0 else range(nch - 1, -1, -1)
        nx = 0
        for c in corder:
            r0 = t0 + c * 64
            tk = slice(c * 64, (c + 1) * 64)
            s = nx % 2
            nx += 1
            xk = K_(f'xb{s}')
            xbs = x['xb'][s]
            P.dma('sp' if d == 0 else 'act', xbs[0:64, :], self.kvn[r0:r0 + 64, 0:768], writes=[xk])
            self.tt('dve', Gm[0:64], bc_last(lad[0:64, c, :], 64), bc_mid(m_cum, 8), ALU.mult, [K_('lad'), 'msk'], [K_('Gm')])
            yield
            x3 = xbs[0:64, 0:512].rearrange("p (h q) -> p h q", h=8)
            self.mm(B0[0:64, :], m_oth, Gm[0:64].rearrange("p h i -> p (h i)"), True, True, ['msk', K_('Gm')], [k0])
            for g in range(2):
                self.mm(B1[0:64, g * 64:(g + 1) * 64], bcT[:, g, tk], ccT[:, g, tk], True, True, ['bcT', 'ccT'], [k1])
            for g in range(2):
                self.mm(B2[0:64, g * 256:(g + 1) * 256], ccT[:, g, tk], hT[:, g * 4:(g + 1) * 4, :].rearrange("p h q -> p (h q)"),
                        True, True, ['ccT', K_('hT')], [k2])
            self.tt('pool', xdt[0:64], x3, bc_last(dtd[0:64, c, :], 64), ALU.mult, [xk, K_('dtd')], [K_('xdt')])
            yield
            self.act(LT[0:64], B0[0:64, :].rearrange("p (h i) -> p h i", h=8), AF.Exp, [k0], [K_('LT')])
            self.tt('dve', yo[0:64], B2[0:64, :].rearrange("p (h q) -> p h q", h=8), bc_last(ecs[0:64, c, :], 64), ALU.mult,
                    [k2, K_('ecs')], [K_('yo')])
            self.tt('pool', xd2[0:64], xdt[0:64], bc_last(edec[0:64, c, :], 64), ALU.mult, [K_('xdt'), K_('edec')], [K_('xd2')])
            yield
            self.tt('pool', LT[0:64], LT[0:64], bc_mid(m_incl, 8), ALU.mult, [K_('LT'), 'msk'], [K_('LT')])
            for g in range(2):
                self.mm(B3[:, g * 256:(g + 1) * 256], xbs[0:64, 512 + g * 128:512 + (g + 1) * 128],
                        xd2[0:64, g * 4:(g + 1) * 4, :].rearrange("p h q -> p (h q)"), True, True, [xk, K_('xd2')], [k3])
            yield
            for g in range(2):
                self.tt('dve', MT[0:64, g * 4:(g + 1) * 4, :], LT[0:64, g * 4:(g + 1) * 4, :], bc_mid(B1[0:64, g * 64:(g + 1) * 64], 4),
                        ALU.mult, [K_('LT'), k1], [K_('MT')])
            self.tt('pool', hT, hT, bc_last(cdec[:, c, :], 64), ALU.mult, [K_('hT'), K_('cdec')], [K_('hT')])
            yield
            for hh in range(8):
                self.mm(B2[0:64, hh * 64:(hh + 1) * 64], MT[0:64, hh, :], xdt[0:64, hh, :], True, True, [K_('MT'), K_('xdt')], [k2])
            self.tt('dve', hT, hT, B3.rearrange("p (h q) -> p h q", h=8), ALU.add, [K_('hT'), k3], [K_('hT')])
            yield
            self.tt('dve', yy[0:64], yo[0:64], B2[0:64, :].rearrange("p (h q) -> p h q", h=8), ALU.add, [K_('yo'), k2], [K_('yy')])
            yield
            P.dma('sp' if d == 0 else 'act', ydst[d][r0:r0 + 64, :], yy[0:64].rearrange("p h q -> p (h q)"), reads=[K_('yy')],
                  writes=[f'ydram{d}'])
        if not is_s:
            hs = h0
            for hh in range(8):
                self.tr(B0[0:64, (hh % 4) * 128:(hh % 4 + 1) * 128] if hh < 4 else B1[0:64, (hh % 4) * 128:(hh % 4 + 1) * 128],
                        hT[:, hh, :], self.ident, [K_('hT'), 'ident'], [k0 if hh < 4 else k1])
            yield
            self.cp('dve', hs[0:64, 0:4, :], B0[0:64, :].rearrange("p (h n) -> p h n", h=4), [k0], [K_('h0')])
            self.cp('act', hs[0:64, 4:8, :], B1[0:64, :].rearrange("p (h n) -> p h n", h=4), [k1], [K_('h0')])
            yield
            P.dma('sp', self.o_ssd[si, j, d].rearrange("h p n -> p h n"), hs[0:64], reads=[K_('h0')])

    for si, (t0, T, is_s) in enumerate(SEQS):
        nch = T // 64
        for g in range(2):
            P.dma('sp', bcT[:, g, 0:T], self.qkn[g * 128:(g + 1) * 128, t0:t0 + T], writes=['bcT'])
            P.dma('act', ccT[:, g, 0:T], self.qkn[256 + g * 128:256 + (g + 1) * 128, t0:t0 + T], writes=['ccT'])
        for c4 in range(0, nch, 4):
            P.dma('sp', sca[0:64, c4:c4 + 4, :], self.sb[t0 + c4 * 64:t0 + (c4 + 4) * 64, :].rearrange("(c t) f -> t c f", t=64), writes=['sca'])
        gens = [chain(0, si, t0, T, is_s), chain(1, si, t0, T, is_s)]
        while gens:
            for g_ in list(gens):
                try:
                    next(g_)
                except StopIteration:
                    gens.remove(g_)
    P.barrier()
    A.reset(self.const_end)
    snw = A.alloc([512])
    self.load_bcast(snw, self.ssd_norm[j, :], 'snw')
    dsk = A.alloc([8])
    self.load_bcast(dsk, self.ssd_d[j, :], 'dsk')
    ya = [A.alloc([512]) for _ in range(2)]
    yb_ = [A.alloc([512]) for _ in range(2)]
    xx = [A.alloc([512]) for _ in range(2)]
    zz = [A.alloc([512]) for _ in range(2)]
    sq = A.alloc([512])
    st = A.alloc([4])
    for t in range(36):
        s = t % 2
        rr = slice(t * 128, (t + 1) * 128)
        P.dma('sp', ya[s], self.yf[rr, :], writes=[f'ya{s}'])
        P.dma('act', yb_[s], self.yb[rr, :], writes=[f'yb{s}'])
        P.dma('sp', xx[s], self.kvn[rr, 0:512], writes=[f'xx{s}'])
        P.dma('act', zz[s], self.projN[rr, 0:512], writes=[f'zz{s}'])
        self.tt('dve', ya[s], ya[s], yb_[s], ALU.add, [f'ya{s}', f'yb{s}'], [f'ya{s}'])
        x3 = xx[s].rearrange("p (h q) -> p h q", h=8)
        self.tt('pool', x3, x3, bc_last(dsk, 64), ALU.mult, [f'xx{s}', 'dsk'], [f'xx{s}'])
        self.act(zz[s], zz[s], AF.Silu, [f'zz{s}'], [f'zz{s}'])
        self.tt('dve', ya[s], ya[s], xx[s], ALU.add, [f'ya{s}', f'xx{s}'], [f'ya{s}'])
        self.tt('dve', ya[s], ya[s], zz[s], ALU.mult, [f'ya{s}', f'zz{s}'], [f'ya{s}'])
        self.act(sq, ya[s], AF.Square, [f'ya{s}'], ['sq', f'ssq{s}'], accum_out=st[:, s:s + 1])
        self.act(st[:, 2 + s:3 + s], st[:, s:s + 1], AF.Sqrt, [f'ssq{s}'], [f'rsq{s}'], bias=EPS, scale=1.0 / 512)
        self.recip(st[:, 2 + s:3 + s], st[:, 2 + s:3 + s], [f'rsq{s}'], [f'rsq{s}'])
        self.stt('dve', yb_[s], ya[s], st[:, 2 + s:3 + s], snw, ALU.mult, ALU.mult, [f'ya{s}', f'rsq{s}', 'snw'], [f'yb{s}'])
        P.dma('sp', self.mixN[rr, :], yb_[s], reads=[f'yb{s}'])


def _na_sample(self, l):
    j = l // 2
    A, P = self.A, self.P
    B = self.banks
    t0 = 512
    scale = 128 ** -0.5
    BIG = 30000.0
    kdT = A.alloc([4, 4096], BF16)
    qdT = A.alloc([4, 4096], BF16)
    Ve = A.alloc([32, 512], BF16)
    Vo = A.alloc([31, 512], BF16)
    kcT = A.alloc([4, 512], BF16)
    Vc = A.alloc([4, 512], BF16)
    Bt = A.alloc([60, 64])
    nm = A.alloc([64])
    ngm = A.alloc([64])
    Z = A.alloc([160])
    ctk = A.alloc([4, 128])
    for h in range(4):
        P.dma('pool', kdT[:, h, :], self.projT[2064 + h * 128:2064 + (h + 1) * 128, t0:t0 + 4096], writes=['kdT'])
        P.dma('pool', qdT[:, h, :], self.projT[1552 + h * 128:1552 + (h + 1) * 128, t0:t0 + 4096], writes=['qdT'])
    for a4 in range(0, 32, 4):
        P.dma('pool', Ve[:, a4:a4 + 4, :], self.projN[t0 + a4 * 128:t0 + (a4 + 4) * 128, 2576:3088].rearrange("(a p) f -> p a f", p=128), writes=['Ve'])
    for a4 in range(0, 31, 4):
        n = min(4, 31 - a4)
        P.dma('pool', Vo[:, a4:a4 + n, :], self.projN[t0 + 64 + a4 * 128:t0 + 64 + (a4 + n) * 128, 2576:3088].rearrange("(a p) f -> p a f", p=128), writes=['Vo'])
    P.dma('pool', Vc, self.cnv[j].rearrange("(a p) h d -> p a (h d)", p=128), writes=['Vc'])
    for h in range(4):
        P.dma('sp', ctk, self.cnk[j, :, h, :].rearrange("(a p) d -> p a d", p=128), writes=['ctk'])
        for a in range(4):
            self.tr(B[0][:, a * 128:(a + 1) * 128], ctk[:, a, :], self.ident, ['ctk', 'ident'], ['b0'])
        self.cp('act', kcT[:, h, :], B[0], ['b0'], ['kcT'])
    self.memset('pool', Z[0:60, :], 0.0, ['Z'])
    P.dma('sp', Z[0:60, 64:95], self.na_rpb[j], writes=['Z'])
    tz = P.dma('sp', self.rpbpad, Z[0:60, :], reads=['Z'], writes=['rpbpad'])
    for q in range(64):
        src = bass.AP(self.rpbpad.tensor, 79 - q, [[0, 1], [160, 60], [1, 64]])
        P.dma('sp' if q % 2 else 'act', Bt[q:q + 1, :, :], src, reads=['rpbpad'], writes=['Bt'])
    P.dma('sp', nm[0:64, :], self.c_namask, writes=['nm'])
    self.ts('dve', ngm[0:64, :], nm[0:64, :], -1.0, BIG, ALU.add, ALU.mult, ['nm'], ['ngm'])
    self.tt('dve', Bt[0:64], Bt[0:64], bc_mid(nm[0:64, :], 60), ALU.mult, ['Bt', 'nm'], ['Bt'])
    self.tt('dve', Bt[0:64], Bt[0:64], bc_mid(ngm[0:64, :], 60), ALU.add, ['Bt', 'ngm'], ['Bt'])
    idb64 = self.identb[0:64, 0:64]
    XN = []
    for ch in range(4):
        XN.append(dict(ssb=A.alloc([1024]), pb=A.alloc([1024], BF16), pT=[A.alloc([512], BF16) for _ in range(2)],
                       st=A.alloc([4]), osb=A.alloc([256])))

    def chain(cn):
        x = XN[cn]
        K_ = lambda n: f'{n}_{cn}'
        BA, BB = B[2 * cn], B[2 * cn + 1]
        kA, kB = f'bk{2 * cn}', f'bk{2 * cn + 1}'
        ptb = BB.bitcast(BF16)[:, 512:1024]
        ssb, pb, st, osb = x['ssb'], x['pb'], x['st'], x['osb']
        pf = ssb
        h = cn
        dq = 'sp' if cn % 2 else 'act'
        npt = 0
        for r in range(64):
            r0 = min(max(r - 4, 0), 56)
            dr0 = r0 - r + 7
            Vx, ti0, vk = (Ve, r0 // 2, 'Ve') if r0 % 2 == 0 else (Vo, (r0 - 1) // 2, 'Vo')
            qs = qdT[:, h, r * 64:(r + 1) * 64]
            self.mm(BA[0:64, :], qs, kdT[:, h, r0 * 64:r0 * 64 + 512], True, True, ['qdT', 'kdT'], [kA])
            yield
            self.stt('dve', ssb[0:64, 0:512].rearrange("p (a k) -> p a k", a=8), BA[0:64, :].rearrange("p (a k) -> p a k", a=8), scale,
                     Bt[0:64, h * 15 + dr0:h * 15 + dr0 + 8, :], ALU.mult, ALU.add, [kA, 'Bt'], [K_('ssb')])
            yield
            self.mm(BA[0:64, :], qs, kcT[:, h, :], True, True, ['qdT', 'kcT'], [kA])
            yield
            self.act(ssb[0:64, 512:1024], BA[0:64, :], AF.Copy, [kA], [K_('ssb')], scale=scale)
            yield
            self.P.op('dve', lambda e, o=st[0:64, 0:1], i=ssb[0:64, :]: e.tensor_reduce(out=o, in_=i, axis=AX.X, op=ALU.max),
                      [K_('ssb')], [K_('mx')])
            yield
            self.ts('dve', st[0:64, 1:2], st[0:64, 0:1], -1.0, None, ALU.mult, None, [K_('mx')], [K_('nmx')])
            yield
            self.act(pf[0:64, :], ssb[0:64, :], AF.Exp, [K_('ssb'), K_('nmx')], [K_('ssb'), K_('sm')], bias=st[0:64, 1:2], accum_out=st[0:64, 2:3])
            yield
            self.recip(st[0:64, 3:4], st[0:64, 2:3], [K_('sm')], [K_('rsm')])
            yield
            self.ts('dve', pb[0:64, :], pf[0:64, :], st[0:64, 3:4], None, ALU.mult, None, [K_('ssb'), K_('rsm')], [K_('pb')])
            yield
            for kc in range(8):
                self.tr(ptb[:, kc * 64:(kc + 1) * 64], pb[0:64, kc * 128:(kc + 1) * 128], idb64, [K_('pb'), 'identb'], [kB])
            yield
            pt_ = x['pT'][npt % 2]
            pk = K_(f'pT{npt % 2}')
            npt += 1
            self.cp('act', pt_, ptb, [kB], [pk])
            yield
            oc = (r % 4) * 64
            for jj in range(8):
                lhs = Vx[:, ti0 + jj, h * 128:(h + 1) * 128] if jj < 4 else Vc[:, jj - 4, h * 128:(h + 1) * 128]
                self.mm(BB[:, oc:oc + 64], lhs, pt_[:, jj * 64:(jj + 1) * 64], jj == 0, jj == 7, [vk, 'Vc', pk], [kB])
            yield
            if r % 4 == 3:
                self.cp('dve' if cn % 2 else 'act', osb, BB[:, 0:256], [kB], [K_('osb')])
                yield
                P.dma(dq, self.mixT[h * 128:(h + 1) * 128, t0 + (r - 3) * 64:t0 + (r + 1) * 64], osb, reads=[K_('osb')])

    gens = [chain(c_) for c_ in range(4)]
    while gens:
        for g_ in list(gens):
            try:
                next(g_)
            except StopIteration:
                gens.remove(g_)


def phase_odd_mix(self, l):
    j = l // 2
    self.phase_begin()
    _ssd(self, l)
    if self.upto == 'S1':
        return
    self.P.barrier()
    self.A.reset(self.const_end)
    _attn_dense(self, l, 1552, 2064, 2576, 4, 4, None, None, False, self.o_nk, self.o_nv, only_prompts=True)
    self.P.barrier()
    self.A.reset(self.const_end)
    _na_sample(self, l)


K.phase_odd_mix = phase_odd_mix
```
